# Optimizing a Trainium2 kernel written in Bass

```python
import jax, jax.numpy as jnp
from jax import lax
import numpy as np

D_MODEL = 1024
BATCH = 2
SEQ = 8192
DEPTH = 2

GRID_W = 64
CTX_LEN = 256
F32 = jnp.float32
ROPE_BASE = 10000.0
ATTN_BLOCK = 128
LN_EPS = 1e-5

GLA_HEADS = 4
GLA_DK = 64
GLA_DV = 128
GLA_QK = GLA_HEADS * GLA_DK
GLA_VW = GLA_HEADS * GLA_DV
GLA_GATE_RANK = 16
GLA_TAU = 16.0
GLA_CHUNK = 64

GQA_HQ = 8
GQA_HKV = 2
GQA_D = 64
WINDOW = 128

RWKV_HEADS = 8
RWKV_N = 64
RWKV_W = RWKV_HEADS * RWKV_N
RWKV_DECAY_RANK = 64
RWKV_AAA_RANK = 64
RWKV_GATE_RANK = 128
RWKV_LNX_EPS = 6.4e-4

MLA_HEADS = 8
MLA_Q_RANK = 256
MLA_KV_RANK = 128
MLA_NOPE = 64
MLA_ROPE = 32
MLA_V = 64
MLA_EPS = 1e-6

MLP_HIDDEN = 4 * D_MODEL

ALPHA = (2 * DEPTH) ** 0.25
BETA = (8 * DEPTH) ** -0.25

AB_SIZES = (GLA_QK, GLA_QK, GLA_VW, GLA_GATE_RANK, GLA_GATE_RANK, GLA_VW, GQA_HQ * GQA_D, GQA_HKV * GQA_D, GQA_HKV * GQA_D)
AB_IN = sum(AB_SIZES)
AB_MIX = GLA_VW + GQA_HQ * GQA_D
RWKV_SIZES = (RWKV_W, RWKV_W, RWKV_W, RWKV_DECAY_RANK, RWKV_DECAY_RANK, RWKV_AAA_RANK, RWKV_GATE_RANK)
RWKV_IN = sum(RWKV_SIZES)
MLA_SIZES = (MLA_Q_RANK, MLA_KV_RANK, MLA_ROPE)
CD_IN = RWKV_IN + sum(MLA_SIZES)
CD_MIX = RWKV_W + MLA_HEADS * MLA_V

kernel_name = 'hybrid_gla_swa_rwkv7_mla_dit'


def layer_norm(x, g, b):
    xf = x.astype(F32)
    mu = jnp.mean(xf, -1, keepdims=True)
    var = jnp.mean(jnp.square(xf - mu), -1, keepdims=True)
    return ((xf - mu) * lax.rsqrt(var + LN_EPS)).astype(x.dtype) * g + b


def rms_norm(x, g, eps):
    xf = x.astype(F32)
    return (xf * lax.rsqrt(jnp.mean(xf * xf, -1, keepdims=True) + eps)).astype(x.dtype) * g


def split_cols(p, sizes):
    return jnp.split(p, [int(s) for s in np.cumsum(sizes)[:-1]], axis=-1)


def axial_rope(x, row, col):
    half = x.shape[-1] // 2

    def rotate(xh, pos):
        nf = xh.shape[-1] // 2
        inv_freq = ROPE_BASE ** (-jnp.arange(nf, dtype=F32) / nf)
        ang = pos.astype(F32)[:, None] * inv_freq[None, :]
        cos = jnp.cos(ang)[None, :, None, :].astype(xh.dtype)
        sin = jnp.sin(ang)[None, :, None, :].astype(xh.dtype)
        x1, x2 = xh[..., :nf], xh[..., nf:]
        return jnp.concatenate([x1 * cos - x2 * sin, x1 * sin + x2 * cos], -1)

    return jnp.concatenate([rotate(x[..., :half], row), rotate(x[..., half:], col)], -1)


def stack_dirs(t_f, t_b, axis):
    return jnp.stack([t_f, jnp.flip(t_b, axis=axis)], axis=0)


def merge_dirs(t, axis):
    return t[0] + jnp.flip(t[1], axis=axis)


def mlp(u, w1, w2):
    return jnp.square(jax.nn.relu(u @ w1)) @ w2


def gla_chunked(q, k, v, logd, s0):
    z, b, h, n, _ = q.shape
    nc = n // GLA_CHUNK

    def to_chunks(t):
        return jnp.moveaxis(t.reshape(z, b, h, nc, GLA_CHUNK, t.shape[-1]), 3, 0)

    causal = jnp.tril(jnp.ones((GLA_CHUNK, GLA_CHUNK), dtype=bool))

    def step(s, inp):
        qc, kc, vc, gc = inp
        cum = jnp.cumsum(gc, axis=-2)
        rel = jnp.where(causal[:, :, None], cum[..., :, None, :] - cum[..., None, :, :], -jnp.inf)
        att = jnp.einsum('zbhtd,zbhsd,zbhtsd->zbhts', qc, kc, jnp.exp(rel))
        o = jnp.einsum('zbhts,zbhse->zbhte', att, vc) + jnp.einsum('zbhtd,zbhde->zbhte', qc * jnp.exp(cum), s)
        last = cum[..., -1:, :]
        s = jnp.exp(last[..., 0, :])[..., None] * s + jnp.einsum('zbhsd,zbhse->zbhde', kc * jnp.exp(last - cum), vc)
        return s, o

    s_fin, o = lax.scan(step, s0, (to_chunks(q), to_chunks(k), to_chunks(v), to_chunks(logd)))
    o = jnp.moveaxis(o, 0, 3).reshape(z, b, h, n, v.shape[-1])
    return o, s_fin


def gla_features(parts, a2, ab):
    q, k, v, a1f, a1b = parts
    b, n = q.shape[:2]

    def heads(t, d):
        return t.reshape(b, n, GLA_HEADS, d).transpose(0, 2, 1, 3)

    ld_f = jax.nn.log_sigmoid(a1f @ a2[0] + ab[0]) / GLA_TAU
    ld_b = jax.nn.log_sigmoid(a1b @ a2[1] + ab[1]) / GLA_TAU
    q = heads(q, GLA_DK) * GLA_DK ** -0.5
    k = heads(k, GLA_DK)
    v = heads(v, GLA_DV)
    return (stack_dirs(q, q, 2), stack_dirs(k, k, 2), stack_dirs(v, v, 2),
            stack_dirs(heads(ld_f, GLA_DK), heads(ld_b, GLA_DK), 2))


def gla_output(o_dirs, og, norm_g):
    o = merge_dirs(o_dirs, 2).transpose(0, 2, 1, 3)
    b, n = o.shape[:2]
    o = rms_norm(o, norm_g, LN_EPS) * jax.nn.silu(og.reshape(b, n, GLA_HEADS, GLA_DV))
    return o.reshape(b, n, GLA_VW)


def windowed_gqa(q, k, v, k_ctx, v_ctx, sink):
    b, n = q.shape[:2]
    nb = n // ATTN_BLOCK
    g = GQA_HQ // GQA_HKV
    scale = GQA_D ** -0.5
    qb = q.reshape(b, nb, ATTN_BLOCK, GQA_HKV, g, GQA_D)

    def band(t):
        tp = jnp.pad(t, ((0, 0), (ATTN_BLOCK, ATTN_BLOCK), (0, 0), (0, 0))).reshape(b, nb + 2, ATTN_BLOCK, GQA_HKV, GQA_D)
        return jnp.concatenate([tp[:, :-2], tp[:, 1:-1], tp[:, 2:]], axis=2)

    kw, vw = band(k), band(v)
    qpos = jnp.arange(nb)[:, None] * ATTN_BLOCK + jnp.arange(ATTN_BLOCK)[None, :]
    kpos = jnp.arange(nb)[:, None] * ATTN_BLOCK - ATTN_BLOCK + jnp.arange(3 * ATTN_BLOCK)[None, :]
    valid = ((jnp.abs(qpos[:, :, None] - kpos[:, None, :]) <= WINDOW)
             & (kpos >= 0)[:, None, :] & (kpos < n)[:, None, :])
    s_loc = jnp.einsum('bnqhgd,bnkhd->bhgnqk', qb, kw).astype(F32) * scale
    s_loc = jnp.where(valid, s_loc, -jnp.inf)
    s_ctx = jnp.einsum('bnqhgd,bkhd->bhgnqk', qb, k_ctx).astype(F32) * scale
    s_sink = jnp.broadcast_to(sink.reshape(GQA_HKV, g)[None, :, :, None, None, None].astype(F32), s_loc.shape[:-1] + (1,))
    p = jax.nn.softmax(jnp.concatenate([s_loc, s_ctx, s_sink], -1), axis=-1).astype(v.dtype)
    nw = 3 * ATTN_BLOCK
    o = (jnp.einsum('bhgnqk,bnkhd->bnqhgd', p[..., :nw], vw)
         + jnp.einsum('bhgnqk,bkhd->bnqhgd', p[..., nw:nw + k_ctx.shape[1]], v_ctx))
    return o.reshape(b, n, GQA_HQ * GQA_D)


def gqa_ctx_attend(q, k, v, sink):
    b, n = q.shape[:2]
    g = GQA_HQ // GQA_HKV
    qg = q.reshape(b, n, GQA_HKV, g, GQA_D)
    s = jnp.einsum('bqhgd,bkhd->bhgqk', qg, k).astype(F32) * GQA_D ** -0.5
    s_sink = jnp.broadcast_to(sink.reshape(GQA_HKV, g)[None, :, :, None, None].astype(F32), s.shape[:-1] + (1,))
    p = jax.nn.softmax(jnp.concatenate([s, s_sink], -1), axis=-1)[..., :-1].astype(v.dtype)
    return jnp.einsum('bhgqk,bkhd->bqhgd', p, v).reshape(b, n, GQA_HQ * GQA_D)


def mixer_ab(u_lat, u_ctx, w_in, w_out, gla_a2, gla_ab, gla_norm, sink, row, col, need_ctx):
    p_lat = split_cols(u_lat @ w_in, AB_SIZES)
    p_ctx = split_cols(u_ctx @ w_in, AB_SIZES)
    b = u_lat.shape[0]
    s0 = jnp.zeros((2, b, GLA_HEADS, GLA_DK, GLA_DV), u_lat.dtype)
    o_ctx, s_ctx = gla_chunked(*gla_features(p_ctx[:5], gla_a2, gla_ab), s0)
    o_lat, _ = gla_chunked(*gla_features(p_lat[:5], gla_a2, gla_ab), s_ctx)
    gla_lat = gla_output(o_lat, p_lat[5], gla_norm)

    def gqa_heads(parts):
        q, k, v = parts
        bb, n = q.shape[:2]
        return (q.reshape(bb, n, GQA_HQ, GQA_D), k.reshape(bb, n, GQA_HKV, GQA_D), v.reshape(bb, n, GQA_HKV, GQA_D))

    q_l, k_l, v_l = gqa_heads(p_lat[6:])
    q_c, k_c, v_c = gqa_heads(p_ctx[6:])
    att_lat = windowed_gqa(axial_rope(q_l, row, col), axial_rope(k_l, row, col), v_l, k_c, v_c, sink)
    y_lat = jnp.concatenate([gla_lat, att_lat], -1) @ w_out
    if not need_ctx:
        return y_lat, None
    gla_c = gla_output(o_ctx, p_ctx[5], gla_norm)
    att_c = gqa_ctx_attend(q_c, k_c, v_c, sink)
    y_ctx = jnp.concatenate([gla_c, att_c], -1) @ w_out
    return y_lat, y_ctx


def centred_shift(p, mu):
    prev = jnp.pad(p, ((0, 0), (1, 0), (0, 0)))[:, :-1]
    nxt = jnp.pad(p, ((0, 0), (0, 1), (0, 0)))[:, 1:]
    return p + mu * (0.5 * (prev + nxt) - p)


def rwkv_features(p, mu, w0, w2, a0, a2, g2, k_k, k_a):
    p = centred_shift(p, mu)
    r, k, v, w1f, w1b, a1, g1 = split_cols(p, RWKV_SIZES)
    b, n = p.shape[:2]

    def heads(t):
        return t.reshape(b, n, RWKV_HEADS, RWKV_N)

    def decay(w1, i):
        w_log = -jax.nn.softplus(-(w0[i] + jnp.tanh(w1) @ w2[i])) - 0.5
        return heads(jnp.exp(-jnp.exp(w_log)))

    a = jax.nn.sigmoid(a0 + a1 @ a2)
    g = jax.nn.sigmoid(g1) @ g2
    kkf = heads(k * k_k).astype(F32)
    kk = (kkf / jnp.maximum(jnp.sqrt(jnp.sum(kkf * kkf, -1, keepdims=True)), 1e-12)).astype(k.dtype)
    k = k * (1 + (a - 1) * k_a)
    return heads(r), decay(w1f, 0), decay(w1b, 1), heads(k), heads(v), kk, heads(a), g


def rwkv_scan(r, w, k, v, a, bb, s0):
    xs = tuple(jnp.moveaxis(t, 2, 0) for t in (r, w, k, v, a, bb))

    def step(s, inp):
        r_t, w_t, k_t, v_t, a_t, b_t = inp
        sa = jnp.einsum('zbhij,zbhj->zbhi', s, a_t)
        s = s * w_t[..., None, :] + sa[..., :, None] * b_t[..., None, :] + v_t[..., :, None] * k_t[..., None, :]
        return s, jnp.einsum('zbhij,zbhj->zbhi', s, r_t)

    s_fin, y = lax.scan(step, s0, xs)
    return jnp.moveaxis(y, 0, 2), s_fin


def rwkv_dirs(f):
    r, wf, wb, k, v, kk, a, _ = f
    return (stack_dirs(r, r, 1), stack_dirs(wf, wb, 1), stack_dirs(k, k, 1), stack_dirs(v, v, 1),
            stack_dirs(-kk, -kk, 1), stack_dirs(kk * a, kk * a, 1))


def rwkv_output(y_dirs, f, r_k, lnx_g, lnx_b):
    r, _, _, k, v, _, _, g = f
    y = merge_dirs(y_dirs, 1)
    b, n = y.shape[:2]
    yf = y.astype(F32)
    mu = jnp.mean(yf, -1, keepdims=True)
    var = jnp.mean(jnp.square(yf - mu), -1, keepdims=True)
    yn = ((yf - mu) * lax.rsqrt(var + RWKV_LNX_EPS)).astype(y.dtype).reshape(b, n, RWKV_W) * lnx_g + lnx_b
    bonus = (jnp.sum(r * k * r_k, -1, keepdims=True) * v).reshape(b, n, RWKV_W)
    return (yn + bonus) * g


def mla_features(parts, q_norm, kv_norm, w_uq, w_ukv, pos):
    q_lat, kv_lat, k_pe = parts
    b, n = q_lat.shape[:2]
    q = (rms_norm(q_lat, q_norm, MLA_EPS) @ w_uq).reshape(b, n, MLA_HEADS, MLA_NOPE + MLA_ROPE)
    kv = (rms_norm(kv_lat, kv_norm, MLA_EPS) @ w_ukv).reshape(b, n, MLA_HEADS, MLA_NOPE + MLA_V)
    q_nope, q_pe = q[..., :MLA_NOPE], q[..., MLA_NOPE:]
    k_nope, v = kv[..., :MLA_NOPE], kv[..., MLA_NOPE:]
    k_pe = k_pe[:, :, None, :]
    if pos is not None:
        q_pe = axial_rope(q_pe, pos[0], pos[1])
        k_pe = axial_rope(k_pe, pos[0], pos[1])
    q = jnp.concatenate([q_nope, q_pe], -1)
    k = jnp.concatenate([k_nope, jnp.broadcast_to(k_pe, (b, n, MLA_HEADS, MLA_ROPE))], -1)
    return q, k, v


def mla_latent_attend(q, k, v):
    b, n = q.shape[:2]
    nb = n // ATTN_BLOCK
    scale = (MLA_NOPE + MLA_ROPE) ** -0.5
    qb = jnp.moveaxis(q.reshape(b, nb, ATTN_BLOCK, MLA_HEADS, MLA_NOPE + MLA_ROPE), 1, 0)

    def block(qi):
        s = jnp.einsum('bqhd,bkhd->bhqk', qi, k).astype(F32) * scale
        p = jax.nn.softmax(s, axis=-1).astype(v.dtype)
        return jnp.einsum('bhqk,bkhd->bqhd', p, v)

    o = lax.map(block, qb)
    return jnp.moveaxis(o, 0, 1).reshape(b, n, MLA_HEADS * MLA_V)


def mla_ctx_attend(q, k, v):
    b, n = q.shape[:2]
    s = jnp.einsum('bqhd,bkhd->bhqk', q, k).astype(F32) * (MLA_NOPE + MLA_ROPE) ** -0.5
    p = jax.nn.softmax(s, axis=-1).astype(v.dtype)
    return jnp.einsum('bhqk,bkhd->bqhd', p, v).reshape(b, n, MLA_HEADS * MLA_V)


def mixer_cd(u_lat, u_ctx, w_in, w_out, mu, w0, w2, a0, a2, g2, k_k, k_a, r_k, lnx_g, lnx_b,
             q_norm, kv_norm, w_uq, w_ukv, row, col, need_ctx):
    proj_lat = u_lat @ w_in
    proj_ctx = u_ctx @ w_in
    b = u_lat.shape[0]
    f_ctx = rwkv_features(proj_ctx[..., :RWKV_IN], mu, w0, w2, a0, a2, g2, k_k, k_a)
    f_lat = rwkv_features(proj_lat[..., :RWKV_IN], mu, w0, w2, a0, a2, g2, k_k, k_a)
    s0 = jnp.zeros((2, b, RWKV_HEADS, RWKV_N, RWKV_N), u_lat.dtype)
    y_ctx_d, s_ctx = rwkv_scan(*rwkv_dirs(f_ctx), s0)
    y_lat_d, _ = rwkv_scan(*rwkv_dirs(f_lat), s_ctx)
    rw_lat = rwkv_output(y_lat_d, f_lat, r_k, lnx_g, lnx_b)
    q_c, k_c, v_c = mla_features(split_cols(proj_ctx[..., RWKV_IN:], MLA_SIZES), q_norm, kv_norm, w_uq, w_ukv, None)
    q_l, k_l, v_l = mla_features(split_cols(proj_lat[..., RWKV_IN:], MLA_SIZES), q_norm, kv_norm, w_uq, w_ukv, (row, col))
    att_lat = mla_latent_attend(q_l, jnp.concatenate([k_l, k_c], 1), jnp.concatenate([v_l, v_c], 1))
    y_lat = jnp.concatenate([rw_lat, att_lat], -1) @ w_out
    if not need_ctx:
        return y_lat, None
    rw_c = rwkv_output(y_ctx_d, f_ctx, r_k, lnx_g, lnx_b)
    att_c = mla_ctx_attend(q_c, k_c, v_c)
    y_ctx = jnp.concatenate([rw_c, att_c], -1) @ w_out
    return y_lat, y_ctx


def setup_inputs(seed: int = 0) -> dict:
    key = jax.random.key(seed)
    ks = iter(jax.random.split(key, 48))
    d = D_MODEL
    ne, no = (DEPTH + 1) // 2, DEPTH // 2

    def nrm(shape, s):
        return jax.random.normal(next(ks), shape, F32) * s

    def uni(shape):
        return jax.random.uniform(next(ks), shape, F32)

    return {
        'x': nrm((BATCH, SEQ, d), 1.0),
        'c': nrm((BATCH, d), 1.0),
        'ctx': nrm((BATCH, CTX_LEN, d), 1.0),
        'c_ctx': nrm((d,), 1.0),
        'ada_w': nrm((DEPTH, d, 6 * d), d ** -0.5),
        'ada_b': nrm((DEPTH, 6 * d), 0.02),
        'ln_g': 1.0 + nrm((DEPTH, 2, d), 0.02),
        'ln_b': nrm((DEPTH, 2, d), 0.02),
        'mlp_w1': nrm((DEPTH, d, MLP_HIDDEN), d ** -0.5),
        'mlp_w2': nrm((DEPTH, MLP_HIDDEN, d), MLP_HIDDEN ** -0.5 * BETA),
        'ab_w_in': nrm((ne, d, AB_IN), d ** -0.5),
        'ab_w_out': nrm((ne, AB_MIX, d), AB_MIX ** -0.5 * BETA),
        'gla_a2': nrm((ne, 2, GLA_GATE_RANK, GLA_QK), GLA_GATE_RANK ** -0.5),
        'gla_ab': 1.0 + nrm((ne, 2, GLA_QK), 1.0),
        'gla_norm': 1.0 + nrm((ne, GLA_DV), 0.02),
        'gqa_sink': nrm((ne, GQA_HQ), 0.5),
        'cd_w_in': nrm((no, d, CD_IN), d ** -0.5),
        'cd_w_out': nrm((no, CD_MIX, d), CD_MIX ** -0.5 * BETA),
        'rwkv_mu': uni((no, RWKV_IN)),
        'rwkv_w0': -6.0 + 5.0 * uni((no, 2, RWKV_W)),
        'rwkv_w2': nrm((no, 2, RWKV_DECAY_RANK, RWKV_W), 0.1 * RWKV_DECAY_RANK ** -0.5),
        'rwkv_a0': nrm((no, RWKV_W), 0.1),
        'rwkv_a2': nrm((no, RWKV_AAA_RANK, RWKV_W), RWKV_AAA_RANK ** -0.5),
        'rwkv_g2': nrm((no, RWKV_GATE_RANK, RWKV_W), RWKV_GATE_RANK ** -0.5),
        'rwkv_kk': 0.85 + nrm((no, RWKV_W), 0.02),
        'rwkv_ka': 1.0 + nrm((no, RWKV_W), 0.02),
        'rwkv_rk': nrm((no, RWKV_HEADS, RWKV_N), 0.1),
        'rwkv_lnx_g': 1.0 + nrm((no, RWKV_W), 0.02),
        'rwkv_lnx_b': nrm((no, RWKV_W), 0.02),
        'mla_q_norm': 1.0 + nrm((no, MLA_Q_RANK), 0.02),
        'mla_kv_norm': 1.0 + nrm((no, MLA_KV_RANK), 0.02),
        'mla_w_uq': nrm((no, MLA_Q_RANK, MLA_HEADS * (MLA_NOPE + MLA_ROPE)), MLA_Q_RANK ** -0.5),
        'mla_w_ukv': nrm((no, MLA_KV_RANK, MLA_HEADS * (MLA_NOPE + MLA_V)), MLA_KV_RANK ** -0.5),
    }


def reference(x, c, ctx, c_ctx, ada_w, ada_b, ln_g, ln_b, mlp_w1, mlp_w2, ab_w_in, ab_w_out, gla_a2, gla_ab,
              gla_norm, gqa_sink, cd_w_in, cd_w_out, rwkv_mu, rwkv_w0, rwkv_w2, rwkv_a0, rwkv_a2, rwkv_g2,
              rwkv_kk, rwkv_ka, rwkv_rk, rwkv_lnx_g, rwkv_lnx_b, mla_q_norm, mla_kv_norm, mla_w_uq, mla_w_ukv):
    n = x.shape[1]
    rows = n // GRID_W
    row = jnp.repeat(jnp.arange(rows, dtype=jnp.int32), GRID_W)
    col = jnp.tile(jnp.arange(GRID_W, dtype=jnp.int32), rows)
    h = ctx
    for l in range(DEPTH):
        need_ctx = l < DEPTH - 1
        m_lat = jnp.split((jax.nn.silu(c) @ ada_w[l] + ada_b[l])[:, None, :], 6, axis=-1)
        m_ctx = jnp.split(jax.nn.silu(c_ctx) @ ada_w[l] + ada_b[l], 6, axis=-1)
        u_lat = x * (1 + m_lat[1]) + m_lat[0]
        u_ctx = h * (1 + m_ctx[1]) + m_ctx[0]
        i = l // 2
        if l % 2 == 0:
            y_lat, y_ctx = mixer_ab(u_lat, u_ctx, ab_w_in[i], ab_w_out[i], gla_a2[i], gla_ab[i], gla_norm[i],
                                    gqa_sink[i], row, col, need_ctx)
        else:
            y_lat, y_ctx = mixer_cd(u_lat, u_ctx, cd_w_in[i], cd_w_out[i], rwkv_mu[i], rwkv_w0[i], rwkv_w2[i],
                                    rwkv_a0[i], rwkv_a2[i], rwkv_g2[i], rwkv_kk[i], rwkv_ka[i], rwkv_rk[i],
                                    rwkv_lnx_g[i], rwkv_lnx_b[i], mla_q_norm[i], mla_kv_norm[i], mla_w_uq[i],
                                    mla_w_ukv[i], row, col, need_ctx)
        x = layer_norm(ALPHA * x + m_lat[2] * y_lat, ln_g[l, 0], ln_b[l, 0])
        x = layer_norm(ALPHA * x + m_lat[5] * mlp(x * (1 + m_lat[4]) + m_lat[3], mlp_w1[l], mlp_w2[l]), ln_g[l, 1], ln_b[l, 1])
        if need_ctx:
            h = layer_norm(ALPHA * h + m_ctx[2] * y_ctx, ln_g[l, 0], ln_b[l, 0])
            h = layer_norm(ALPHA * h + m_ctx[5] * mlp(h * (1 + m_ctx[4]) + m_ctx[3], mlp_w1[l], mlp_w2[l]), ln_g[l, 1], ln_b[l, 1])
    return x
```

```python
import numpy as np
from contextlib import ExitStack
import concourse.bass as bass
import concourse.mybir as mybir
from concourse.bass_utils import run_bass_kernel_spmd

F32 = mybir.dt.float32
BF16 = mybir.dt.bfloat16
AF = mybir.ActivationFunctionType
ALU = mybir.AluOpType
AX = mybir.AxisListType

D = 1024
NCORES = 8
ALPHA = 4.0 ** 0.25
LN_EPS = 1e-5


class Prog:
    def __init__(self, nc, n_dma_sems=32, same_engine_sync=("act", "pool", "dve")):
        self.nc = nc
        self.eng = {"pe": nc.tensor, "dve": nc.vector, "act": nc.scalar, "pool": nc.gpsimd, "sp": nc.sync}
        self.sem, self.cnt, self.stack = {}, {}, ExitStack()
        for e in self.eng:
            self.sem[e] = self.stack.enter_context(nc.semaphore("c_" + e))
            self.cnt[e] = 0
        self.dsem = [self.stack.enter_context(nc.semaphore("d_%d" % i)) for i in range(n_dma_sems)]
        self.dcnt = [0] * n_dma_sems
        self.dnext = 0
        self.waited = {e: {} for e in self.eng}
        self.last_w, self.readers = {}, {}
        self.same = set(same_engine_sync)
        self.rr = 0
        self.csem = self.stack.enter_context(nc.semaphore("c_coll"))
        self.ccnt = 0

    def _semobj(self, k):
        return self.sem[k] if isinstance(k, str) else self.dsem[k]

    def _wait(self, e, k, v):
        if self.waited[e].get(k, 0) >= v:
            return
        self.eng[e].wait_ge(self._semobj(k), v)
        self.waited[e][k] = v

    def _deps(self, e, reads, writes):
        deps = {}

        def add(kv):
            if kv is not None and deps.get(kv[0], 0) < kv[1]:
                deps[kv[0]] = kv[1]
        for b in reads:
            add(self.last_w.get(b))
        for b in writes:
            add(self.last_w.get(b))
            for kv in self.readers.get(b, {}).items():
                add(kv)
        for k, v in deps.items():
            if k == e and e not in self.same:
                continue
            self._wait(e, k, v)

    def _record(self, mark, reads, writes):
        for b in reads:
            r = self.readers.setdefault(b, {})
            if r.get(mark[0], 0) < mark[1]:
                r[mark[0]] = mark[1]
        for b in writes:
            self.last_w[b] = mark
            self.readers[b] = {}

    def op(self, e, fn, reads=(), writes=()):
        self._deps(e, reads, writes)
        ins = fn(self.eng[e])
        self.cnt[e] += 1
        ins.then_inc(self.sem[e], 1)
        self._record((e, self.cnt[e]), reads, writes)

    def dma(self, e, out, in_, reads=(), writes=(), **kw):
        i = self.dnext
        self.dnext = (self.dnext + 1) % len(self.dsem)
        if self.dcnt[i] > 0:
            self._wait(e, i, self.dcnt[i])
        self._deps(e, reads, writes)
        ins = self.eng[e].dma_start(out=out, in_=in_, **kw)
        self.dcnt[i] += 16
        ins.then_inc(self.dsem[i], 16)
        self._record((i, self.dcnt[i]), reads, writes)

    def finish(self, e="sp"):
        self.coll_wait()
        for i in range(len(self.dsem)):
            if self.dcnt[i] > 0:
                self._wait(e, i, self.dcnt[i])
        for k in self.eng:
            if k != e and self.cnt[k] > 0:
                self._wait(e, k, self.cnt[k])

    def barrier(self):
        for e in self.eng:
            for i in range(len(self.dsem)):
                if self.dcnt[i] > 0:
                    self._wait(e, i, self.dcnt[i])
            for k in self.eng:
                if k != e and self.cnt[k] > 0:
                    self._wait(e, k, self.cnt[k])

    def coll_issue(self, kind, op, groups, in_ap, out_ap, barrier=True):
        if barrier:
            self.barrier()
        ins = self.nc.gpsimd.collective_compute(kind, op, replica_groups=groups, ins=[in_ap], outs=[out_ap])
        self.ccnt += 1
        ins.then_inc(self.csem)

    def coll_wait(self):
        for e in self.eng:
            if self.ccnt > 0 and self.waited[e].get("cc", 0) < self.ccnt:
                self.eng[e].wait_ge(self.csem, self.ccnt)
                self.waited[e]["cc"] = self.ccnt

    def coll(self, kind, op, groups, in_ap, out_ap):
        self.coll_issue(kind, op, groups, in_ap, out_ap)
        self.coll_wait()

    def close(self):
        self.stack.close()

    def alt(self, engs=("dve", "pool")):
        self.rr += 1
        return engs[self.rr % len(engs)]


class Ctx:
    def __init__(self, shared=False):
        self.nc = bass.Bass("TRN2", target_bir_lowering=False)
        self.es = ExitStack()
        self.P = None
        self.shared = shared
        self.bind = {}
        self.pref = ""
        self.ext_in = []

    def _io(self, name, shape, dt, kind):
        if name in self.bind:
            ap = self.bind[name]
            assert [int(x) for x in ap.shape] == [int(x) for x in shape], (name, ap.shape, shape)
            return ap
        full = self.pref + name
        if kind == "ExternalInput":
            self.ext_in.append(full)
        return self.nc.dram_tensor(full, list(shape), dt, kind=kind).ap()

    def din(self, name, shape, dt=F32):
        return self._io(name, shape, dt, "ExternalInput")

    def dout(self, name, shape, dt=F32):
        return self._io(name, shape, dt, "ExternalOutput")

    def scratch(self, name, shape, dt=F32):
        return self.nc.dram_tensor(name, list(shape), dt).ap()

    def sb(self, name, shape, dt=F32):
        return self.es.enter_context(self.nc.sbuf_tensor("s_" + self.pref + name, list(shape), dt))

    def ps(self, name, shape, dt=F32):
        return self.es.enter_context(self.nc.psum_tensor("p_" + self.pref + name, list(shape), dt))

    def start(self, **kw):
        if self.P is None:
            self.P = Prog(self.nc, **kw)
        return self.P

    def stage(self, pref, bind):
        self.pref = pref
        self.bind = dict(bind)

    def end(self):
        if self.shared:
            self.P.barrier()
            self.es.close()
            self.es = ExitStack()
            self.bind = {}
            return None
        self.P.finish("sp")
        self.P.close()
        self.es.close()
        return self.nc

    def finalize(self):
        self.P.finish("sp")
        self.P.close()
        return self.nc


def run_spmd(nc, in_maps):
    res = run_bass_kernel_spmd(nc, in_maps, core_ids=list(range(len(in_maps))))
    return res.results


def load_w_bf16(C, P, wdst, wkey, wsrc, K, ncols, stage, skey, col0=0, engs=("dve", "pool", "act")):
    KC = (K + 127) // 128
    for kc in range(KC):
        kp = min(128, K - kc * 128)
        b = kc % 2
        P.dma("sp", stage[0:kp, b, 0:ncols], wsrc[kc * 128:kc * 128 + kp, col0:col0 + ncols], writes=[(skey, b)])
        e = engs[kc % len(engs)]
        if e == "act":
            P.op("act", lambda a, kp=kp, b=b, kc=kc: a.copy(wdst[0:kp, kc, 0:ncols], stage[0:kp, b, 0:ncols]),
                 reads=[(skey, b)], writes=[(wkey, kc)])
        else:
            P.op(e, lambda v, kp=kp, b=b, kc=kc: v.tensor_copy(wdst[0:kp, kc, 0:ncols], stage[0:kp, b, 0:ncols]),
                 reads=[(skey, b)], writes=[(wkey, kc)])


def build_k0(ncol, C=None):
    C = C or Ctx()
    cT = C.din("cT", [128, 8, 3])
    w = C.din("w", [2, 1024, ncol])
    bia = C.din("b", [3, 2, ncol])
    out = C.dout("out", [3, 2, ncol])
    cs = C.sb("cs", [128, 8, 3])
    cs2 = C.sb("cs2", [128, 8, 3])
    wt = C.sb("wt", [128, 2, 8, ncol])
    bt = C.sb("bt", [3, 2, ncol])
    ot = C.sb("ot", [3, 2, ncol])
    pss = [C.ps("ps%d" % i, [128, 512]) for i in range(2)]
    P = C.start()
    P.dma("sp", cs[:], cT[:, :, :], writes=["cs"])
    P.dma("sp", bt[:], bia[:, :, :], writes=["bt"])
    for l in range(2):
        P.dma("sp", wt[:, l, :, :], w[l].rearrange("(kc p) n -> p kc n", p=128), writes=[("wt", l)])
    P.op("act", lambda a: a.activation(cs2[:], cs[:], AF.Silu), reads=["cs"], writes=["cs2"])
    i = 0
    for l in range(2):
        for n0 in range(0, ncol, 512):
            n1 = min(ncol, n0 + 512)
            ps = pss[i % 2]; pk = "ps%d" % (i % 2); i += 1
            for kc in range(8):
                P.op("pe", lambda t, kc=kc, l=l, n0=n0, n1=n1, ps=ps: t.matmul(ps[0:3, 0:n1 - n0], cs2[:, kc, :], wt[:, l, kc, n0:n1], start=(kc == 0), stop=(kc == 7)),
                     reads=["cs2", ("wt", l)], writes=[pk])
            P.op("dve", lambda v, l=l, n0=n0, n1=n1, ps=ps: v.tensor_tensor(ot[:, l, n0:n1], ps[0:3, 0:n1 - n0], bt[:, l, n0:n1], ALU.add),
                 reads=[pk, "bt"], writes=["ot"])
    P.dma("sp", out[:, :, :], ot[:], reads=["ot"])
    return C.end()


def stage_k0(c, c_ctx, ada_w, ada_b):
    ncol = 6144 // NCORES
    nc = build_k0(ncol)
    cc = np.stack([c[0], c[1], c_ctx], axis=1).astype(np.float32)
    cT = np.ascontiguousarray(cc.reshape(8, 128, 3).transpose(1, 0, 2))
    maps = []
    for i in range(NCORES):
        sl = slice(i * ncol, (i + 1) * ncol)
        maps.append({"cT": cT, "w": np.ascontiguousarray(ada_w[:, :, sl]),
                     "b": np.ascontiguousarray(np.broadcast_to(ada_b[None, :, sl], (3, 2, ncol)))})
    res = run_spmd(nc, maps)
    m = np.concatenate([r["out"] for r in res], axis=2)
    return np.ascontiguousarray(m.transpose(1, 0, 2))


def mod_layout(mrow):
    return np.ascontiguousarray(mrow.reshape(6, 8, 128).transpose(2, 0, 1))


def build_k1(ncols, segs, C=None):
    C = C or Ctx()
    W = C.din("w", [1024, ncols])
    xs, mods, outs = [], [], []
    for si, (ntok, tile) in enumerate(segs):
        xs.append(C.din("x%d" % si, [1024, ntok]))
        mods.append(C.din("mod%d" % si, [128, 6, 8]))
        outs.append(C.dout("p%d" % si, [ncols, ntok]))
    TM = max(t for _, t in segs)
    NCH = (ncols + 127) // 128
    wb = C.sb("wb", [128, 8, ncols], BF16)
    stage = C.sb("stage", [128, 2, ncols])
    xt = C.sb("xt", [128, 2, 8, TM])
    ub = C.sb("ub", [128, 2, 8, TM], BF16)
    ost = C.sb("ost", [128, 4, TM])
    mt = C.sb("mt", [128, len(segs), 6, 8])
    sc1 = C.sb("sc1", [128, len(segs), 8])
    pss = [C.ps("ps%d" % i, [128, 512]) for i in range(4)]
    P = C.start()
    for si in range(len(segs)):
        P.dma("sp", mt[:, si, :, :], mods[si][:, :, :], writes=[("mt", si)])
        P.op("dve", lambda v, si=si: v.tensor_scalar_add(sc1[:, si, :], mt[:, si, 1, :], 1.0), reads=[("mt", si)], writes=[("sc1", si)])
    load_w_bf16(C, P, wb, "wb", W, 1024, ncols, stage, "stage")
    it = 0
    oi = 0
    for si, (ntok, tile) in enumerate(segs):
        xv = xs[si].rearrange("(kc p) t -> p kc t", p=128)
        for t0 in range(0, ntok, tile):
            b = it % 2; it += 1
            P.dma("sp", xt[:, b, :, 0:tile], xv[:, :, t0:t0 + tile], writes=[("xt", b)])
            for kc in range(8):
                e = P.alt()
                P.op(e, lambda v, b=b, kc=kc, si=si, tile=tile: v.tensor_scalar(ub[:, b, kc, 0:tile], xt[:, b, kc, 0:tile], sc1[:, si, kc:kc + 1], mt[:, si, 0, kc:kc + 1], ALU.mult, ALU.add),
                     reads=[("xt", b), ("sc1", si), ("mt", si)], writes=[("ub", b, kc)])
            for ch in range(NCH):
                c0 = ch * 128; cn = min(128, ncols - c0)
                pi = oi % 4; ob = oi % 4; oi += 1
                ps = pss[pi]
                for kc in range(8):
                    P.op("pe", lambda t, ps=ps, cn=cn, c0=c0, kc=kc, b=b, tile=tile: t.matmul(ps[0:cn, 0:tile], wb[:, kc, c0:c0 + cn], ub[:, b, kc, 0:tile], start=(kc == 0), stop=(kc == 7)),
                         reads=[("wb", kc), ("ub", b, kc)], writes=[("ps", pi)])
                if oi % 2:
                    P.op("act", lambda a, ps=ps, cn=cn, ob=ob, tile=tile: a.copy(ost[0:cn, ob, 0:tile], ps[0:cn, 0:tile]), reads=[("ps", pi)], writes=[("ost", ob)])
                else:
                    P.op("dve", lambda v, ps=ps, cn=cn, ob=ob, tile=tile: v.tensor_copy(ost[0:cn, ob, 0:tile], ps[0:cn, 0:tile]), reads=[("ps", pi)], writes=[("ost", ob)])
                P.dma("sp", outs[si][c0:c0 + cn, t0:t0 + tile], ost[0:cn, ob, 0:tile], reads=[("ost", ob)])
    return C.end()


def stage_k1(x, ctx, m_l, w_in):
    ncols = w_in.shape[1]
    nc = build_k1(ncols, [(2048, 512), (64, 64)])
    maps = []
    for c in range(NCORES):
        b, s = c // 4, c % 4
        maps.append({"w": np.ascontiguousarray(w_in),
                     "x0": np.ascontiguousarray(x[b, s * 2048:(s + 1) * 2048].T), "mod0": mod_layout(m_l[b]),
                     "x1": np.ascontiguousarray(ctx[b, s * 64:(s + 1) * 64].T), "mod1": mod_layout(m_l[2])})
    res = run_spmd(nc, maps)
    pl = np.empty((2, 8192, ncols), np.float32)
    pc = np.empty((2, 256, ncols), np.float32)
    for c in range(NCORES):
        b, s = c // 4, c % 4
        pl[b, s * 2048:(s + 1) * 2048] = res[c]["p0"].T
        pc[b, s * 64:(s + 1) * 64] = res[c]["p1"].T
    return pl, pc


def build_gla(T, stop=99, C=None):
    NT = T // 128
    BLK = 6
    assert NT % BLK == 0
    C = C or Ctx()
    qT = C.din("qT", [2, 64, T]); kT = C.din("kT", [2, 64, T])
    kk = C.din("k", [2, T, 64]); vv = C.din("v", [2, T, 128])
    a1T = C.din("a1T", [2, 16, T]); a2 = C.din("a2", [2, 16, 64]); ab = C.din("ab", [2, 1, 64])
    tri_d = C.din("tri", [128, 128]); mask_d = C.din("mask", [128, 128])
    o_d = C.dout("o", [2, T, 128])
    W = BLK * 128
    qs = C.sb("qs", [64, 2, 2, W]); ks = C.sb("ks", [64, 2, 2, W])
    kt = C.sb("kt", [128, 2, 2, BLK, 64]); vt = C.sb("vt", [128, 2, 2, BLK, 128])
    vb = C.sb("vb", [128, 2, 2, 128], BF16)
    a1s = C.sb("a1s", [16, 2, 2, W]); a2s = C.sb("a2s", [16, 2, 64]); abs_ = C.sb("abs", [1, 2, 64])
    ones = C.sb("ones", [1, 128])
    tri = C.sb("tri", [128, 128]); mask = C.sb("mask", [128, 128])
    ez = C.sb("ez", [128, 2, 2, 64]); sp = C.sb("sp", [128, 2, 2, 64])
    ecT = C.sb("ecT", [64, 2, 2, 128]); encT = C.sb("encT", [64, 2, 2, 128]); enc = C.sb("enc", [128, 2, 2, 64])
    el = C.sb("el", [64, 2, 2, 2])
    qa = C.sb("qa", [64, 2, 2, 128], BF16); qb_ = C.sb("qb", [64, 2, 2, 128], BF16)
    kTb = C.sb("kTb", [64, 2, 2, 128], BF16); kb = C.sb("kb", [128, 2, 2, 64], BF16)
    att = C.sb("att", [128, 2, 2, 128], BF16)
    S = C.sb("S", [64, 2, 128]); Sb = C.sb("Sb", [64, 2, 3, 128], BF16); tmp = C.sb("tmp", [64, 2, 128])
    osb = C.sb("osb", [128, 2, 2, 128])
    NPS = 8
    pss = [C.ps("ps%d" % i, [128, 512]) for i in range(NPS)]
    P = C.start()
    pctr = [0]

    def newps():
        i = pctr[0] % NPS; pctr[0] += 1
        return pss[i], ("ps", i)

    P.dma("sp", tri[:], tri_d[:, :], writes=["tri"]); P.dma("sp", mask[:], mask_d[:, :], writes=["mask"])
    P.dma("sp", a2s[:], a2.rearrange("z r d -> r z d"), writes=["a2s"])
    P.dma("sp", abs_[:], ab.rearrange("z o d -> o z d"), writes=["abs"])
    P.op("dve", lambda v: v.memset(ones[:], 1.0), writes=["ones"])
    P.op("dve", lambda v: v.memset(S[:], 0.0), writes=[("S", 0), ("S", 1)])
    P.op("pool", lambda v: v.memset(Sb[:], 0.0), writes=[("Sb", 0, 0), ("Sb", 1, 0), ("Sb", 0, 1), ("Sb", 1, 1), ("Sb", 0, 2), ("Sb", 1, 2)])
    P.op("pool", lambda v: v.memset(qa[:], 0.0), writes=[("qa", z, p) for z in range(2) for p in range(2)])
    P.op("pool", lambda v: v.memset(qb_[:], 0.0), writes=[("qb", z, p) for z in range(2) for p in range(2)])
    sbi = [0, 0]
    for blk in range(NT // BLK):
        bb = blk % 2
        t0 = blk * W
        for z in range(2):
            P.dma("sp", qs[:, z, bb, :], qT[z, :, t0:t0 + W], writes=[("qs", z, bb)])
            P.dma("sp", ks[:, z, bb, :], kT[z, :, t0:t0 + W], writes=[("ks", z, bb)])
            P.dma("sp", kt[:, z, bb, :, :], kk[z, t0:t0 + W, :].rearrange("(n p) d -> p n d", p=128), writes=[("kt", z, bb)])
            P.dma("sp", vt[:, z, bb, :, :], vv[z, t0:t0 + W, :].rearrange("(n p) d -> p n d", p=128), writes=[("vt", z, bb)])
            P.dma("sp", a1s[:, z, bb, :], a1T[z, :, t0:t0 + W], writes=[("a1s", z, bb)])
        for ti in range(BLK):
            tg = blk * BLK + ti
            p = tg % 2
            c0 = ti * 128
            for z in range(2):
                ps, pk = newps()
                P.op("pe", lambda t, ps=ps, z=z, bb=bb, c0=c0: t.matmul(ps[:, 0:64], a1s[:, z, bb, c0:c0 + 128], a2s[:, z, :], start=True, stop=False),
                     reads=[("a1s", z, bb), "a2s"], writes=[pk])
                P.op("pe", lambda t, ps=ps, z=z: t.matmul(ps[:, 0:64], ones[:, :], abs_[:, z, :], start=False, stop=True),
                     reads=["ones", "abs"], writes=[pk])
                P.op("act", lambda a, ps=ps, z=z, p=p: a.activation(ez[:, z, p, :], ps[:, 0:64], AF.Exp, scale=-1.0), reads=[pk], writes=[("ez", z, p)])
                P.op("act", lambda a, z=z, p=p: a.activation(sp[:, z, p, :], ez[:, z, p, :], AF.Ln, bias=1.0), reads=[("ez", z, p)], writes=[("sp", z, p)])
                if stop <= 1:
                    continue
                ps1, pk1 = newps()
                P.op("pe", lambda t, ps1=ps1, z=z, p=p: t.matmul(ps1[0:64, 0:128], sp[:, z, p, :], tri[:, :], start=True, stop=True),
                     reads=[("sp", z, p), "tri"], writes=[pk1])
                ps2, pk2 = newps()
                P.op("pe", lambda t, ps2=ps2, z=z, p=p: t.matmul(ps2[:, 0:64], tri[:, :], sp[:, z, p, :], start=True, stop=True),
                     reads=[("sp", z, p), "tri"], writes=[pk2])
                P.op("act", lambda a, ps1=ps1, z=z, p=p: a.activation(ecT[:, z, p, :], ps1[0:64, 0:128], AF.Exp), reads=[pk1], writes=[("ecT", z, p)])
                P.op("act", lambda a, ps1=ps1, z=z, p=p: a.activation(encT[:, z, p, :], ps1[0:64, 0:128], AF.Exp, scale=-1.0), reads=[pk1], writes=[("encT", z, p)])
                P.op("act", lambda a, ps2=ps2, z=z, p=p: a.activation(enc[:, z, p, :], ps2[:, 0:64], AF.Exp, scale=-1.0), reads=[pk2], writes=[("enc", z, p)])
                P.op("dve", lambda v, z=z, p=p: v.tensor_copy(el[:, z, p, :], ecT[:, z, p, 63:128:64]), reads=[("ecT", z, p)], writes=[("el", z, p)])
                if stop <= 2:
                    continue
                P.op("dve", lambda v, z=z, p=p, bb=bb, c0=c0: v.scalar_tensor_tensor(qa[:, z, p, 0:64], qs[:, z, bb, c0:c0 + 64], 0.125, ecT[:, z, p, 0:64], ALU.mult, ALU.mult),
                     reads=[("qs", z, bb), ("ecT", z, p)], writes=[("qa", z, p)])
                P.op("dve", lambda v, z=z, p=p, bb=bb, c0=c0: v.scalar_tensor_tensor(qb_[:, z, p, 64:128], qs[:, z, bb, c0 + 64:c0 + 128], 0.125, ecT[:, z, p, 64:128], ALU.mult, ALU.mult),
                     reads=[("qs", z, bb), ("ecT", z, p)], writes=[("qb", z, p)])
                P.op("dve", lambda v, z=z, p=p, bb=bb, c0=c0: v.tensor_tensor(kTb[:, z, p, :], ks[:, z, bb, c0:c0 + 128], encT[:, z, p, :], ALU.mult),
                     reads=[("ks", z, bb), ("encT", z, p)], writes=[("kTb", z, p)])
                P.op("dve", lambda v, z=z, p=p, bb=bb, ti=ti: v.tensor_tensor(kb[:, z, p, :], kt[:, z, bb, ti, :], enc[:, z, p, :], ALU.mult),
                     reads=[("kt", z, bb), ("enc", z, p)], writes=[("kb", z, p)])
                P.op("dve", lambda v, z=z, p=p, bb=bb, ti=ti: v.tensor_copy(vb[:, z, p, :], vt[:, z, bb, ti, :]), reads=[("vt", z, bb)], writes=[("vb", z, p)])
                if stop <= 3:
                    continue
                ps3, pk3 = newps()
                P.op("pe", lambda t, ps3=ps3, z=z, p=p: t.matmul(ps3[:, 0:64], kTb[:, z, p, :], qa[:, z, p, 0:64], start=True, stop=True),
                     reads=[("kTb", z, p), ("qa", z, p)], writes=[pk3])
                P.op("pe", lambda t, ps3=ps3, z=z, p=p: t.matmul(ps3[:, 64:128], kTb[:, z, p, :], qb_[:, z, p, 64:128], start=True, stop=True),
                     reads=[("kTb", z, p), ("qb", z, p)], writes=[pk3])
                P.op("dve", lambda v, ps3=ps3, z=z, p=p: v.tensor_tensor(att[:, z, p, :], ps3[:, 0:128], mask[:, :], ALU.mult), reads=[pk3, "mask"], writes=[("att", z, p)])
                if stop <= 4:
                    continue
                ps4, pk4 = newps()
                P.op("pe", lambda t, ps4=ps4, z=z, p=p: t.matmul(ps4[0:64, 0:128], kb[0:64, z, p, :], vb[0:64, z, p, :], start=True, stop=True),
                     reads=[("kb", z, p), ("vb", z, p)], writes=[pk4])
                ps4b, pk4b = newps()
                P.op("pe", lambda t, ps4b=ps4b, z=z, p=p: t.matmul(ps4b[0:64, 128:256], kb[64:128, z, p, :], vb[64:128, z, p, :], start=True, stop=True),
                     reads=[("kb", z, p), ("vb", z, p)], writes=[pk4b])
                s0 = sbi[z]; s1 = (s0 + 1) % 3; s2 = (s0 + 2) % 3
                P.op("dve", lambda v, ps4=ps4, z=z: v.tensor_tensor(tmp[:, z, :], ps4[0:64, 0:128], S[:, z, :], ALU.add), reads=[pk4, ("S", z)], writes=[("tmp", z)])
                P.op("dve", lambda v, z=z, p=p: v.tensor_scalar_mul(S[:, z, :], tmp[:, z, :], el[:, z, p, 0:1]), reads=[("tmp", z), ("el", z, p)], writes=[("S", z)])
                P.op("act", lambda a, z=z, s1=s1: a.copy(Sb[:, z, s1, :], S[:, z, :]), reads=[("S", z)], writes=[("Sb", z, s1)])
                P.op("dve", lambda v, ps4b=ps4b, z=z: v.tensor_tensor(tmp[:, z, :], ps4b[0:64, 128:256], S[:, z, :], ALU.add), reads=[pk4b, ("S", z)], writes=[("tmp", z)])
                P.op("dve", lambda v, z=z, p=p: v.tensor_scalar_mul(S[:, z, :], tmp[:, z, :], el[:, z, p, 1:2]), reads=[("tmp", z), ("el", z, p)], writes=[("S", z)])
                P.op("act", lambda a, z=z, s2=s2: a.copy(Sb[:, z, s2, :], S[:, z, :]), reads=[("S", z)], writes=[("Sb", z, s2)])
                if stop <= 5:
                    continue
                ps5, pk5 = newps()
                P.op("pe", lambda t, ps5=ps5, z=z, p=p: t.matmul(ps5[:, 0:128], att[:, z, p, :], vb[:, z, p, :], start=True, stop=False),
                     reads=[("att", z, p), ("vb", z, p)], writes=[pk5])
                P.op("pe", lambda t, ps5=ps5, z=z, p=p, s0=s0: t.matmul(ps5[:, 0:128], qa[:, z, p, :], Sb[:, z, s0, :], start=False, stop=False),
                     reads=[("qa", z, p), ("Sb", z, s0)], writes=[pk5])
                P.op("pe", lambda t, ps5=ps5, z=z, p=p, s1=s1: t.matmul(ps5[:, 0:128], qb_[:, z, p, :], Sb[:, z, s1, :], start=False, stop=True),
                     reads=[("qb", z, p), ("Sb", z, s1)], writes=[pk5])
                sbi[z] = s2
                P.op("act", lambda a, ps5=ps5, z=z, p=p: a.copy(osb[:, z, p, :], ps5[:, 0:128]), reads=[pk5], writes=[("osb", z, p)])
                P.dma("sp", o_d[z, tg * 128:(tg + 1) * 128, :], osb[:, z, p, :], reads=[("osb", z, p)])
    return C.end()


def gla_consts():
    t = np.arange(128)
    same = (t[:, None] // 64) == (t[None, :] // 64)
    m = (same & (t[:, None] <= t[None, :])).astype(np.float32)
    return np.ascontiguousarray(m * (-1.0 / 16.0)), np.ascontiguousarray(m)


def stage_gla(pl, pc, gla_a2, gla_ab):
    T = 8448
    nc = build_gla(T)
    tri, mask = gla_consts()
    maps = []
    for c in range(NCORES):
        b, h = c // 4, c % 4
        p = np.concatenate([pc[b], pl[b]], axis=0)
        pr = np.concatenate([pc[b][::-1], pl[b][::-1]], axis=0)
        def pack(fn):
            return np.ascontiguousarray(np.stack([fn(p, 0), fn(pr, 1)], 0))
        maps.append({
            "qT": pack(lambda a, z: a[:, h * 64:(h + 1) * 64].T),
            "kT": pack(lambda a, z: a[:, 256 + h * 64:256 + (h + 1) * 64].T),
            "k": pack(lambda a, z: a[:, 256 + h * 64:256 + (h + 1) * 64]),
            "v": pack(lambda a, z: a[:, 512 + h * 128:512 + (h + 1) * 128]),
            "a1T": pack(lambda a, z: a[:, 1024 + 16 * z:1024 + 16 * (z + 1)].T),
            "a2": np.ascontiguousarray(gla_a2[:, :, h * 64:(h + 1) * 64]),
            "ab": np.ascontiguousarray(gla_ab[:, None, h * 64:(h + 1) * 64]),
            "tri": tri, "mask": mask})
    res = run_spmd(nc, maps)
    o_f = np.empty((2, 4, T, 128), np.float32); o_b = np.empty((2, 4, T, 128), np.float32)
    for c in range(NCORES):
        b, h = c // 4, c % 4
        o = res[c]["o"]
        o_f[b, h] = o[0]
        o_b[b, h, :256] = o[1][:256][::-1]
        o_b[b, h, 256:] = o[1][256:][::-1]
    return o_f, o_b


def ln_tail(P, nt, z, zsq, ones, newps, lng, lnb, outT, out_cols, ost, tagp):
    ps_s, pk_s = newps()
    ps_q, pk_q = newps()
    for kc in range(8):
        P.op("act", lambda a, kc=kc: a.activation(zsq[:, kc, 0:nt], z[:, kc, 0:nt], AF.Square), reads=[("z", kc)], writes=[("zsq", kc)])
    for kc in range(8):
        P.op("pe", lambda t, kc=kc: t.matmul(ps_s[:, 0:nt], ones[:, :], z[:, kc, 0:nt], start=(kc == 0), stop=(kc == 7)), reads=[("z", kc), "ones"], writes=[pk_s])
    for kc in range(8):
        P.op("pe", lambda t, kc=kc: t.matmul(ps_q[:, 0:nt], ones[:, :], zsq[:, kc, 0:nt], start=(kc == 0), stop=(kc == 7)), reads=[("zsq", kc), "ones"], writes=[pk_q])
    mean, var = ost[:, 0, 0:nt], ost[:, 1, 0:nt]
    P.op("dve", lambda v: v.tensor_scalar_mul(mean, ps_s[:, 0:nt], 1.0 / 1024.0), reads=[pk_s], writes=[("ost", 0)])
    P.op("dve", lambda v: v.tensor_tensor(var, mean, mean, ALU.mult), reads=[("ost", 0)], writes=[("ost", 1)])
    P.op("dve", lambda v: v.scalar_tensor_tensor(var, ps_q[:, 0:nt], 1.0 / 1024.0, var, ALU.mult, ALU.subtract), reads=[pk_q, ("ost", 1)], writes=[("ost", 1)])
    P.op("dve", lambda v: v.tensor_scalar_add(var, var, LN_EPS), reads=[("ost", 1)], writes=[("ost", 1)])
    P.op("act", lambda a: a.activation(var, var, AF.Sqrt), reads=[("ost", 1)], writes=[("ost", 1)])
    P.op("dve", lambda v: v.reciprocal(var, var), reads=[("ost", 1)], writes=[("ost", 1)])
    for kc in range(8):
        e = P.alt()
        P.op(e, lambda v, kc=kc: v.tensor_tensor(zsq[:, kc, 0:nt], z[:, kc, 0:nt], mean, ALU.subtract), reads=[("z", kc), ("ost", 0)], writes=[("zsq", kc)])
        P.op(e, lambda v, kc=kc: v.tensor_tensor(zsq[:, kc, 0:nt], zsq[:, kc, 0:nt], var, ALU.mult), reads=[("zsq", kc), ("ost", 1)], writes=[("zsq", kc)])
        P.op(e, lambda v, kc=kc: v.tensor_scalar(zsq[:, kc, 0:nt], zsq[:, kc, 0:nt], lng[:, kc:kc + 1], lnb[:, kc:kc + 1], ALU.mult, ALU.add), reads=[("zsq", kc), "ln"], writes=[("zsq", kc)])
    P.dma("sp", outT.rearrange("(kc p) t -> p kc t", p=128)[:, :, out_cols[0]:out_cols[1]], zsq[:, :, 0:nt], reads=[("zsq", kc) for kc in range(8)])


def build_k4b(segs, C=None):
    C = C or Ctx()
    W1 = C.din("w1", [1024, 4096]); W2 = C.din("w2", [4096, 1024])
    lnd = C.din("ln", [128, 2, 8]); ones_d = C.din("ones", [128, 128])
    xs, mods, outs = [], [], []
    for si, (ntok, tile) in enumerate(segs):
        xs.append(C.din("x%d" % si, [1024, ntok])); mods.append(C.din("mod%d" % si, [128, 6, 8])); outs.append(C.dout("o%d" % si, [1024, ntok]))
    TM = max(t for _, t in segs)
    w1b = C.sb("w1b", [128, 8, 4096], BF16); w2b = C.sb("w2b", [128, 32, 1024], BF16)
    stage = C.sb("stage", [128, 2, 2048])
    xt = C.sb("xt", [128, 8, TM]); ub = C.sb("ub", [128, 8, TM], BF16); hid = C.sb("hid", [128, 32, TM], BF16)
    z = C.sb("z", [128, 8, TM]); zsq = C.sb("zsq", [128, 8, TM]); ost = C.sb("ost", [128, 2, TM])
    ones = C.sb("ones", [128, 128]); lnt = C.sb("lnt", [128, 2, 8])
    mt = C.sb("mt", [128, len(segs), 6, 8]); sc1 = C.sb("sc1", [128, len(segs), 8])
    pss = [C.ps("ps%d" % i, [128, 512]) for i in range(8)]
    P = C.start()
    pctr = [0]

    def newps():
        i = pctr[0] % 8; pctr[0] += 1
        return pss[i], ("ps", i)
    P.dma("sp", ones[:], ones_d[:, :], writes=["ones"]); P.dma("sp", lnt[:], lnd[:, :, :], writes=["ln"])
    for si in range(len(segs)):
        P.dma("sp", mt[:, si, :, :], mods[si][:, :, :], writes=[("mt", si)])
        P.op("dve", lambda v, si=si: v.tensor_scalar_add(sc1[:, si, :], mt[:, si, 4, :], 1.0), reads=[("mt", si)], writes=[("sc1", si)])
    for half in range(2):
        load_w_bf16(C, P, w1b[:, :, half * 2048:(half + 1) * 2048], ("w1", half), W1, 1024, 2048, stage, "stage", col0=half * 2048)
    load_w_bf16(C, P, w2b, "w2", W2, 4096, 1024, stage, "stage")
    for si, (ntok, tile) in enumerate(segs):
        xv = xs[si].rearrange("(kc p) t -> p kc t", p=128)
        for t0 in range(0, ntok, tile):
            nt = tile
            P.dma("sp", xt[:, :, 0:nt], xv[:, :, t0:t0 + nt], writes=["xt"])
            for kc in range(8):
                P.op(P.alt(), lambda v, kc=kc, si=si: v.tensor_scalar(ub[:, kc, 0:nt], xt[:, kc, 0:nt], sc1[:, si, kc:kc + 1], mt[:, si, 3, kc:kc + 1], ALU.mult, ALU.add),
                     reads=["xt", ("sc1", si), ("mt", si)], writes=[("ub", kc)])
            for hc in range(32):
                ps, pk = newps()
                for kc in range(8):
                    P.op("pe", lambda t, ps=ps, hc=hc, kc=kc: t.matmul(ps[:, 0:nt], w1b[:, kc, hc * 128:(hc + 1) * 128], ub[:, kc, 0:nt], start=(kc == 0), stop=(kc == 7)),
                         reads=[(("w1", hc // 16), kc), ("ub", kc)], writes=[pk])
                P.op("act", lambda a, ps=ps, hc=hc: a.activation(zsq[:, hc % 8, 0:nt], ps[:, 0:nt], AF.Relu), reads=[pk], writes=[("zsq", hc % 8)])
                P.op(P.alt(), lambda v, hc=hc: v.tensor_tensor(hid[:, hc, 0:nt], zsq[:, hc % 8, 0:nt], zsq[:, hc % 8, 0:nt], ALU.mult), reads=[("zsq", hc % 8)], writes=[("hid", hc)])
            for dc in range(8):
                ps, pk = newps()
                for hc in range(32):
                    P.op("pe", lambda t, ps=ps, hc=hc, dc=dc: t.matmul(ps[:, 0:nt], w2b[:, hc, dc * 128:(dc + 1) * 128], hid[:, hc, 0:nt], start=(hc == 0), stop=(hc == 31)),
                         reads=[("w2", hc), ("hid", hc)], writes=[pk])
                P.op("dve", lambda v, dc=dc: v.tensor_scalar_mul(z[:, dc, 0:nt], xt[:, dc, 0:nt], ALPHA), reads=["xt"], writes=[("z", dc)])
                P.op("dve", lambda v, ps=ps, dc=dc, si=si: v.scalar_tensor_tensor(z[:, dc, 0:nt], ps[:, 0:nt], mt[:, si, 5, dc:dc + 1], z[:, dc, 0:nt], ALU.mult, ALU.add),
                     reads=[pk, ("z", dc), ("mt", si)], writes=[("z", dc)])
            ln_tail(P, nt, z, zsq, ones, newps, lnt[:, 0, :], lnt[:, 1, :], outs[si], (t0, t0 + nt), ost, None)
    return C.end()


def ln_layout(g, b):
    return np.ascontiguousarray(np.stack([g.reshape(8, 128).T, b.reshape(8, 128).T], axis=1))


def shard_tok(a, c):
    b, s = c // 4, c % 4
    return np.ascontiguousarray(a[b, s * 2048:(s + 1) * 2048].T)


def shard_ctx(a, c):
    b, s = c // 4, c % 4
    return np.ascontiguousarray(a[b, s * 64:(s + 1) * 64].T)


def unshard(res, key_lat, key_ctx, F):
    xl = np.empty((2, 8192, F), np.float32); xc = np.empty((2, 256, F), np.float32)
    for c in range(NCORES):
        b, s = c // 4, c % 4
        xl[b, s * 2048:(s + 1) * 2048] = res[c][key_lat].T
        if key_ctx is not None:
            xc[b, s * 64:(s + 1) * 64] = res[c][key_ctx].T
    return xl, xc


def stage_k4b(x1, h1, m_l, w1, w2, g, bb, with_ctx=True):
    segs = [(2048, 256)] + ([(64, 64)] if with_ctx else [])
    nc = build_k4b(segs)
    ones = np.ones((128, 128), np.float32)
    maps = []
    for c in range(NCORES):
        m = {"w1": w1, "w2": w2, "ln": ln_layout(g, bb), "ones": ones, "x0": shard_tok(x1, c), "mod0": mod_layout(m_l[c // 4])}
        if with_ctx:
            m["x1"] = shard_ctx(h1, c); m["mod1"] = mod_layout(m_l[2])
        maps.append(m)
    res = run_spmd(nc, maps)
    return unshard(res, "o0", "o1" if with_ctx else None, 1024)


def build_gqa(first, last, C=None, oc_rows=64, nmk=2):
    NQ, NK = 2048, 2304
    C = C or Ctx()
    qT = C.din("qT", [64, 8, NQ]); qsT = C.din("qsT", [64, 8, NQ])
    kT = C.din("kT", [64, 2, NK]); ksT = C.din("ksT", [64, 2, NK])
    cq = C.din("cq", [64, NQ]); sq = C.din("sq", [64, NQ]); ck = C.din("ck", [64, NK]); sk = C.din("sk", [64, NK])
    vv = C.din("v", [128, 18, 2, 64]); kcT = C.din("kcT", [64, 2, 256]); vc = C.din("vc", [128, 2, 2, 64])
    qcT = C.din("qcT", [64, 8, 64]); sink = C.din("sink", [128, 8]); mk = C.din("mk", [128, nmk, 512])
    o_d = C.dout("o", [NQ, 512]); oc_d = C.dout("oc", [oc_rows, 512])
    q = C.sb("q", [64, 8, NQ]); t1 = C.sb("t1", [64, 2304]); t2 = C.sb("t2", [64, 2304])
    cqs = C.sb("cqs", [64, NQ]); sqs = C.sb("sqs", [64, NQ]); cks = C.sb("cks", [64, NK]); sks = C.sb("sks", [64, NK])
    qr = C.sb("qr", [64, 8, NQ], BF16); kr = C.sb("kr", [64, 2, NK], BF16)
    kc = C.sb("kc", [64, 2, 256]); kcb = C.sb("kcb", [64, 2, 256], BF16)
    qc = C.sb("qc", [64, 8, 64]); qcb = C.sb("qcb", [64, 8, 64], BF16)
    vf = C.sb("vf", [128, 20, 2, 64]); va = C.sb("va", [128, 20, 2, 65], BF16)
    mks = C.sb("mks", [128, nmk, 512]); mkb = C.sb("mkb", [128, nmk, 512], BF16)
    es = C.sb("es", [128, 8]); pt = C.sb("pt", [128, 4, 512], BF16)
    den = C.sb("den", [128, 2, 4]); osb = C.sb("osb", [128, 2, 512])
    pss = [C.ps("ps%d" % i, [128, 512]) for i in range(8)]
    P = C.start()
    pctr = [0]

    def newps():
        i = pctr[0] % 8; pctr[0] += 1
        return pss[i], ("ps", i)
    P.dma("sp", cqs[:], cq[:, :], writes=["cqs"]); P.dma("sp", sqs[:], sq[:, :], writes=["sqs"])
    P.dma("sp", cks[:], ck[:, :], writes=["cks"]); P.dma("sp", sks[:], sk[:, :], writes=["sks"])
    P.dma("sp", vf[:, 0:18, :, :], vv[:, :, :, :], writes=["vf"]); P.dma("sp", vf[:, 18:20, :, :], vc[:, :, :, :], writes=["vf"])
    P.dma("sp", kc[:], kcT[:, :, :], writes=["kc"]); P.dma("sp", qc[:], qcT[:, :, :], writes=["qc"])
    P.dma("sp", mks[:], mk[:, :, :], writes=["mks"]); P.dma("sp", es[:], sink[:, :], writes=["es"])
    P.op("act", lambda a: a.activation(es[:], es[:], AF.Exp), reads=["es"], writes=["es"])
    P.op("dve", lambda v: v.tensor_copy(mkb[:], mks[:]), reads=["mks"], writes=["mkb"])
    P.op("dve", lambda v: v.tensor_copy(kcb[:], kc[:]), reads=["kc"], writes=["kcb"])
    P.op("dve", lambda v: v.tensor_copy(qcb[:], qc[:]), reads=["qc"], writes=["qcb"])
    P.op("pool", lambda v: v.memset(va[:], 1.0), writes=["va"])
    P.op("dve", lambda v: v.tensor_copy(va[:, :, :, 0:64], vf[:]), reads=["vf"], writes=["va"])
    for h in range(8):
        P.dma("sp", q[:, h, :], qT[:, h, :], writes=[("q", h)])
        P.dma("sp", t1[:, 0:NQ], qsT[:, h, :], writes=["t1"])
        P.op("dve", lambda v: v.tensor_tensor(t1[:, 0:NQ], t1[:, 0:NQ], sqs[:], ALU.mult), reads=["t1", "sqs"], writes=["t1"])
        P.op("pool", lambda v, h=h: v.tensor_tensor(q[:, h, :], q[:, h, :], cqs[:], ALU.mult), reads=[("q", h), "cqs"], writes=[("q", h)])
        P.op("dve", lambda v, h=h: v.tensor_tensor(qr[:, h, :], q[:, h, :], t1[:, 0:NQ], ALU.add), reads=[("q", h), "t1"], writes=[("qr", h)])
    for g in range(2):
        P.dma("sp", t2[:, :], kT[:, g, :], writes=["t2"])
        P.dma("sp", t1[:, :], ksT[:, g, :], writes=["t1"])
        P.op("dve", lambda v: v.tensor_tensor(t1[:, :], t1[:, :], sks[:], ALU.mult), reads=["t1", "sks"], writes=["t1"])
        P.op("pool", lambda v: v.tensor_tensor(t2[:, :], t2[:, :], cks[:], ALU.mult), reads=["t2", "cks"], writes=["t2"])
        P.op("dve", lambda v, g=g: v.tensor_tensor(kr[:, g, :], t2[:, :], t1[:, :], ALU.add), reads=["t2", "t1"], writes=[("kr", g)])
    it = 0
    for qb in range(17):
        ob = qb % 2
        for g in range(2):
            if qb < 16:
                tiles = []
                if not (first and qb == 0):
                    tiles.append((qb, 2 if (nmk == 4 and qb == 0) else 0))
                tiles.append((qb + 1, None))
                if not (last and qb == 15):
                    tiles.append((qb + 2, 3 if (nmk == 4 and qb == 15) else 1))
                tiles += [("c", 0), ("c", 1)]
                nq = 128
            else:
                tiles = [("c", 0), ("c", 1)]
                nq = 64
            ops, opk = newps()

            def emit_qk(ti, g=g, qb=qb, nq=nq, tiles=tiles):
                kt_, mi = tiles[ti]
                ps, pk = newps()
                if kt_ == "c":
                    lhs = kcb[:, g, mi * 128:(mi + 1) * 128]; lk = "kcb"; vi = 18 + mi; mi = None
                else:
                    lhs = kr[:, g, kt_ * 128:(kt_ + 1) * 128]; lk = ("kr", g); vi = kt_
                if qb < 16:
                    rhs = qr[:, 4 * g:4 * g + 4, qb * 128:(qb + 1) * 128]; rk_ = [("qr", 4 * g + j) for j in range(4)]
                else:
                    rhs = qcb[:, 4 * g:4 * g + 4, :]; rk_ = ["qcb"]
                P.op("pe", lambda t, ps=ps, lhs=lhs, rhs=rhs, nq=nq: t.matmul(ps[:, 0:4 * nq], lhs, rhs, start=True, stop=True), reads=[lk] + rk_, writes=[pk])
                return ps, pk, vi, mi
            pend = [emit_qk(0)]
            for ti in range(len(tiles)):
                ps, pk, vi, mi = pend.pop(0)
                if ti + 1 < len(tiles):
                    pend.append(emit_qk(ti + 1))
                pb = it % 4; it += 1
                P.op("act", lambda a, ps=ps, pb=pb, nq=nq: a.activation(pt[:, pb, 0:4 * nq], ps[:, 0:4 * nq], AF.Exp, scale=0.125), reads=[pk], writes=[("pt", pb)])
                if mi is not None:
                    P.op("dve", lambda v, pb=pb, mi=mi: v.tensor_tensor(pt[:, pb, :], pt[:, pb, :], mkb[:, mi, :], ALU.mult), reads=[("pt", pb), "mkb"], writes=[("pt", pb)])
                for j in range(4):
                    P.op("pe", lambda t, ops=ops, pb=pb, j=j, vi=vi, g=g, nq=nq, ti=ti, nt_=len(tiles): t.matmul(ops[0:nq, j * 65:(j + 1) * 65], pt[:, pb, j * nq:(j + 1) * nq], va[:, vi, g, :], start=(ti == 0 and j == 0), stop=(ti == nt_ - 1 and j == 3)),
                         reads=[("pt", pb), "va"], writes=[opk])
            P.op("dve", lambda v, ops=ops, g=g, ob=ob, nq=nq: v.tensor_tensor(den[0:nq, ob, :], ops[0:nq, 64:260:65], es[0:nq, 4 * g:4 * g + 4], ALU.add), reads=[opk, "es"], writes=[("den", ob)])
            P.op("dve", lambda v, ob=ob, nq=nq: v.reciprocal(den[0:nq, ob, :], den[0:nq, ob, :]), reads=[("den", ob)], writes=[("den", ob)])
            for j in range(4):
                hh = 4 * g + j
                P.op("dve" if j % 2 else "act", (lambda v, ops=ops, ob=ob, j=j, hh=hh, nq=nq: v.tensor_scalar_mul(osb[0:nq, ob, hh * 64:(hh + 1) * 64], ops[0:nq, j * 65:j * 65 + 64], den[0:nq, ob, j:j + 1])) if j % 2 else
                     (lambda a, ops=ops, ob=ob, j=j, hh=hh, nq=nq: a.activation(osb[0:nq, ob, hh * 64:(hh + 1) * 64], ops[0:nq, j * 65:j * 65 + 64], AF.Copy, scale=den[0:nq, ob, j:j + 1])),
                     reads=[opk, ("den", ob)], writes=[("osb", ob)])
        if qb < 16:
            P.dma("sp", o_d[qb * 128:(qb + 1) * 128, :], osb[:, ob, :], reads=[("osb", ob)])
        else:
            P.dma("sp", oc_d[:, :], osb[0:oc_rows, ob, :], reads=[("osb", ob)])
    return C.end()


GQ0, GK0, GV0 = 1568, 2080, 2208
SWAP64 = np.array([d + 16 if (d % 32) < 16 else d - 16 for d in range(64)])


def rope_tables(pos_tok):
    row = (pos_tok // 64).astype(np.float32); col = (pos_tok % 64).astype(np.float32)
    inv = (10000.0 ** (-np.arange(16, dtype=np.float32) / np.float32(16))).astype(np.float32)
    ar = row[None, :] * inv[:, None]; ac = col[None, :] * inv[:, None]
    Cc = np.concatenate([np.cos(ar), np.cos(ar), np.cos(ac), np.cos(ac)], 0).astype(np.float32)
    Ss = np.concatenate([-np.sin(ar), np.sin(ar), -np.sin(ac), np.sin(ac)], 0).astype(np.float32)
    return Cc, Ss


def stage_gqa(pl, pc, sink):
    ncs = {(f, l): build_gqa(f, l) for (f, l) in [(True, False), (False, False), (False, True)]}
    t = np.arange(128)
    mk = np.stack([np.tile((t[:, None] >= t[None, :]).astype(np.float32), (1, 4)), np.tile((t[:, None] <= t[None, :]).astype(np.float32), (1, 4))], 1)
    outs = {}
    att = np.empty((2, 8192, 512), np.float32); attc = np.empty((2, 256, 512), np.float32)
    maps = []
    for c in range(NCORES):
        b, s = c // 4, c % 4
        lo, hi = s * 2048, (s + 1) * 2048
        pos = np.arange(lo, hi); posk = np.clip(np.arange(lo - 128, hi + 128), 0, 8191)
        cq, sq = rope_tables(pos); ck, sk = rope_tables(posk)
        pk = np.zeros((2304, pl.shape[2]), np.float32)
        a0, a1 = max(lo - 128, 0), min(hi + 128, 8192)
        pk[a0 - (lo - 128):a1 - (lo - 128)] = pl[b, a0:a1]
        m = {"qT": np.ascontiguousarray(pl[b, lo:hi, GQ0:GQ0 + 512].reshape(2048, 8, 64).transpose(2, 1, 0)),
             "qsT": np.ascontiguousarray(pl[b, lo:hi, 2336:2848].reshape(2048, 8, 64).transpose(2, 1, 0)),
             "kT": np.ascontiguousarray(pk[:, GK0:GK0 + 128].reshape(2304, 2, 64).transpose(2, 1, 0)),
             "ksT": np.ascontiguousarray(pk[:, 2848:2976].reshape(2304, 2, 64).transpose(2, 1, 0)),
             "cq": cq, "sq": sq, "ck": ck, "sk": sk,
             "v": np.ascontiguousarray(pk[:, GV0:GV0 + 128].reshape(18, 128, 2, 64).transpose(1, 0, 2, 3)),
             "kcT": np.ascontiguousarray(pc[b, :, GK0:GK0 + 128].reshape(256, 2, 64).transpose(2, 1, 0)),
             "vc": np.ascontiguousarray(pc[b, :, GV0:GV0 + 128].reshape(2, 128, 2, 64).transpose(1, 0, 2, 3)),
             "qcT": np.ascontiguousarray(pc[b, s * 64:(s + 1) * 64, GQ0:GQ0 + 512].reshape(64, 8, 64).transpose(2, 1, 0)),
             "sink": np.ascontiguousarray(np.broadcast_to(sink[None, :], (128, 8))).astype(np.float32), "mk": np.ascontiguousarray(mk)}
        maps.append(m)
    for key, nc in ncs.items():
        res = run_spmd(nc, maps)
        for c in range(NCORES):
            b, s = c // 4, c % 4
            if (s == 0, s == 3) == key:
                att[b, s * 2048:(s + 1) * 2048] = res[c]["o"]
                attc[b, s * 64:(s + 1) * 64] = res[c]["oc"]
    return att, attc


def aug_w_in_ab(w_in):
    qs = np.concatenate([w_in[:, GQ0 + h * 64 + SWAP64] for h in range(8)], 1)
    ks = np.concatenate([w_in[:, GK0 + h * 64 + SWAP64] for h in range(2)], 1)
    return np.ascontiguousarray(np.concatenate([w_in, qs, ks], 1))


def build_k4a0(segs, C=None):
    C = C or Ctx()
    W = C.din("w", [1024, 1024]); lnd = C.din("ln", [128, 2, 8]); ones_d = C.din("ones", [128, 128]); gn_d = C.din("gn", [128, 1])
    xs, mods, outs, ofs, obs, ogs, ats = [], [], [], [], [], [], []
    for si, (ntok, tile) in enumerate(segs):
        xs.append(C.din("x%d" % si, [1024, ntok])); mods.append(C.din("mod%d" % si, [128, 6, 8])); outs.append(C.dout("o%d" % si, [1024, ntok]))
        ofs.append(C.din("of%d" % si, [512, ntok])); obs.append(C.din("ob%d" % si, [512, ntok])); ogs.append(C.din("og%d" % si, [512, ntok])); ats.append(C.din("at%d" % si, [512, ntok]))
    TM = max(t for _, t in segs)
    wb = C.sb("wb", [128, 8, 1024], BF16); stage = C.sb("stage", [128, 2, 1024])
    xt = C.sb("xt", [128, 8, TM]); mix = C.sb("mix", [128, 8, TM], BF16)
    a = C.sb("a", [128, 4, TM]); b_ = C.sb("b", [128, 4, TM]); g_ = C.sb("g", [128, 4, TM]); at = C.sb("at", [128, 4, TM])
    z = C.sb("z", [128, 8, TM]); zsq = C.sb("zsq", [128, 8, TM]); ost = C.sb("ost", [128, 2, TM])
    ones = C.sb("ones", [128, 128]); lnt = C.sb("lnt", [128, 2, 8]); gn = C.sb("gn", [128, 1])
    mt = C.sb("mt", [128, len(segs), 6, 8])
    pss = [C.ps("ps%d" % i, [128, 512]) for i in range(8)]
    P = C.start()
    pctr = [0]

    def newps():
        i = pctr[0] % 8; pctr[0] += 1
        return pss[i], ("ps", i)
    P.dma("sp", ones[:], ones_d[:, :], writes=["ones"]); P.dma("sp", lnt[:], lnd[:, :, :], writes=["ln"]); P.dma("sp", gn[:], gn_d[:, :], writes=["gn"])
    for si in range(len(segs)):
        P.dma("sp", mt[:, si, :, :], mods[si][:, :, :], writes=[("mt", si)])
    load_w_bf16(C, P, wb, "wb", W, 1024, 1024, stage, "stage")
    for si, (ntok, tile) in enumerate(segs):
        v3 = lambda ap: ap.rearrange("(kc p) t -> p kc t", p=128)
        for t0 in range(0, ntok, tile):
            nt = tile
            P.dma("sp", xt[:, :, 0:nt], v3(xs[si])[:, :, t0:t0 + nt], writes=["xt"])
            P.dma("sp", a[:, :, 0:nt], v3(ofs[si])[:, :, t0:t0 + nt], writes=["a"])
            P.dma("sp", b_[:, :, 0:nt], v3(obs[si])[:, :, t0:t0 + nt], writes=["b"])
            P.dma("sp", g_[:, :, 0:nt], v3(ogs[si])[:, :, t0:t0 + nt], writes=["g"])
            P.dma("sp", at[:, :, 0:nt], v3(ats[si])[:, :, t0:t0 + nt], writes=["at"])
            P.op("dve", lambda v: v.tensor_tensor(a[:, :, 0:nt], a[:, :, 0:nt], b_[:, :, 0:nt], ALU.add), reads=["a", "b"], writes=["a"])
            P.op("act", lambda a_: a_.activation(b_[:, :, 0:nt], a[:, :, 0:nt], AF.Square), reads=["a"], writes=["b"])
            P.op("act", lambda a_: a_.activation(g_[:, :, 0:nt], g_[:, :, 0:nt], AF.Silu), reads=["g"], writes=["g"])
            P.op("pool", lambda v: v.tensor_copy(mix[:, 4:8, 0:nt], at[:, :, 0:nt]), reads=["at"], writes=[("mix", k) for k in range(4, 8)])
            for h in range(4):
                ps, pk = newps()
                P.op("pe", lambda t, ps=ps, h=h: t.matmul(ps[:, 0:nt], ones[:, :], b_[:, h, 0:nt], start=True, stop=True), reads=["b", "ones"], writes=[pk])
                P.op("dve", lambda v, ps=ps, h=h: v.tensor_scalar(zsq[:, h, 0:nt], ps[:, 0:nt], 1.0 / 128.0, LN_EPS, ALU.mult, ALU.add), reads=[pk], writes=[("zsq", h)])
                P.op("act", lambda a_, h=h: a_.activation(zsq[:, h, 0:nt], zsq[:, h, 0:nt], AF.Sqrt), reads=[("zsq", h)], writes=[("zsq", h)])
                P.op("dve", lambda v, h=h: v.reciprocal(zsq[:, h, 0:nt], zsq[:, h, 0:nt]), reads=[("zsq", h)], writes=[("zsq", h)])
                P.op("dve", lambda v, h=h: v.tensor_tensor(zsq[:, h, 0:nt], zsq[:, h, 0:nt], a[:, h, 0:nt], ALU.mult), reads=[("zsq", h), "a"], writes=[("zsq", h)])
                P.op("dve", lambda v, h=h: v.scalar_tensor_tensor(mix[:, h, 0:nt], zsq[:, h, 0:nt], gn[:, 0:1], g_[:, h, 0:nt], ALU.mult, ALU.mult), reads=[("zsq", h), "g", "gn"], writes=[("mix", h)])
            for dc in range(8):
                ps, pk = newps()
                for kc in range(8):
                    P.op("pe", lambda t, ps=ps, kc=kc, dc=dc: t.matmul(ps[:, 0:nt], wb[:, kc, dc * 128:(dc + 1) * 128], mix[:, kc, 0:nt], start=(kc == 0), stop=(kc == 7)),
                         reads=[("wb", kc), ("mix", kc)], writes=[pk])
                P.op("dve", lambda v, dc=dc: v.tensor_scalar_mul(z[:, dc, 0:nt], xt[:, dc, 0:nt], ALPHA), reads=["xt"], writes=[("z", dc)])
                P.op("dve", lambda v, ps=ps, dc=dc, si=si: v.scalar_tensor_tensor(z[:, dc, 0:nt], ps[:, 0:nt], mt[:, si, 2, dc:dc + 1], z[:, dc, 0:nt], ALU.mult, ALU.add),
                     reads=[pk, ("z", dc), ("mt", si)], writes=[("z", dc)])
            ln_tail(P, nt, z, zsq, ones, newps, lnt[:, 0, :], lnt[:, 1, :], outs[si], (t0, t0 + nt), ost, None)
    return C.end()


def stage_k4a0(x, ctx, m_l, o_f, o_b, pl, pc, att, attc, w_out, gla_norm, g, bb):
    nc = build_k4a0([(2048, 256), (64, 64)])
    ones = np.ones((128, 128), np.float32)
    def heads_to_tok(o):
        return np.ascontiguousarray(o.transpose(0, 2, 1, 3).reshape(2, o.shape[2], 512))
    ofl, obl = heads_to_tok(o_f[:, :, 256:]), heads_to_tok(o_b[:, :, 256:])
    ofc, obc = heads_to_tok(o_f[:, :, :256]), heads_to_tok(o_b[:, :, :256])
    maps = []
    for c in range(NCORES):
        maps.append({"w": w_out, "ln": ln_layout(g, bb), "ones": ones, "gn": np.ascontiguousarray(gla_norm.reshape(128, 1)),
                     "x0": shard_tok(x, c), "mod0": mod_layout(m_l[c // 4]), "x1": shard_ctx(ctx, c), "mod1": mod_layout(m_l[2]),
                     "of0": shard_tok(ofl, c), "ob0": shard_tok(obl, c), "og0": shard_tok(pl[:, :, 1056:1568], c), "at0": shard_tok(att, c),
                     "of1": shard_ctx(ofc, c), "ob1": shard_ctx(obc, c), "og1": shard_ctx(pc[:, :, 1056:1568], c), "at1": shard_ctx(attc, c)})
    res = run_spmd(nc, maps)
    return unshard(res, "o0", "o1", 1024)


def build_rwf(segs, C=None, NH=4):
    NCHK = 3 * NH + 3
    C = C or Ctx()
    Wl = C.din("wl", [128, 512 * NH]); sc_d = C.din("sc", [128, NCHK + 5 * NH]); bo_d = C.din("bo", [128, 128])
    ps_in, outs = [], []
    names = ["r", "km", "v", "an", "bb", "wf", "wb", "g"]
    for si, (ntok, tile) in enumerate(segs):
        ps_in.append(C.din("p%d" % si, [NCHK * 128, ntok + 2]))
        outs.append({n: C.dout("%s%d" % (n, si), [128 * NH, ntok]) for n in names})
    TM = max(t for _, t in segs)
    wl = C.sb("wl", [128, 1, 512 * NH], BF16); stage = C.sb("stage", [128, 2, 512 * NH])
    sc = C.sb("sc", [128, NCHK + 5 * NH]); hmu = C.sb("hmu", [128, NCHK]); omm = C.sb("omm", [128, NCHK]); omka = C.sb("omka", [128, NH])
    bo = C.sb("bo", [128, 128])
    pt = C.sb("pt", [128, NCHK, TM + 2]); sh = C.sb("sh", [128, NCHK, TM]); tmp = C.sb("tmp", [128, 2, TM])
    th = C.sb("th", [128, 3, TM], BF16)
    o = C.sb("o", [128, 6, NH, TM]); ag = C.sb("ag", [128, NH, TM]); kx = C.sb("kx", [128, 2, TM]); sq = C.sb("sq", [128, 2, TM])
    pss = [C.ps("ps%d" % i, [128, 512]) for i in range(8)]
    P = C.start()
    pctr = [0]

    def newps():
        i = pctr[0] % 8; pctr[0] += 1
        return pss[i], ("ps", i)
    P.dma("sp", sc[:], sc_d[:, :], writes=["sc"]); P.dma("sp", bo[:], bo_d[:, :], writes=["bo"])
    load_w_bf16(C, P, wl, "wl", Wl, 128, 512 * NH, stage, "stage")
    P.op("dve", lambda v: v.tensor_scalar_mul(hmu[:], sc[:, 0:NCHK], 0.5), reads=["sc"], writes=["hmu"])
    P.op("dve", lambda v: v.tensor_scalar(omm[:], sc[:, 0:NCHK], -1.0, 1.0, ALU.mult, ALU.add), reads=["sc"], writes=["omm"])
    P.op("dve", lambda v: v.tensor_scalar(omka[:], sc[:, NCHK + 4 * NH:NCHK + 5 * NH], -1.0, 1.0, ALU.mult, ALU.add), reads=["sc"], writes=["omka"])
    W0F, W0B, A0, KK, KA = NCHK, NCHK + NH, NCHK + 2 * NH, NCHK + 3 * NH, NCHK + 4 * NH
    for si, (ntok, tile) in enumerate(segs):
        pv = ps_in[si].rearrange("(kc p) t -> p kc t", p=128)
        for t0 in range(0, ntok, tile):
            nt = tile
            P.dma("sp", pt[:, :, 0:nt + 2], pv[:, :, t0:t0 + nt + 2], writes=["pt"])
            for ch in range(NCHK):
                e = P.alt(); tb = ch % 2
                P.op(e, lambda v, ch=ch, tb=tb: v.tensor_tensor(tmp[:, tb, 0:nt], pt[:, ch, 0:nt], pt[:, ch, 2:nt + 2], ALU.add), reads=["pt"], writes=[("tmp", tb)])
                P.op(e, lambda v, ch=ch, tb=tb: v.tensor_scalar_mul(tmp[:, tb, 0:nt], tmp[:, tb, 0:nt], hmu[:, ch:ch + 1]), reads=[("tmp", tb), "hmu"], writes=[("tmp", tb)])
                P.op("dve", lambda v, ch=ch, tb=tb: v.scalar_tensor_tensor(sh[:, ch, 0:nt], pt[:, ch, 1:nt + 1], omm[:, ch:ch + 1], tmp[:, tb, 0:nt], ALU.mult, ALU.add), reads=["pt", ("tmp", tb), "omm"], writes=[("sh", ch)])
            P.op("act", lambda a: a.activation(th[:, 0, 0:nt], sh[:, 3 * NH, 0:nt], AF.Tanh), reads=[("sh", 3 * NH)], writes=[("th", 0)])
            P.op("act", lambda a: a.copy(th[:, 1, 0:nt], sh[:, 3 * NH + 1, 0:nt]), reads=[("sh", 3 * NH + 1)], writes=[("th", 1)])
            P.op("act", lambda a: a.activation(th[:, 2, 0:nt], sh[:, 3 * NH + 2, 0:nt], AF.Sigmoid), reads=[("sh", 3 * NH + 2)], writes=[("th", 2)])
            for c in range(NH):
                for (oi, wc0, ti_, bcol) in ((3, 0, 0, W0F), (4, 128 * NH, 0, W0B)):
                    ps, pk = newps()
                    P.op("pe", lambda t, ps=ps, wc0=wc0, c=c, ti_=ti_: t.matmul(ps[:, 0:nt], wl[:, 0, wc0 + c * 128:wc0 + (c + 1) * 128], th[:, ti_, 0:nt], start=True, stop=True), reads=[("wl", 0), ("th", ti_)], writes=[pk])
                    P.op("act", lambda a, ps=ps, oi=oi, c=c, bcol=bcol: a.activation(o[:, oi, c, 0:nt], ps[:, 0:nt], AF.Sigmoid, bias=sc[:, bcol + c:bcol + c + 1]), reads=[pk, "sc"], writes=[("o", oi, c)])
                    P.op("act", lambda a, oi=oi, c=c: a.activation(o[:, oi, c, 0:nt], o[:, oi, c, 0:nt], AF.Exp, scale=-0.6065306597126334), reads=[("o", oi, c)], writes=[("o", oi, c)])
                ps, pk = newps()
                P.op("pe", lambda t, ps=ps, c=c: t.matmul(ps[:, 0:nt], wl[:, 0, 256 * NH + c * 128:256 * NH + (c + 1) * 128], th[:, 1, 0:nt], start=True, stop=True), reads=[("wl", 0), ("th", 1)], writes=[pk])
                P.op("act", lambda a, ps=ps, c=c: a.activation(ag[:, c, 0:nt], ps[:, 0:nt], AF.Sigmoid, bias=sc[:, A0 + c:A0 + c + 1]), reads=[pk, "sc"], writes=[("ag", c)])
                ps, pk = newps()
                P.op("pe", lambda t, ps=ps, c=c: t.matmul(ps[:, 0:nt], wl[:, 0, 384 * NH + c * 128:384 * NH + (c + 1) * 128], th[:, 2, 0:nt], start=True, stop=True), reads=[("wl", 0), ("th", 2)], writes=[pk])
                P.op("act", lambda a, ps=ps, c=c: a.copy(o[:, 5, c, 0:nt], ps[:, 0:nt]), reads=[pk], writes=[("o", 5, c)])
                kb_ = c % 2
                P.op("dve", lambda v, c=c, kb_=kb_: v.tensor_scalar_mul(kx[:, kb_, 0:nt], sh[:, NH + c, 0:nt], sc[:, KK + c:KK + c + 1]), reads=[("sh", NH + c), "sc"], writes=[("kx", kb_)])
                P.op("pool", lambda v, kb_=kb_: v.tensor_tensor(sq[:, kb_, 0:nt], kx[:, kb_, 0:nt], kx[:, kb_, 0:nt], ALU.mult), reads=[("kx", kb_)], writes=[("sq", kb_)])
                ps, pk = newps()
                P.op("pe", lambda t, ps=ps, kb_=kb_: t.matmul(ps[:, 0:nt], bo[:, :], sq[:, kb_, 0:nt], start=True, stop=True), reads=["bo", ("sq", kb_)], writes=[pk])
                P.op("act", lambda a, ps=ps, kb_=kb_: a.activation(sq[:, kb_, 0:nt], ps[:, 0:nt], AF.Sqrt), reads=[pk], writes=[("sq", kb_)])
                P.op("dve", lambda v, kb_=kb_: v.tensor_scalar_max(sq[:, kb_, 0:nt], sq[:, kb_, 0:nt], 1e-12), reads=[("sq", kb_)], writes=[("sq", kb_)])
                P.op("dve", lambda v, kb_=kb_: v.reciprocal(sq[:, kb_, 0:nt], sq[:, kb_, 0:nt]), reads=[("sq", kb_)], writes=[("sq", kb_)])
                P.op("dve", lambda v, c=c, kb_=kb_: v.scalar_tensor_tensor(o[:, 1, c, 0:nt], kx[:, kb_, 0:nt], -1.0, sq[:, kb_, 0:nt], ALU.mult, ALU.mult), reads=[("kx", kb_), ("sq", kb_)], writes=[("o", 1, c)])
                P.op("dve", lambda v, c=c: v.scalar_tensor_tensor(o[:, 2, c, 0:nt], o[:, 1, c, 0:nt], -1.0, ag[:, c, 0:nt], ALU.mult, ALU.mult), reads=[("o", 1, c), ("ag", c)], writes=[("o", 2, c)])
                P.op("pool", lambda v, c=c: v.tensor_scalar(ag[:, c, 0:nt], ag[:, c, 0:nt], sc[:, KA + c:KA + c + 1], omka[:, c:c + 1], ALU.mult, ALU.add), reads=[("ag", c), "sc", "omka"], writes=[("ag", c)])
                P.op("pool", lambda v, c=c: v.tensor_tensor(o[:, 0, c, 0:nt], sh[:, NH + c, 0:nt], ag[:, c, 0:nt], ALU.mult), reads=[("sh", NH + c), ("ag", c)], writes=[("o", 0, c)])
            v3 = lambda ap: ap.rearrange("(kc p) t -> p kc t", p=128)
            P.dma("sp", v3(outs[si]["r"])[:, :, t0:t0 + nt], sh[:, 0:NH, 0:nt], reads=[("sh", k) for k in range(NH)])
            P.dma("sp", v3(outs[si]["v"])[:, :, t0:t0 + nt], sh[:, 2 * NH:3 * NH, 0:nt], reads=[("sh", k) for k in range(2 * NH, 3 * NH)])
            for oi, n in ((0, "km"), (1, "an"), (2, "bb"), (3, "wf"), (4, "wb"), (5, "g")):
                P.dma("sp", v3(outs[si][n])[:, :, t0:t0 + nt], o[:, oi, :, 0:nt], reads=[("o", oi, c) for c in range(NH)])
    return C.end()


def cd_w_in_layout(w):
    z64 = np.zeros((1024, 64), np.float32)
    kpe = w[:, 2240:2272]
    sw = np.array([d + 8 if (d % 16) < 8 else d - 8 for d in range(32)])
    return np.ascontiguousarray(np.concatenate([w[:, 0:1536], w[:, 1536:1664], w[:, 1664:1728], z64, w[:, 1728:1856],
                                                w[:, 1856:2112], w[:, 2112:2240], kpe, kpe[:, sw], z64], 1))


def mu_layout(mu):
    z64 = np.zeros(64, np.float32)
    m = np.concatenate([mu[0:1536], mu[1536:1664], mu[1664:1728], z64, mu[1728:1856]])
    return m.reshape(15, 128).T


def stage_rwf(pl, pc, I):
    nc = build_rwf([(2048, 256), (64, 64)])
    w2, a2, g2 = I['rwkv_w2'][0], I['rwkv_a2'][0], I['rwkv_g2'][0]
    wl = np.zeros((128, 2048), np.float32)
    wl[0:64, 0:512] = w2[0]; wl[64:128, 512:1024] = w2[1]; wl[0:64, 1024:1536] = a2; wl[:, 1536:2048] = g2
    col = lambda v: v.reshape(4, 128).T
    sc = np.concatenate([mu_layout(I['rwkv_mu'][0]), col(I['rwkv_w0'][0, 0]), col(I['rwkv_w0'][0, 1]), col(I['rwkv_a0'][0]), col(I['rwkv_kk'][0]), col(I['rwkv_ka'][0])], 1).astype(np.float32)
    t = np.arange(128)
    bo = ((t[:, None] // 64) == (t[None, :] // 64)).astype(np.float32)
    maps = []
    for c in range(NCORES):
        b, s = c // 4, c % 4
        pp = np.zeros((2050, 1920), np.float32)
        a0, a1 = max(s * 2048 - 1, 0), min((s + 1) * 2048 + 1, 8192)
        pp[a0 - (s * 2048 - 1):a1 - (s * 2048 - 1)] = pl[b, a0:a1, :1920]
        pq = np.zeros((66, 1920), np.float32)
        a0, a1 = max(s * 64 - 1, 0), min((s + 1) * 64 + 1, 256)
        pq[a0 - (s * 64 - 1):a1 - (s * 64 - 1)] = pc[b, a0:a1, :1920]
        maps.append({"wl": wl, "sc": np.ascontiguousarray(sc), "bo": bo, "p0": np.ascontiguousarray(pp.T), "p1": np.ascontiguousarray(pq.T)})
    res = run_spmd(nc, maps)
    out = {}
    for n in ["r", "km", "v", "an", "bb", "wf", "wb", "g"]:
        xl, xc = unshard(res, n + "0", n + "1", 512)
        out[n] = np.concatenate([xc, xl], 1)
    return out


def build_rws(T, Tskip, C=None):
    NB = T // 64
    C = C or Ctx()
    Ab = C.din("Ab", [128, T, 2]); Rb = C.din("Rb", [128, T, 2]); Wt = C.din("Wt", [128, T])
    BK = C.din("BK", [4, T, 128]); VVd = C.din("VV", [4, T, 64])
    y_d = C.dout("y", [2, T, 64])
    ab = C.sb("ab", [128, 2, 64, 2]); rb = C.sb("rb", [128, 2, 64, 2]); wt = C.sb("wt", [128, 2, 64])
    bk = C.sb("bk", [4, 2, 64, 128]); vv = C.sb("vv", [4, 2, 64, 64]); ysb = C.sb("ysb", [2, 2, 64, 64])
    ST = C.sb("ST", [128, 64])
    pa = [C.ps("pa%d" % i, [128, 512]) for i in range(2)]
    po = [C.ps("po%d" % i, [128, 512]) for i in range(2)]
    py = [C.ps("py%d" % i, [128, 512]) for i in range(2)]
    P = C.start()
    P.op("dve", lambda v: v.memset(ST[:], 0.0), writes=["ST"])
    P.op("dve", lambda v: v.memset(ysb[:], 0.0), writes=[("ysb", 0), ("ysb", 1)])
    for blk in range(NB):
        b = blk % 2
        s0 = blk * 64
        P.dma("sp", ab[:, b, :, :], Ab[:, s0:s0 + 64, :], writes=[("ab", b)])
        P.dma("sp", rb[:, b, :, :], Rb[:, s0:s0 + 64, :], writes=[("rb", b)])
        P.dma("sp", wt[:, b, :], Wt[:, s0:s0 + 64], writes=[("wt", b)])
        P.dma("sp", bk[:, b, :, :], BK[:, s0:s0 + 64, :], writes=[("bk", b)])
        P.dma("sp", vv[:, b, :, :], VVd[:, s0:s0 + 64, :], writes=[("vv", b)])
        for t in range(64):
            i2 = t % 2
            P.op("pe", lambda e, b=b, t=t, i2=i2: e.matmul(pa[i2][0:2, 0:64], ab[:, b, t, :], ST[:, :], start=True, stop=True), reads=[("ab", b), "ST"], writes=[("pa", i2)])
            P.op("act", lambda a, b=b, t=t, i2=i2: a.copy(vv[0:2, b, t, :], pa[i2][0:2, 0:64]), reads=[("pa", i2)], writes=[("vv", b)])
            P.op("pe", lambda e, b=b, t=t, i2=i2: e.matmul(po[i2][:, 0:64], bk[:, b, t, :], vv[:, b, t, :], start=True, stop=True), reads=[("bk", b), ("vv", b)], writes=[("po", i2)])
            P.op("dve", lambda v, b=b, t=t, i2=i2: v.scalar_tensor_tensor(ST[:, :], ST[:, :], wt[:, b, t:t + 1], po[i2][:, 0:64], ALU.mult, ALU.add), reads=["ST", ("wt", b), ("po", i2)], writes=["ST"])
            if s0 >= Tskip:
                g8 = (t // 8) % 2
                P.op("pe", lambda e, b=b, t=t, g8=g8: e.matmul(py[g8][0:2, (t % 8) * 64:(t % 8 + 1) * 64], rb[:, b, t, :], ST[:, :], start=True, stop=True), reads=[("rb", b), "ST"], writes=[("py", g8)])
                if t % 8 == 7:
                    P.op("act", lambda a, b=b, t=t, g8=g8: a.copy(ysb[:, b, t - 7:t + 1, :], py[g8][0:2, 0:512].rearrange("p (a c) -> p a c", c=64)), reads=[("py", g8)], writes=[("ysb", b)])
        P.dma("sp", y_d[:, s0:s0 + 64, :], ysb[:, b, :, :], reads=[("ysb", b)])
    return C.end()


def stage_rws(F):
    T = 8448
    nc = build_rws(T, 256)
    ys = []
    for d_, wname in ((0, "wf"), (1, "wb")):
        def st(a):
            return a if d_ == 0 else np.concatenate([a[:, :256][:, ::-1], a[:, 256:][:, ::-1]], 1)
        r, km, v, an, bb, w = [st(F[n]) for n in ("r", "km", "v", "an", "bb", wname)]
        maps = []
        for c in range(NCORES):
            b, hp = c // 4, c % 4
            Ab = np.zeros((128, T, 2), np.float32); Rb = np.zeros((128, T, 2), np.float32); Wt = np.zeros((128, T), np.float32)
            BK = np.zeros((4, T, 128), np.float32); VV = np.zeros((4, T, 64), np.float32)
            for ci in range(2):
                h = 2 * hp + ci
                sl = slice(h * 64, (h + 1) * 64)
                Ab[ci * 64:(ci + 1) * 64, :, ci] = an[b, :, sl].T
                Rb[ci * 64:(ci + 1) * 64, :, ci] = r[b, :, sl].T
                Wt[ci * 64:(ci + 1) * 64, :] = w[b, :, sl].T
                BK[ci, :, ci * 64:(ci + 1) * 64] = bb[b, :, sl]
                BK[2 + ci, :, ci * 64:(ci + 1) * 64] = km[b, :, sl]
                VV[2 + ci] = v[b, :, sl]
            maps.append({"Ab": Ab, "Rb": Rb, "Wt": Wt, "BK": BK, "VV": VV})
        res = run_spmd(nc, maps)
        y = np.empty((2, 8192, 512), np.float32)
        for c in range(NCORES):
            b, hp = c // 4, c % 4
            yy = res[c]["y"][:, 256:]
            if d_ == 1:
                yy = yy[:, ::-1]
            for ci in range(2):
                h = 2 * hp + ci
                y[b, :, h * 64:(h + 1) * 64] = yy[ci]
        ys.append(y)
    return ys[0], ys[1]


def build_mlf(ntok, tile, C=None, with_q=True, with_kv=True):
    C = C or Ctx()
    p_d = C.din("p", [640, ntok]); wuq_d = C.din("wuq", [256, 1536]); wukv_d = C.din("wukv", [128, 1024])
    gq_d = C.din("gq", [128, 3]); ones_d = C.din("ones", [128, 128])
    cq_d = C.din("cq", [96, ntok]); sq_d = C.din("sq", [96, ntok])
    q_o = C.dout("q", [96, 8, ntok]); kn_o = C.dout("kn", [64, 8, ntok]); kpe_o = C.dout("kpe", [32, ntok]); v_o = C.dout("v", [ntok, 512])
    wuq = C.sb("wuq", [128, 2, 1536], BF16); wukv = C.sb("wukv", [128, 1, 1024], BF16); stage = C.sb("stage", [128, 2, 1536])
    gq = C.sb("gq", [128, 3]); ones = C.sb("ones", [128, 128])
    pt = C.sb("pt", [128, 5, tile]); sqq = C.sb("sqq", [128, 3, tile]); rs = C.sb("rs", [128, 2, tile])
    qn = C.sb("qn", [128, 3, tile], BF16)
    cq = C.sb("cq", [96, tile]); sq = C.sb("sq", [96, tile])
    qo = C.sb("qo", [96, 8, tile]); qs = C.sb("qs", [96, 2, tile]); ko = C.sb("ko", [64, 8, tile]); kp = C.sb("kp", [32, 2, tile]); vo = C.sb("vo", [128, 2, 512])
    pss = [C.ps("ps%d" % i, [128, 512]) for i in range(8)]
    P = C.start()
    pctr = [0]

    def newps():
        i = pctr[0] % 8; pctr[0] += 1
        return pss[i], ("ps", i)
    P.dma("sp", gq[:], gq_d[:, :], writes=["gq"]); P.dma("sp", ones[:], ones_d[:, :], writes=["ones"])
    load_w_bf16(C, P, wuq, "wuq", wuq_d, 256, 1536, stage, "stage")
    load_w_bf16(C, P, wukv, "wukv", wukv_d, 128, 1024, stage, "stage")
    pv = p_d.rearrange("(kc p) t -> p kc t", p=128)
    for t0 in range(0, ntok, tile):
        nt = tile
        P.dma("sp", pt[:, :, :], pv[:, :, t0:t0 + nt], writes=["pt"])
        P.dma("sp", cq[:], cq_d[:, t0:t0 + nt], writes=["cq"]); P.dma("sp", sq[:], sq_d[:, t0:t0 + nt], writes=["sq"])
        P.op("act", lambda a: a.activation(sqq[:, :, :], pt[:, 0:3, :], AF.Square), reads=["pt"], writes=["sqq"])
        for (ri, chs, n) in ((0, (0, 1), 256.0), (1, (2,), 128.0)):
            ps, pk = newps()
            for i, ch in enumerate(chs):
                P.op("pe", lambda t, ps=ps, ch=ch, i=i, chs=chs: t.matmul(ps[:, 0:nt], ones[:, :], sqq[:, ch, :], start=(i == 0), stop=(i == len(chs) - 1)), reads=["ones", "sqq"], writes=[pk])
            P.op("dve", lambda v, ps=ps, ri=ri, n=n: v.tensor_scalar(rs[:, ri, :], ps[:, 0:nt], 1.0 / n, 1e-6, ALU.mult, ALU.add), reads=[pk], writes=[("rs", ri)])
            P.op("act", lambda a, ri=ri: a.activation(rs[:, ri, :], rs[:, ri, :], AF.Sqrt), reads=[("rs", ri)], writes=[("rs", ri)])
            P.op("dve", lambda v, ri=ri: v.reciprocal(rs[:, ri, :], rs[:, ri, :]), reads=[("rs", ri)], writes=[("rs", ri)])
            for ch in chs:
                P.op("dve", lambda v, ch=ch, ri=ri: v.scalar_tensor_tensor(qn[:, ch, :], pt[:, ch, :], gq[:, ch:ch + 1], rs[:, ri, :], ALU.mult, ALU.mult), reads=["pt", "gq", ("rs", ri)], writes=[("qn", ch)])
        for h in range(8 if with_q else 0):
            ps, pk = newps()
            for kc in range(2):
                P.op("pe", lambda t, ps=ps, h=h, kc=kc: t.matmul(ps[0:96, 0:nt], wuq[:, kc, h * 96:(h + 1) * 96], qn[:, kc, :], start=(kc == 0), stop=(kc == 1)), reads=[("wuq", kc), ("qn", kc)], writes=[pk])
            P.op("act", lambda a, ps=ps, h=h: a.copy(qo[:, h, :], ps[0:96, 0:nt]), reads=[pk], writes=[("qo", h)])
            ps2, pk2 = newps()
            for kc in range(2):
                P.op("pe", lambda t, ps2=ps2, h=h, kc=kc: t.matmul(ps2[0:96, 0:nt], wuq[:, kc, 768 + h * 96:768 + (h + 1) * 96], qn[:, kc, :], start=(kc == 0), stop=(kc == 1)), reads=[("wuq", kc), ("qn", kc)], writes=[pk2])
            sb_ = h % 2
            P.op("dve", lambda v, ps2=ps2, sb_=sb_: v.tensor_tensor(qs[64:96, sb_, :], ps2[64:96, 0:nt], sq[64:96, :], ALU.mult), reads=[pk2, "sq"], writes=[("qs", sb_)])
            P.op("dve", lambda v, h=h: v.tensor_tensor(qo[64:96, h, :], qo[64:96, h, :], cq[64:96, :], ALU.mult), reads=[("qo", h), "cq"], writes=[("qo", h)])
            P.op("dve", lambda v, h=h, sb_=sb_: v.tensor_tensor(qo[64:96, h, :], qo[64:96, h, :], qs[64:96, sb_, :], ALU.add), reads=[("qo", h), ("qs", sb_)], writes=[("qo", h)])
        for h in range(8 if with_kv else 0):
            ps3, pk3 = newps()
            P.op("pe", lambda t, ps3=ps3, h=h: t.matmul(ps3[0:64, 0:nt], wukv[:, 0, h * 64:(h + 1) * 64], qn[:, 2, :], start=True, stop=True), reads=[("wukv", 0), ("qn", 2)], writes=[pk3])
            P.op("act" if h % 2 else "dve", (lambda a, ps3=ps3, h=h: a.copy(ko[:, h, :], ps3[0:64, 0:nt])) if h % 2 else (lambda v, ps3=ps3, h=h: v.tensor_copy(ko[:, h, :], ps3[0:64, 0:nt])), reads=[pk3], writes=[("ko", h)])
        P.op("dve", lambda v: v.tensor_tensor(kp[:, 0, :], pt[0:32, 3, :], cq[0:32, :], ALU.mult), reads=["pt", "cq"], writes=[("kp", 0)])
        P.op("dve", lambda v: v.tensor_tensor(kp[:, 1, :], pt[0:32, 4, :], sq[0:32, :], ALU.mult), reads=["pt", "sq"], writes=[("kp", 1)])
        P.op("dve", lambda v: v.tensor_tensor(kp[:, 0, :], kp[:, 0, :], kp[:, 1, :], ALU.add), reads=[("kp", 0), ("kp", 1)], writes=[("kp", 0)])
        for sub in range(nt // 128 if with_kv else 0):
            ps4, pk4 = newps()
            vb_ = sub % 2
            P.op("pe", lambda t, ps4=ps4, sub=sub: t.matmul(ps4[:, 0:512], qn[:, 2, sub * 128:(sub + 1) * 128], wukv[:, 0, 512:1024], start=True, stop=True), reads=[("wukv", 0), ("qn", 2)], writes=[pk4])
            P.op("act", lambda a, ps4=ps4, vb_=vb_: a.copy(vo[:, vb_, :], ps4[:, 0:512]), reads=[pk4], writes=[("vo", vb_)])
            P.dma("sp", v_o[t0 + sub * 128:t0 + (sub + 1) * 128, :], vo[:, vb_, :], reads=[("vo", vb_)])
        if with_q:
            P.dma("sp", q_o[:, :, t0:t0 + nt], qo[:, :, :], reads=[("qo", h) for h in range(8)])
        if with_kv:
            P.dma("sp", kn_o[:, :, t0:t0 + nt], ko[:, :, :], reads=[("ko", h) for h in range(8)])
            P.dma("sp", kpe_o[:, t0:t0 + nt], kp[:, 0, :], reads=[("kp", 0)])
    return C.end()


def build_mla(C=None, split_k=False):
    NQ, NKT = 2048, 66
    C = C or Ctx()
    q_d = C.din("q", [96, 8, NQ]); v_d = C.din("v", [128, NKT, 8, 64])
    if split_k:
        kn_d = C.din("kn", [64, 8, NKT * 128]); kpe_d = C.din("kpe", [32, NKT * 128])
    else:
        k_d = C.din("k", [8, 96, NKT * 128])
    o_d = C.dout("o", [NQ, 512])
    qf = C.sb("qf", [96, 2, NQ]); qb = C.sb("qb", [96, 2, NQ], BF16)
    kf = C.sb("kf", [96, 2, 2112]); kb = C.sb("kb", [96, 2, NKT * 128], BF16)
    vf = C.sb("vf", [128, NKT, 64]); va = C.sb("va", [128, 2, NKT, 65], BF16)
    pt = C.sb("pt", [128, 4, 512], BF16); den = C.sb("den", [128, 2, 4]); osb = C.sb("osb", [128, 16, 512])
    pss = [C.ps("ps%d" % i, [128, 512]) for i in range(8)]
    P = C.start()
    P.op("pool", lambda v: v.memset(va[:], 1.0), writes=[("va", 0), ("va", 1)])
    it = 0; sc = 0; oc = 0
    for h in range(8):
        hb = h % 2
        P.dma("sp", qf[:, hb, :], q_d[:, h, :], writes=[("qf", hb)])
        P.op("dve", lambda v, hb=hb: v.tensor_copy(qb[:, hb, :], qf[:, hb, :]), reads=[("qf", hb)], writes=[("qb", hb)])
        for part in range(4):
            pb_ = part % 2
            if split_k:
                P.dma("sp", kf[0:64, pb_, :], kn_d[:, h, part * 2112:(part + 1) * 2112], writes=[("kf", pb_)])
                P.dma("sp", kf[64:96, pb_, :], kpe_d[:, part * 2112:(part + 1) * 2112], writes=[("kf", pb_)])
            else:
                P.dma("sp", kf[:, pb_, :], k_d[h, :, part * 2112:(part + 1) * 2112], writes=[("kf", pb_)])
            P.op("pool" if part % 2 else "dve", lambda v, hb=hb, pb_=pb_, part=part: v.tensor_copy(kb[:, hb, part * 2112:(part + 1) * 2112], kf[:, pb_, :]), reads=[("kf", pb_)], writes=[("kb", hb)])
        P.dma("sp", vf[:, :, :], v_d[:, :, h, :], writes=["vf"])
        P.op("dve", lambda v, hb=hb: v.tensor_copy(va[:, hb, :, 0:64], vf[:, :, :]), reads=["vf"], writes=[("va", hb)])
        for qt in range(4):
            ops = pss[4 + oc % 4]; opk = ("ps", 4 + oc % 4); oc += 1
            def emit_qk(kt, hb=hb, qt=qt):
                nonlocal sc
                ps = pss[sc % 4]; pk = ("ps", sc % 4); sc += 1
                P.op("pe", lambda t, ps=ps, hb=hb, kt=kt, qt=qt: t.matmul(ps[:, 0:512], kb[:, hb, kt * 128:(kt + 1) * 128], qb[:, hb, qt * 512:(qt + 1) * 512], start=True, stop=True), reads=[("kb", hb), ("qb", hb)], writes=[pk])
                return ps, pk
            pend = [emit_qk(0), emit_qk(1)]
            for kt in range(NKT):
                ps, pk = pend.pop(0)
                if kt + 2 < NKT:
                    pend.append(emit_qk(kt + 2))
                pb = it % 4; it += 1
                P.op("act", lambda a, ps=ps, pb=pb: a.activation(pt[:, pb, :], ps[:, 0:512], AF.Exp, scale=96.0 ** -0.5), reads=[pk], writes=[("pt", pb)])
                for j in range(4):
                    P.op("pe", lambda t, ops=ops, pb=pb, j=j, hb=hb, kt=kt: t.matmul(ops[:, j * 65:(j + 1) * 65], pt[:, pb, j * 128:(j + 1) * 128], va[:, hb, kt, :], start=(kt == 0 and j == 0), stop=(kt == NKT - 1 and j == 3)),
                         reads=[("pt", pb), ("va", hb)], writes=[opk])
            db = qt % 2
            P.op("dve", lambda v, ops=ops, db=db: v.reciprocal(den[:, db, :], ops[:, 64:260:65]), reads=[opk], writes=[("den", db)])
            for j in range(4):
                P.op("dve", lambda v, ops=ops, db=db, j=j, qt=qt, h=h: v.tensor_scalar_mul(osb[:, qt * 4 + j, h * 64:(h + 1) * 64], ops[:, j * 65:j * 65 + 64], den[:, db, j:j + 1]), reads=[opk, ("den", db)], writes=[("osb", qt * 4 + j)])
    P.dma("sp", o_d.rearrange("(n p) c -> p n c", p=128), osb[:, :, :], reads=[("osb", i) for i in range(16)])
    return C.end()


def build_k4a1(ntok, tile, C=None):
    C = C or Ctx()
    W = C.din("w", [1024, 1024]); lnd = C.din("ln", [128, 2, 8]); ones_d = C.din("ones", [128, 128]); bo_d = C.din("bo", [128, 128]); sc_d = C.din("sc", [128, 12])
    x_d = C.din("x0", [1024, ntok]); mod_d = C.din("mod0", [128, 6, 8]); out = C.dout("o0", [1024, ntok])
    ins = {n: C.din(n, [512, ntok]) for n in ("yf", "yb", "r", "km", "v", "g", "at")}
    TM = tile
    wb = C.sb("wb", [128, 8, 1024], BF16); stage = C.sb("stage", [128, 2, 1024])
    xt = C.sb("xt", [128, 8, TM]); mix = C.sb("mix", [128, 8, TM], BF16)
    tl = {n: C.sb("t_" + n, [128, 4, TM]) for n in ins}
    z = C.sb("z", [128, 8, TM]); zsq = C.sb("zsq", [128, 8, TM]); ost = C.sb("ost", [128, 2, TM])
    ones = C.sb("ones", [128, 128]); bo = C.sb("bo", [128, 128]); lnt = C.sb("lnt", [128, 2, 8]); sc = C.sb("sc", [128, 12]); mt = C.sb("mt", [128, 1, 6, 8])
    pss = [C.ps("ps%d" % i, [128, 512]) for i in range(8)]
    P = C.start()
    pctr = [0]

    def newps():
        i = pctr[0] % 8; pctr[0] += 1
        return pss[i], ("ps", i)
    P.dma("sp", ones[:], ones_d[:, :], writes=["ones"]); P.dma("sp", bo[:], bo_d[:, :], writes=["bo"]); P.dma("sp", lnt[:], lnd[:, :, :], writes=["ln"])
    P.dma("sp", sc[:], sc_d[:, :], writes=["sc"]); P.dma("sp", mt[:, 0, :, :], mod_d[:, :, :], writes=[("mt", 0)])
    load_w_bf16(C, P, wb, "wb", W, 1024, 1024, stage, "stage")
    v3 = lambda ap: ap.rearrange("(kc p) t -> p kc t", p=128)
    for t0 in range(0, ntok, tile):
        nt = tile
        P.dma("sp", xt[:, :, :], v3(x_d)[:, :, t0:t0 + nt], writes=["xt"])
        for n in ins:
            P.dma("sp", tl[n][:, :, :], v3(ins[n])[:, :, t0:t0 + nt], writes=[n])
        y, yb_, r_, km_, v_, g_, at_ = (tl[n] for n in ("yf", "yb", "r", "km", "v", "g", "at"))
        P.op("dve", lambda v: v.tensor_tensor(y[:, :, :], y[:, :, :], yb_[:, :, :], ALU.add), reads=["yf", "yb"], writes=["yf"])
        P.op("act", lambda a: a.activation(yb_[:, :, :], y[:, :, :], AF.Square), reads=["yf"], writes=["yb"])
        P.op("pool", lambda v: v.tensor_copy(mix[:, 4:8, :], at_[:, :, :]), reads=["at"], writes=[("mix", k) for k in range(4, 8)])
        P.op("pool", lambda v: v.tensor_tensor(r_[:, :, :], r_[:, :, :], km_[:, :, :], ALU.mult), reads=["r", "km"], writes=["r"])
        for c in range(4):
            psm, pkm = newps(); psq, pkq = newps(); psb, pkb = newps()
            P.op("pe", lambda t, psm=psm, c=c: t.matmul(psm[:, 0:nt], bo[:, :], y[:, c, :], start=True, stop=True), reads=["bo", "yf"], writes=[pkm])
            P.op("pe", lambda t, psq=psq, c=c: t.matmul(psq[:, 0:nt], bo[:, :], yb_[:, c, :], start=True, stop=True), reads=["bo", "yb"], writes=[pkq])
            P.op("dve", lambda v, c=c: v.tensor_scalar_mul(km_[:, c, :], r_[:, c, :], sc[:, c:c + 1]), reads=["r", "sc"], writes=[("km", c)])
            P.op("pe", lambda t, psb=psb, c=c: t.matmul(psb[:, 0:nt], bo[:, :], km_[:, c, :], start=True, stop=True), reads=["bo", ("km", c)], writes=[pkb])
            mean, var = ost[:, 0, :], ost[:, 1, :]
            P.op("dve", lambda v, psm=psm: v.tensor_scalar_mul(mean, psm[:, 0:nt], 1.0 / 64.0), reads=[pkm], writes=[("ost", 0)])
            P.op("dve", lambda v: v.tensor_tensor(var, mean, mean, ALU.mult), reads=[("ost", 0)], writes=[("ost", 1)])
            P.op("dve", lambda v, psq=psq: v.scalar_tensor_tensor(var, psq[:, 0:nt], 1.0 / 64.0, var, ALU.mult, ALU.subtract), reads=[pkq, ("ost", 1)], writes=[("ost", 1)])
            P.op("dve", lambda v: v.tensor_scalar_add(var, var, 6.4e-4), reads=[("ost", 1)], writes=[("ost", 1)])
            P.op("act", lambda a: a.activation(var, var, AF.Sqrt), reads=[("ost", 1)], writes=[("ost", 1)])
            P.op("dve", lambda v: v.reciprocal(var, var), reads=[("ost", 1)], writes=[("ost", 1)])
            P.op("dve", lambda v, c=c: v.tensor_tensor(zsq[:, c, :], y[:, c, :], mean, ALU.subtract), reads=["yf", ("ost", 0)], writes=[("zsq", c)])
            P.op("dve", lambda v, c=c: v.tensor_tensor(zsq[:, c, :], zsq[:, c, :], var, ALU.mult), reads=[("zsq", c), ("ost", 1)], writes=[("zsq", c)])
            P.op("dve", lambda v, c=c: v.tensor_scalar(zsq[:, c, :], zsq[:, c, :], sc[:, 4 + c:5 + c], sc[:, 8 + c:9 + c], ALU.mult, ALU.add), reads=[("zsq", c), "sc"], writes=[("zsq", c)])
            P.op("dve", lambda v, psb=psb, c=c: v.tensor_tensor(z[:, c, :], psb[:, 0:nt], v_[:, c, :], ALU.mult), reads=[pkb, "v"], writes=[("z", c)])
            P.op("dve", lambda v, c=c: v.tensor_tensor(zsq[:, c, :], zsq[:, c, :], z[:, c, :], ALU.add), reads=[("zsq", c), ("z", c)], writes=[("zsq", c)])
            P.op("dve", lambda v, c=c: v.tensor_tensor(mix[:, c, :], zsq[:, c, :], g_[:, c, :], ALU.mult), reads=[("zsq", c), "g"], writes=[("mix", c)])
        for dc in range(8):
            ps, pk = newps()
            for kc in range(8):
                P.op("pe", lambda t, ps=ps, kc=kc, dc=dc: t.matmul(ps[:, 0:nt], wb[:, kc, dc * 128:(dc + 1) * 128], mix[:, kc, :], start=(kc == 0), stop=(kc == 7)), reads=[("wb", kc), ("mix", kc)], writes=[pk])
            P.op("dve", lambda v, dc=dc: v.tensor_scalar_mul(z[:, dc, :], xt[:, dc, :], ALPHA), reads=["xt"], writes=[("z", dc)])
            P.op("dve", lambda v, ps=ps, dc=dc: v.scalar_tensor_tensor(z[:, dc, :], ps[:, 0:nt], mt[:, 0, 2, dc:dc + 1], z[:, dc, :], ALU.mult, ALU.add), reads=[pk, ("z", dc), ("mt", 0)], writes=[("z", dc)])
        ln_tail(P, nt, z, zsq, ones, newps, lnt[:, 0, :], lnt[:, 1, :], out, (t0, t0 + nt), ost, None)
    return C.end()


def cd_w_in_layout(w):
    z64 = np.zeros((1024, 64), np.float32); z96 = np.zeros((1024, 96), np.float32)
    kpe = w[:, 2240:2272]
    sw = np.array([d + 8 if (d % 16) < 8 else d - 8 for d in range(32)])
    return np.ascontiguousarray(np.concatenate([w[:, 0:1536], w[:, 1536:1664], w[:, 1664:1728], z64, w[:, 1728:1856],
                                                w[:, 1856:2112], w[:, 2112:2240], kpe, z96, kpe[:, sw], z96], 1))


def rope32_tables(pos_tok, rope):
    n = len(pos_tok)
    if not rope:
        return np.ones((32, n), np.float32), np.zeros((32, n), np.float32)
    row = (pos_tok // 64).astype(np.float32); col = (pos_tok % 64).astype(np.float32)
    inv = (10000.0 ** (-np.arange(8, dtype=np.float32) / np.float32(8))).astype(np.float32)
    ar = row[None, :] * inv[:, None]; ac = col[None, :] * inv[:, None]
    Cc = np.concatenate([np.cos(ar), np.cos(ar), np.cos(ac), np.cos(ac)], 0).astype(np.float32)
    Ss = np.concatenate([-np.sin(ar), np.sin(ar), -np.sin(ac), np.sin(ac)], 0).astype(np.float32)
    return Cc, Ss


def stage_mla(pl, pc, I):
    wuq = I['mla_w_uq'][0]; wukv = I['mla_w_ukv'][0]
    sw = np.array([d + 8 if (d % 16) < 8 else d - 8 for d in range(32)])
    wuq_sw = wuq.reshape(256, 8, 96).copy()
    wuq_sw[:, :, 64:96] = wuq_sw[:, :, 64:96][:, :, sw]
    wuq_all = np.ascontiguousarray(np.concatenate([wuq, wuq_sw.reshape(256, 768)], 1))
    wk = wukv.reshape(128, 8, 128)
    wukv_all = np.ascontiguousarray(np.concatenate([wk[:, :, :64].reshape(128, 512), wk[:, :, 64:].reshape(128, 512)], 1))
    gq = np.stack([I['mla_q_norm'][0][:128], I['mla_q_norm'][0][128:], I['mla_kv_norm'][0]], 1).astype(np.float32)
    ones = np.ones((128, 128), np.float32)
    ncf = build_mlf(2048, 256); ncc = build_mlf(64, 64)
    def tabs(pos, rope):
        c32, s32 = rope32_tables(pos, rope)
        cq = np.ones((96, len(pos)), np.float32); sq = np.zeros((96, len(pos)), np.float32)
        cq[64:96] = c32; sq[64:96] = s32; cq[0:32] = c32; sq[0:32] = s32
        return cq, sq
    mapsl, mapsc = [], []
    for c in range(NCORES):
        b, s = c // 4, c % 4
        cq, sq = tabs(np.arange(s * 2048, (s + 1) * 2048), True)
        mapsl.append({"p": np.ascontiguousarray(pl[b, s * 2048:(s + 1) * 2048, 1920:2560].T), "wuq": wuq_all, "wukv": wukv_all, "gq": gq, "ones": ones, "cq": cq, "sq": sq})
        cq, sq = tabs(np.arange(64), False)
        mapsc.append({"p": np.ascontiguousarray(pc[b, s * 64:(s + 1) * 64, 1920:2560].T), "wuq": wuq_all, "wukv": wukv_all, "gq": gq, "ones": ones, "cq": cq, "sq": sq})
    rl = run_spmd(ncf, mapsl); rc = run_spmd(ncc, mapsc)
    KT = np.empty((2, 8, 96, 8448), np.float32); V = np.empty((2, 8448, 512), np.float32)
    for c in range(NCORES):
        b, s = c // 4, c % 4
        sl = slice(s * 2048, (s + 1) * 2048); sc_ = slice(8192 + s * 64, 8192 + (s + 1) * 64)
        KT[b, :, 0:64, sl] = rl[c]["kn"].transpose(1, 0, 2); KT[b, :, 64:96, sl] = rl[c]["kpe"][None]
        KT[b, :, 0:64, sc_] = rc[c]["kn"].transpose(1, 0, 2); KT[b, :, 64:96, sc_] = rc[c]["kpe"][None]
        V[b, sl] = rl[c]["v"]; V[b, sc_] = rc[c]["v"]
    nca = build_mla()
    maps = []
    for c in range(NCORES):
        b = c // 4
        maps.append({"q": rl[c]["q"], "k": KT[b], "v": np.ascontiguousarray(V[b].reshape(66, 128, 8, 64).transpose(1, 0, 2, 3))})
    ra = run_spmd(nca, maps)
    att = np.empty((2, 8192, 512), np.float32)
    for c in range(NCORES):
        b, s = c // 4, c % 4
        att[b, s * 2048:(s + 1) * 2048] = ra[c]["o"]
    return att


def stage_k4a1(x, m_l, yf, yb, F, att, I):
    nc = build_k4a1(2048, 256)
    ones = np.ones((128, 128), np.float32)
    t = np.arange(128)
    bo = ((t[:, None] // 64) == (t[None, :] // 64)).astype(np.float32)
    col = lambda v: v.reshape(4, 128).T
    sc = np.ascontiguousarray(np.concatenate([col(I['rwkv_rk'][0].reshape(512)), col(I['rwkv_lnx_g'][0]), col(I['rwkv_lnx_b'][0])], 1).astype(np.float32))
    maps = []
    for c in range(NCORES):
        maps.append({"w": I['cd_w_out'][0], "ln": ln_layout(I['ln_g'][1, 0], I['ln_b'][1, 0]), "ones": ones, "bo": bo, "sc": sc,
                     "x0": shard_tok(x, c), "mod0": mod_layout(m_l[c // 4]),
                     "yf": shard_tok(yf, c), "yb": shard_tok(yb, c), "r": shard_tok(F["r"][:, 256:], c), "km": shard_tok(F["km"][:, 256:], c),
                     "v": shard_tok(F["v"][:, 256:], c), "g": shard_tok(F["g"][:, 256:], c), "at": shard_tok(att, c)})
    res = run_spmd(nc, maps)
    return unshard(res, "o0", None, 1024)[0]


def kernel(**I):
    I = {k: np.asarray(v, dtype=np.float32) for k, v in I.items()}
    m = stage_k0(I['c'], I['c_ctx'], I['ada_w'], I['ada_b'])
    pl, pc = stage_k1(I['x'], I['ctx'], m[0], aug_w_in_ab(I['ab_w_in'][0]))
    o_f, o_b = stage_gla(pl, pc, I['gla_a2'][0], I['gla_ab'][0])
    att, attc = stage_gqa(pl, pc, I['gqa_sink'][0])
    xa, ha = stage_k4a0(I['x'], I['ctx'], m[0], o_f, o_b, pl, pc, att, attc, I['ab_w_out'][0], I['gla_norm'][0], I['ln_g'][0, 0], I['ln_b'][0, 0])
    xb, hb = stage_k4b(xa, ha, m[0], I['mlp_w1'][0], I['mlp_w2'][0], I['ln_g'][0, 1], I['ln_b'][0, 1])
    pl1, pc1 = stage_k1(xb, hb, m[1], cd_w_in_layout(I['cd_w_in'][0]))
    F = stage_rwf(pl1, pc1, I)
    yf, yb = stage_rws(F)
    att1 = stage_mla(pl1, pc1, I)
    xa1 = stage_k4a1(xb, m[1], yf, yb, F, att1, I)
    xb1, _ = stage_k4b(xa1, None, m[1], I['mlp_w1'][1], I['mlp_w2'][1], I['ln_g'][1, 1], I['ln_b'][1, 1], with_ctx=False)
    return xb1


GROUPS = [[0, 1, 2, 3], [4, 5, 6, 7]]


def build_k0f(C):
    cT = C.din("cT", [128, 8, 2]); w = C.din("w", [2, 1024, 6144]); bia = C.din("b", [2, 2, 6144]); out = C.dout("out", [2, 2, 6144])
    NB = 768
    cs = C.sb("cs", [128, 8, 2]); cs2 = C.sb("cs2", [128, 8, 2]); wt = C.sb("wt", [128, 2, 8, NB]); bt = C.sb("bt", [2, 2, 6144]); ot = C.sb("ot", [2, 2, 6144])
    pss = [C.ps("ps%d" % i, [128, 512]) for i in range(2)]
    P = C.start()
    P.dma("sp", cs[:], cT[:, :, :], writes=["cs"]); P.dma("sp", bt[:], bia[:, :, :], writes=["bt"])
    P.op("act", lambda a: a.activation(cs2[:], cs[:], AF.Silu), reads=["cs"], writes=["cs2"])
    i = 0; bi = 0
    for l in range(2):
        for blk in range(6144 // NB):
            b = bi % 2; bi += 1
            P.dma("sp", wt[:, b, :, :], w[l][:, blk * NB:(blk + 1) * NB].rearrange("(kc p) n -> p kc n", p=128), writes=[("wt", b)])
            for n0 in range(0, NB, 384):
                ps = pss[i % 2]; pk = ("ps", i % 2); i += 1
                for kc in range(8):
                    P.op("pe", lambda t, kc=kc, b=b, n0=n0, ps=ps: t.matmul(ps[0:2, 0:384], cs2[:, kc, :], wt[:, b, kc, n0:n0 + 384], start=(kc == 0), stop=(kc == 7)), reads=["cs2", ("wt", b)], writes=[pk])
                g0 = blk * NB + n0
                P.op("dve", lambda v, l=l, g0=g0, ps=ps: v.tensor_tensor(ot[:, l, g0:g0 + 384], ps[0:2, 0:384], bt[:, l, g0:g0 + 384], ALU.add), reads=[pk, "bt"], writes=["ot"])
    P.dma("sp", out[:, :, :], ot[:], reads=["ot"])
    return C.end()


def build_k1x(C, ncols, nfm, tm, segs):
    W = C.din("w", [1024, ncols])
    xs, mods, outs, outt = [], [], [], []
    for si, (ntok, tile) in enumerate(segs):
        xs.append(C.bind["xv%d" % si] if ("xv%d" % si) in C.bind else C.din("x%d" % si, [1024, ntok]).rearrange("(kc p) t -> p kc t", p=128))
        mods.append(C.din("mod%d" % si, [128, 6, 8])); outs.append(C.dout("p%d" % si, [nfm, ntok]))
        outt.append(C.dout("pt%d" % si, [ntok, tm[1] - tm[0]]) if tm else None)
    TM = max(t for _, t in segs)
    NCH = (nfm + 127) // 128
    wb = C.sb("wb", [128, 8, ncols], BF16); stage = C.sb("stage", [128, 2, ncols])
    xt = C.sb("xt", [128, 2, 8, TM]); ub = C.sb("ub", [128, 2, 8, TM], BF16); ost = C.sb("ost", [128, 4, 512])
    mt = C.sb("mt", [128, len(segs), 6, 8]); sc1 = C.sb("sc1", [128, len(segs), 8])
    pss = [C.ps("ps%d" % i, [128, 512]) for i in range(4)]
    P = C.start()
    for si in range(len(segs)):
        P.dma("sp", mt[:, si, :, :], mods[si][:, :, :], writes=[("mt", si)])
        P.op("dve", lambda v, si=si: v.tensor_scalar_add(sc1[:, si, :], mt[:, si, 1, :], 1.0), reads=[("mt", si)], writes=[("sc1", si)])
    load_w_bf16(C, P, wb, "wb", W, 1024, ncols, stage, "stage")
    it = 0; oi = 0
    for si, (ntok, tile) in enumerate(segs):
        xv = xs[si]
        for t0 in range(0, ntok, tile):
            b = it % 2; it += 1
            P.dma("sp", xt[:, b, :, 0:tile], xv[:, :, t0:t0 + tile], writes=[("xt", b)])
            for kc in range(8):
                P.op(P.alt(), lambda v, b=b, kc=kc, si=si, tile=tile: v.tensor_scalar(ub[:, b, kc, 0:tile], xt[:, b, kc, 0:tile], sc1[:, si, kc:kc + 1], mt[:, si, 0, kc:kc + 1], ALU.mult, ALU.add),
                     reads=[("xt", b), ("sc1", si), ("mt", si)], writes=[("ub", b, kc)])
            for ch in range(NCH):
                c0 = ch * 128; cn = min(128, nfm - c0)
                pi = oi % 4; ob = oi % 4; oi += 1
                ps = pss[pi]
                for kc in range(8):
                    P.op("pe", lambda t, ps=ps, cn=cn, c0=c0, kc=kc, b=b, tile=tile: t.matmul(ps[0:cn, 0:tile], wb[:, kc, c0:c0 + cn], ub[:, b, kc, 0:tile], start=(kc == 0), stop=(kc == 7)),
                         reads=[("wb", kc), ("ub", b, kc)], writes=[("ps", pi)])
                P.op("act" if oi % 2 else "dve", (lambda a, ps=ps, cn=cn, ob=ob, tile=tile: a.copy(ost[0:cn, ob, 0:tile], ps[0:cn, 0:tile])) if oi % 2 else
                     (lambda v, ps=ps, cn=cn, ob=ob, tile=tile: v.tensor_copy(ost[0:cn, ob, 0:tile], ps[0:cn, 0:tile])), reads=[("ps", pi)], writes=[("ost", ob)])
                P.dma("sp", outs[si][c0:c0 + cn, t0:t0 + tile], ost[0:cn, ob, 0:tile], reads=[("ost", ob)])
            if tm:
                nt_ = tm[1] - tm[0]
                for s0 in range(0, tile, 128):
                    sn = min(128, tile - s0)
                    pi = oi % 4; ob = oi % 4; oi += 1
                    ps = pss[pi]
                    for kc in range(8):
                        P.op("pe", lambda t, ps=ps, kc=kc, b=b, s0=s0, sn=sn: t.matmul(ps[0:sn, 0:nt_], ub[:, b, kc, s0:s0 + sn], wb[:, kc, tm[0]:tm[1]], start=(kc == 0), stop=(kc == 7)),
                             reads=[("wb", kc), ("ub", b, kc)], writes=[("ps", pi)])
                    P.op("act", lambda a, ps=ps, ob=ob, sn=sn: a.copy(ost[0:sn, ob, 0:nt_], ps[0:sn, 0:nt_]), reads=[("ps", pi)], writes=[("ost", ob)])
                    P.dma("sp", outt[si][t0 + s0:t0 + s0 + sn, :], ost[0:sn, ob, 0:nt_], reads=[("ost", ob)])
    return C.end()


def build_tr(C, R, Cc, rev_src=False, plain2=False):
    src = C.din("src", [R, Cc]); idn = C.din("idn", [128, 128]); dst = C.dout("dst", [Cc, R])
    src2 = C.din("src2", [R, Cc]) if rev_src else None
    jd = C.din("jdn", [128, 128]) if rev_src else None
    NCc = Cc // 128
    st = C.sb("st", [128, 2, 4, Cc]); st2 = C.sb("st2", [128, 2, 4, Cc]) if rev_src else None
    idt = C.sb("idt", [128, 128]); jt = C.sb("jt", [128, 128]) if rev_src else None
    ot = C.sb("ot", [128, 4, 512])
    pss = [C.ps("ps%d" % i, [128, 512]) for i in range(4)]
    P = C.start()
    P.dma("sp", idt[:], idn[:, :], writes=["idt"])
    if rev_src:
        P.dma("sp", jt[:], jd[:, :], writes=["jt"])
    oi = 0
    for g, r0 in enumerate(range(0, R, 512)):
        nr = min(4, (R - r0) // 128)
        b = g % 2
        P.dma("sp", st[:, b, 0:nr, :], src[r0:r0 + nr * 128, :].rearrange("(n p) c -> p n c", p=128), writes=[("st", b)])
        if rev_src:
            if plain2:
                P.dma("sp", st2[:, b, 0:nr, :], src2[r0:r0 + nr * 128, :].rearrange("(n p) c -> p n c", p=128), writes=[("st2", b)])
            else:
                P.dma("sp", st2[:, b, 0:nr, :], src2[R - r0 - nr * 128:R - r0, :].rearrange("(n p) c -> p n c", p=128), writes=[("st2", b)])
        for c in range(NCc):
            pi = oi % 4; oi += 1
            ps = pss[pi]
            for n in range(nr):
                P.op("pe", lambda t, ps=ps, b=b, n=n, c=c: t.matmul(ps[:, n * 128:(n + 1) * 128], st[:, b, n, c * 128:(c + 1) * 128], idt[:, :], start=True, stop=not rev_src),
                     reads=[("st", b), "idt"], writes=[("ps", pi)])
                if rev_src:
                    P.op("pe", lambda t, ps=ps, b=b, n=n, c=c: t.matmul(ps[:, n * 128:(n + 1) * 128], st2[:, b, (n if plain2 else nr - 1 - n), c * 128:(c + 1) * 128], jt[:, :], start=False, stop=True),
                         reads=[("st2", b), "jt"], writes=[("ps", pi)])
            P.op("act" if oi % 2 else "dve", (lambda a, ps=ps, pi=pi: a.copy(ot[:, pi, 0:nr * 128], ps[:, 0:nr * 128])) if oi % 2 else (lambda v, ps=ps, pi=pi: v.tensor_copy(ot[:, pi, 0:nr * 128], ps[:, 0:nr * 128])),
                 reads=[("ps", pi)], writes=[("ot", pi)])
            P.dma("sp", dst[c * 128:(c + 1) * 128, r0:r0 + nr * 128], ot[:, pi, 0:nr * 128], reads=[("ot", pi)])
    return C.end()


def build_glaepi(C, T=8448):
    oT = C.din("oT", [128, T]); ogT = C.din("ogT", [128, T]); gn_d = C.din("gn", [128, 1]); ones_d = C.din("ones", [128, 128]); W = C.din("w", [128, 1024])
    rs_in = C.dout("rs_in", [4096, 2112])
    wb = C.sb("wb", [128, 1, 1024], BF16); stage = C.sb("stage", [128, 2, 1024]); ones = C.sb("ones", [128, 128]); gn = C.sb("gn", [128, 1])
    o = C.sb("o", [128, 2, 512]); og = C.sb("og", [128, 2, 512]); sq = C.sb("sq", [128, 2, 512]); mixb = C.sb("mixb", [128, 2, 512], BF16); ost = C.sb("ost", [128, 4, 512])
    pss = [C.ps("ps%d" % i, [128, 512]) for i in range(8)]
    P = C.start()
    pctr = [0]

    def newps():
        i = pctr[0] % 8; pctr[0] += 1
        return pss[i], ("ps", i)
    P.dma("sp", ones[:], ones_d[:, :], writes=["ones"]); P.dma("sp", gn[:], gn_d[:, :], writes=["gn"])
    load_w_bf16(C, P, wb, "wb", W, 128, 1024, stage, "stage")
    tiles = [(i * 64, 64, i, 2048) for i in range(4)] + [(256 + n * 512, 512, n // 4, (n % 4) * 512) for n in range(16)]
    oi = 0
    for it, (s0, nt, rank, d0) in enumerate(tiles):
        b = it % 2
        P.dma("sp", o[:, b, 0:nt], oT[:, s0:s0 + nt], writes=[("o", b)]); P.dma("sp", og[:, b, 0:nt], ogT[:, s0:s0 + nt], writes=[("og", b)])
        P.op("act", lambda a, b=b, nt=nt: a.activation(sq[:, b, 0:nt], o[:, b, 0:nt], AF.Square), reads=[("o", b)], writes=[("sq", b)])
        P.op("act", lambda a, b=b, nt=nt: a.activation(og[:, b, 0:nt], og[:, b, 0:nt], AF.Silu), reads=[("og", b)], writes=[("og", b)])
        ps, pk = newps()
        P.op("pe", lambda t, ps=ps, b=b, nt=nt: t.matmul(ps[:, 0:nt], ones[:, :], sq[:, b, 0:nt], start=True, stop=True), reads=[("sq", b), "ones"], writes=[pk])
        P.op("dve", lambda v, ps=ps, b=b, nt=nt: v.tensor_scalar(sq[:, b, 0:nt], ps[:, 0:nt], 1.0 / 128.0, LN_EPS, ALU.mult, ALU.add), reads=[pk], writes=[("sq", b)])
        P.op("act", lambda a, b=b, nt=nt: a.activation(sq[:, b, 0:nt], sq[:, b, 0:nt], AF.Sqrt), reads=[("sq", b)], writes=[("sq", b)])
        P.op("dve", lambda v, b=b, nt=nt: v.reciprocal(sq[:, b, 0:nt], sq[:, b, 0:nt]), reads=[("sq", b)], writes=[("sq", b)])
        P.op("dve", lambda v, b=b, nt=nt: v.tensor_tensor(sq[:, b, 0:nt], sq[:, b, 0:nt], o[:, b, 0:nt], ALU.mult), reads=[("sq", b), ("o", b)], writes=[("sq", b)])
        P.op("dve", lambda v, b=b, nt=nt: v.scalar_tensor_tensor(mixb[:, b, 0:nt], sq[:, b, 0:nt], gn[:, 0:1], og[:, b, 0:nt], ALU.mult, ALU.mult), reads=[("sq", b), ("og", b), "gn"], writes=[("mixb", b)])
        for dc in range(8):
            ps, pk = newps()
            ob = oi % 4; oi += 1
            P.op("pe", lambda t, ps=ps, dc=dc, b=b, nt=nt: t.matmul(ps[:, 0:nt], wb[:, 0, dc * 128:(dc + 1) * 128], mixb[:, b, 0:nt], start=True, stop=True), reads=[("wb", 0), ("mixb", b)], writes=[pk])
            P.op("act" if oi % 2 else "dve", (lambda a, ps=ps, ob=ob, nt=nt: a.copy(ost[:, ob, 0:nt], ps[:, 0:nt])) if oi % 2 else (lambda v, ps=ps, ob=ob, nt=nt: v.tensor_copy(ost[:, ob, 0:nt], ps[:, 0:nt])),
                 reads=[pk], writes=[("ost", ob)])
            P.dma("sp", rs_in[rank * 1024 + dc * 128:rank * 1024 + (dc + 1) * 128, d0:d0 + nt], ost[:, ob, 0:nt], reads=[("ost", ob)])
    return C.end()


def build_k4af(C, segs, KC):
    W = C.din("w", [KC * 128, 1024]); lnd = C.din("ln", [128, 2, 8]); ones_d = C.din("ones", [128, 128])
    xs, mods, outs, rss, ats = [], [], [], [], []
    for si, (ntok, tile) in enumerate(segs):
        xs.append(C.din("x%d" % si, [1024, ntok])); mods.append(C.din("mod%d" % si, [128, 6, 8])); outs.append(C.dout("o%d" % si, [1024, ntok]))
        rss.append(C.din("rs%d" % si, [1024, ntok])); ats.append(C.din("at%d" % si, [KC * 128, ntok]))
    TM = max(t for _, t in segs)
    wb = C.sb("wb", [128, KC, 1024], BF16); stage = C.sb("stage", [128, 2, 1024])
    xt = C.sb("xt", [128, 8, TM]); rt = C.sb("rt", [128, 8, TM]); at = C.sb("at", [128, KC, TM]); mix = C.sb("mix", [128, KC, TM], BF16)
    z = C.sb("z", [128, 8, TM]); zsq = C.sb("zsq", [128, 8, TM]); ost = C.sb("ost", [128, 2, TM])
    ones = C.sb("ones", [128, 128]); lnt = C.sb("lnt", [128, 2, 8]); mt = C.sb("mt", [128, len(segs), 6, 8])
    pss = [C.ps("ps%d" % i, [128, 512]) for i in range(8)]
    P = C.start()
    pctr = [0]

    def newps():
        i = pctr[0] % 8; pctr[0] += 1
        return pss[i], ("ps", i)
    P.dma("sp", ones[:], ones_d[:, :], writes=["ones"]); P.dma("sp", lnt[:], lnd[:, :, :], writes=["ln"])
    for si in range(len(segs)):
        P.dma("sp", mt[:, si, :, :], mods[si][:, :, :], writes=[("mt", si)])
    load_w_bf16(C, P, wb, "wb", W, KC * 128, 1024, stage, "stage")
    v3 = lambda ap: ap.rearrange("(kc p) t -> p kc t", p=128)
    for si, (ntok, tile) in enumerate(segs):
        for t0 in range(0, ntok, tile):
            nt = tile
            P.dma("sp", xt[:, :, 0:nt], v3(xs[si])[:, :, t0:t0 + nt], writes=["xt"])
            P.dma("sp", rt[:, :, 0:nt], v3(rss[si])[:, :, t0:t0 + nt], writes=["rt"])
            P.dma("sp", at[:, :, 0:nt], v3(ats[si])[:, :, t0:t0 + nt], writes=["at"])
            P.op("pool", lambda v: v.tensor_copy(mix[:, :, 0:nt], at[:, :, 0:nt]), reads=["at"], writes=[("mix", k) for k in range(KC)])
            for dc in range(8):
                ps, pk = newps()
                for kc in range(KC):
                    P.op("pe", lambda t, ps=ps, kc=kc, dc=dc: t.matmul(ps[:, 0:nt], wb[:, kc, dc * 128:(dc + 1) * 128], mix[:, kc, 0:nt], start=(kc == 0), stop=(kc == KC - 1)),
                         reads=[("wb", kc), ("mix", kc)], writes=[pk])
                P.op("dve", lambda v, ps=ps, dc=dc: v.tensor_tensor(rt[:, dc, 0:nt], rt[:, dc, 0:nt], ps[:, 0:nt], ALU.add), reads=[pk, "rt"], writes=["rt"])
                P.op("pool", lambda v, dc=dc: v.tensor_scalar_mul(z[:, dc, 0:nt], xt[:, dc, 0:nt], ALPHA), reads=["xt"], writes=[("z", dc)])
                P.op("dve", lambda v, dc=dc, si=si: v.scalar_tensor_tensor(z[:, dc, 0:nt], rt[:, dc, 0:nt], mt[:, si, 2, dc:dc + 1], z[:, dc, 0:nt], ALU.mult, ALU.add),
                     reads=["rt", ("z", dc), ("mt", si)], writes=[("z", dc)])
            ln_tail(P, nt, z, zsq, ones, newps, lnt[:, 0, :], lnt[:, 1, :], outs[si], (t0, t0 + nt), ost, None)
    return C.end()


ADA_PERM = np.array([j * 1024 + kc * 128 + p for p in range(128) for j in range(6) for kc in range(8)])


def fused_program(upto=99):
    C = Ctx(shared=True)
    sc = C.scratch
    T = 8448
    m_scr = sc("m_scr", [2, 2, 6144])
    modv = lambda r, l: m_scr[r, l, :].rearrange("(p j kc) -> p j kc", p=128, j=6)
    C.stage("k0_", {"out": m_scr}); build_k0f(C)
    pT_gla = sc("pT_gla", [2, 384, T]); pt_gla = sc("pt_gla", [2, T, 192])
    for z in range(2):
        C.stage("g%d_" % z, {"mod0": modv(1, 0), "mod1": modv(0, 0), "p0": pT_gla[z, :, 0:256], "p1": pT_gla[z, :, 256:T], "pt0": pt_gla[z, 0:256, :], "pt1": pt_gla[z, 256:T, :]})
        build_k1x(C, 576, 384, (384, 576), [(256, 256), (8192, 512)])
    o_gla = sc("o_gla", [2, T, 128])
    C.stage("gl_", {"qT": pT_gla[:, 0:64, :], "kT": pT_gla[:, 64:128, :], "a1T": pT_gla[:, 128:144, :], "k": pt_gla[:, :, 0:64], "v": pt_gla[:, :, 64:192], "o": o_gla})
    build_gla(T, C=C)
    oT = sc("oT_gla", [128, T])
    C.stage("t0_", {"src": o_gla[0, 0:256, :], "src2": o_gla[1, 0:256, :], "dst": oT[:, 0:256]}); build_tr(C, 256, 128, rev_src=True)
    C.stage("t1_", {"src": o_gla[0, 256:T, :], "src2": o_gla[1, 256:T, :], "dst": oT[:, 256:T]}); build_tr(C, 8192, 128, rev_src=True)
    rs_in = sc("rs_in", [4096, 2112]); rs_out = sc("rs_out", [1024, 2112])
    C.stage("ge_", {"oT": oT, "ogT": pT_gla[0, 256:384, :], "rs_in": rs_in}); build_glaepi(C, T)
    C.P.coll_issue("ReduceScatter", ALU.add, GROUPS, rs_in[:, :], rs_out[:, :])
    pq0 = sc("pq0", [1280, 2304]); pqt0 = sc("pqt0", [2304, 128]); pq1 = sc("pq1", [1280, 256]); pqt1 = sc("pqt1", [256, 128]); pq2 = sc("pq2", [1280, 64]); pqt2 = sc("pqt2", [64, 128])
    C.stage("q_", {"mod0": modv(0, 0), "mod1": modv(1, 0), "mod2": modv(1, 0), "p0": pq0, "pt0": pqt0, "p1": pq1, "pt1": pqt1, "p2": pq2, "pt2": pqt2})
    build_k1x(C, 1408, 1280, (1280, 1408), [(2304, 256), (256, 256), (64, 64)])
    att_tm = sc("att_tm", [2176, 512]); attT = sc("attT", [512, 2176])
    hd = lambda ap: ap.rearrange("(h d) t -> d h t", d=64)
    tmv = lambda ap: ap.rearrange("(n p) (g d) -> p n g d", p=128, d=64)
    C.stage("a_", {"qT": hd(pq0[0:512, 128:2176]), "qsT": hd(pq0[512:1024, 128:2176]), "kT": hd(pq0[1024:1152, :]), "ksT": hd(pq0[1152:1280, :]), "v": tmv(pqt0),
                   "kcT": hd(pq1[1024:1152, :]), "vc": tmv(pqt1), "qcT": hd(pq2[0:512, :]), "o": att_tm[0:2048, :], "oc": att_tm[2048:2176, :]})
    build_gqa(False, False, C=C, oc_rows=128, nmk=4)
    C.stage("ta_", {"src": att_tm, "dst": attT}); build_tr(C, 2176, 512)
    C.P.coll_wait()
    x1T = sc("x1T", [1024, 2048]); h1T = sc("h1T", [1024, 64]); xbT = sc("xbT", [1024, 2048]); hbT = sc("hbT", [1024, 64])
    C.stage("f_", {"mod0": modv(0, 0), "mod1": modv(1, 0), "rs0": rs_out[:, 0:2048], "rs1": rs_out[:, 2048:2112], "at0": attT[:, 0:2048], "at1": attT[:, 2048:2112], "o0": x1T, "o1": h1T})
    build_k4af(C, [(2048, 256), (64, 64)], 4)
    ag_lat = sc("ag_lat", [8, 128, 2048]); ag_ctx = sc("ag_ctx", [1024, 64]); agl_out = sc("agl_out", [8, 512, 2048]); agc_out = sc("agc_out", [4096, 64])
    xb_own = ag_lat.rearrange("kc p t -> (kc p) t")
    C.stage("m0_", {"mod0": modv(0, 0), "mod1": modv(1, 0), "x0": x1T, "x1": h1T, "o0": xb_own, "o1": ag_ctx})
    build_k4b([(2048, 256), (64, 64)], C=C)
    if upto == 0:
        dbg = C.nc.dram_tensor("dbg_xb", [1024, 2048], F32, kind="ExternalOutput").ap()
        dbg2 = C.nc.dram_tensor("dbg_hb", [1024, 64], F32, kind="ExternalOutput").ap()
        C.P.dma("sp", dbg[:, :], xb_own); C.P.dma("sp", dbg2[:, :], ag_ctx[:, :])
        return C.finalize(), C
    for kc in range(8):
        C.P.coll_issue("AllGather", ALU.bypass, GROUPS, ag_lat[kc], agl_out[kc], barrier=(kc == 0))
    C.P.coll_issue("AllGather", ALU.bypass, GROUPS, ag_ctx[:, :], agc_out[:, :], barrier=False)
    p_q = sc("p_q", [640, 2048])
    C.stage("mq_", {"x0": xb_own, "mod0": modv(0, 1), "p0": p_q}); build_k1x(C, 640, 640, None, [(2048, 512)])
    q_own = sc("q_own", [96, 8, 2048]); kn_d = sc("kn_dum", [64, 8, 2048]); kpe_dm = sc("kpe_dum", [32, 2048]); v_dm = sc("v_dum", [2048, 512])
    C.stage("mqf_", {"p": p_q, "q": q_own, "kn": kn_d, "kpe": kpe_dm, "v": v_dm}); build_mlf(2048, 256, C=C, with_kv=False)
    C.P.coll_wait()
    latv = lambda r: agl_out[:, r * 128:(r + 1) * 128, :].rearrange("kc p t -> p kc t")
    ctxv = lambda r: agc_out[r * 1024:(r + 1) * 1024, :]
    pTr_c = sc("pTr_c", [768, 258]); pTr_l = sc("pTr_l", [768, 8194])
    C.stage("z_", {})
    zt = C.sb("zt", [128, 6, 1]); P = C.start()
    P.op("dve", lambda v: v.memset(zt[:], 0.0), writes=["zt"])
    for ap_, col in ((pTr_c, 0), (pTr_c, 257), (pTr_l, 0), (pTr_l, 8193)):
        P.dma("sp", ap_.rearrange("(kc p) t -> p kc t", p=128)[:, :, col:col + 1], zt[:], reads=["zt"], allow_slow_non_contiguous=True)
    C.end()
    bd = {}
    for i in range(4):
        bd["x%d" % i] = ctxv(i); bd["mod%d" % i] = modv(1, 1); bd["p%d" % i] = pTr_c[:, 1 + 64 * i:1 + 64 * (i + 1)]
        bd["xv%d" % (4 + i)] = latv(i); bd["mod%d" % (4 + i)] = modv(0, 1); bd["p%d" % (4 + i)] = pTr_l[:, 1 + 2048 * i:1 + 2048 * (i + 1)]
    C.stage("r_", bd); build_k1x(C, 768, 768, None, [(64, 64)] * 4 + [(2048, 512)] * 4)
    FN = ["r", "km", "v", "an", "bb", "wf", "wb", "g"]
    Ff = {n: sc("F_" + n, [128, T]) for n in FN}
    bd = {"p0": pTr_c, "p1": pTr_l}
    for n in FN:
        bd[n + "0"] = Ff[n][:, 0:256]; bd[n + "1"] = Ff[n][:, 256:T]
    C.stage("rf_", bd); build_rwf([(256, 256), (8192, 256)], C=C, NH=1)
    Ftm = {n: sc("Ftm_" + n, [T, 128]) for n in ("bb", "km", "v", "wf", "wb")}
    bd = {}
    for n in ("bb", "km", "v", "wf", "wb"):
        bd["src_" + n] = Ff[n]; bd["dst_" + n] = Ftm[n]
    C.stage("xt_", bd); build_tr_fm(C, ("bb", "km", "v", "wf", "wb"), T)
    y_scr = sc("y_scr", [2, T, 128])
    C.stage("s_", {"an": Ff["an"], "r": Ff["r"], "bb": Ff["bb"], "km": Ff["km"], "bb_tm": Ftm["bb"], "km_tm": Ftm["km"], "v_tm": Ftm["v"], "wf_tm": Ftm["wf"], "wb_tm": Ftm["wb"], "y": y_scr}); build_rwc2(C, T, 4)
    yT = sc("yT", [128, 8192])
    C.stage("ty_", {"src": y_scr[0, 256:T, :], "src2": y_scr[1, 256:T, :], "dst": yT}); build_tr(C, 8192, 128, rev_src=True, plain2=True)
    rs2_in = sc("rs2_in", [4096, 2048]); rs2_out = sc("rs2_out", [1024, 2048])
    C.stage("re_", {"yT": yT, "r": Ff["r"][:, 256:T], "km": Ff["km"][:, 256:T], "v": Ff["v"][:, 256:T], "g": Ff["g"][:, 256:T], "rs_in": rs2_in}); build_rwepi(C)
    C.P.coll_issue("ReduceScatter", ALU.add, GROUPS, rs2_in[:, :], rs2_out[:, :])
    p_mla = sc("p_mla", [640, T])
    bd = {}
    for i in range(4):
        bd["xv%d" % i] = latv(i); bd["mod%d" % i] = modv(0, 1); bd["p%d" % i] = p_mla[:, 2048 * i:2048 * (i + 1)]
        bd["x%d" % (4 + i)] = ctxv(i); bd["mod%d" % (4 + i)] = modv(1, 1); bd["p%d" % (4 + i)] = p_mla[:, 8192 + 64 * i:8192 + 64 * (i + 1)]
    C.stage("mk_", bd); build_k1x(C, 640, 640, None, [(2048, 512)] * 4 + [(64, 64)] * 4)
    q_all = sc("q_all", [96, 8, T]); kn_all = sc("kn_all", [64, 8, T]); kpe_all = sc("kpe_all", [32, T]); v_all = sc("v_all", [T, 512])
    C.stage("mkf_", {"p": p_mla, "q": q_all, "kn": kn_all, "kpe": kpe_all, "v": v_all}); build_mlf(T, 256, C=C, with_q=False)
    att1 = sc("att1_tm", [2048, 512]); attT1 = sc("attT1", [512, 2048])
    C.stage("ma_", {"q": q_own, "kn": kn_all, "kpe": kpe_all, "v": v_all.rearrange("(n p) (h d) -> p n h d", p=128, d=64), "o": att1}); build_mla(C=C, split_k=True)
    C.stage("tb_", {"src": att1, "dst": attT1}); build_tr(C, 2048, 512)
    C.P.coll_wait()
    C.stage("f1_", {"x0": xb_own, "mod0": modv(0, 1), "rs0": rs2_out, "at0": attT1, "o0": x1T}); build_k4af(C, [(2048, 256)], 4)
    C.stage("m1_", {"x0": x1T, "mod0": modv(0, 1)}); build_k4b([(2048, 256)], C=C)
    return C.finalize(), C


def fused_inputs(I, c):
    b, s = c // 4, c % 4
    f = lambda a: np.ascontiguousarray(a, dtype=np.float32)
    x, ctx = I['x'], I['ctx']
    m = {}
    cc = np.stack([I['c'][b], I['c_ctx']], axis=1)
    m["k0_cT"] = f(cc.reshape(8, 128, 2).transpose(1, 0, 2))
    m["k0_w"] = f(I['ada_w'][:, :, ADA_PERM])
    m["k0_b"] = f(np.broadcast_to(I['ada_b'][None, :, ADA_PERM], (2, 2, 6144)))
    w_in = I['ab_w_in'][0]
    h = s
    z96 = np.zeros((1024, 96), np.float32)
    q_, k_, v_ = w_in[:, h * 64:(h + 1) * 64], w_in[:, 256 + h * 64:256 + (h + 1) * 64], w_in[:, 512 + h * 128:512 + (h + 1) * 128]
    a1f, a1b, og = w_in[:, 1024:1040], w_in[:, 1040:1056], w_in[:, 1056 + h * 128:1056 + (h + 1) * 128]
    m["g0_w"] = f(np.concatenate([q_, k_, a1f, a1b, z96, og, k_, v_], 1))
    m["g1_w"] = f(np.concatenate([q_, k_, a1b, a1f, z96, og, k_, v_], 1))
    m["g0_x0"] = f(ctx[b].T); m["g0_x1"] = f(x[b].T)
    m["g1_x0"] = f(ctx[b][::-1].T); m["g1_x1"] = f(x[b][::-1].T)
    m["gl_a2"] = f(I['gla_a2'][0][:, :, h * 64:(h + 1) * 64]); m["gl_ab"] = f(I['gla_ab'][0][:, None, h * 64:(h + 1) * 64])
    tri, mask = gla_consts()
    m["gl_tri"] = tri; m["gl_mask"] = mask
    idn = np.eye(128, dtype=np.float32); jdn = f(idn[::-1])
    for p in ("t0_", "t1_"):
        m[p + "idn"] = idn; m[p + "jdn"] = jdn
    ones = np.ones((128, 128), np.float32)
    m["ge_gn"] = f(I['gla_norm'][0].reshape(128, 1)); m["ge_ones"] = ones; m["ge_w"] = f(I['ab_w_out'][0][h * 128:(h + 1) * 128])
    qs = np.concatenate([w_in[:, GQ0 + hh * 64 + SWAP64] for hh in range(8)], 1)
    ks = np.concatenate([w_in[:, GK0 + hh * 64 + SWAP64] for hh in range(2)], 1)
    m["q_w"] = f(np.concatenate([w_in[:, GQ0:GQ0 + 512], qs, w_in[:, GK0:GK0 + 128], ks, w_in[:, GV0:GV0 + 128]], 1))
    lo, hi = s * 2048, (s + 1) * 2048
    xh = np.zeros((2304, 1024), np.float32)
    a0, a1 = max(lo - 128, 0), min(hi + 128, 8192)
    xh[a0 - (lo - 128):a1 - (lo - 128)] = x[b, a0:a1]
    m["q_x0"] = f(xh.T); m["q_x1"] = f(ctx[b].T); m["q_x2"] = f(ctx[b, s * 64:(s + 1) * 64].T)
    pos = np.arange(lo, hi); posk = np.clip(np.arange(lo - 128, hi + 128), 0, 8191)
    m["a_cq"], m["a_sq"] = rope_tables(pos); m["a_ck"], m["a_sk"] = rope_tables(posk)
    m["a_sink"] = f(np.broadcast_to(I['gqa_sink'][0][None, :], (128, 8)))
    t = np.arange(128)
    mprev = np.tile((t[:, None] >= t[None, :]).astype(np.float32), (1, 4))
    mnext = np.tile((t[:, None] <= t[None, :]).astype(np.float32), (1, 4))
    m["a_mk"] = f(np.stack([mprev, mnext, mprev * (0.0 if s == 0 else 1.0), mnext * (0.0 if s == 3 else 1.0)], 1))
    m["ta_idn"] = idn
    m["f_w"] = f(I['ab_w_out'][0][512:1024]); m["f_ln"] = ln_layout(I['ln_g'][0, 0], I['ln_b'][0, 0]); m["f_ones"] = ones
    m["f_x0"] = f(x[b, lo:hi].T); m["f_x1"] = f(ctx[b, s * 64:(s + 1) * 64].T)
    m["m0_w1"] = f(I['mlp_w1'][0]); m["m0_w2"] = f(I['mlp_w2'][0]); m["m0_ln"] = ln_layout(I['ln_g'][0, 1], I['ln_b'][0, 1]); m["m0_ones"] = ones
    return m


def build_rws2(C, T=8448, BS=32):
    NCTX = 256 // BS
    NB = T // BS
    fm = {n: C.din(n, [128, T]) for n in ("an", "r", "wf", "wb")}
    tmd = {n: C.din(n, [T, 128]) for n in ("bb", "km", "v")}
    mk_d = C.din("mk2", [128, 2])
    y_d = C.dout("y", [2, T, 128])
    an = C.sb("an", [128, 2, 2, BS]); rr = C.sb("rr", [128, 2, 2, BS]); wt = C.sb("wt", [128, 2, 2, BS])
    ab = C.sb("ab", [128, 2, 2, BS, 2]); rb = C.sb("rb", [128, 2, 2, BS, 2])
    bk = C.sb("bk", [4, 2, 2, BS, 128]); vv = C.sb("vv", [4, 2, 2, BS, 64]); ysb = C.sb("ysb", [2, 2, 2, BS, 64])
    ST = C.sb("ST", [128, 2, 64]); mk = C.sb("mk", [128, 2])
    pa = [C.ps("pa%d" % i, [128, 512]) for i in range(2)]
    po = [C.ps("po%d" % i, [128, 512]) for i in range(2)]
    py = [C.ps("py%d" % i, [128, 512]) for i in range(2)]
    P = C.start()
    P.dma("sp", mk[:], mk_d[:, :], writes=["mk"])
    P.op("dve", lambda v: v.memset(ST[:], 0.0), writes=[("ST", 0), ("ST", 1)])
    P.op("pool", lambda v: v.memset(bk[:], 0.0), writes=[("bk", z, b) for z in range(2) for b in range(2)])
    P.op("pool", lambda v: v.memset(vv[:], 0.0), writes=[("vv", z, b) for z in range(2) for b in range(2)])
    P.op("dve", lambda v: v.memset(ysb[:], 0.0), writes=[("ysb", z, b) for z in range(2) for b in range(2)])
    order = [list(range(NB)), list(range(NCTX - 1, -1, -1)) + list(range(NB - 1, NCTX - 1, -1))]
    for i in range(NB):
        b = i % 2
        for z in range(2):
            blk = order[z][i]; s0 = blk * BS
            wn = "wf" if z == 0 else "wb"
            P.dma("sp", an[:, z, b, :], fm["an"][:, s0:s0 + BS], writes=[("an", z, b)])
            P.dma("sp", rr[:, z, b, :], fm["r"][:, s0:s0 + BS], writes=[("rr", z, b)])
            P.dma("sp", wt[:, z, b, :], fm[wn][:, s0:s0 + BS], writes=[("wt", z, b)])
            for row, (nm, lo) in enumerate((("bb", 0), ("bb", 64), ("km", 0), ("km", 64))):
                P.dma("sp", bk[row:row + 1, z, b, :, lo:lo + 64], tmd[nm][s0:s0 + BS, lo:lo + 64], writes=[("bk", z, b)])
            for row, lo in ((2, 0), (3, 64)):
                P.dma("sp", vv[row:row + 1, z, b, :, :], tmd["v"][s0:s0 + BS, lo:lo + 64], writes=[("vv", z, b)])
            for c in range(2):
                P.op("pool", lambda v, z=z, b=b, c=c: v.tensor_scalar_mul(ab[:, z, b, :, c], an[:, z, b, :], mk[:, c:c + 1]), reads=[("an", z, b), "mk"], writes=[("ab", z, b)])
                P.op("pool", lambda v, z=z, b=b, c=c: v.tensor_scalar_mul(rb[:, z, b, :, c], rr[:, z, b, :], mk[:, c:c + 1]), reads=[("rr", z, b), "mk"], writes=[("rb", z, b)])
        for k in range(BS):
            for z in range(2):
                blk = order[z][i]
                t = k if z == 0 else BS - 1 - k
                P.op("pe", lambda e, z=z, b=b, t=t: e.matmul(pa[z][0:2, 0:64], ab[:, z, b, t, :], ST[:, z, :], start=True, stop=True), reads=[("ab", z, b), ("ST", z)], writes=[("pa", z)])
                P.op("act", lambda a, z=z, b=b, t=t: a.copy(vv[0:2, z, b, t, :], pa[z][0:2, 0:64]), reads=[("pa", z)], writes=[("vv", z, b)])
                P.op("pe", lambda e, z=z, b=b, t=t: e.matmul(po[z][:, 0:64], bk[:, z, b, t, :], vv[:, z, b, t, :], start=True, stop=True), reads=[("bk", z, b), ("vv", z, b)], writes=[("po", z)])
                P.op("dve", lambda v, z=z, b=b, t=t: v.scalar_tensor_tensor(ST[:, z, :], ST[:, z, :], wt[:, z, b, t:t + 1], po[z][:, 0:64], ALU.mult, ALU.add), reads=[("ST", z), ("wt", z, b), ("po", z)], writes=[("ST", z)])
                if blk >= NCTX:
                    P.op("pe", lambda e, z=z, b=b, t=t: e.matmul(py[z][0:2, (t % 8) * 64:(t % 8 + 1) * 64], rb[:, z, b, t, :], ST[:, z, :], start=True, stop=True), reads=[("rb", z, b), ("ST", z)], writes=[("py", z)])
                    last = (t % 8 == 7) if z == 0 else (t % 8 == 0)
                    if last:
                        t8 = (t // 8) * 8
                        P.op("act", lambda a, z=z, b=b, t8=t8: a.copy(ysb[:, z, b, t8:t8 + 8, :], py[z][0:2, 0:512].rearrange("p (a c) -> p a c", c=64)), reads=[("py", z)], writes=[("ysb", z, b)])
        for z in range(2):
            blk = order[z][i]; s0 = blk * BS
            if blk >= NCTX:
                P.dma("sp", y_d[z, s0:s0 + BS, :].rearrange("t (c i) -> c t i", c=2), ysb[:, z, b, :, :], reads=[("ysb", z, b)])
    return C.end()


def build_rwepi(C, N=8192):
    ins = {n: C.din(n, [128, N]) for n in ("yT", "r", "km", "v", "g")}
    sc_d = C.din("sc", [128, 3]); bo_d = C.din("bo", [128, 128]); W = C.din("w", [128, 1024]); rs_in = C.dout("rs_in", [4096, 2048])
    wb = C.sb("wb", [128, 1, 1024], BF16); stage = C.sb("stage", [128, 2, 1024]); bo = C.sb("bo", [128, 128]); sc = C.sb("sc", [128, 3])
    tl = {n: C.sb("t_" + n, [128, 2, 512]) for n in ins}
    sq = C.sb("sq", [128, 2, 512]); mean = C.sb("mean", [128, 512]); var = C.sb("var", [128, 512]); mixb = C.sb("mixb", [128, 2, 512], BF16); ost = C.sb("ost", [128, 4, 512])
    pss = [C.ps("ps%d" % i, [128, 512]) for i in range(8)]
    P = C.start()
    pctr = [0]

    def newps():
        i = pctr[0] % 8; pctr[0] += 1
        return pss[i], ("ps", i)
    P.dma("sp", bo[:], bo_d[:, :], writes=["bo"]); P.dma("sp", sc[:], sc_d[:, :], writes=["sc"])
    load_w_bf16(C, P, wb, "wb", W, 128, 1024, stage, "stage")
    oi = 0
    for it in range(N // 512):
        b = it % 2; t0 = it * 512
        for n in ins:
            P.dma("sp", tl[n][:, b, :], ins[n][:, t0:t0 + 512], writes=[(n, b)])
        y, r_, km_, v_, g_ = (tl[n] for n in ("yT", "r", "km", "v", "g"))
        P.op("act", lambda a, b=b: a.activation(sq[:, b, :], y[:, b, :], AF.Square), reads=[("yT", b)], writes=[("sq", b)])
        P.op("pool", lambda v, b=b: v.tensor_tensor(r_[:, b, :], r_[:, b, :], km_[:, b, :], ALU.mult), reads=[("r", b), ("km", b)], writes=[("r", b)])
        P.op("pool", lambda v, b=b: v.tensor_scalar_mul(r_[:, b, :], r_[:, b, :], sc[:, 0:1]), reads=[("r", b), "sc"], writes=[("r", b)])
        psm, pkm = newps(); psq, pkq = newps(); psb, pkb = newps()
        P.op("pe", lambda t, psm=psm, b=b: t.matmul(psm[:, 0:512], bo[:, :], y[:, b, :], start=True, stop=True), reads=["bo", ("yT", b)], writes=[pkm])
        P.op("pe", lambda t, psq=psq, b=b: t.matmul(psq[:, 0:512], bo[:, :], sq[:, b, :], start=True, stop=True), reads=["bo", ("sq", b)], writes=[pkq])
        P.op("pe", lambda t, psb=psb, b=b: t.matmul(psb[:, 0:512], bo[:, :], r_[:, b, :], start=True, stop=True), reads=["bo", ("r", b)], writes=[pkb])
        P.op("dve", lambda v, psm=psm: v.tensor_scalar_mul(mean[:, :], psm[:, 0:512], 1.0 / 64.0), reads=[pkm], writes=["mean"])
        P.op("dve", lambda v: v.tensor_tensor(var[:, :], mean[:, :], mean[:, :], ALU.mult), reads=["mean"], writes=["var"])
        P.op("dve", lambda v, psq=psq: v.scalar_tensor_tensor(var[:, :], psq[:, 0:512], 1.0 / 64.0, var[:, :], ALU.mult, ALU.subtract), reads=[pkq, "var"], writes=["var"])
        P.op("dve", lambda v: v.tensor_scalar_add(var[:, :], var[:, :], 6.4e-4), reads=["var"], writes=["var"])
        P.op("act", lambda a: a.activation(var[:, :], var[:, :], AF.Sqrt), reads=["var"], writes=["var"])
        P.op("dve", lambda v: v.reciprocal(var[:, :], var[:, :]), reads=["var"], writes=["var"])
        P.op("dve", lambda v, b=b: v.tensor_tensor(sq[:, b, :], y[:, b, :], mean[:, :], ALU.subtract), reads=[("yT", b), "mean"], writes=[("sq", b)])
        P.op("dve", lambda v, b=b: v.tensor_tensor(sq[:, b, :], sq[:, b, :], var[:, :], ALU.mult), reads=[("sq", b), "var"], writes=[("sq", b)])
        P.op("dve", lambda v, b=b: v.tensor_scalar(sq[:, b, :], sq[:, b, :], sc[:, 1:2], sc[:, 2:3], ALU.mult, ALU.add), reads=[("sq", b), "sc"], writes=[("sq", b)])
        P.op("dve", lambda v, psb=psb, b=b: v.tensor_tensor(v_[:, b, :], psb[:, 0:512], v_[:, b, :], ALU.mult), reads=[pkb, ("v", b)], writes=[("v", b)])
        P.op("dve", lambda v, b=b: v.tensor_tensor(sq[:, b, :], sq[:, b, :], v_[:, b, :], ALU.add), reads=[("sq", b), ("v", b)], writes=[("sq", b)])
        P.op("dve", lambda v, b=b: v.tensor_tensor(mixb[:, b, :], sq[:, b, :], g_[:, b, :], ALU.mult), reads=[("sq", b), ("g", b)], writes=[("mixb", b)])
        rank, d0 = t0 // 2048, t0 % 2048
        for dc in range(8):
            ps, pk = newps()
            ob = oi % 4; oi += 1
            P.op("pe", lambda t, ps=ps, dc=dc, b=b: t.matmul(ps[:, 0:512], wb[:, 0, dc * 128:(dc + 1) * 128], mixb[:, b, :], start=True, stop=True), reads=[("wb", 0), ("mixb", b)], writes=[pk])
            P.op("act" if oi % 2 else "dve", (lambda a, ps=ps, ob=ob: a.copy(ost[:, ob, :], ps[:, 0:512])) if oi % 2 else (lambda v, ps=ps, ob=ob: v.tensor_copy(ost[:, ob, :], ps[:, 0:512])), reads=[pk], writes=[("ost", ob)])
            P.dma("sp", rs_in[rank * 1024 + dc * 128:rank * 1024 + (dc + 1) * 128, d0:d0 + 512], ost[:, ob, :], reads=[("ost", ob)])
    return C.end()


def fused_inputs1(I, c, m):
    b, s = c // 4, c % 4
    f = lambda a: np.ascontiguousarray(a, dtype=np.float32)
    hp = s
    w = I['cd_w_in'][0]
    z64 = np.zeros((1024, 64), np.float32); z96 = np.zeros((1024, 96), np.float32)
    pr = slice(hp * 128, (hp + 1) * 128)
    m["r_w"] = f(np.concatenate([w[:, 0:512][:, pr], w[:, 512:1024][:, pr], w[:, 1024:1536][:, pr], w[:, 1536:1664], w[:, 1664:1728], z64, w[:, 1728:1856]], 1))
    w2, a2, g2 = I['rwkv_w2'][0], I['rwkv_a2'][0], I['rwkv_g2'][0]
    wl = np.zeros((128, 512), np.float32)
    wl[0:64, 0:128] = w2[0][:, pr]; wl[64:128, 128:256] = w2[1][:, pr]; wl[0:64, 256:384] = a2[:, pr]; wl[:, 384:512] = g2[:, pr]
    m["rf_wl"] = wl
    mu = I['rwkv_mu'][0]
    col = lambda v: v.reshape(128, 1)
    sc = np.concatenate([col(mu[0:512][pr]), col(mu[512:1024][pr]), col(mu[1024:1536][pr]), col(mu[1536:1664]), col(np.concatenate([mu[1664:1728], np.zeros(64, np.float32)])), col(mu[1728:1856]),
                         col(I['rwkv_w0'][0, 0][pr]), col(I['rwkv_w0'][0, 1][pr]), col(I['rwkv_a0'][0][pr]), col(I['rwkv_kk'][0][pr]), col(I['rwkv_ka'][0][pr])], 1)
    m["rf_sc"] = f(sc)
    t = np.arange(128)
    bo = ((t[:, None] // 64) == (t[None, :] // 64)).astype(np.float32)
    m["rf_bo"] = bo
    idn = np.eye(128, dtype=np.float32)
    m["xt_idn"] = idn
    m["s_U"], m["s_M"] = rwc_consts()
    m["ty_idn"] = idn; m["ty_jdn"] = idn
    m["re_sc"] = f(np.stack([I['rwkv_rk'][0].reshape(512)[pr], I['rwkv_lnx_g'][0][pr], I['rwkv_lnx_b'][0][pr]], 1))
    m["re_bo"] = bo; m["re_w"] = f(I['cd_w_out'][0][hp * 128:(hp + 1) * 128])
    kpe = w[:, 2240:2272]
    sw = np.array([d + 8 if (d % 16) < 8 else d - 8 for d in range(32)])
    wm = f(np.concatenate([w[:, 1856:2112], w[:, 2112:2240], kpe, z96, kpe[:, sw], z96], 1))
    m["mk_w"] = wm; m["mq_w"] = wm
    wuq = I['mla_w_uq'][0]; wukv = I['mla_w_ukv'][0]
    wuq_sw = wuq.reshape(256, 8, 96).copy()
    wuq_sw[:, :, 64:96] = wuq_sw[:, :, 64:96][:, :, sw]
    wuq_all = f(np.concatenate([wuq, wuq_sw.reshape(256, 768)], 1))
    wk = wukv.reshape(128, 8, 128)
    wukv_all = f(np.concatenate([wk[:, :, :64].reshape(128, 512), wk[:, :, 64:].reshape(128, 512)], 1))
    gq = f(np.stack([I['mla_q_norm'][0][:128], I['mla_q_norm'][0][128:], I['mla_kv_norm'][0]], 1))
    ones = np.ones((128, 128), np.float32)

    def tabs(pos, nrope):
        c32, s32 = rope32_tables(pos, True)
        n = len(pos)
        cq = np.ones((96, n), np.float32); sq = np.zeros((96, n), np.float32)
        cq[64:96, :nrope] = c32[:, :nrope]; sq[64:96, :nrope] = s32[:, :nrope]; cq[0:32, :nrope] = c32[:, :nrope]; sq[0:32, :nrope] = s32[:, :nrope]
        return cq, sq
    cqa, sqa = tabs(np.concatenate([np.arange(8192), np.zeros(256, np.int64)]), 8192)
    cqo, sqo = tabs(np.arange(s * 2048, (s + 1) * 2048), 2048)
    for p_, (cq, sq) in (("mkf_", (cqa, sqa)), ("mqf_", (cqo, sqo))):
        m[p_ + "wuq"] = wuq_all; m[p_ + "wukv"] = wukv_all; m[p_ + "gq"] = gq; m[p_ + "ones"] = ones; m[p_ + "cq"] = f(cq); m[p_ + "sq"] = f(sq)
    m["tb_idn"] = idn
    m["f1_w"] = f(I['cd_w_out'][0][512:1024]); m["f1_ln"] = ln_layout(I['ln_g'][1, 0], I['ln_b'][1, 0]); m["f1_ones"] = ones
    m["m1_w1"] = f(I['mlp_w1'][1]); m["m1_w2"] = f(I['mlp_w2'][1]); m["m1_ln"] = ln_layout(I['ln_g'][1, 1], I['ln_b'][1, 1]); m["m1_ones"] = ones
    return m


_FUSED_CACHE = {}


def kernel(**I):
    I = {k: np.asarray(v, dtype=np.float32) for k, v in I.items()}
    if "nc" not in _FUSED_CACHE:
        _FUSED_CACHE["nc"] = fused_program()
    nc, C = _FUSED_CACHE["nc"]
    maps = []
    for c in range(NCORES):
        m = fused_inputs(I, c)
        fused_inputs1(I, c, m)
        maps.append(m)
    res = run_spmd(nc, maps)
    out = np.empty((2, 8192, 1024), np.float32)
    for c in range(NCORES):
        b, s = c // 4, c % 4
        out[b, s * 2048:(s + 1) * 2048] = res[c]["m1_o0"].T
    return out


def rwc_consts():
    t = np.arange(64)
    le = (t[:, None] <= t[None, :]).astype(np.float32); lt = (t[:, None] < t[None, :]).astype(np.float32)
    ge = le.T.copy(); gt = lt.T.copy()
    U = np.stack([le, lt, ge, gt], 1)
    q4 = lambda f, b: np.stack([f, f, b, b], 1)
    return np.ascontiguousarray(U), np.ascontiguousarray(np.stack([q4(lt, gt), q4(gt, lt), q4(le, ge), q4(np.eye(64, dtype=np.float32), np.eye(64, dtype=np.float32))], 1))


def build_rwc(C, T=8448, NCTX=4):
    NB = T // 64
    fm = {n: C.din(n, [128, T]) for n in ("an", "r", "bb", "km")}
    tmd = {n: C.din(n + "_tm", [T, 128]) for n in ("bb", "km", "v", "wf", "wb")}
    U_d = C.din("U", [64, 4, 64]); M_d = C.din("M", [64, 4, 4, 64])
    y_d = C.dout("y", [2, T, 128])
    Uc = C.sb("Uc", [64, 4, 64]); Mc = C.sb("Mc", [64, 4, 4, 64])
    ffm = {n: C.sb("f_" + n, [64, 2, 4, 64]) for n in fm}
    ftm = {n: C.sb("t_" + n, [64, 2, 2, 128]) for n in ("bb", "km", "v", "w")}
    lw = C.sb("lw", [64, 2, 128]); Eg = C.sb("Eg", [64, 4, 64]); Egi = C.sb("Egi", [64, 4, 64]); Egx = C.sb("Egx", [64, 4, 64]); Egt = C.sb("Egt", [64, 2, 128])
    At = C.sb("At", [64, 4, 64]); Rt = C.sb("Rt", [64, 4, 64]); Bt = C.sb("Bt", [64, 4, 64]); Kt = C.sb("Kt", [64, 4, 64])
    Btm = C.sb("Btm", [64, 2, 128]); Ktm = C.sb("Ktm", [64, 2, 128]); gC = C.sb("gC", [64, 4])
    Pm = C.sb("Pm", [64, 2, 4, 64]); PT = C.sb("PT", [64, 2, 4, 64]); Tm = C.sb("Tm", [64, 2, 4, 64])
    Mka = C.sb("Mka", [64, 4, 64]); Mbr = C.sb("Mbr", [64, 4, 64]); Mkr = C.sb("Mkr", [64, 4, 64])
    ST = C.sb("ST", [64, 4, 64]); W1 = C.sb("W1", [64, 4, 64]); UT = C.sb("UT", [64, 4, 64]); tmp = C.sb("tmp", [64, 4, 64]); ysb = C.sb("ysb", [64, 2, 4, 64])
    pss = [C.ps("ps%d" % i, [128, 512]) for i in range(8)]
    P = C.start()
    pctr = [0]

    def newps():
        i = pctr[0] % 8; pctr[0] += 1
        return pss[i], ("ps", i)
    P.dma("sp", Uc[:], U_d[:, :, :], writes=["Uc"]); P.dma("sp", Mc[:], M_d[:, :, :, :], writes=["Mc"])
    P.op("dve", lambda v: v.memset(ST[:], 0.0), writes=["ST"])
    order = [list(range(NB)), list(range(NCTX - 1, -1, -1)) + list(range(NB - 1, NCTX - 1, -1))]
    for i in range(NB):
        b = i % 2
        blk = [order[0][i], order[1][i]]
        for z in range(2):
            s0 = blk[z] * 64
            for n in fm:
                P.dma("sp", ffm[n][:, b, 2 * z:2 * z + 2, :], fm[n][:, s0:s0 + 64].rearrange("(c j) t -> j c t", c=2), writes=[("f" + n, b)])
            for n, src in (("bb", "bb"), ("km", "km"), ("v", "v"), ("w", "wf" if z == 0 else "wb")):
                P.dma("sp", ftm[n][:, b, z, :], tmd[src][s0:s0 + 64, :], writes=[("t" + n, b)])
        P.op("act", lambda a, b=b: a.activation(lw[:, :, :], ftm["w"][:, b, :, :], AF.Ln), reads=[("tw", b)], writes=["lw"])
        pg, pkg = newps(); px, pkx = newps(); pt_, pkt = newps()
        for z in range(2):
            for c in range(2):
                q = 2 * z + c
                P.op("pe", lambda t, z=z, c=c, q=q: t.matmul(pg[0:64, q * 64:(q + 1) * 64], lw[:, z, c * 64:(c + 1) * 64], Uc[:, 2 * z, :], start=True, stop=True), reads=["lw", "Uc"], writes=[pkg])
                P.op("pe", lambda t, z=z, c=c, q=q: t.matmul(px[0:64, q * 64:(q + 1) * 64], lw[:, z, c * 64:(c + 1) * 64], Uc[:, 2 * z + 1, :], start=True, stop=True), reads=["lw", "Uc"], writes=[pkx])
            P.op("pe", lambda t, z=z: t.matmul(pt_[0:64, z * 128:(z + 1) * 128], Uc[:, 2 * z, :], lw[:, z, :], start=True, stop=True), reads=["lw", "Uc"], writes=[pkt])
        f4 = lambda ap: ap.rearrange("p (q t) -> p q t", q=4)
        P.op("act", lambda a: a.activation(Eg[:], f4(pg[0:64, 0:256]), AF.Exp), reads=[pkg], writes=["Eg"])
        P.op("act", lambda a: a.activation(Egi[:], f4(pg[0:64, 0:256]), AF.Exp, scale=-1.0), reads=[pkg], writes=["Egi"])
        P.op("act", lambda a: a.activation(Egx[:], f4(px[0:64, 0:256]), AF.Exp), reads=[pkx], writes=["Egx"])
        P.op("act", lambda a: a.activation(Egt[:], pt_[0:64, 0:256].rearrange("p (z f) -> p z f", z=2), AF.Exp, scale=-1.0), reads=[pkt], writes=["Egt"])
        P.op("dve", lambda v, b=b: v.tensor_tensor(At[:], ffm["an"][:, b, :, :], Egx[:], ALU.mult), reads=[("fan", b), "Egx"], writes=["At"])
        P.op("pool", lambda v, b=b: v.tensor_tensor(Rt[:], ffm["r"][:, b, :, :], Eg[:], ALU.mult), reads=[("fr", b), "Eg"], writes=["Rt"])
        P.op("dve", lambda v, b=b: v.tensor_tensor(Bt[:], ffm["bb"][:, b, :, :], Egi[:], ALU.mult), reads=[("fbb", b), "Egi"], writes=["Bt"])
        P.op("pool", lambda v, b=b: v.tensor_tensor(Kt[:], ffm["km"][:, b, :, :], Egi[:], ALU.mult), reads=[("fkm", b), "Egi"], writes=["Kt"])
        P.op("dve", lambda v, b=b: v.tensor_tensor(Btm[:], ftm["bb"][:, b, :, :], Egt[:], ALU.mult), reads=[("tbb", b), "Egt"], writes=["Btm"])
        P.op("pool", lambda v, b=b: v.tensor_tensor(Ktm[:], ftm["km"][:, b, :, :], Egt[:], ALU.mult), reads=[("tkm", b), "Egt"], writes=["Ktm"])
        P.op("dve", lambda v: v.tensor_copy(gC[:, 0:2], Eg[:, 0:2, 63]), reads=["Eg"], writes=["gC"])
        P.op("dve", lambda v: v.tensor_copy(gC[:, 2:4], Eg[:, 2:4, 0]), reads=["Eg"], writes=["gC"])
        grams = [("N", Bt, At, 0, Pm, 0), ("NT", At, Bt, 1, PT, 0), ("Mka", Kt, At, 0, Mka, None), ("Mbr", Bt, Rt, 2, Mbr, None), ("Mkr", Kt, Rt, 2, Mkr, None)]
        for nm, L, Rr, mi, dst, slot in grams:
            ps, pk = newps()
            for q in range(4):
                P.op("pe", lambda t, ps=ps, q=q, L=L, Rr=Rr: t.matmul(ps[0:64, q * 64:(q + 1) * 64], L[:, q, :], Rr[:, q, :], start=True, stop=True), reads=[L.name if hasattr(L, "name") else nm, "At", "Bt", "Kt", "Rt"], writes=[pk])
            d_ap = dst[:, slot, :, :] if slot is not None else dst[:]
            P.op("dve", lambda v, ps=ps, d_ap=d_ap, mi=mi: v.tensor_tensor(d_ap, f4(ps[0:64, 0:256]), Mc[:, mi, :, :], ALU.mult), reads=[pk, "Mc"], writes=[nm + "m"])
        P.op("dve", lambda v: v.tensor_tensor(Tm[:, 0, :, :], Pm[:, 0, :, :], Mc[:, 3, :, :], ALU.add), reads=["Nm", "Mc"], writes=["Tm0"])
        cur = 0
        for lev in range(5):
            nxt = 1 - cur
            pp, pkp = newps(); pq, pkq = newps()
            for q in range(4):
                P.op("pe", lambda t, q=q, cur=cur: t.matmul(pp[0:64, q * 64:(q + 1) * 64], PT[:, cur, q, :], Pm[:, cur, q, :], start=True, stop=True), reads=["Nm", "NTm", "Pl%d" % cur, "PTl%d" % cur], writes=[pkp])
                P.op("pe", lambda t, q=q, cur=cur: t.matmul(pq[0:64, q * 64:(q + 1) * 64], Pm[:, cur, q, :], PT[:, cur, q, :], start=True, stop=True), reads=["Nm", "NTm", "Pl%d" % cur, "PTl%d" % cur], writes=[pkq])
            P.op("act", lambda a, nxt=nxt: a.copy(Pm[:, nxt, :, :], f4(pp[0:64, 0:256])), reads=[pkp], writes=["Pl%d" % nxt])
            P.op("dve", lambda v, nxt=nxt: v.tensor_copy(PT[:, nxt, :, :], f4(pq[0:64, 0:256])), reads=[pkq], writes=["PTl%d" % nxt])
            pt2, pkt2 = newps()
            for q in range(4):
                P.op("pe", lambda t, q=q, cur=cur, nxt=nxt: t.matmul(pt2[0:64, q * 64:(q + 1) * 64], PT[:, nxt, q, :], Tm[:, cur, q, :], start=True, stop=True), reads=["PTl%d" % nxt, "Tm%d" % cur], writes=[pkt2])
            P.op("dve", lambda v, cur=cur, nxt=nxt: v.tensor_tensor(Tm[:, nxt, :, :], f4(pt2[0:64, 0:256]), Tm[:, cur, :, :], ALU.add), reads=[pkt2, "Tm%d" % cur], writes=["Tm%d" % nxt])
            cur = nxt
        Tf = cur
        vsl = lambda q: ftm["v"][:, b, q // 2, (q % 2) * 64:(q % 2 + 1) * 64]
        pw, pkw = newps()
        for q in range(4):
            P.op("pe", lambda t, q=q: t.matmul(pw[0:64, q * 64:(q + 1) * 64], Mka[:, q, :], vsl(q), start=True, stop=False), reads=["Mkam", ("tv", b)], writes=[pkw])
            P.op("pe", lambda t, q=q: t.matmul(pw[0:64, q * 64:(q + 1) * 64], At[:, q, :], ST[:, q, :], start=False, stop=True), reads=["At", "ST"], writes=[pkw])
        P.op("act", lambda a: a.copy(W1[:], f4(pw[0:64, 0:256])), reads=[pkw], writes=["W1"])
        pu, pku = newps()
        for q in range(4):
            P.op("pe", lambda t, q=q, Tf=Tf: t.matmul(pu[0:64, q * 64:(q + 1) * 64], Tm[:, Tf, q, :], W1[:, q, :], start=True, stop=True), reads=["Tm%d" % Tf, "W1"], writes=[pku])
        P.op("act", lambda a: a.copy(UT[:], f4(pu[0:64, 0:256])), reads=[pku], writes=["UT"])
        psn, pksn = newps()
        for q in range(4):
            P.op("pe", lambda t, q=q: t.matmul(psn[0:64, q * 64:(q + 1) * 64], Ktm[:, q // 2, (q % 2) * 64:(q % 2 + 1) * 64], vsl(q), start=True, stop=False), reads=["Ktm", ("tv", b)], writes=[pksn])
            P.op("pe", lambda t, q=q: t.matmul(psn[0:64, q * 64:(q + 1) * 64], Btm[:, q // 2, (q % 2) * 64:(q % 2 + 1) * 64], UT[:, q, :], start=False, stop=True), reads=["Btm", "UT"], writes=[pksn])
        if min(blk) >= 0:
            py_, pky = newps()
            for q in range(4):
                P.op("pe", lambda t, q=q: t.matmul(py_[0:64, q * 64:(q + 1) * 64], Mkr[:, q, :], vsl(q), start=True, stop=False), reads=["Mkrm", ("tv", b)], writes=[pky])
                P.op("pe", lambda t, q=q: t.matmul(py_[0:64, q * 64:(q + 1) * 64], Rt[:, q, :], ST[:, q, :], start=False, stop=False), reads=["Rt", "ST"], writes=[pky])
                P.op("pe", lambda t, q=q: t.matmul(py_[0:64, q * 64:(q + 1) * 64], Mbr[:, q, :], UT[:, q, :], start=False, stop=True), reads=["Mbrm", "UT"], writes=[pky])
            P.op("act", lambda a, b=b: a.copy(ysb[:, b, :, :], f4(py_[0:64, 0:256])), reads=[pky], writes=[("ysb", b)])
            for z in range(2):
                if blk[z] >= NCTX:
                    P.dma("sp", y_d[z, blk[z] * 64:(blk[z] + 1) * 64, :].rearrange("t (c i) -> t c i", c=2), ysb[:, b, 2 * z:2 * z + 2, :], reads=[("ysb", b)])
        P.op("dve", lambda v: v.tensor_tensor(tmp[:], f4(psn[0:64, 0:256]), ST[:], ALU.add), reads=[pksn, "ST"], writes=["tmp"])
        for q in range(4):
            P.op("dve" if q % 2 else "pool", lambda v, q=q: v.tensor_scalar_mul(ST[:, q, :], tmp[:, q, :], gC[:, q:q + 1]), reads=["tmp", "gC"], writes=["ST"])
    return C.end()


def build_rwc2(C, T=8448, NCTX=4, stop_at=99):
    NB = T // 64
    fm = {n: C.din(n, [128, T]) for n in ("an", "r", "bb", "km")}
    tmd = {n: C.din(n + "_tm", [T, 128]) for n in ("bb", "km", "v", "wf", "wb")}
    U_d = C.din("U", [64, 4, 64]); M_d = C.din("M", [64, 4, 4, 64])
    y_d = C.dout("y", [2, T, 128])
    Uc = C.sb("Uc", [64, 4, 64]); Mc = C.sb("Mc", [64, 4, 4, 64])
    ffm = {n: C.sb("f_" + n, [64, 2, 2, 2, 64]) for n in fm}
    ftm = {n: C.sb("t_" + n, [64, 2, 2, 128]) for n in ("bb", "km", "v", "w")}
    lw = C.sb("lw", [64, 2, 128]); Eg = C.sb("Eg", [64, 2, 2, 64]); Egi = C.sb("Egi", [64, 2, 2, 64]); Egx = C.sb("Egx", [64, 2, 2, 64]); Egt = C.sb("Egt", [64, 2, 128])
    At = C.sb("At", [64, 2, 2, 64], BF16); Rt = C.sb("Rt", [64, 2, 2, 64], BF16); Bt = C.sb("Bt", [64, 2, 2, 64], BF16); Kt = C.sb("Kt", [64, 2, 2, 64], BF16)
    Btm = C.sb("Btm", [64, 2, 128], BF16); Ktm = C.sb("Ktm", [64, 2, 128], BF16); vb = C.sb("vb", [64, 2, 128], BF16); gC = C.sb("gC", [64, 2, 2])
    Pm = C.sb("Pm", [64, 2, 2, 2, 64]); PT = C.sb("PT", [64, 2, 2, 2, 64]); Tm = C.sb("Tm", [64, 2, 2, 2, 64])
    Mka = C.sb("Mka", [64, 2, 2, 64], BF16); Mbr = C.sb("Mbr", [64, 2, 2, 64], BF16); Mkr = C.sb("Mkr", [64, 2, 2, 64], BF16)
    ST = C.sb("ST", [64, 2, 2, 64]); STb = C.sb("STb", [64, 2, 2, 64], BF16); W1 = C.sb("W1", [64, 2, 2, 64]); UT = C.sb("UT", [64, 2, 2, 64], BF16)
    tmp = C.sb("tmp", [64, 2, 2, 64]); ysb = C.sb("ysb", [64, 2, 2, 2, 64])
    pss = [C.ps("ps%d" % i, [128, 512]) for i in range(8)]
    P = C.start()
    pctr = [0]

    def newps():
        i = pctr[0] % 8; pctr[0] += 1
        return pss[i], ("ps", i)
    P.dma("sp", Uc[:], U_d[:, :, :], writes=["Uc"]); P.dma("sp", Mc[:], M_d[:, :, :, :], writes=["Mc"])
    P.op("dve", lambda v: v.memset(ST[:], 0.0), writes=[("ST", 0), ("ST", 1)])
    P.op("dve", lambda v: v.memset(STb[:], 0.0), writes=[("STb", 0), ("STb", 1)])
    order = [list(range(NB)), list(range(NCTX - 1, -1, -1)) + list(range(NB - 1, NCTX - 1, -1))]
    c2 = lambda ap: ap.rearrange("p (c t) -> p c t", c=2)

    def make_steps(i, z):
        b = i % 2
        blk = order[z][i]; s0 = blk * 64
        steps = []
        K = lambda n: (n, z)

        def s_load():
            for n in fm:
                P.dma("sp", ffm[n][:, b, z, :, :], fm[n][:, s0:s0 + 64].rearrange("(c j) t -> j c t", c=2), writes=[("f" + n, b, z)])
            for n, src in (("bb", "bb"), ("km", "km"), ("v", "v"), ("w", "wf" if z == 0 else "wb")):
                P.dma("sp", ftm[n][:, b, z, :], tmd[src][s0:s0 + 64, :], writes=[("t" + n, b, z)])
        steps.append(s_load)

        def s_ln():
            P.op("act", lambda a: a.activation(lw[:, z, :], ftm["w"][:, b, z, :], AF.Ln), reads=[("tw", b, z)], writes=[K("lw")])
            P.op("dve", lambda v: v.tensor_copy(vb[:, z, :], ftm["v"][:, b, z, :]), reads=[("tv", b, z)], writes=[K("vb")])
        steps.append(s_ln)
        st = {}

        def s_cum():
            pg, pkg = newps(); px, pkx = newps(); pt_, pkt = newps()
            st.update(pg=pg, pkg=pkg, px=px, pkx=pkx, pt=pt_, pkt=pkt)
            for c in range(2):
                P.op("pe", lambda t, c=c: t.matmul(pg[0:64, c * 64:(c + 1) * 64], lw[:, z, c * 64:(c + 1) * 64], Uc[:, 2 * z, :], start=True, stop=True), reads=[K("lw"), "Uc"], writes=[pkg])
                P.op("pe", lambda t, c=c: t.matmul(px[0:64, c * 64:(c + 1) * 64], lw[:, z, c * 64:(c + 1) * 64], Uc[:, 2 * z + 1, :], start=True, stop=True), reads=[K("lw"), "Uc"], writes=[pkx])
            P.op("pe", lambda t: t.matmul(pt_[0:64, 0:128], Uc[:, 2 * z, :], lw[:, z, :], start=True, stop=True), reads=[K("lw"), "Uc"], writes=[pkt])
        steps.append(s_cum)

        def s_exp():
            pg, pkg, px, pkx, pt_, pkt = st["pg"], st["pkg"], st["px"], st["pkx"], st["pt"], st["pkt"]
            P.op("act", lambda a: a.activation(Eg[:, z, :, :], c2(pg[0:64, 0:128]), AF.Exp), reads=[pkg], writes=[K("Eg")])
            P.op("act", lambda a: a.activation(Egi[:, z, :, :], c2(pg[0:64, 0:128]), AF.Exp, scale=-1.0), reads=[pkg], writes=[K("Egi")])
            P.op("act", lambda a: a.activation(Egx[:, z, :, :], c2(px[0:64, 0:128]), AF.Exp), reads=[pkx], writes=[K("Egx")])
            P.op("act", lambda a: a.activation(Egt[:, z, :], pt_[0:64, 0:128], AF.Exp, scale=-1.0), reads=[pkt], writes=[K("Egt")])
        steps.append(s_exp)

        def s_prod():
            P.op("dve", lambda v: v.tensor_tensor(At[:, z, :, :], ffm["an"][:, b, z, :, :], Egx[:, z, :, :], ALU.mult), reads=[("fan", b, z), K("Egx")], writes=[K("At")])
            P.op("dve", lambda v: v.tensor_tensor(Rt[:, z, :, :], ffm["r"][:, b, z, :, :], Eg[:, z, :, :], ALU.mult), reads=[("fr", b, z), K("Eg")], writes=[K("Rt")])
            P.op("dve", lambda v: v.tensor_tensor(Bt[:, z, :, :], ffm["bb"][:, b, z, :, :], Egi[:, z, :, :], ALU.mult), reads=[("fbb", b, z), K("Egi")], writes=[K("Bt")])
            P.op("dve", lambda v: v.tensor_tensor(Kt[:, z, :, :], ffm["km"][:, b, z, :, :], Egi[:, z, :, :], ALU.mult), reads=[("fkm", b, z), K("Egi")], writes=[K("Kt")])
            P.op("dve", lambda v: v.tensor_tensor(Btm[:, z, :], ftm["bb"][:, b, z, :], Egt[:, z, :], ALU.mult), reads=[("tbb", b, z), K("Egt")], writes=[K("Btm")])
            P.op("dve", lambda v: v.tensor_tensor(Ktm[:, z, :], ftm["km"][:, b, z, :], Egt[:, z, :], ALU.mult), reads=[("tkm", b, z), K("Egt")], writes=[K("Ktm")])
            col = 63 if z == 0 else 0
            P.op("dve", lambda v: v.tensor_copy(gC[:, z, :], Eg[:, z, :, col]), reads=[K("Eg")], writes=[K("gC")])
        steps.append(s_prod)

        def s_gram():
            pa, pka = newps(); pb, pkb = newps()
            for gi, (L, Rr) in enumerate(((Bt, At), (At, Bt), (Kt, At))):
                for c in range(2):
                    P.op("pe", lambda t, gi=gi, c=c, L=L, Rr=Rr: t.matmul(pa[0:64, gi * 128 + c * 64:gi * 128 + (c + 1) * 64], L[:, z, c, :], Rr[:, z, c, :], start=True, stop=True), reads=[K("At"), K("Bt"), K("Kt")], writes=[pka])
            for gi, (L, Rr) in enumerate(((Bt, Rt), (Kt, Rt))):
                for c in range(2):
                    P.op("pe", lambda t, gi=gi, c=c, L=L, Rr=Rr: t.matmul(pb[0:64, gi * 128 + c * 64:gi * 128 + (c + 1) * 64], L[:, z, c, :], Rr[:, z, c, :], start=True, stop=True), reads=[K("Rt"), K("Bt"), K("Kt")], writes=[pkb])
            mq = slice(2 * z, 2 * z + 2)
            P.op("dve", lambda v: v.tensor_tensor(Pm[:, 0, z, :, :], c2(pa[0:64, 0:128]), Mc[:, 0, mq, :], ALU.mult), reads=[pka, "Mc"], writes=[K("P0")])
            P.op("dve", lambda v: v.tensor_tensor(PT[:, 0, z, :, :], c2(pa[0:64, 128:256]), Mc[:, 1, mq, :], ALU.mult), reads=[pka, "Mc"], writes=[K("PT0")])
            P.op("dve", lambda v: v.tensor_tensor(Mka[:, z, :, :], c2(pa[0:64, 256:384]), Mc[:, 0, mq, :], ALU.mult), reads=[pka, "Mc"], writes=[K("Mka")])
            P.op("dve", lambda v: v.tensor_tensor(Mbr[:, z, :, :], c2(pb[0:64, 0:128]), Mc[:, 2, mq, :], ALU.mult), reads=[pkb, "Mc"], writes=[K("Mbr")])
            P.op("dve", lambda v: v.tensor_tensor(Mkr[:, z, :, :], c2(pb[0:64, 128:256]), Mc[:, 2, mq, :], ALU.mult), reads=[pkb, "Mc"], writes=[K("Mkr")])
            P.op("dve", lambda v: v.tensor_tensor(Tm[:, 0, z, :, :], Pm[:, 0, z, :, :], Mc[:, 3, mq, :], ALU.add), reads=[K("P0"), "Mc"], writes=[K("T0")])
        steps.append(s_gram)
        for lev in range(5):
            cur, nxt = lev % 2, 1 - lev % 2

            def s_sq(cur=cur, nxt=nxt):
                pp, pkp = newps(); pq, pkq = newps()
                for c in range(2):
                    P.op("pe", lambda t, c=c: t.matmul(pp[0:64, c * 64:(c + 1) * 64], PT[:, cur, z, c, :], Pm[:, cur, z, c, :], start=True, stop=True), reads=[K("P%d" % cur), K("PT%d" % cur)], writes=[pkp])
                    P.op("pe", lambda t, c=c: t.matmul(pq[0:64, c * 64:(c + 1) * 64], Pm[:, cur, z, c, :], PT[:, cur, z, c, :], start=True, stop=True), reads=[K("P%d" % cur), K("PT%d" % cur)], writes=[pkq])
                P.op("act", lambda a: a.copy(Pm[:, nxt, z, :, :], c2(pp[0:64, 0:128])), reads=[pkp], writes=[K("P%d" % nxt)])
                P.op("dve", lambda v: v.tensor_copy(PT[:, nxt, z, :, :], c2(pq[0:64, 0:128])), reads=[pkq], writes=[K("PT%d" % nxt)])
            steps.append(s_sq)

            def s_t(cur=cur, nxt=nxt):
                pt2, pkt2 = newps()
                for c in range(2):
                    P.op("pe", lambda t, c=c: t.matmul(pt2[0:64, c * 64:(c + 1) * 64], PT[:, nxt, z, c, :], Tm[:, cur, z, c, :], start=True, stop=True), reads=[K("PT%d" % nxt), K("T%d" % cur)], writes=[pkt2])
                P.op("dve", lambda v: v.tensor_tensor(Tm[:, nxt, z, :, :], c2(pt2[0:64, 0:128]), Tm[:, cur, z, :, :], ALU.add), reads=[pkt2, K("T%d" % cur)], writes=[K("T%d" % nxt)])
            steps.append(s_t)
        Tf = 1

        def s_w1():
            pw, pkw = newps(); st.update(pw=pw, pkw=pkw)
            for c in range(2):
                P.op("pe", lambda t, c=c: t.matmul(pw[0:64, c * 64:(c + 1) * 64], Mka[:, z, c, :], vb[:, z, c * 64:(c + 1) * 64], start=True, stop=False), reads=[K("Mka"), K("vb")], writes=[pkw])
                P.op("pe", lambda t, c=c: t.matmul(pw[0:64, c * 64:(c + 1) * 64], At[:, z, c, :], STb[:, z, c, :], start=False, stop=True), reads=[K("At"), K("STb")], writes=[pkw])
            P.op("act", lambda a: a.copy(W1[:, z, :, :], c2(pw[0:64, 0:128])), reads=[pkw], writes=[K("W1")])
        steps.append(s_w1)

        def s_ut():
            pu, pku = newps()
            for c in range(2):
                P.op("pe", lambda t, c=c: t.matmul(pu[0:64, c * 64:(c + 1) * 64], Tm[:, Tf, z, c, :], W1[:, z, c, :], start=True, stop=True), reads=[K("T%d" % Tf), K("W1")], writes=[pku])
            P.op("act", lambda a: a.copy(UT[:, z, :, :], c2(pu[0:64, 0:128])), reads=[pku], writes=[K("UT")])
        steps.append(s_ut)

        def s_fin():
            psn, pksn = newps()
            for c in range(2):
                P.op("pe", lambda t, c=c: t.matmul(psn[0:64, c * 64:(c + 1) * 64], Ktm[:, z, c * 64:(c + 1) * 64], vb[:, z, c * 64:(c + 1) * 64], start=True, stop=False), reads=[K("Ktm"), K("vb")], writes=[pksn])
                P.op("pe", lambda t, c=c: t.matmul(psn[0:64, c * 64:(c + 1) * 64], Btm[:, z, c * 64:(c + 1) * 64], UT[:, z, c, :], start=False, stop=True), reads=[K("Btm"), K("UT")], writes=[pksn])
            if blk >= NCTX:
                py_, pky = newps()
                for c in range(2):
                    P.op("pe", lambda t, c=c: t.matmul(py_[0:64, c * 64:(c + 1) * 64], Mkr[:, z, c, :], vb[:, z, c * 64:(c + 1) * 64], start=True, stop=False), reads=[K("Mkr"), K("vb")], writes=[pky])
                    P.op("pe", lambda t, c=c: t.matmul(py_[0:64, c * 64:(c + 1) * 64], Rt[:, z, c, :], STb[:, z, c, :], start=False, stop=False), reads=[K("Rt"), K("STb")], writes=[pky])
                    P.op("pe", lambda t, c=c: t.matmul(py_[0:64, c * 64:(c + 1) * 64], Mbr[:, z, c, :], UT[:, z, c, :], start=False, stop=True), reads=[K("Mbr"), K("UT")], writes=[pky])
                P.op("act", lambda a: a.copy(ysb[:, b, z, :, :], c2(py_[0:64, 0:128])), reads=[pky], writes=[("ysb", b, z)])
                P.dma("sp", y_d[z, s0:s0 + 64, :].rearrange("t (c i) -> t c i", c=2), ysb[:, b, z, :, :], reads=[("ysb", b, z)])
            P.op("dve", lambda v: v.tensor_tensor(tmp[:, z, :, :], c2(psn[0:64, 0:128]), ST[:, z, :, :], ALU.add), reads=[pksn, K("ST")], writes=[K("tmp")])
            for c in range(2):
                P.op("dve", lambda v, c=c: v.tensor_scalar_mul(ST[:, z, c, :], tmp[:, z, c, :], gC[:, z, c:c + 1]), reads=[K("tmp"), K("gC")], writes=[K("ST")])
            P.op("act", lambda a: a.copy(STb[:, z, :, :], ST[:, z, :, :]), reads=[K("ST")], writes=[K("STb")])
        steps.append(s_fin)
        return steps

    for i in range(NB):
        sa, sb_ = make_steps(i, 0), make_steps(i, 1)
        for k in range(min(len(sa), stop_at)):
            sa[k](); sb_[k]()
    return C.end()


def build_tr_fm(C, names, T):
    srcs = {n: C.din("src_" + n, [128, T]) for n in names}
    dsts = {n: C.dout("dst_" + n, [T, 128]) for n in names}
    idn = C.din("idn", [128, 128])
    st = C.sb("st", [128, 3, 512]); ot = C.sb("ot", [128, 3, 4, 128]); idt = C.sb("idt", [128, 128])
    pss = [C.ps("ps%d" % i, [128, 512]) for i in range(4)]
    P = C.start()
    P.dma("sp", idt[:], idn[:, :], writes=["idt"])
    it = 0
    for n in names:
        for c0 in range(0, T, 512):
            nr = min(4, (T - c0) // 128)
            b = it % 3; pi = it % 4; it += 1
            P.dma("sp", st[:, b, 0:nr * 128], srcs[n][:, c0:c0 + nr * 128], writes=[("st", b)])
            for k in range(nr):
                P.op("pe", lambda t, b=b, k=k, pi=pi: t.matmul(pss[pi][:, k * 128:(k + 1) * 128], st[:, b, k * 128:(k + 1) * 128], idt[:, :], start=True, stop=True), reads=[("st", b), "idt"], writes=[("ps", pi)])
            P.op("act" if it % 2 else "dve", (lambda a, b=b, pi=pi, nr=nr: a.copy(ot[:, b, 0:nr, :], pss[pi][:, 0:nr * 128].rearrange("p (n c) -> p n c", c=128))) if it % 2 else
                 (lambda v, b=b, pi=pi, nr=nr: v.tensor_copy(ot[:, b, 0:nr, :], pss[pi][:, 0:nr * 128].rearrange("p (n c) -> p n c", c=128))), reads=[("ps", pi)], writes=[("ot", b)])
            P.dma("sp", dsts[n][c0:c0 + nr * 128, :].rearrange("(n p) c -> p n c", p=128), ot[:, b, 0:nr, :], reads=[("ot", b)])
    return C.end()
```

```python
import numpy as np
from contextlib import ExitStack
import concourse.bass as bass
import concourse.mybir as mybir
from concourse.bass_utils import run_bass_kernel_spmd

F32 = mybir.dt.float32
BF16 = mybir.dt.bfloat16
AF = mybir.ActivationFunctionType
ALU = mybir.AluOpType
AX = mybir.AxisListType

D = 1024
NCORES = 8
ALPHA = 4.0 ** 0.25
LN_EPS = 1e-5


class Prog:
    def __init__(self, nc, n_dma_sems=32, same_engine_sync=("act", "pool", "dve")):
        self.nc = nc
        self.eng = {"pe": nc.tensor, "dve": nc.vector, "act": nc.scalar, "pool": nc.gpsimd, "sp": nc.sync}
        self.sem, self.cnt, self.stack = {}, {}, ExitStack()
        for e in self.eng:
            self.sem[e] = self.stack.enter_context(nc.semaphore("c_" + e))
            self.cnt[e] = 0
        self.dsem = [self.stack.enter_context(nc.semaphore("d_%d" % i)) for i in range(n_dma_sems)]
        self.dcnt = [0] * n_dma_sems
        self.dnext = 0
        self.waited = {e: {} for e in self.eng}
        self.last_w, self.readers = {}, {}
        self.same = set(same_engine_sync)
        self.rr = 0
        self.csem = self.stack.enter_context(nc.semaphore("c_coll"))
        self.ccnt = 0

    def _semobj(self, k):
        return self.sem[k] if isinstance(k, str) else self.dsem[k]

    def _wait(self, e, k, v):
        if self.waited[e].get(k, 0) >= v:
            return
        self.eng[e].wait_ge(self._semobj(k), v)
        self.waited[e][k] = v

    def _deps(self, e, reads, writes):
        deps = {}

        def add(kv):
            if kv is not None and deps.get(kv[0], 0) < kv[1]:
                deps[kv[0]] = kv[1]
        for b in reads:
            add(self.last_w.get(b))
        for b in writes:
            add(self.last_w.get(b))
            for kv in self.readers.get(b, {}).items():
                add(kv)
        for k, v in deps.items():
            if k == e and e not in self.same:
                continue
            self._wait(e, k, v)

    def _record(self, mark, reads, writes):
        for b in reads:
            r = self.readers.setdefault(b, {})
            if r.get(mark[0], 0) < mark[1]:
                r[mark[0]] = mark[1]
        for b in writes:
            self.last_w[b] = mark
            self.readers[b] = {}

    def op(self, e, fn, reads=(), writes=()):
        self._deps(e, reads, writes)
        ins = fn(self.eng[e])
        self.cnt[e] += 1
        ins.then_inc(self.sem[e], 1)
        self._record((e, self.cnt[e]), reads, writes)

    def dma(self, e, out, in_, reads=(), writes=(), **kw):
        i = self.dnext
        self.dnext = (self.dnext + 1) % len(self.dsem)
        if self.dcnt[i] > 0:
            self._wait(e, i, self.dcnt[i])
        self._deps(e, reads, writes)
        ins = self.eng[e].dma_start(out=out, in_=in_, **kw)
        self.dcnt[i] += 16
        ins.then_inc(self.dsem[i], 16)
        self._record((i, self.dcnt[i]), reads, writes)

    def finish(self, e="sp"):
        self.coll_wait()
        for i in range(len(self.dsem)):
            if self.dcnt[i] > 0:
                self._wait(e, i, self.dcnt[i])
        for k in self.eng:
            if k != e and self.cnt[k] > 0:
                self._wait(e, k, self.cnt[k])

    def barrier(self):
        for e in self.eng:
            for i in range(len(self.dsem)):
                if self.dcnt[i] > 0:
                    self._wait(e, i, self.dcnt[i])
            for k in self.eng:
                if k != e and self.cnt[k] > 0:
                    self._wait(e, k, self.cnt[k])

    def coll_issue(self, kind, op, groups, in_ap, out_ap, barrier=True):
        if barrier:
            self.barrier()
        ins = self.nc.gpsimd.collective_compute(kind, op, replica_groups=groups, ins=[in_ap], outs=[out_ap])
        self.ccnt += 1
        ins.then_inc(self.csem)

    def coll_wait(self):
        for e in self.eng:
            if self.ccnt > 0 and self.waited[e].get("cc", 0) < self.ccnt:
                self.eng[e].wait_ge(self.csem, self.ccnt)
                self.waited[e]["cc"] = self.ccnt

    def coll(self, kind, op, groups, in_ap, out_ap):
        self.coll_issue(kind, op, groups, in_ap, out_ap)
        self.coll_wait()

    def close(self):
        self.stack.close()

    def alt(self, engs=("dve", "pool")):
        self.rr += 1
        return engs[self.rr % len(engs)]


class Ctx:
    def __init__(self, shared=False):
        self.nc = bass.Bass("TRN2", target_bir_lowering=False)
        self.es = ExitStack()
        self.P = None
        self.shared = shared
        self.bind = {}
        self.pref = ""
        self.ext_in = []

    def _io(self, name, shape, dt, kind):
        if name in self.bind:
            ap = self.bind[name]
            assert [int(x) for x in ap.shape] == [int(x) for x in shape], (name, ap.shape, shape)
            return ap
        full = self.pref + name
        if kind == "ExternalInput":
            self.ext_in.append(full)
        return self.nc.dram_tensor(full, list(shape), dt, kind=kind).ap()

    def din(self, name, shape, dt=F32):
        return self._io(name, shape, dt, "ExternalInput")

    def dout(self, name, shape, dt=F32):
        return self._io(name, shape, dt, "ExternalOutput")

    def scratch(self, name, shape, dt=F32):
        return self.nc.dram_tensor(name, list(shape), dt).ap()

    def sb(self, name, shape, dt=F32):
        return self.es.enter_context(self.nc.sbuf_tensor("s_" + self.pref + name, list(shape), dt))

    def ps(self, name, shape, dt=F32):
        return self.es.enter_context(self.nc.psum_tensor("p_" + self.pref + name, list(shape), dt))

    def start(self, **kw):
        if self.P is None:
            self.P = Prog(self.nc, **kw)
        return self.P

    def stage(self, pref, bind):
        self.pref = pref
        self.bind = dict(bind)

    def end(self):
        if self.shared:
            self.P.barrier()
            self.es.close()
            self.es = ExitStack()
            self.bind = {}
            return None
        self.P.finish("sp")
        self.P.close()
        self.es.close()
        return self.nc

    def finalize(self):
        self.P.finish("sp")
        self.P.close()
        return self.nc


def run_spmd(nc, in_maps):
    res = run_bass_kernel_spmd(nc, in_maps, core_ids=list(range(len(in_maps))))
    return res.results


def load_w_bf16(C, P, wdst, wkey, wsrc, K, ncols, stage, skey, col0=0, engs=("dve", "pool", "act")):
    KC = (K + 127) // 128
    for kc in range(KC):
        kp = min(128, K - kc * 128)
        b = kc % 2
        P.dma("sp", stage[0:kp, b, 0:ncols], wsrc[kc * 128:kc * 128 + kp, col0:col0 + ncols], writes=[(skey, b)])
        e = engs[kc % len(engs)]
        if e == "act":
            P.op("act", lambda a, kp=kp, b=b, kc=kc: a.copy(wdst[0:kp, kc, 0:ncols], stage[0:kp, b, 0:ncols]),
                 reads=[(skey, b)], writes=[(wkey, kc)])
        else:
            P.op(e, lambda v, kp=kp, b=b, kc=kc: v.tensor_copy(wdst[0:kp, kc, 0:ncols], stage[0:kp, b, 0:ncols]),
                 reads=[(skey, b)], writes=[(wkey, kc)])


def build_k0(ncol, C=None):
    C = C or Ctx()
    cT = C.din("cT", [128, 8, 3])
    w = C.din("w", [2, 1024, ncol])
    bia = C.din("b", [3, 2, ncol])
    out = C.dout("out", [3, 2, ncol])
    cs = C.sb("cs", [128, 8, 3])
    cs2 = C.sb("cs2", [128, 8, 3])
    wt = C.sb("wt", [128, 2, 8, ncol])
    bt = C.sb("bt", [3, 2, ncol])
    ot = C.sb("ot", [3, 2, ncol])
    pss = [C.ps("ps%d" % i, [128, 512]) for i in range(2)]
    P = C.start()
    P.dma("sp", cs[:], cT[:, :, :], writes=["cs"])
    P.dma("sp", bt[:], bia[:, :, :], writes=["bt"])
    for l in range(2):
        P.dma("sp", wt[:, l, :, :], w[l].rearrange("(kc p) n -> p kc n", p=128), writes=[("wt", l)])
    P.op("act", lambda a: a.activation(cs2[:], cs[:], AF.Silu), reads=["cs"], writes=["cs2"])
    i = 0
    for l in range(2):
        for n0 in range(0, ncol, 512):
            n1 = min(ncol, n0 + 512)
            ps = pss[i % 2]; pk = "ps%d" % (i % 2); i += 1
            for kc in range(8):
                P.op("pe", lambda t, kc=kc, l=l, n0=n0, n1=n1, ps=ps: t.matmul(ps[0:3, 0:n1 - n0], cs2[:, kc, :], wt[:, l, kc, n0:n1], start=(kc == 0), stop=(kc == 7)),
                     reads=["cs2", ("wt", l)], writes=[pk])
            P.op("dve", lambda v, l=l, n0=n0, n1=n1, ps=ps: v.tensor_tensor(ot[:, l, n0:n1], ps[0:3, 0:n1 - n0], bt[:, l, n0:n1], ALU.add),
                 reads=[pk, "bt"], writes=["ot"])
    P.dma("sp", out[:, :, :], ot[:], reads=["ot"])
    return C.end()


def stage_k0(c, c_ctx, ada_w, ada_b):
    ncol = 6144 // NCORES
    nc = build_k0(ncol)
    cc = np.stack([c[0], c[1], c_ctx], axis=1).astype(np.float32)
    cT = np.ascontiguousarray(cc.reshape(8, 128, 3).transpose(1, 0, 2))
    maps = []
    for i in range(NCORES):
        sl = slice(i * ncol, (i + 1) * ncol)
        maps.append({"cT": cT, "w": np.ascontiguousarray(ada_w[:, :, sl]),
                     "b": np.ascontiguousarray(np.broadcast_to(ada_b[None, :, sl], (3, 2, ncol)))})
    res = run_spmd(nc, maps)
    m = np.concatenate([r["out"] for r in res], axis=2)
    return np.ascontiguousarray(m.transpose(1, 0, 2))


def mod_layout(mrow):
    return np.ascontiguousarray(mrow.reshape(6, 8, 128).transpose(2, 0, 1))


def build_k1(ncols, segs, C=None):
    C = C or Ctx()
    W = C.din("w", [1024, ncols])
    xs, mods, outs = [], [], []
    for si, (ntok, tile) in enumerate(segs):
        xs.append(C.din("x%d" % si, [1024, ntok]))
        mods.append(C.din("mod%d" % si, [128, 6, 8]))
        outs.append(C.dout("p%d" % si, [ncols, ntok]))
    TM = max(t for _, t in segs)
    NCH = (ncols + 127) // 128
    wb = C.sb("wb", [128, 8, ncols], BF16)
    stage = C.sb("stage", [128, 2, ncols])
    xt = C.sb("xt", [128, 2, 8, TM])
    ub = C.sb("ub", [128, 2, 8, TM], BF16)
    ost = C.sb("ost", [128, 4, TM])
    mt = C.sb("mt", [128, len(segs), 6, 8])
    sc1 = C.sb("sc1", [128, len(segs), 8])
    pss = [C.ps("ps%d" % i, [128, 512]) for i in range(4)]
    P = C.start()
    for si in range(len(segs)):
        P.dma("sp", mt[:, si, :, :], mods[si][:, :, :], writes=[("mt", si)])
        P.op("dve", lambda v, si=si: v.tensor_scalar_add(sc1[:, si, :], mt[:, si, 1, :], 1.0), reads=[("mt", si)], writes=[("sc1", si)])
    load_w_bf16(C, P, wb, "wb", W, 1024, ncols, stage, "stage")
    it = 0
    oi = 0
    for si, (ntok, tile) in enumerate(segs):
        xv = xs[si].rearrange("(kc p) t -> p kc t", p=128)
        for t0 in range(0, ntok, tile):
            b = it % 2; it += 1
            P.dma("sp", xt[:, b, :, 0:tile], xv[:, :, t0:t0 + tile], writes=[("xt", b)])
            for kc in range(8):
                e = P.alt()
                P.op(e, lambda v, b=b, kc=kc, si=si, tile=tile: v.tensor_scalar(ub[:, b, kc, 0:tile], xt[:, b, kc, 0:tile], sc1[:, si, kc:kc + 1], mt[:, si, 0, kc:kc + 1], ALU.mult, ALU.add),
                     reads=[("xt", b), ("sc1", si), ("mt", si)], writes=[("ub", b, kc)])
            for ch in range(NCH):
                c0 = ch * 128; cn = min(128, ncols - c0)
                pi = oi % 4; ob = oi % 4; oi += 1
                ps = pss[pi]
                for kc in range(8):
                    P.op("pe", lambda t, ps=ps, cn=cn, c0=c0, kc=kc, b=b, tile=tile: t.matmul(ps[0:cn, 0:tile], wb[:, kc, c0:c0 + cn], ub[:, b, kc, 0:tile], start=(kc == 0), stop=(kc == 7)),
                         reads=[("wb", kc), ("ub", b, kc)], writes=[("ps", pi)])
                if oi % 2:
                    P.op("act", lambda a, ps=ps, cn=cn, ob=ob, tile=tile: a.copy(ost[0:cn, ob, 0:tile], ps[0:cn, 0:tile]), reads=[("ps", pi)], writes=[("ost", ob)])
                else:
                    P.op("dve", lambda v, ps=ps, cn=cn, ob=ob, tile=tile: v.tensor_copy(ost[0:cn, ob, 0:tile], ps[0:cn, 0:tile]), reads=[("ps", pi)], writes=[("ost", ob)])
                P.dma("sp", outs[si][c0:c0 + cn, t0:t0 + tile], ost[0:cn, ob, 0:tile], reads=[("ost", ob)])
    return C.end()


def stage_k1(x, ctx, m_l, w_in):
    ncols = w_in.shape[1]
    nc = build_k1(ncols, [(2048, 512), (64, 64)])
    maps = []
    for c in range(NCORES):
        b, s = c // 4, c % 4
        maps.append({"w": np.ascontiguousarray(w_in),
                     "x0": np.ascontiguousarray(x[b, s * 2048:(s + 1) * 2048].T), "mod0": mod_layout(m_l[b]),
                     "x1": np.ascontiguousarray(ctx[b, s * 64:(s + 1) * 64].T), "mod1": mod_layout(m_l[2])})
    res = run_spmd(nc, maps)
    pl = np.empty((2, 8192, ncols), np.float32)
    pc = np.empty((2, 256, ncols), np.float32)
    for c in range(NCORES):
        b, s = c // 4, c % 4
        pl[b, s * 2048:(s + 1) * 2048] = res[c]["p0"].T
        pc[b, s * 64:(s + 1) * 64] = res[c]["p1"].T
    return pl, pc


def build_gla(T, stop=99, C=None):
    NT = T // 128
    BLK = 6
    assert NT % BLK == 0
    C = C or Ctx()
    qT = C.din("qT", [2, 64, T]); kT = C.din("kT", [2, 64, T])
    kk = C.din("k", [2, T, 64]); vv = C.din("v", [2, T, 128])
    a1T = C.din("a1T", [2, 16, T]); a2 = C.din("a2", [2, 16, 64]); ab = C.din("ab", [2, 1, 64])
    tri_d = C.din("tri", [128, 128]); mask_d = C.din("mask", [128, 128])
    o_d = C.dout("o", [2, T, 128])
    W = BLK * 128
    qs = C.sb("qs", [64, 2, 2, W]); ks = C.sb("ks", [64, 2, 2, W])
    kt = C.sb("kt", [128, 2, 2, BLK, 64]); vt = C.sb("vt", [128, 2, 2, BLK, 128])
    vb = C.sb("vb", [128, 2, 2, 128], BF16)
    a1s = C.sb("a1s", [16, 2, 2, W]); a2s = C.sb("a2s", [16, 2, 64]); abs_ = C.sb("abs", [1, 2, 64])
    ones = C.sb("ones", [1, 128])
    tri = C.sb("tri", [128, 128]); mask = C.sb("mask", [128, 128])
    ez = C.sb("ez", [128, 2, 2, 64]); sp = C.sb("sp", [128, 2, 2, 64])
    ecT = C.sb("ecT", [64, 2, 2, 128]); encT = C.sb("encT", [64, 2, 2, 128]); enc = C.sb("enc", [128, 2, 2, 64])
    el = C.sb("el", [64, 2, 2, 2])
    qa = C.sb("qa", [64, 2, 2, 128], BF16); qb_ = C.sb("qb", [64, 2, 2, 128], BF16)
    kTb = C.sb("kTb", [64, 2, 2, 128], BF16); kb = C.sb("kb", [128, 2, 2, 64], BF16)
    att = C.sb("att", [128, 2, 2, 128], BF16)
    S = C.sb("S", [64, 2, 128]); Sb = C.sb("Sb", [64, 2, 3, 128], BF16); tmp = C.sb("tmp", [64, 2, 128])
    osb = C.sb("osb", [128, 2, 2, 128])
    NPS = 8
    pss = [C.ps("ps%d" % i, [128, 512]) for i in range(NPS)]
    P = C.start()
    pctr = [0]

    def newps():
        i = pctr[0] % NPS; pctr[0] += 1
        return pss[i], ("ps", i)

    P.dma("sp", tri[:], tri_d[:, :], writes=["tri"]); P.dma("sp", mask[:], mask_d[:, :], writes=["mask"])
    P.dma("sp", a2s[:], a2.rearrange("z r d -> r z d"), writes=["a2s"])
    P.dma("sp", abs_[:], ab.rearrange("z o d -> o z d"), writes=["abs"])
    P.op("dve", lambda v: v.memset(ones[:], 1.0), writes=["ones"])
    P.op("dve", lambda v: v.memset(S[:], 0.0), writes=[("S", 0), ("S", 1)])
    P.op("pool", lambda v: v.memset(Sb[:], 0.0), writes=[("Sb", 0, 0), ("Sb", 1, 0), ("Sb", 0, 1), ("Sb", 1, 1), ("Sb", 0, 2), ("Sb", 1, 2)])
    P.op("pool", lambda v: v.memset(qa[:], 0.0), writes=[("qa", z, p) for z in range(2) for p in range(2)])
    P.op("pool", lambda v: v.memset(qb_[:], 0.0), writes=[("qb", z, p) for z in range(2) for p in range(2)])
    sbi = [0, 0]
    for blk in range(NT // BLK):
        bb = blk % 2
        t0 = blk * W
        for z in range(2):
            P.dma("sp", qs[:, z, bb, :], qT[z, :, t0:t0 + W], writes=[("qs", z, bb)])
            P.dma("sp", ks[:, z, bb, :], kT[z, :, t0:t0 + W], writes=[("ks", z, bb)])
            P.dma("sp", kt[:, z, bb, :, :], kk[z, t0:t0 + W, :].rearrange("(n p) d -> p n d", p=128), writes=[("kt", z, bb)])
            P.dma("sp", vt[:, z, bb, :, :], vv[z, t0:t0 + W, :].rearrange("(n p) d -> p n d", p=128), writes=[("vt", z, bb)])
            P.dma("sp", a1s[:, z, bb, :], a1T[z, :, t0:t0 + W], writes=[("a1s", z, bb)])
        for ti in range(BLK):
            tg = blk * BLK + ti
            p = tg % 2
            c0 = ti * 128
            for z in range(2):
                ps, pk = newps()
                P.op("pe", lambda t, ps=ps, z=z, bb=bb, c0=c0: t.matmul(ps[:, 0:64], a1s[:, z, bb, c0:c0 + 128], a2s[:, z, :], start=True, stop=False),
                     reads=[("a1s", z, bb), "a2s"], writes=[pk])
                P.op("pe", lambda t, ps=ps, z=z: t.matmul(ps[:, 0:64], ones[:, :], abs_[:, z, :], start=False, stop=True),
                     reads=["ones", "abs"], writes=[pk])
                P.op("act", lambda a, ps=ps, z=z, p=p: a.activation(ez[:, z, p, :], ps[:, 0:64], AF.Exp, scale=-1.0), reads=[pk], writes=[("ez", z, p)])
                P.op("act", lambda a, z=z, p=p: a.activation(sp[:, z, p, :], ez[:, z, p, :], AF.Ln, bias=1.0), reads=[("ez", z, p)], writes=[("sp", z, p)])
                if stop <= 1:
                    continue
                ps1, pk1 = newps()
                P.op("pe", lambda t, ps1=ps1, z=z, p=p: t.matmul(ps1[0:64, 0:128], sp[:, z, p, :], tri[:, :], start=True, stop=True),
                     reads=[("sp", z, p), "tri"], writes=[pk1])
                ps2, pk2 = newps()
                P.op("pe", lambda t, ps2=ps2, z=z, p=p: t.matmul(ps2[:, 0:64], tri[:, :], sp[:, z, p, :], start=True, stop=True),
                     reads=[("sp", z, p), "tri"], writes=[pk2])
                P.op("act", lambda a, ps1=ps1, z=z, p=p: a.activation(ecT[:, z, p, :], ps1[0:64, 0:128], AF.Exp), reads=[pk1], writes=[("ecT", z, p)])
                P.op("act", lambda a, ps1=ps1, z=z, p=p: a.activation(encT[:, z, p, :], ps1[0:64, 0:128], AF.Exp, scale=-1.0), reads=[pk1], writes=[("encT", z, p)])
                P.op("act", lambda a, ps2=ps2, z=z, p=p: a.activation(enc[:, z, p, :], ps2[:, 0:64], AF.Exp, scale=-1.0), reads=[pk2], writes=[("enc", z, p)])
                P.op("dve", lambda v, z=z, p=p: v.tensor_copy(el[:, z, p, :], ecT[:, z, p, 63:128:64]), reads=[("ecT", z, p)], writes=[("el", z, p)])
                if stop <= 2:
                    continue
                P.op("dve", lambda v, z=z, p=p, bb=bb, c0=c0: v.scalar_tensor_tensor(qa[:, z, p, 0:64], qs[:, z, bb, c0:c0 + 64], 0.125, ecT[:, z, p, 0:64], ALU.mult, ALU.mult),
                     reads=[("qs", z, bb), ("ecT", z, p)], writes=[("qa", z, p)])
                P.op("dve", lambda v, z=z, p=p, bb=bb, c0=c0: v.scalar_tensor_tensor(qb_[:, z, p, 64:128], qs[:, z, bb, c0 + 64:c0 + 128], 0.125, ecT[:, z, p, 64:128], ALU.mult, ALU.mult),
                     reads=[("qs", z, bb), ("ecT", z, p)], writes=[("qb", z, p)])
                P.op("dve", lambda v, z=z, p=p, bb=bb, c0=c0: v.tensor_tensor(kTb[:, z, p, :], ks[:, z, bb, c0:c0 + 128], encT[:, z, p, :], ALU.mult),
                     reads=[("ks", z, bb), ("encT", z, p)], writes=[("kTb", z, p)])
                P.op("dve", lambda v, z=z, p=p, bb=bb, ti=ti: v.tensor_tensor(kb[:, z, p, :], kt[:, z, bb, ti, :], enc[:, z, p, :], ALU.mult),
                     reads=[("kt", z, bb), ("enc", z, p)], writes=[("kb", z, p)])
                P.op("dve", lambda v, z=z, p=p, bb=bb, ti=ti: v.tensor_copy(vb[:, z, p, :], vt[:, z, bb, ti, :]), reads=[("vt", z, bb)], writes=[("vb", z, p)])
                if stop <= 3:
                    continue
                ps3, pk3 = newps()
                P.op("pe", lambda t, ps3=ps3, z=z, p=p: t.matmul(ps3[:, 0:64], kTb[:, z, p, :], qa[:, z, p, 0:64], start=True, stop=True),
                     reads=[("kTb", z, p), ("qa", z, p)], writes=[pk3])
                P.op("pe", lambda t, ps3=ps3, z=z, p=p: t.matmul(ps3[:, 64:128], kTb[:, z, p, :], qb_[:, z, p, 64:128], start=True, stop=True),
                     reads=[("kTb", z, p), ("qb", z, p)], writes=[pk3])
                P.op("dve", lambda v, ps3=ps3, z=z, p=p: v.tensor_tensor(att[:, z, p, :], ps3[:, 0:128], mask[:, :], ALU.mult), reads=[pk3, "mask"], writes=[("att", z, p)])
                if stop <= 4:
                    continue
                ps4, pk4 = newps()
                P.op("pe", lambda t, ps4=ps4, z=z, p=p: t.matmul(ps4[0:64, 0:128], kb[0:64, z, p, :], vb[0:64, z, p, :], start=True, stop=True),
                     reads=[("kb", z, p), ("vb", z, p)], writes=[pk4])
                ps4b, pk4b = newps()
                P.op("pe", lambda t, ps4b=ps4b, z=z, p=p: t.matmul(ps4b[0:64, 128:256], kb[64:128, z, p, :], vb[64:128, z, p, :], start=True, stop=True),
                     reads=[("kb", z, p), ("vb", z, p)], writes=[pk4b])
                s0 = sbi[z]; s1 = (s0 + 1) % 3; s2 = (s0 + 2) % 3
                P.op("dve", lambda v, ps4=ps4, z=z: v.tensor_tensor(tmp[:, z, :], ps4[0:64, 0:128], S[:, z, :], ALU.add), reads=[pk4, ("S", z)], writes=[("tmp", z)])
                P.op("dve", lambda v, z=z, p=p: v.tensor_scalar_mul(S[:, z, :], tmp[:, z, :], el[:, z, p, 0:1]), reads=[("tmp", z), ("el", z, p)], writes=[("S", z)])
                P.op("act", lambda a, z=z, s1=s1: a.copy(Sb[:, z, s1, :], S[:, z, :]), reads=[("S", z)], writes=[("Sb", z, s1)])
                P.op("dve", lambda v, ps4b=ps4b, z=z: v.tensor_tensor(tmp[:, z, :], ps4b[0:64, 128:256], S[:, z, :], ALU.add), reads=[pk4b, ("S", z)], writes=[("tmp", z)])
                P.op("dve", lambda v, z=z, p=p: v.tensor_scalar_mul(S[:, z, :], tmp[:, z, :], el[:, z, p, 1:2]), reads=[("tmp", z), ("el", z, p)], writes=[("S", z)])
                P.op("act", lambda a, z=z, s2=s2: a.copy(Sb[:, z, s2, :], S[:, z, :]), reads=[("S", z)], writes=[("Sb", z, s2)])
                if stop <= 5:
                    continue
                ps5, pk5 = newps()
                P.op("pe", lambda t, ps5=ps5, z=z, p=p: t.matmul(ps5[:, 0:128], att[:, z, p, :], vb[:, z, p, :], start=True, stop=False),
                     reads=[("att", z, p), ("vb", z, p)], writes=[pk5])
                P.op("pe", lambda t, ps5=ps5, z=z, p=p, s0=s0: t.matmul(ps5[:, 0:128], qa[:, z, p, :], Sb[:, z, s0, :], start=False, stop=False),
                     reads=[("qa", z, p), ("Sb", z, s0)], writes=[pk5])
                P.op("pe", lambda t, ps5=ps5, z=z, p=p, s1=s1: t.matmul(ps5[:, 0:128], qb_[:, z, p, :], Sb[:, z, s1, :], start=False, stop=True),
                     reads=[("qb", z, p), ("Sb", z, s1)], writes=[pk5])
                sbi[z] = s2
                P.op("act", lambda a, ps5=ps5, z=z, p=p: a.copy(osb[:, z, p, :], ps5[:, 0:128]), reads=[pk5], writes=[("osb", z, p)])
                P.dma("sp", o_d[z, tg * 128:(tg + 1) * 128, :], osb[:, z, p, :], reads=[("osb", z, p)])
    return C.end()


def gla_consts():
    t = np.arange(128)
    same = (t[:, None] // 64) == (t[None, :] // 64)
    m = (same & (t[:, None] <= t[None, :])).astype(np.float32)
    return np.ascontiguousarray(m * (-1.0 / 16.0)), np.ascontiguousarray(m)


def stage_gla(pl, pc, gla_a2, gla_ab):
    T = 8448
    nc = build_gla(T)
    tri, mask = gla_consts()
    maps = []
    for c in range(NCORES):
        b, h = c // 4, c % 4
        p = np.concatenate([pc[b], pl[b]], axis=0)
        pr = np.concatenate([pc[b][::-1], pl[b][::-1]], axis=0)
        def pack(fn):
            return np.ascontiguousarray(np.stack([fn(p, 0), fn(pr, 1)], 0))
        maps.append({
            "qT": pack(lambda a, z: a[:, h * 64:(h + 1) * 64].T),
            "kT": pack(lambda a, z: a[:, 256 + h * 64:256 + (h + 1) * 64].T),
            "k": pack(lambda a, z: a[:, 256 + h * 64:256 + (h + 1) * 64]),
            "v": pack(lambda a, z: a[:, 512 + h * 128:512 + (h + 1) * 128]),
            "a1T": pack(lambda a, z: a[:, 1024 + 16 * z:1024 + 16 * (z + 1)].T),
            "a2": np.ascontiguousarray(gla_a2[:, :, h * 64:(h + 1) * 64]),
            "ab": np.ascontiguousarray(gla_ab[:, None, h * 64:(h + 1) * 64]),
            "tri": tri, "mask": mask})
    res = run_spmd(nc, maps)
    o_f = np.empty((2, 4, T, 128), np.float32); o_b = np.empty((2, 4, T, 128), np.float32)
    for c in range(NCORES):
        b, h = c // 4, c % 4
        o = res[c]["o"]
        o_f[b, h] = o[0]
        o_b[b, h, :256] = o[1][:256][::-1]
        o_b[b, h, 256:] = o[1][256:][::-1]
    return o_f, o_b


def ln_tail(P, nt, z, zsq, ones, newps, lng, lnb, outT, out_cols, ost, tagp):
    ps_s, pk_s = newps()
    ps_q, pk_q = newps()
    for kc in range(8):
        P.op("act", lambda a, kc=kc: a.activation(zsq[:, kc, 0:nt], z[:, kc, 0:nt], AF.Square), reads=[("z", kc)], writes=[("zsq", kc)])
    for kc in range(8):
        P.op("pe", lambda t, kc=kc: t.matmul(ps_s[:, 0:nt], ones[:, :], z[:, kc, 0:nt], start=(kc == 0), stop=(kc == 7)), reads=[("z", kc), "ones"], writes=[pk_s])
    for kc in range(8):
        P.op("pe", lambda t, kc=kc: t.matmul(ps_q[:, 0:nt], ones[:, :], zsq[:, kc, 0:nt], start=(kc == 0), stop=(kc == 7)), reads=[("zsq", kc), "ones"], writes=[pk_q])
    mean, var = ost[:, 0, 0:nt], ost[:, 1, 0:nt]
    P.op("dve", lambda v: v.tensor_scalar_mul(mean, ps_s[:, 0:nt], 1.0 / 1024.0), reads=[pk_s], writes=[("ost", 0)])
    P.op("dve", lambda v: v.tensor_tensor(var, mean, mean, ALU.mult), reads=[("ost", 0)], writes=[("ost", 1)])
    P.op("dve", lambda v: v.scalar_tensor_tensor(var, ps_q[:, 0:nt], 1.0 / 1024.0, var, ALU.mult, ALU.subtract), reads=[pk_q, ("ost", 1)], writes=[("ost", 1)])
    P.op("dve", lambda v: v.tensor_scalar_add(var, var, LN_EPS), reads=[("ost", 1)], writes=[("ost", 1)])
    P.op("act", lambda a: a.activation(var, var, AF.Sqrt), reads=[("ost", 1)], writes=[("ost", 1)])
    P.op("dve", lambda v: v.reciprocal(var, var), reads=[("ost", 1)], writes=[("ost", 1)])
    for kc in range(8):
        e = P.alt()
        P.op(e, lambda v, kc=kc: v.tensor_tensor(zsq[:, kc, 0:nt], z[:, kc, 0:nt], mean, ALU.subtract), reads=[("z", kc), ("ost", 0)], writes=[("zsq", kc)])
        P.op(e, lambda v, kc=kc: v.tensor_tensor(zsq[:, kc, 0:nt], zsq[:, kc, 0:nt], var, ALU.mult), reads=[("zsq", kc), ("ost", 1)], writes=[("zsq", kc)])
        P.op(e, lambda v, kc=kc: v.tensor_scalar(zsq[:, kc, 0:nt], zsq[:, kc, 0:nt], lng[:, kc:kc + 1], lnb[:, kc:kc + 1], ALU.mult, ALU.add), reads=[("zsq", kc), "ln"], writes=[("zsq", kc)])
    P.dma("sp", outT.rearrange("(kc p) t -> p kc t", p=128)[:, :, out_cols[0]:out_cols[1]], zsq[:, :, 0:nt], reads=[("zsq", kc) for kc in range(8)])


def build_k4b(segs, C=None):
    C = C or Ctx()
    W1 = C.din("w1", [1024, 4096]); W2 = C.din("w2", [4096, 1024])
    lnd = C.din("ln", [128, 2, 8]); ones_d = C.din("ones", [128, 128])
    xs, mods, outs = [], [], []
    for si, (ntok, tile) in enumerate(segs):
        xs.append(C.din("x%d" % si, [1024, ntok])); mods.append(C.din("mod%d" % si, [128, 6, 8])); outs.append(C.dout("o%d" % si, [1024, ntok]))
    TM = max(t for _, t in segs)
    w1b = C.sb("w1b", [128, 8, 4096], BF16); w2b = C.sb("w2b", [128, 32, 1024], BF16)
    stage = C.sb("stage", [128, 2, 2048])
    xt = C.sb("xt", [128, 8, TM]); ub = C.sb("ub", [128, 8, TM], BF16); hid = C.sb("hid", [128, 32, TM], BF16)
    z = C.sb("z", [128, 8, TM]); zsq = C.sb("zsq", [128, 8, TM]); ost = C.sb("ost", [128, 2, TM])
    ones = C.sb("ones", [128, 128]); lnt = C.sb("lnt", [128, 2, 8])
    mt = C.sb("mt", [128, len(segs), 6, 8]); sc1 = C.sb("sc1", [128, len(segs), 8])
    pss = [C.ps("ps%d" % i, [128, 512]) for i in range(8)]
    P = C.start()
    pctr = [0]

    def newps():
        i = pctr[0] % 8; pctr[0] += 1
        return pss[i], ("ps", i)
    P.dma("sp", ones[:], ones_d[:, :], writes=["ones"]); P.dma("sp", lnt[:], lnd[:, :, :], writes=["ln"])
    for si in range(len(segs)):
        P.dma("sp", mt[:, si, :, :], mods[si][:, :, :], writes=[("mt", si)])
        P.op("dve", lambda v, si=si: v.tensor_scalar_add(sc1[:, si, :], mt[:, si, 4, :], 1.0), reads=[("mt", si)], writes=[("sc1", si)])
    for half in range(2):
        load_w_bf16(C, P, w1b[:, :, half * 2048:(half + 1) * 2048], ("w1", half), W1, 1024, 2048, stage, "stage", col0=half * 2048)
    load_w_bf16(C, P, w2b, "w2", W2, 4096, 1024, stage, "stage")
    for si, (ntok, tile) in enumerate(segs):
        xv = xs[si].rearrange("(kc p) t -> p kc t", p=128)
        for t0 in range(0, ntok, tile):
            nt = tile
            P.dma("sp", xt[:, :, 0:nt], xv[:, :, t0:t0 + nt], writes=["xt"])
            for kc in range(8):
                P.op(P.alt(), lambda v, kc=kc, si=si: v.tensor_scalar(ub[:, kc, 0:nt], xt[:, kc, 0:nt], sc1[:, si, kc:kc + 1], mt[:, si, 3, kc:kc + 1], ALU.mult, ALU.add),
                     reads=["xt", ("sc1", si), ("mt", si)], writes=[("ub", kc)])
            for hc in range(32):
                ps, pk = newps()
                for kc in range(8):
                    P.op("pe", lambda t, ps=ps, hc=hc, kc=kc: t.matmul(ps[:, 0:nt], w1b[:, kc, hc * 128:(hc + 1) * 128], ub[:, kc, 0:nt], start=(kc == 0), stop=(kc == 7)),
                         reads=[(("w1", hc // 16), kc), ("ub", kc)], writes=[pk])
                P.op("act", lambda a, ps=ps, hc=hc: a.activation(zsq[:, hc % 8, 0:nt], ps[:, 0:nt], AF.Relu), reads=[pk], writes=[("zsq", hc % 8)])
                P.op(P.alt(), lambda v, hc=hc: v.tensor_tensor(hid[:, hc, 0:nt], zsq[:, hc % 8, 0:nt], zsq[:, hc % 8, 0:nt], ALU.mult), reads=[("zsq", hc % 8)], writes=[("hid", hc)])
            for dc in range(8):
                ps, pk = newps()
                for hc in range(32):
                    P.op("pe", lambda t, ps=ps, hc=hc, dc=dc: t.matmul(ps[:, 0:nt], w2b[:, hc, dc * 128:(dc + 1) * 128], hid[:, hc, 0:nt], start=(hc == 0), stop=(hc == 31)),
                         reads=[("w2", hc), ("hid", hc)], writes=[pk])
                P.op("dve", lambda v, dc=dc: v.tensor_scalar_mul(z[:, dc, 0:nt], xt[:, dc, 0:nt], ALPHA), reads=["xt"], writes=[("z", dc)])
                P.op("dve", lambda v, ps=ps, dc=dc, si=si: v.scalar_tensor_tensor(z[:, dc, 0:nt], ps[:, 0:nt], mt[:, si, 5, dc:dc + 1], z[:, dc, 0:nt], ALU.mult, ALU.add),
                     reads=[pk, ("z", dc), ("mt", si)], writes=[("z", dc)])
            ln_tail(P, nt, z, zsq, ones, newps, lnt[:, 0, :], lnt[:, 1, :], outs[si], (t0, t0 + nt), ost, None)
    return C.end()


def ln_layout(g, b):
    return np.ascontiguousarray(np.stack([g.reshape(8, 128).T, b.reshape(8, 128).T], axis=1))


def shard_tok(a, c):
    b, s = c // 4, c % 4
    return np.ascontiguousarray(a[b, s * 2048:(s + 1) * 2048].T)


def shard_ctx(a, c):
    b, s = c // 4, c % 4
    return np.ascontiguousarray(a[b, s * 64:(s + 1) * 64].T)


def unshard(res, key_lat, key_ctx, F):
    xl = np.empty((2, 8192, F), np.float32); xc = np.empty((2, 256, F), np.float32)
    for c in range(NCORES):
        b, s = c // 4, c % 4
        xl[b, s * 2048:(s + 1) * 2048] = res[c][key_lat].T
        if key_ctx is not None:
            xc[b, s * 64:(s + 1) * 64] = res[c][key_ctx].T
    return xl, xc


def stage_k4b(x1, h1, m_l, w1, w2, g, bb, with_ctx=True):
    segs = [(2048, 256)] + ([(64, 64)] if with_ctx else [])
    nc = build_k4b(segs)
    ones = np.ones((128, 128), np.float32)
    maps = []
    for c in range(NCORES):
        m = {"w1": w1, "w2": w2, "ln": ln_layout(g, bb), "ones": ones, "x0": shard_tok(x1, c), "mod0": mod_layout(m_l[c // 4])}
        if with_ctx:
            m["x1"] = shard_ctx(h1, c); m["mod1"] = mod_layout(m_l[2])
        maps.append(m)
    res = run_spmd(nc, maps)
    return unshard(res, "o0", "o1" if with_ctx else None, 1024)


def build_gqa(first, last, C=None, oc_rows=64, nmk=2):
    NQ, NK = 2048, 2304
    C = C or Ctx()
    qT = C.din("qT", [64, 8, NQ]); qsT = C.din("qsT", [64, 8, NQ])
    kT = C.din("kT", [64, 2, NK]); ksT = C.din("ksT", [64, 2, NK])
    cq = C.din("cq", [64, NQ]); sq = C.din("sq", [64, NQ]); ck = C.din("ck", [64, NK]); sk = C.din("sk", [64, NK])
    vv = C.din("v", [128, 18, 2, 64]); kcT = C.din("kcT", [64, 2, 256]); vc = C.din("vc", [128, 2, 2, 64])
    qcT = C.din("qcT", [64, 8, 64]); sink = C.din("sink", [128, 8]); mk = C.din("mk", [128, nmk, 512])
    o_d = C.dout("o", [NQ, 512]); oc_d = C.dout("oc", [oc_rows, 512])
    q = C.sb("q", [64, 8, NQ]); t1 = C.sb("t1", [64, 2304]); t2 = C.sb("t2", [64, 2304])
    cqs = C.sb("cqs", [64, NQ]); sqs = C.sb("sqs", [64, NQ]); cks = C.sb("cks", [64, NK]); sks = C.sb("sks", [64, NK])
    qr = C.sb("qr", [64, 8, NQ], BF16); kr = C.sb("kr", [64, 2, NK], BF16)
    kc = C.sb("kc", [64, 2, 256]); kcb = C.sb("kcb", [64, 2, 256], BF16)
    qc = C.sb("qc", [64, 8, 64]); qcb = C.sb("qcb", [64, 8, 64], BF16)
    vf = C.sb("vf", [128, 20, 2, 64]); va = C.sb("va", [128, 20, 2, 65], BF16)
    mks = C.sb("mks", [128, nmk, 512]); mkb = C.sb("mkb", [128, nmk, 512], BF16)
    es = C.sb("es", [128, 8]); pt = C.sb("pt", [128, 4, 512], BF16)
    den = C.sb("den", [128, 2, 4]); osb = C.sb("osb", [128, 2, 512])
    pss = [C.ps("ps%d" % i, [128, 512]) for i in range(8)]
    P = C.start()
    pctr = [0]

    def newps():
        i = pctr[0] % 8; pctr[0] += 1
        return pss[i], ("ps", i)
    P.dma("sp", cqs[:], cq[:, :], writes=["cqs"]); P.dma("sp", sqs[:], sq[:, :], writes=["sqs"])
    P.dma("sp", cks[:], ck[:, :], writes=["cks"]); P.dma("sp", sks[:], sk[:, :], writes=["sks"])
    P.dma("sp", vf[:, 0:18, :, :], vv[:, :, :, :], writes=["vf"]); P.dma("sp", vf[:, 18:20, :, :], vc[:, :, :, :], writes=["vf"])
    P.dma("sp", kc[:], kcT[:, :, :], writes=["kc"]); P.dma("sp", qc[:], qcT[:, :, :], writes=["qc"])
    P.dma("sp", mks[:], mk[:, :, :], writes=["mks"]); P.dma("sp", es[:], sink[:, :], writes=["es"])
    P.op("act", lambda a: a.activation(es[:], es[:], AF.Exp), reads=["es"], writes=["es"])
    P.op("dve", lambda v: v.tensor_copy(mkb[:], mks[:]), reads=["mks"], writes=["mkb"])
    P.op("dve", lambda v: v.tensor_copy(kcb[:], kc[:]), reads=["kc"], writes=["kcb"])
    P.op("dve", lambda v: v.tensor_copy(qcb[:], qc[:]), reads=["qc"], writes=["qcb"])
    P.op("pool", lambda v: v.memset(va[:], 1.0), writes=["va"])
    P.op("dve", lambda v: v.tensor_copy(va[:, :, :, 0:64], vf[:]), reads=["vf"], writes=["va"])
    for h in range(8):
        P.dma("sp", q[:, h, :], qT[:, h, :], writes=[("q", h)])
        P.dma("sp", t1[:, 0:NQ], qsT[:, h, :], writes=["t1"])
        P.op("dve", lambda v: v.tensor_tensor(t1[:, 0:NQ], t1[:, 0:NQ], sqs[:], ALU.mult), reads=["t1", "sqs"], writes=["t1"])
        P.op("pool", lambda v, h=h: v.tensor_tensor(q[:, h, :], q[:, h, :], cqs[:], ALU.mult), reads=[("q", h), "cqs"], writes=[("q", h)])
        P.op("dve", lambda v, h=h: v.tensor_tensor(qr[:, h, :], q[:, h, :], t1[:, 0:NQ], ALU.add), reads=[("q", h), "t1"], writes=[("qr", h)])
    for g in range(2):
        P.dma("sp", t2[:, :], kT[:, g, :], writes=["t2"])
        P.dma("sp", t1[:, :], ksT[:, g, :], writes=["t1"])
        P.op("dve", lambda v: v.tensor_tensor(t1[:, :], t1[:, :], sks[:], ALU.mult), reads=["t1", "sks"], writes=["t1"])
        P.op("pool", lambda v: v.tensor_tensor(t2[:, :], t2[:, :], cks[:], ALU.mult), reads=["t2", "cks"], writes=["t2"])
        P.op("dve", lambda v, g=g: v.tensor_tensor(kr[:, g, :], t2[:, :], t1[:, :], ALU.add), reads=["t2", "t1"], writes=[("kr", g)])
    it = 0
    for qb in range(17):
        ob = qb % 2
        for g in range(2):
            if qb < 16:
                tiles = []
                if not (first and qb == 0):
                    tiles.append((qb, 2 if (nmk == 4 and qb == 0) else 0))
                tiles.append((qb + 1, None))
                if not (last and qb == 15):
                    tiles.append((qb + 2, 3 if (nmk == 4 and qb == 15) else 1))
                tiles += [("c", 0), ("c", 1)]
                nq = 128
            else:
                tiles = [("c", 0), ("c", 1)]
                nq = 64
            ops, opk = newps()

            def emit_qk(ti, g=g, qb=qb, nq=nq, tiles=tiles):
                kt_, mi = tiles[ti]
                ps, pk = newps()
                if kt_ == "c":
                    lhs = kcb[:, g, mi * 128:(mi + 1) * 128]; lk = "kcb"; vi = 18 + mi; mi = None
                else:
                    lhs = kr[:, g, kt_ * 128:(kt_ + 1) * 128]; lk = ("kr", g); vi = kt_
                if qb < 16:
                    rhs = qr[:, 4 * g:4 * g + 4, qb * 128:(qb + 1) * 128]; rk_ = [("qr", 4 * g + j) for j in range(4)]
                else:
                    rhs = qcb[:, 4 * g:4 * g + 4, :]; rk_ = ["qcb"]
                P.op("pe", lambda t, ps=ps, lhs=lhs, rhs=rhs, nq=nq: t.matmul(ps[:, 0:4 * nq], lhs, rhs, start=True, stop=True), reads=[lk] + rk_, writes=[pk])
                return ps, pk, vi, mi
            pend = [emit_qk(0)]
            for ti in range(len(tiles)):
                ps, pk, vi, mi = pend.pop(0)
                if ti + 1 < len(tiles):
                    pend.append(emit_qk(ti + 1))
                pb = it % 4; it += 1
                P.op("act", lambda a, ps=ps, pb=pb, nq=nq: a.activation(pt[:, pb, 0:4 * nq], ps[:, 0:4 * nq], AF.Exp, scale=0.125), reads=[pk], writes=[("pt", pb)])
                if mi is not None:
                    P.op("dve", lambda v, pb=pb, mi=mi: v.tensor_tensor(pt[:, pb, :], pt[:, pb, :], mkb[:, mi, :], ALU.mult), reads=[("pt", pb), "mkb"], writes=[("pt", pb)])
                for j in range(4):
                    P.op("pe", lambda t, ops=ops, pb=pb, j=j, vi=vi, g=g, nq=nq, ti=ti, nt_=len(tiles): t.matmul(ops[0:nq, j * 65:(j + 1) * 65], pt[:, pb, j * nq:(j + 1) * nq], va[:, vi, g, :], start=(ti == 0 and j == 0), stop=(ti == nt_ - 1 and j == 3)),
                         reads=[("pt", pb), "va"], writes=[opk])
            P.op("dve", lambda v, ops=ops, g=g, ob=ob, nq=nq: v.tensor_tensor(den[0:nq, ob, :], ops[0:nq, 64:260:65], es[0:nq, 4 * g:4 * g + 4], ALU.add), reads=[opk, "es"], writes=[("den", ob)])
            P.op("dve", lambda v, ob=ob, nq=nq: v.reciprocal(den[0:nq, ob, :], den[0:nq, ob, :]), reads=[("den", ob)], writes=[("den", ob)])
            for j in range(4):
                hh = 4 * g + j
                P.op("dve" if j % 2 else "act", (lambda v, ops=ops, ob=ob, j=j, hh=hh, nq=nq: v.tensor_scalar_mul(osb[0:nq, ob, hh * 64:(hh + 1) * 64], ops[0:nq, j * 65:j * 65 + 64], den[0:nq, ob, j:j + 1])) if j % 2 else
                     (lambda a, ops=ops, ob=ob, j=j, hh=hh, nq=nq: a.activation(osb[0:nq, ob, hh * 64:(hh + 1) * 64], ops[0:nq, j * 65:j * 65 + 64], AF.Copy, scale=den[0:nq, ob, j:j + 1])),
                     reads=[opk, ("den", ob)], writes=[("osb", ob)])
        if qb < 16:
            P.dma("sp", o_d[qb * 128:(qb + 1) * 128, :], osb[:, ob, :], reads=[("osb", ob)])
        else:
            P.dma("sp", oc_d[:, :], osb[0:oc_rows, ob, :], reads=[("osb", ob)])
    return C.end()


GQ0, GK0, GV0 = 1568, 2080, 2208
SWAP64 = np.array([d + 16 if (d % 32) < 16 else d - 16 for d in range(64)])


def rope_tables(pos_tok):
    row = (pos_tok // 64).astype(np.float32); col = (pos_tok % 64).astype(np.float32)
    inv = (10000.0 ** (-np.arange(16, dtype=np.float32) / np.float32(16))).astype(np.float32)
    ar = row[None, :] * inv[:, None]; ac = col[None, :] * inv[:, None]
    Cc = np.concatenate([np.cos(ar), np.cos(ar), np.cos(ac), np.cos(ac)], 0).astype(np.float32)
    Ss = np.concatenate([-np.sin(ar), np.sin(ar), -np.sin(ac), np.sin(ac)], 0).astype(np.float32)
    return Cc, Ss


def stage_gqa(pl, pc, sink):
    ncs = {(f, l): build_gqa(f, l) for (f, l) in [(True, False), (False, False), (False, True)]}
    t = np.arange(128)
    mk = np.stack([np.tile((t[:, None] >= t[None, :]).astype(np.float32), (1, 4)), np.tile((t[:, None] <= t[None, :]).astype(np.float32), (1, 4))], 1)
    outs = {}
    att = np.empty((2, 8192, 512), np.float32); attc = np.empty((2, 256, 512), np.float32)
    maps = []
    for c in range(NCORES):
        b, s = c // 4, c % 4
        lo, hi = s * 2048, (s + 1) * 2048
        pos = np.arange(lo, hi); posk = np.clip(np.arange(lo - 128, hi + 128), 0, 8191)
        cq, sq = rope_tables(pos); ck, sk = rope_tables(posk)
        pk = np.zeros((2304, pl.shape[2]), np.float32)
        a0, a1 = max(lo - 128, 0), min(hi + 128, 8192)
        pk[a0 - (lo - 128):a1 - (lo - 128)] = pl[b, a0:a1]
        m = {"qT": np.ascontiguousarray(pl[b, lo:hi, GQ0:GQ0 + 512].reshape(2048, 8, 64).transpose(2, 1, 0)),
             "qsT": np.ascontiguousarray(pl[b, lo:hi, 2336:2848].reshape(2048, 8, 64).transpose(2, 1, 0)),
             "kT": np.ascontiguousarray(pk[:, GK0:GK0 + 128].reshape(2304, 2, 64).transpose(2, 1, 0)),
             "ksT": np.ascontiguousarray(pk[:, 2848:2976].reshape(2304, 2, 64).transpose(2, 1, 0)),
             "cq": cq, "sq": sq, "ck": ck, "sk": sk,
             "v": np.ascontiguousarray(pk[:, GV0:GV0 + 128].reshape(18, 128, 2, 64).transpose(1, 0, 2, 3)),
             "kcT": np.ascontiguousarray(pc[b, :, GK0:GK0 + 128].reshape(256, 2, 64).transpose(2, 1, 0)),
             "vc": np.ascontiguousarray(pc[b, :, GV0:GV0 + 128].reshape(2, 128, 2, 64).transpose(1, 0, 2, 3)),
             "qcT": np.ascontiguousarray(pc[b, s * 64:(s + 1) * 64, GQ0:GQ0 + 512].reshape(64, 8, 64).transpose(2, 1, 0)),
             "sink": np.ascontiguousarray(np.broadcast_to(sink[None, :], (128, 8))).astype(np.float32), "mk": np.ascontiguousarray(mk)}
        maps.append(m)
    for key, nc in ncs.items():
        res = run_spmd(nc, maps)
        for c in range(NCORES):
            b, s = c // 4, c % 4
            if (s == 0, s == 3) == key:
                att[b, s * 2048:(s + 1) * 2048] = res[c]["o"]
                attc[b, s * 64:(s + 1) * 64] = res[c]["oc"]
    return att, attc


def aug_w_in_ab(w_in):
    qs = np.concatenate([w_in[:, GQ0 + h * 64 + SWAP64] for h in range(8)], 1)
    ks = np.concatenate([w_in[:, GK0 + h * 64 + SWAP64] for h in range(2)], 1)
    return np.ascontiguousarray(np.concatenate([w_in, qs, ks], 1))


def build_k4a0(segs, C=None):
    C = C or Ctx()
    W = C.din("w", [1024, 1024]); lnd = C.din("ln", [128, 2, 8]); ones_d = C.din("ones", [128, 128]); gn_d = C.din("gn", [128, 1])
    xs, mods, outs, ofs, obs, ogs, ats = [], [], [], [], [], [], []
    for si, (ntok, tile) in enumerate(segs):
        xs.append(C.din("x%d" % si, [1024, ntok])); mods.append(C.din("mod%d" % si, [128, 6, 8])); outs.append(C.dout("o%d" % si, [1024, ntok]))
        ofs.append(C.din("of%d" % si, [512, ntok])); obs.append(C.din("ob%d" % si, [512, ntok])); ogs.append(C.din("og%d" % si, [512, ntok])); ats.append(C.din("at%d" % si, [512, ntok]))
    TM = max(t for _, t in segs)
    wb = C.sb("wb", [128, 8, 1024], BF16); stage = C.sb("stage", [128, 2, 1024])
    xt = C.sb("xt", [128, 8, TM]); mix = C.sb("mix", [128, 8, TM], BF16)
    a = C.sb("a", [128, 4, TM]); b_ = C.sb("b", [128, 4, TM]); g_ = C.sb("g", [128, 4, TM]); at = C.sb("at", [128, 4, TM])
    z = C.sb("z", [128, 8, TM]); zsq = C.sb("zsq", [128, 8, TM]); ost = C.sb("ost", [128, 2, TM])
    ones = C.sb("ones", [128, 128]); lnt = C.sb("lnt", [128, 2, 8]); gn = C.sb("gn", [128, 1])
    mt = C.sb("mt", [128, len(segs), 6, 8])
    pss = [C.ps("ps%d" % i, [128, 512]) for i in range(8)]
    P = C.start()
    pctr = [0]

    def newps():
        i = pctr[0] % 8; pctr[0] += 1
        return pss[i], ("ps", i)
    P.dma("sp", ones[:], ones_d[:, :], writes=["ones"]); P.dma("sp", lnt[:], lnd[:, :, :], writes=["ln"]); P.dma("sp", gn[:], gn_d[:, :], writes=["gn"])
    for si in range(len(segs)):
        P.dma("sp", mt[:, si, :, :], mods[si][:, :, :], writes=[("mt", si)])
    load_w_bf16(C, P, wb, "wb", W, 1024, 1024, stage, "stage")
    for si, (ntok, tile) in enumerate(segs):
        v3 = lambda ap: ap.rearrange("(kc p) t -> p kc t", p=128)
        for t0 in range(0, ntok, tile):
            nt = tile
            P.dma("sp", xt[:, :, 0:nt], v3(xs[si])[:, :, t0:t0 + nt], writes=["xt"])
            P.dma("sp", a[:, :, 0:nt], v3(ofs[si])[:, :, t0:t0 + nt], writes=["a"])
            P.dma("sp", b_[:, :, 0:nt], v3(obs[si])[:, :, t0:t0 + nt], writes=["b"])
            P.dma("sp", g_[:, :, 0:nt], v3(ogs[si])[:, :, t0:t0 + nt], writes=["g"])
            P.dma("sp", at[:, :, 0:nt], v3(ats[si])[:, :, t0:t0 + nt], writes=["at"])
            P.op("dve", lambda v: v.tensor_tensor(a[:, :, 0:nt], a[:, :, 0:nt], b_[:, :, 0:nt], ALU.add), reads=["a", "b"], writes=["a"])
            P.op("act", lambda a_: a_.activation(b_[:, :, 0:nt], a[:, :, 0:nt], AF.Square), reads=["a"], writes=["b"])
            P.op("act", lambda a_: a_.activation(g_[:, :, 0:nt], g_[:, :, 0:nt], AF.Silu), reads=["g"], writes=["g"])
            P.op("pool", lambda v: v.tensor_copy(mix[:, 4:8, 0:nt], at[:, :, 0:nt]), reads=["at"], writes=[("mix", k) for k in range(4, 8)])
            for h in range(4):
                ps, pk = newps()
                P.op("pe", lambda t, ps=ps, h=h: t.matmul(ps[:, 0:nt], ones[:, :], b_[:, h, 0:nt], start=True, stop=True), reads=["b", "ones"], writes=[pk])
                P.op("dve", lambda v, ps=ps, h=h: v.tensor_scalar(zsq[:, h, 0:nt], ps[:, 0:nt], 1.0 / 128.0, LN_EPS, ALU.mult, ALU.add), reads=[pk], writes=[("zsq", h)])
                P.op("act", lambda a_, h=h: a_.activation(zsq[:, h, 0:nt], zsq[:, h, 0:nt], AF.Sqrt), reads=[("zsq", h)], writes=[("zsq", h)])
                P.op("dve", lambda v, h=h: v.reciprocal(zsq[:, h, 0:nt], zsq[:, h, 0:nt]), reads=[("zsq", h)], writes=[("zsq", h)])
                P.op("dve", lambda v, h=h: v.tensor_tensor(zsq[:, h, 0:nt], zsq[:, h, 0:nt], a[:, h, 0:nt], ALU.mult), reads=[("zsq", h), "a"], writes=[("zsq", h)])
                P.op("dve", lambda v, h=h: v.scalar_tensor_tensor(mix[:, h, 0:nt], zsq[:, h, 0:nt], gn[:, 0:1], g_[:, h, 0:nt], ALU.mult, ALU.mult), reads=[("zsq", h), "g", "gn"], writes=[("mix", h)])
            for dc in range(8):
                ps, pk = newps()
                for kc in range(8):
                    P.op("pe", lambda t, ps=ps, kc=kc, dc=dc: t.matmul(ps[:, 0:nt], wb[:, kc, dc * 128:(dc + 1) * 128], mix[:, kc, 0:nt], start=(kc == 0), stop=(kc == 7)),
                         reads=[("wb", kc), ("mix", kc)], writes=[pk])
                P.op("dve", lambda v, dc=dc: v.tensor_scalar_mul(z[:, dc, 0:nt], xt[:, dc, 0:nt], ALPHA), reads=["xt"], writes=[("z", dc)])
                P.op("dve", lambda v, ps=ps, dc=dc, si=si: v.scalar_tensor_tensor(z[:, dc, 0:nt], ps[:, 0:nt], mt[:, si, 2, dc:dc + 1], z[:, dc, 0:nt], ALU.mult, ALU.add),
                     reads=[pk, ("z", dc), ("mt", si)], writes=[("z", dc)])
            ln_tail(P, nt, z, zsq, ones, newps, lnt[:, 0, :], lnt[:, 1, :], outs[si], (t0, t0 + nt), ost, None)
    return C.end()


def stage_k4a0(x, ctx, m_l, o_f, o_b, pl, pc, att, attc, w_out, gla_norm, g, bb):
    nc = build_k4a0([(2048, 256), (64, 64)])
    ones = np.ones((128, 128), np.float32)
    def heads_to_tok(o):
        return np.ascontiguousarray(o.transpose(0, 2, 1, 3).reshape(2, o.shape[2], 512))
    ofl, obl = heads_to_tok(o_f[:, :, 256:]), heads_to_tok(o_b[:, :, 256:])
    ofc, obc = heads_to_tok(o_f[:, :, :256]), heads_to_tok(o_b[:, :, :256])
    maps = []
    for c in range(NCORES):
        maps.append({"w": w_out, "ln": ln_layout(g, bb), "ones": ones, "gn": np.ascontiguousarray(gla_norm.reshape(128, 1)),
                     "x0": shard_tok(x, c), "mod0": mod_layout(m_l[c // 4]), "x1": shard_ctx(ctx, c), "mod1": mod_layout(m_l[2]),
                     "of0": shard_tok(ofl, c), "ob0": shard_tok(obl, c), "og0": shard_tok(pl[:, :, 1056:1568], c), "at0": shard_tok(att, c),
                     "of1": shard_ctx(ofc, c), "ob1": shard_ctx(obc, c), "og1": shard_ctx(pc[:, :, 1056:1568], c), "at1": shard_ctx(attc, c)})
    res = run_spmd(nc, maps)
    return unshard(res, "o0", "o1", 1024)


def build_rwf(segs, C=None, NH=4):
    NCHK = 3 * NH + 3
    C = C or Ctx()
    Wl = C.din("wl", [128, 512 * NH]); sc_d = C.din("sc", [128, NCHK + 5 * NH]); bo_d = C.din("bo", [128, 128])
    ps_in, outs = [], []
    names = ["r", "km", "v", "an", "bb", "wf", "wb", "g"]
    for si, (ntok, tile) in enumerate(segs):
        ps_in.append(C.din("p%d" % si, [NCHK * 128, ntok + 2]))
        outs.append({n: C.dout("%s%d" % (n, si), [128 * NH, ntok]) for n in names})
    TM = max(t for _, t in segs)
    wl = C.sb("wl", [128, 1, 512 * NH], BF16); stage = C.sb("stage", [128, 2, 512 * NH])
    sc = C.sb("sc", [128, NCHK + 5 * NH]); hmu = C.sb("hmu", [128, NCHK]); omm = C.sb("omm", [128, NCHK]); omka = C.sb("omka", [128, NH])
    bo = C.sb("bo", [128, 128])
    pt = C.sb("pt", [128, NCHK, TM + 2]); sh = C.sb("sh", [128, NCHK, TM]); tmp = C.sb("tmp", [128, 2, TM])
    th = C.sb("th", [128, 3, TM], BF16)
    o = C.sb("o", [128, 6, NH, TM]); ag = C.sb("ag", [128, NH, TM]); kx = C.sb("kx", [128, 2, TM]); sq = C.sb("sq", [128, 2, TM])
    pss = [C.ps("ps%d" % i, [128, 512]) for i in range(8)]
    P = C.start()
    pctr = [0]

    def newps():
        i = pctr[0] % 8; pctr[0] += 1
        return pss[i], ("ps", i)
    P.dma("sp", sc[:], sc_d[:, :], writes=["sc"]); P.dma("sp", bo[:], bo_d[:, :], writes=["bo"])
    load_w_bf16(C, P, wl, "wl", Wl, 128, 512 * NH, stage, "stage")
    P.op("dve", lambda v: v.tensor_scalar_mul(hmu[:], sc[:, 0:NCHK], 0.5), reads=["sc"], writes=["hmu"])
    P.op("dve", lambda v: v.tensor_scalar(omm[:], sc[:, 0:NCHK], -1.0, 1.0, ALU.mult, ALU.add), reads=["sc"], writes=["omm"])
    P.op("dve", lambda v: v.tensor_scalar(omka[:], sc[:, NCHK + 4 * NH:NCHK + 5 * NH], -1.0, 1.0, ALU.mult, ALU.add), reads=["sc"], writes=["omka"])
    W0F, W0B, A0, KK, KA = NCHK, NCHK + NH, NCHK + 2 * NH, NCHK + 3 * NH, NCHK + 4 * NH
    for si, (ntok, tile) in enumerate(segs):
        pv = ps_in[si].rearrange("(kc p) t -> p kc t", p=128)
        for t0 in range(0, ntok, tile):
            nt = tile
            P.dma("sp", pt[:, :, 0:nt + 2], pv[:, :, t0:t0 + nt + 2], writes=["pt"])
            for ch in range(NCHK):
                e = P.alt(); tb = ch % 2
                P.op(e, lambda v, ch=ch, tb=tb: v.tensor_tensor(tmp[:, tb, 0:nt], pt[:, ch, 0:nt], pt[:, ch, 2:nt + 2], ALU.add), reads=["pt"], writes=[("tmp", tb)])
                P.op(e, lambda v, ch=ch, tb=tb: v.tensor_scalar_mul(tmp[:, tb, 0:nt], tmp[:, tb, 0:nt], hmu[:, ch:ch + 1]), reads=[("tmp", tb), "hmu"], writes=[("tmp", tb)])
                P.op("dve", lambda v, ch=ch, tb=tb: v.scalar_tensor_tensor(sh[:, ch, 0:nt], pt[:, ch, 1:nt + 1], omm[:, ch:ch + 1], tmp[:, tb, 0:nt], ALU.mult, ALU.add), reads=["pt", ("tmp", tb), "omm"], writes=[("sh", ch)])
            P.op("act", lambda a: a.activation(th[:, 0, 0:nt], sh[:, 3 * NH, 0:nt], AF.Tanh), reads=[("sh", 3 * NH)], writes=[("th", 0)])
            P.op("act", lambda a: a.copy(th[:, 1, 0:nt], sh[:, 3 * NH + 1, 0:nt]), reads=[("sh", 3 * NH + 1)], writes=[("th", 1)])
            P.op("act", lambda a: a.activation(th[:, 2, 0:nt], sh[:, 3 * NH + 2, 0:nt], AF.Sigmoid), reads=[("sh", 3 * NH + 2)], writes=[("th", 2)])
            for c in range(NH):
                for (oi, wc0, ti_, bcol) in ((3, 0, 0, W0F), (4, 128 * NH, 0, W0B)):
                    ps, pk = newps()
                    P.op("pe", lambda t, ps=ps, wc0=wc0, c=c, ti_=ti_: t.matmul(ps[:, 0:nt], wl[:, 0, wc0 + c * 128:wc0 + (c + 1) * 128], th[:, ti_, 0:nt], start=True, stop=True), reads=[("wl", 0), ("th", ti_)], writes=[pk])
                    P.op("act", lambda a, ps=ps, oi=oi, c=c, bcol=bcol: a.activation(o[:, oi, c, 0:nt], ps[:, 0:nt], AF.Sigmoid, bias=sc[:, bcol + c:bcol + c + 1]), reads=[pk, "sc"], writes=[("o", oi, c)])
                    P.op("act", lambda a, oi=oi, c=c: a.activation(o[:, oi, c, 0:nt], o[:, oi, c, 0:nt], AF.Exp, scale=-0.6065306597126334), reads=[("o", oi, c)], writes=[("o", oi, c)])
                ps, pk = newps()
                P.op("pe", lambda t, ps=ps, c=c: t.matmul(ps[:, 0:nt], wl[:, 0, 256 * NH + c * 128:256 * NH + (c + 1) * 128], th[:, 1, 0:nt], start=True, stop=True), reads=[("wl", 0), ("th", 1)], writes=[pk])
                P.op("act", lambda a, ps=ps, c=c: a.activation(ag[:, c, 0:nt], ps[:, 0:nt], AF.Sigmoid, bias=sc[:, A0 + c:A0 + c + 1]), reads=[pk, "sc"], writes=[("ag", c)])
                ps, pk = newps()
                P.op("pe", lambda t, ps=ps, c=c: t.matmul(ps[:, 0:nt], wl[:, 0, 384 * NH + c * 128:384 * NH + (c + 1) * 128], th[:, 2, 0:nt], start=True, stop=True), reads=[("wl", 0), ("th", 2)], writes=[pk])
                P.op("act", lambda a, ps=ps, c=c: a.copy(o[:, 5, c, 0:nt], ps[:, 0:nt]), reads=[pk], writes=[("o", 5, c)])
                kb_ = c % 2
                P.op("dve", lambda v, c=c, kb_=kb_: v.tensor_scalar_mul(kx[:, kb_, 0:nt], sh[:, NH + c, 0:nt], sc[:, KK + c:KK + c + 1]), reads=[("sh", NH + c), "sc"], writes=[("kx", kb_)])
                P.op("pool", lambda v, kb_=kb_: v.tensor_tensor(sq[:, kb_, 0:nt], kx[:, kb_, 0:nt], kx[:, kb_, 0:nt], ALU.mult), reads=[("kx", kb_)], writes=[("sq", kb_)])
                ps, pk = newps()
                P.op("pe", lambda t, ps=ps, kb_=kb_: t.matmul(ps[:, 0:nt], bo[:, :], sq[:, kb_, 0:nt], start=True, stop=True), reads=["bo", ("sq", kb_)], writes=[pk])
                P.op("act", lambda a, ps=ps, kb_=kb_: a.activation(sq[:, kb_, 0:nt], ps[:, 0:nt], AF.Sqrt), reads=[pk], writes=[("sq", kb_)])
                P.op("dve", lambda v, kb_=kb_: v.tensor_scalar_max(sq[:, kb_, 0:nt], sq[:, kb_, 0:nt], 1e-12), reads=[("sq", kb_)], writes=[("sq", kb_)])
                P.op("dve", lambda v, kb_=kb_: v.reciprocal(sq[:, kb_, 0:nt], sq[:, kb_, 0:nt]), reads=[("sq", kb_)], writes=[("sq", kb_)])
                P.op("dve", lambda v, c=c, kb_=kb_: v.scalar_tensor_tensor(o[:, 1, c, 0:nt], kx[:, kb_, 0:nt], -1.0, sq[:, kb_, 0:nt], ALU.mult, ALU.mult), reads=[("kx", kb_), ("sq", kb_)], writes=[("o", 1, c)])
                P.op("dve", lambda v, c=c: v.scalar_tensor_tensor(o[:, 2, c, 0:nt], o[:, 1, c, 0:nt], -1.0, ag[:, c, 0:nt], ALU.mult, ALU.mult), reads=[("o", 1, c), ("ag", c)], writes=[("o", 2, c)])
                P.op("pool", lambda v, c=c: v.tensor_scalar(ag[:, c, 0:nt], ag[:, c, 0:nt], sc[:, KA + c:KA + c + 1], omka[:, c:c + 1], ALU.mult, ALU.add), reads=[("ag", c), "sc", "omka"], writes=[("ag", c)])
                P.op("pool", lambda v, c=c: v.tensor_tensor(o[:, 0, c, 0:nt], sh[:, NH + c, 0:nt], ag[:, c, 0:nt], ALU.mult), reads=[("sh", NH + c), ("ag", c)], writes=[("o", 0, c)])
            v3 = lambda ap: ap.rearrange("(kc p) t -> p kc t", p=128)
            P.dma("sp", v3(outs[si]["r"])[:, :, t0:t0 + nt], sh[:, 0:NH, 0:nt], reads=[("sh", k) for k in range(NH)])
            P.dma("sp", v3(outs[si]["v"])[:, :, t0:t0 + nt], sh[:, 2 * NH:3 * NH, 0:nt], reads=[("sh", k) for k in range(2 * NH, 3 * NH)])
            for oi, n in ((0, "km"), (1, "an"), (2, "bb"), (3, "wf"), (4, "wb"), (5, "g")):
                P.dma("sp", v3(outs[si][n])[:, :, t0:t0 + nt], o[:, oi, :, 0:nt], reads=[("o", oi, c) for c in range(NH)])
    return C.end()


def cd_w_in_layout(w):
    z64 = np.zeros((1024, 64), np.float32)
    kpe = w[:, 2240:2272]
    sw = np.array([d + 8 if (d % 16) < 8 else d - 8 for d in range(32)])
    return np.ascontiguousarray(np.concatenate([w[:, 0:1536], w[:, 1536:1664], w[:, 1664:1728], z64, w[:, 1728:1856],
                                                w[:, 1856:2112], w[:, 2112:2240], kpe, kpe[:, sw], z64], 1))


def mu_layout(mu):
    z64 = np.zeros(64, np.float32)
    m = np.concatenate([mu[0:1536], mu[1536:1664], mu[1664:1728], z64, mu[1728:1856]])
    return m.reshape(15, 128).T


def stage_rwf(pl, pc, I):
    nc = build_rwf([(2048, 256), (64, 64)])
    w2, a2, g2 = I['rwkv_w2'][0], I['rwkv_a2'][0], I['rwkv_g2'][0]
    wl = np.zeros((128, 2048), np.float32)
    wl[0:64, 0:512] = w2[0]; wl[64:128, 512:1024] = w2[1]; wl[0:64, 1024:1536] = a2; wl[:, 1536:2048] = g2
    col = lambda v: v.reshape(4, 128).T
    sc = np.concatenate([mu_layout(I['rwkv_mu'][0]), col(I['rwkv_w0'][0, 0]), col(I['rwkv_w0'][0, 1]), col(I['rwkv_a0'][0]), col(I['rwkv_kk'][0]), col(I['rwkv_ka'][0])], 1).astype(np.float32)
    t = np.arange(128)
    bo = ((t[:, None] // 64) == (t[None, :] // 64)).astype(np.float32)
    maps = []
    for c in range(NCORES):
        b, s = c // 4, c % 4
        pp = np.zeros((2050, 1920), np.float32)
        a0, a1 = max(s * 2048 - 1, 0), min((s + 1) * 2048 + 1, 8192)
        pp[a0 - (s * 2048 - 1):a1 - (s * 2048 - 1)] = pl[b, a0:a1, :1920]
        pq = np.zeros((66, 1920), np.float32)
        a0, a1 = max(s * 64 - 1, 0), min((s + 1) * 64 + 1, 256)
        pq[a0 - (s * 64 - 1):a1 - (s * 64 - 1)] = pc[b, a0:a1, :1920]
        maps.append({"wl": wl, "sc": np.ascontiguousarray(sc), "bo": bo, "p0": np.ascontiguousarray(pp.T), "p1": np.ascontiguousarray(pq.T)})
    res = run_spmd(nc, maps)
    out = {}
    for n in ["r", "km", "v", "an", "bb", "wf", "wb", "g"]:
        xl, xc = unshard(res, n + "0", n + "1", 512)
        out[n] = np.concatenate([xc, xl], 1)
    return out


def build_rws(T, Tskip, C=None):
    NB = T // 64
    C = C or Ctx()
    Ab = C.din("Ab", [128, T, 2]); Rb = C.din("Rb", [128, T, 2]); Wt = C.din("Wt", [128, T])
    BK = C.din("BK", [4, T, 128]); VVd = C.din("VV", [4, T, 64])
    y_d = C.dout("y", [2, T, 64])
    ab = C.sb("ab", [128, 2, 64, 2]); rb = C.sb("rb", [128, 2, 64, 2]); wt = C.sb("wt", [128, 2, 64])
    bk = C.sb("bk", [4, 2, 64, 128]); vv = C.sb("vv", [4, 2, 64, 64]); ysb = C.sb("ysb", [2, 2, 64, 64])
    ST = C.sb("ST", [128, 64])
    pa = [C.ps("pa%d" % i, [128, 512]) for i in range(2)]
    po = [C.ps("po%d" % i, [128, 512]) for i in range(2)]
    py = [C.ps("py%d" % i, [128, 512]) for i in range(2)]
    P = C.start()
    P.op("dve", lambda v: v.memset(ST[:], 0.0), writes=["ST"])
    P.op("dve", lambda v: v.memset(ysb[:], 0.0), writes=[("ysb", 0), ("ysb", 1)])
    for blk in range(NB):
        b = blk % 2
        s0 = blk * 64
        P.dma("sp", ab[:, b, :, :], Ab[:, s0:s0 + 64, :], writes=[("ab", b)])
        P.dma("sp", rb[:, b, :, :], Rb[:, s0:s0 + 64, :], writes=[("rb", b)])
        P.dma("sp", wt[:, b, :], Wt[:, s0:s0 + 64], writes=[("wt", b)])
        P.dma("sp", bk[:, b, :, :], BK[:, s0:s0 + 64, :], writes=[("bk", b)])
        P.dma("sp", vv[:, b, :, :], VVd[:, s0:s0 + 64, :], writes=[("vv", b)])
        for t in range(64):
            i2 = t % 2
            P.op("pe", lambda e, b=b, t=t, i2=i2: e.matmul(pa[i2][0:2, 0:64], ab[:, b, t, :], ST[:, :], start=True, stop=True), reads=[("ab", b), "ST"], writes=[("pa", i2)])
            P.op("act", lambda a, b=b, t=t, i2=i2: a.copy(vv[0:2, b, t, :], pa[i2][0:2, 0:64]), reads=[("pa", i2)], writes=[("vv", b)])
            P.op("pe", lambda e, b=b, t=t, i2=i2: e.matmul(po[i2][:, 0:64], bk[:, b, t, :], vv[:, b, t, :], start=True, stop=True), reads=[("bk", b), ("vv", b)], writes=[("po", i2)])
            P.op("dve", lambda v, b=b, t=t, i2=i2: v.scalar_tensor_tensor(ST[:, :], ST[:, :], wt[:, b, t:t + 1], po[i2][:, 0:64], ALU.mult, ALU.add), reads=["ST", ("wt", b), ("po", i2)], writes=["ST"])
            if s0 >= Tskip:
                g8 = (t // 8) % 2
                P.op("pe", lambda e, b=b, t=t, g8=g8: e.matmul(py[g8][0:2, (t % 8) * 64:(t % 8 + 1) * 64], rb[:, b, t, :], ST[:, :], start=True, stop=True), reads=[("rb", b), "ST"], writes=[("py", g8)])
                if t % 8 == 7:
                    P.op("act", lambda a, b=b, t=t, g8=g8: a.copy(ysb[:, b, t - 7:t + 1, :], py[g8][0:2, 0:512].rearrange("p (a c) -> p a c", c=64)), reads=[("py", g8)], writes=[("ysb", b)])
        P.dma("sp", y_d[:, s0:s0 + 64, :], ysb[:, b, :, :], reads=[("ysb", b)])
    return C.end()


def stage_rws(F):
    T = 8448
    nc = build_rws(T, 256)
    ys = []
    for d_, wname in ((0, "wf"), (1, "wb")):
        def st(a):
            return a if d_ == 0 else np.concatenate([a[:, :256][:, ::-1], a[:, 256:][:, ::-1]], 1)
        r, km, v, an, bb, w = [st(F[n]) for n in ("r", "km", "v", "an", "bb", wname)]
        maps = []
        for c in range(NCORES):
            b, hp = c // 4, c % 4
            Ab = np.zeros((128, T, 2), np.float32); Rb = np.zeros((128, T, 2), np.float32); Wt = np.zeros((128, T), np.float32)
            BK = np.zeros((4, T, 128), np.float32); VV = np.zeros((4, T, 64), np.float32)
            for ci in range(2):
                h = 2 * hp + ci
                sl = slice(h * 64, (h + 1) * 64)
                Ab[ci * 64:(ci + 1) * 64, :, ci] = an[b, :, sl].T
                Rb[ci * 64:(ci + 1) * 64, :, ci] = r[b, :, sl].T
                Wt[ci * 64:(ci + 1) * 64, :] = w[b, :, sl].T
                BK[ci, :, ci * 64:(ci + 1) * 64] = bb[b, :, sl]
                BK[2 + ci, :, ci * 64:(ci + 1) * 64] = km[b, :, sl]
                VV[2 + ci] = v[b, :, sl]
            maps.append({"Ab": Ab, "Rb": Rb, "Wt": Wt, "BK": BK, "VV": VV})
        res = run_spmd(nc, maps)
        y = np.empty((2, 8192, 512), np.float32)
        for c in range(NCORES):
            b, hp = c // 4, c % 4
            yy = res[c]["y"][:, 256:]
            if d_ == 1:
                yy = yy[:, ::-1]
            for ci in range(2):
                h = 2 * hp + ci
                y[b, :, h * 64:(h + 1) * 64] = yy[ci]
        ys.append(y)
    return ys[0], ys[1]


def build_mlf(ntok, tile, C=None, with_q=True, with_kv=True):
    C = C or Ctx()
    p_d = C.din("p", [640, ntok]); wuq_d = C.din("wuq", [256, 1536]); wukv_d = C.din("wukv", [128, 1024])
    gq_d = C.din("gq", [128, 3]); ones_d = C.din("ones", [128, 128])
    cq_d = C.din("cq", [96, ntok]); sq_d = C.din("sq", [96, ntok])
    q_o = C.dout("q", [96, 8, ntok]); kn_o = C.dout("kn", [64, 8, ntok]); kpe_o = C.dout("kpe", [32, ntok]); v_o = C.dout("v", [ntok, 512])
    wuq = C.sb("wuq", [128, 2, 1536], BF16); wukv = C.sb("wukv", [128, 1, 1024], BF16); stage = C.sb("stage", [128, 2, 1536])
    gq = C.sb("gq", [128, 3]); ones = C.sb("ones", [128, 128])
    pt = C.sb("pt", [128, 5, tile]); sqq = C.sb("sqq", [128, 3, tile]); rs = C.sb("rs", [128, 2, tile])
    qn = C.sb("qn", [128, 3, tile], BF16)
    cq = C.sb("cq", [96, tile]); sq = C.sb("sq", [96, tile])
    qo = C.sb("qo", [96, 8, tile]); qs = C.sb("qs", [96, 2, tile]); ko = C.sb("ko", [64, 8, tile]); kp = C.sb("kp", [32, 2, tile]); vo = C.sb("vo", [128, 2, 512])
    pss = [C.ps("ps%d" % i, [128, 512]) for i in range(8)]
    P = C.start()
    pctr = [0]

    def newps():
        i = pctr[0] % 8; pctr[0] += 1
        return pss[i], ("ps", i)
    P.dma("sp", gq[:], gq_d[:, :], writes=["gq"]); P.dma("sp", ones[:], ones_d[:, :], writes=["ones"])
    load_w_bf16(C, P, wuq, "wuq", wuq_d, 256, 1536, stage, "stage")
    load_w_bf16(C, P, wukv, "wukv", wukv_d, 128, 1024, stage, "stage")
    pv = p_d.rearrange("(kc p) t -> p kc t", p=128)
    for t0 in range(0, ntok, tile):
        nt = tile
        P.dma("sp", pt[:, :, :], pv[:, :, t0:t0 + nt], writes=["pt"])
        P.dma("sp", cq[:], cq_d[:, t0:t0 + nt], writes=["cq"]); P.dma("sp", sq[:], sq_d[:, t0:t0 + nt], writes=["sq"])
        P.op("act", lambda a: a.activation(sqq[:, :, :], pt[:, 0:3, :], AF.Square), reads=["pt"], writes=["sqq"])
        for (ri, chs, n) in ((0, (0, 1), 256.0), (1, (2,), 128.0)):
            ps, pk = newps()
            for i, ch in enumerate(chs):
                P.op("pe", lambda t, ps=ps, ch=ch, i=i, chs=chs: t.matmul(ps[:, 0:nt], ones[:, :], sqq[:, ch, :], start=(i == 0), stop=(i == len(chs) - 1)), reads=["ones", "sqq"], writes=[pk])
            P.op("dve", lambda v, ps=ps, ri=ri, n=n: v.tensor_scalar(rs[:, ri, :], ps[:, 0:nt], 1.0 / n, 1e-6, ALU.mult, ALU.add), reads=[pk], writes=[("rs", ri)])
            P.op("act", lambda a, ri=ri: a.activation(rs[:, ri, :], rs[:, ri, :], AF.Sqrt), reads=[("rs", ri)], writes=[("rs", ri)])
            P.op("dve", lambda v, ri=ri: v.reciprocal(rs[:, ri, :], rs[:, ri, :]), reads=[("rs", ri)], writes=[("rs", ri)])
            for ch in chs:
                P.op("dve", lambda v, ch=ch, ri=ri: v.scalar_tensor_tensor(qn[:, ch, :], pt[:, ch, :], gq[:, ch:ch + 1], rs[:, ri, :], ALU.mult, ALU.mult), reads=["pt", "gq", ("rs", ri)], writes=[("qn", ch)])
        for h in range(8 if with_q else 0):
            ps, pk = newps()
            for kc in range(2):
                P.op("pe", lambda t, ps=ps, h=h, kc=kc: t.matmul(ps[0:96, 0:nt], wuq[:, kc, h * 96:(h + 1) * 96], qn[:, kc, :], start=(kc == 0), stop=(kc == 1)), reads=[("wuq", kc), ("qn", kc)], writes=[pk])
            P.op("act", lambda a, ps=ps, h=h: a.copy(qo[:, h, :], ps[0:96, 0:nt]), reads=[pk], writes=[("qo", h)])
            ps2, pk2 = newps()
            for kc in range(2):
                P.op("pe", lambda t, ps2=ps2, h=h, kc=kc: t.matmul(ps2[0:96, 0:nt], wuq[:, kc, 768 + h * 96:768 + (h + 1) * 96], qn[:, kc, :], start=(kc == 0), stop=(kc == 1)), reads=[("wuq", kc), ("qn", kc)], writes=[pk2])
            sb_ = h % 2
            P.op("dve", lambda v, ps2=ps2, sb_=sb_: v.tensor_tensor(qs[64:96, sb_, :], ps2[64:96, 0:nt], sq[64:96, :], ALU.mult), reads=[pk2, "sq"], writes=[("qs", sb_)])
            P.op("dve", lambda v, h=h: v.tensor_tensor(qo[64:96, h, :], qo[64:96, h, :], cq[64:96, :], ALU.mult), reads=[("qo", h), "cq"], writes=[("qo", h)])
            P.op("dve", lambda v, h=h, sb_=sb_: v.tensor_tensor(qo[64:96, h, :], qo[64:96, h, :], qs[64:96, sb_, :], ALU.add), reads=[("qo", h), ("qs", sb_)], writes=[("qo", h)])
        for h in range(8 if with_kv else 0):
            ps3, pk3 = newps()
            P.op("pe", lambda t, ps3=ps3, h=h: t.matmul(ps3[0:64, 0:nt], wukv[:, 0, h * 64:(h + 1) * 64], qn[:, 2, :], start=True, stop=True), reads=[("wukv", 0), ("qn", 2)], writes=[pk3])
            P.op("act" if h % 2 else "dve", (lambda a, ps3=ps3, h=h: a.copy(ko[:, h, :], ps3[0:64, 0:nt])) if h % 2 else (lambda v, ps3=ps3, h=h: v.tensor_copy(ko[:, h, :], ps3[0:64, 0:nt])), reads=[pk3], writes=[("ko", h)])
        P.op("dve", lambda v: v.tensor_tensor(kp[:, 0, :], pt[0:32, 3, :], cq[0:32, :], ALU.mult), reads=["pt", "cq"], writes=[("kp", 0)])
        P.op("dve", lambda v: v.tensor_tensor(kp[:, 1, :], pt[0:32, 4, :], sq[0:32, :], ALU.mult), reads=["pt", "sq"], writes=[("kp", 1)])
        P.op("dve", lambda v: v.tensor_tensor(kp[:, 0, :], kp[:, 0, :], kp[:, 1, :], ALU.add), reads=[("kp", 0), ("kp", 1)], writes=[("kp", 0)])
        for sub in range(nt // 128 if with_kv else 0):
            ps4, pk4 = newps()
            vb_ = sub % 2
            P.op("pe", lambda t, ps4=ps4, sub=sub: t.matmul(ps4[:, 0:512], qn[:, 2, sub * 128:(sub + 1) * 128], wukv[:, 0, 512:1024], start=True, stop=True), reads=[("wukv", 0), ("qn", 2)], writes=[pk4])
            P.op("act", lambda a, ps4=ps4, vb_=vb_: a.copy(vo[:, vb_, :], ps4[:, 0:512]), reads=[pk4], writes=[("vo", vb_)])
            P.dma("sp", v_o[t0 + sub * 128:t0 + (sub + 1) * 128, :], vo[:, vb_, :], reads=[("vo", vb_)])
        if with_q:
            P.dma("sp", q_o[:, :, t0:t0 + nt], qo[:, :, :], reads=[("qo", h) for h in range(8)])
        if with_kv:
            P.dma("sp", kn_o[:, :, t0:t0 + nt], ko[:, :, :], reads=[("ko", h) for h in range(8)])
            P.dma("sp", kpe_o[:, t0:t0 + nt], kp[:, 0, :], reads=[("kp", 0)])
    return C.end()


def build_mla(C=None, split_k=False):
    NQ, NKT = 2048, 66
    C = C or Ctx()
    q_d = C.din("q", [96, 8, NQ]); v_d = C.din("v", [128, NKT, 8, 64])
    if split_k:
        kn_d = C.din("kn", [64, 8, NKT * 128]); kpe_d = C.din("kpe", [32, NKT * 128])
    else:
        k_d = C.din("k", [8, 96, NKT * 128])
    o_d = C.dout("o", [NQ, 512])
    qf = C.sb("qf", [96, 2, NQ]); qb = C.sb("qb", [96, 2, NQ], BF16)
    kf = C.sb("kf", [96, 2, 2112]); kb = C.sb("kb", [96, 2, NKT * 128], BF16)
    vf = C.sb("vf", [128, NKT, 64]); va = C.sb("va", [128, 2, NKT, 65], BF16)
    pt = C.sb("pt", [128, 4, 512], BF16); den = C.sb("den", [128, 2, 4]); osb = C.sb("osb", [128, 16, 512])
    pss = [C.ps("ps%d" % i, [128, 512]) for i in range(8)]
    P = C.start()
    P.op("pool", lambda v: v.memset(va[:], 1.0), writes=[("va", 0), ("va", 1)])
    it = 0; sc = 0; oc = 0
    for h in range(8):
        hb = h % 2
        P.dma("sp", qf[:, hb, :], q_d[:, h, :], writes=[("qf", hb)])
        P.op("dve", lambda v, hb=hb: v.tensor_copy(qb[:, hb, :], qf[:, hb, :]), reads=[("qf", hb)], writes=[("qb", hb)])
        for part in range(4):
            pb_ = part % 2
            if split_k:
                P.dma("sp", kf[0:64, pb_, :], kn_d[:, h, part * 2112:(part + 1) * 2112], writes=[("kf", pb_)])
                P.dma("sp", kf[64:96, pb_, :], kpe_d[:, part * 2112:(part + 1) * 2112], writes=[("kf", pb_)])
            else:
                P.dma("sp", kf[:, pb_, :], k_d[h, :, part * 2112:(part + 1) * 2112], writes=[("kf", pb_)])
            P.op("pool" if part % 2 else "dve", lambda v, hb=hb, pb_=pb_, part=part: v.tensor_copy(kb[:, hb, part * 2112:(part + 1) * 2112], kf[:, pb_, :]), reads=[("kf", pb_)], writes=[("kb", hb)])
        P.dma("sp", vf[:, :, :], v_d[:, :, h, :], writes=["vf"])
        P.op("dve", lambda v, hb=hb: v.tensor_copy(va[:, hb, :, 0:64], vf[:, :, :]), reads=["vf"], writes=[("va", hb)])
        for qt in range(4):
            ops = pss[4 + oc % 4]; opk = ("ps", 4 + oc % 4); oc += 1
            def emit_qk(kt, hb=hb, qt=qt):
                nonlocal sc
                ps = pss[sc % 4]; pk = ("ps", sc % 4); sc += 1
                P.op("pe", lambda t, ps=ps, hb=hb, kt=kt, qt=qt: t.matmul(ps[:, 0:512], kb[:, hb, kt * 128:(kt + 1) * 128], qb[:, hb, qt * 512:(qt + 1) * 512], start=True, stop=True), reads=[("kb", hb), ("qb", hb)], writes=[pk])
                return ps, pk
            pend = [emit_qk(0), emit_qk(1)]
            for kt in range(NKT):
                ps, pk = pend.pop(0)
                if kt + 2 < NKT:
                    pend.append(emit_qk(kt + 2))
                pb = it % 4; it += 1
                P.op("act", lambda a, ps=ps, pb=pb: a.activation(pt[:, pb, :], ps[:, 0:512], AF.Exp, scale=96.0 ** -0.5), reads=[pk], writes=[("pt", pb)])
                for j in range(4):
                    P.op("pe", lambda t, ops=ops, pb=pb, j=j, hb=hb, kt=kt: t.matmul(ops[:, j * 65:(j + 1) * 65], pt[:, pb, j * 128:(j + 1) * 128], va[:, hb, kt, :], start=(kt == 0 and j == 0), stop=(kt == NKT - 1 and j == 3)),
                         reads=[("pt", pb), ("va", hb)], writes=[opk])
            db = qt % 2
            P.op("dve", lambda v, ops=ops, db=db: v.reciprocal(den[:, db, :], ops[:, 64:260:65]), reads=[opk], writes=[("den", db)])
            for j in range(4):
                P.op("dve", lambda v, ops=ops, db=db, j=j, qt=qt, h=h: v.tensor_scalar_mul(osb[:, qt * 4 + j, h * 64:(h + 1) * 64], ops[:, j * 65:j * 65 + 64], den[:, db, j:j + 1]), reads=[opk, ("den", db)], writes=[("osb", qt * 4 + j)])
    P.dma("sp", o_d.rearrange("(n p) c -> p n c", p=128), osb[:, :, :], reads=[("osb", i) for i in range(16)])
    return C.end()


def build_k4a1(ntok, tile, C=None):
    C = C or Ctx()
    W = C.din("w", [1024, 1024]); lnd = C.din("ln", [128, 2, 8]); ones_d = C.din("ones", [128, 128]); bo_d = C.din("bo", [128, 128]); sc_d = C.din("sc", [128, 12])
    x_d = C.din("x0", [1024, ntok]); mod_d = C.din("mod0", [128, 6, 8]); out = C.dout("o0", [1024, ntok])
    ins = {n: C.din(n, [512, ntok]) for n in ("yf", "yb", "r", "km", "v", "g", "at")}
    TM = tile
    wb = C.sb("wb", [128, 8, 1024], BF16); stage = C.sb("stage", [128, 2, 1024])
    xt = C.sb("xt", [128, 8, TM]); mix = C.sb("mix", [128, 8, TM], BF16)
    tl = {n: C.sb("t_" + n, [128, 4, TM]) for n in ins}
    z = C.sb("z", [128, 8, TM]); zsq = C.sb("zsq", [128, 8, TM]); ost = C.sb("ost", [128, 2, TM])
    ones = C.sb("ones", [128, 128]); bo = C.sb("bo", [128, 128]); lnt = C.sb("lnt", [128, 2, 8]); sc = C.sb("sc", [128, 12]); mt = C.sb("mt", [128, 1, 6, 8])
    pss = [C.ps("ps%d" % i, [128, 512]) for i in range(8)]
    P = C.start()
    pctr = [0]

    def newps():
        i = pctr[0] % 8; pctr[0] += 1
        return pss[i], ("ps", i)
    P.dma("sp", ones[:], ones_d[:, :], writes=["ones"]); P.dma("sp", bo[:], bo_d[:, :], writes=["bo"]); P.dma("sp", lnt[:], lnd[:, :, :], writes=["ln"])
    P.dma("sp", sc[:], sc_d[:, :], writes=["sc"]); P.dma("sp", mt[:, 0, :, :], mod_d[:, :, :], writes=[("mt", 0)])
    load_w_bf16(C, P, wb, "wb", W, 1024, 1024, stage, "stage")
    v3 = lambda ap: ap.rearrange("(kc p) t -> p kc t", p=128)
    for t0 in range(0, ntok, tile):
        nt = tile
        P.dma("sp", xt[:, :, :], v3(x_d)[:, :, t0:t0 + nt], writes=["xt"])
        for n in ins:
            P.dma("sp", tl[n][:, :, :], v3(ins[n])[:, :, t0:t0 + nt], writes=[n])
        y, yb_, r_, km_, v_, g_, at_ = (tl[n] for n in ("yf", "yb", "r", "km", "v", "g", "at"))
        P.op("dve", lambda v: v.tensor_tensor(y[:, :, :], y[:, :, :], yb_[:, :, :], ALU.add), reads=["yf", "yb"], writes=["yf"])
        P.op("act", lambda a: a.activation(yb_[:, :, :], y[:, :, :], AF.Square), reads=["yf"], writes=["yb"])
        P.op("pool", lambda v: v.tensor_copy(mix[:, 4:8, :], at_[:, :, :]), reads=["at"], writes=[("mix", k) for k in range(4, 8)])
        P.op("pool", lambda v: v.tensor_tensor(r_[:, :, :], r_[:, :, :], km_[:, :, :], ALU.mult), reads=["r", "km"], writes=["r"])
        for c in range(4):
            psm, pkm = newps(); psq, pkq = newps(); psb, pkb = newps()
            P.op("pe", lambda t, psm=psm, c=c: t.matmul(psm[:, 0:nt], bo[:, :], y[:, c, :], start=True, stop=True), reads=["bo", "yf"], writes=[pkm])
            P.op("pe", lambda t, psq=psq, c=c: t.matmul(psq[:, 0:nt], bo[:, :], yb_[:, c, :], start=True, stop=True), reads=["bo", "yb"], writes=[pkq])
            P.op("dve", lambda v, c=c: v.tensor_scalar_mul(km_[:, c, :], r_[:, c, :], sc[:, c:c + 1]), reads=["r", "sc"], writes=[("km", c)])
            P.op("pe", lambda t, psb=psb, c=c: t.matmul(psb[:, 0:nt], bo[:, :], km_[:, c, :], start=True, stop=True), reads=["bo", ("km", c)], writes=[pkb])
            mean, var = ost[:, 0, :], ost[:, 1, :]
            P.op("dve", lambda v, psm=psm: v.tensor_scalar_mul(mean, psm[:, 0:nt], 1.0 / 64.0), reads=[pkm], writes=[("ost", 0)])
            P.op("dve", lambda v: v.tensor_tensor(var, mean, mean, ALU.mult), reads=[("ost", 0)], writes=[("ost", 1)])
            P.op("dve", lambda v, psq=psq: v.scalar_tensor_tensor(var, psq[:, 0:nt], 1.0 / 64.0, var, ALU.mult, ALU.subtract), reads=[pkq, ("ost", 1)], writes=[("ost", 1)])
            P.op("dve", lambda v: v.tensor_scalar_add(var, var, 6.4e-4), reads=[("ost", 1)], writes=[("ost", 1)])
            P.op("act", lambda a: a.activation(var, var, AF.Sqrt), reads=[("ost", 1)], writes=[("ost", 1)])
            P.op("dve", lambda v: v.reciprocal(var, var), reads=[("ost", 1)], writes=[("ost", 1)])
            P.op("dve", lambda v, c=c: v.tensor_tensor(zsq[:, c, :], y[:, c, :], mean, ALU.subtract), reads=["yf", ("ost", 0)], writes=[("zsq", c)])
            P.op("dve", lambda v, c=c: v.tensor_tensor(zsq[:, c, :], zsq[:, c, :], var, ALU.mult), reads=[("zsq", c), ("ost", 1)], writes=[("zsq", c)])
            P.op("dve", lambda v, c=c: v.tensor_scalar(zsq[:, c, :], zsq[:, c, :], sc[:, 4 + c:5 + c], sc[:, 8 + c:9 + c], ALU.mult, ALU.add), reads=[("zsq", c), "sc"], writes=[("zsq", c)])
            P.op("dve", lambda v, psb=psb, c=c: v.tensor_tensor(z[:, c, :], psb[:, 0:nt], v_[:, c, :], ALU.mult), reads=[pkb, "v"], writes=[("z", c)])
            P.op("dve", lambda v, c=c: v.tensor_tensor(zsq[:, c, :], zsq[:, c, :], z[:, c, :], ALU.add), reads=[("zsq", c), ("z", c)], writes=[("zsq", c)])
            P.op("dve", lambda v, c=c: v.tensor_tensor(mix[:, c, :], zsq[:, c, :], g_[:, c, :], ALU.mult), reads=[("zsq", c), "g"], writes=[("mix", c)])
        for dc in range(8):
            ps, pk = newps()
            for kc in range(8):
                P.op("pe", lambda t, ps=ps, kc=kc, dc=dc: t.matmul(ps[:, 0:nt], wb[:, kc, dc * 128:(dc + 1) * 128], mix[:, kc, :], start=(kc == 0), stop=(kc == 7)), reads=[("wb", kc), ("mix", kc)], writes=[pk])
            P.op("dve", lambda v, dc=dc: v.tensor_scalar_mul(z[:, dc, :], xt[:, dc, :], ALPHA), reads=["xt"], writes=[("z", dc)])
            P.op("dve", lambda v, ps=ps, dc=dc: v.scalar_tensor_tensor(z[:, dc, :], ps[:, 0:nt], mt[:, 0, 2, dc:dc + 1], z[:, dc, :], ALU.mult, ALU.add), reads=[pk, ("z", dc), ("mt", 0)], writes=[("z", dc)])
        ln_tail(P, nt, z, zsq, ones, newps, lnt[:, 0, :], lnt[:, 1, :], out, (t0, t0 + nt), ost, None)
    return C.end()


def cd_w_in_layout(w):
    z64 = np.zeros((1024, 64), np.float32); z96 = np.zeros((1024, 96), np.float32)
    kpe = w[:, 2240:2272]
    sw = np.array([d + 8 if (d % 16) < 8 else d - 8 for d in range(32)])
    return np.ascontiguousarray(np.concatenate([w[:, 0:1536], w[:, 1536:1664], w[:, 1664:1728], z64, w[:, 1728:1856],
                                                w[:, 1856:2112], w[:, 2112:2240], kpe, z96, kpe[:, sw], z96], 1))


def rope32_tables(pos_tok, rope):
    n = len(pos_tok)
    if not rope:
        return np.ones((32, n), np.float32), np.zeros((32, n), np.float32)
    row = (pos_tok // 64).astype(np.float32); col = (pos_tok % 64).astype(np.float32)
    inv = (10000.0 ** (-np.arange(8, dtype=np.float32) / np.float32(8))).astype(np.float32)
    ar = row[None, :] * inv[:, None]; ac = col[None, :] * inv[:, None]
    Cc = np.concatenate([np.cos(ar), np.cos(ar), np.cos(ac), np.cos(ac)], 0).astype(np.float32)
    Ss = np.concatenate([-np.sin(ar), np.sin(ar), -np.sin(ac), np.sin(ac)], 0).astype(np.float32)
    return Cc, Ss


def stage_mla(pl, pc, I):
    wuq = I['mla_w_uq'][0]; wukv = I['mla_w_ukv'][0]
    sw = np.array([d + 8 if (d % 16) < 8 else d - 8 for d in range(32)])
    wuq_sw = wuq.reshape(256, 8, 96).copy()
    wuq_sw[:, :, 64:96] = wuq_sw[:, :, 64:96][:, :, sw]
    wuq_all = np.ascontiguousarray(np.concatenate([wuq, wuq_sw.reshape(256, 768)], 1))
    wk = wukv.reshape(128, 8, 128)
    wukv_all = np.ascontiguousarray(np.concatenate([wk[:, :, :64].reshape(128, 512), wk[:, :, 64:].reshape(128, 512)], 1))
    gq = np.stack([I['mla_q_norm'][0][:128], I['mla_q_norm'][0][128:], I['mla_kv_norm'][0]], 1).astype(np.float32)
    ones = np.ones((128, 128), np.float32)
    ncf = build_mlf(2048, 256); ncc = build_mlf(64, 64)
    def tabs(pos, rope):
        c32, s32 = rope32_tables(pos, rope)
        cq = np.ones((96, len(pos)), np.float32); sq = np.zeros((96, len(pos)), np.float32)
        cq[64:96] = c32; sq[64:96] = s32; cq[0:32] = c32; sq[0:32] = s32
        return cq, sq
    mapsl, mapsc = [], []
    for c in range(NCORES):
        b, s = c // 4, c % 4
        cq, sq = tabs(np.arange(s * 2048, (s + 1) * 2048), True)
        mapsl.append({"p": np.ascontiguousarray(pl[b, s * 2048:(s + 1) * 2048, 1920:2560].T), "wuq": wuq_all, "wukv": wukv_all, "gq": gq, "ones": ones, "cq": cq, "sq": sq})
        cq, sq = tabs(np.arange(64), False)
        mapsc.append({"p": np.ascontiguousarray(pc[b, s * 64:(s + 1) * 64, 1920:2560].T), "wuq": wuq_all, "wukv": wukv_all, "gq": gq, "ones": ones, "cq": cq, "sq": sq})
    rl = run_spmd(ncf, mapsl); rc = run_spmd(ncc, mapsc)
    KT = np.empty((2, 8, 96, 8448), np.float32); V = np.empty((2, 8448, 512), np.float32)
    for c in range(NCORES):
        b, s = c // 4, c % 4
        sl = slice(s * 2048, (s + 1) * 2048); sc_ = slice(8192 + s * 64, 8192 + (s + 1) * 64)
        KT[b, :, 0:64, sl] = rl[c]["kn"].transpose(1, 0, 2); KT[b, :, 64:96, sl] = rl[c]["kpe"][None]
        KT[b, :, 0:64, sc_] = rc[c]["kn"].transpose(1, 0, 2); KT[b, :, 64:96, sc_] = rc[c]["kpe"][None]
        V[b, sl] = rl[c]["v"]; V[b, sc_] = rc[c]["v"]
    nca = build_mla()
    maps = []
    for c in range(NCORES):
        b = c // 4
        maps.append({"q": rl[c]["q"], "k": KT[b], "v": np.ascontiguousarray(V[b].reshape(66, 128, 8, 64).transpose(1, 0, 2, 3))})
    ra = run_spmd(nca, maps)
    att = np.empty((2, 8192, 512), np.float32)
    for c in range(NCORES):
        b, s = c // 4, c % 4
        att[b, s * 2048:(s + 1) * 2048] = ra[c]["o"]
    return att


def stage_k4a1(x, m_l, yf, yb, F, att, I):
    nc = build_k4a1(2048, 256)
    ones = np.ones((128, 128), np.float32)
    t = np.arange(128)
    bo = ((t[:, None] // 64) == (t[None, :] // 64)).astype(np.float32)
    col = lambda v: v.reshape(4, 128).T
    sc = np.ascontiguousarray(np.concatenate([col(I['rwkv_rk'][0].reshape(512)), col(I['rwkv_lnx_g'][0]), col(I['rwkv_lnx_b'][0])], 1).astype(np.float32))
    maps = []
    for c in range(NCORES):
        maps.append({"w": I['cd_w_out'][0], "ln": ln_layout(I['ln_g'][1, 0], I['ln_b'][1, 0]), "ones": ones, "bo": bo, "sc": sc,
                     "x0": shard_tok(x, c), "mod0": mod_layout(m_l[c // 4]),
                     "yf": shard_tok(yf, c), "yb": shard_tok(yb, c), "r": shard_tok(F["r"][:, 256:], c), "km": shard_tok(F["km"][:, 256:], c),
                     "v": shard_tok(F["v"][:, 256:], c), "g": shard_tok(F["g"][:, 256:], c), "at": shard_tok(att, c)})
    res = run_spmd(nc, maps)
    return unshard(res, "o0", None, 1024)[0]


def kernel(**I):
    I = {k: np.asarray(v, dtype=np.float32) for k, v in I.items()}
    m = stage_k0(I['c'], I['c_ctx'], I['ada_w'], I['ada_b'])
    pl, pc = stage_k1(I['x'], I['ctx'], m[0], aug_w_in_ab(I['ab_w_in'][0]))
    o_f, o_b = stage_gla(pl, pc, I['gla_a2'][0], I['gla_ab'][0])
    att, attc = stage_gqa(pl, pc, I['gqa_sink'][0])
    xa, ha = stage_k4a0(I['x'], I['ctx'], m[0], o_f, o_b, pl, pc, att, attc, I['ab_w_out'][0], I['gla_norm'][0], I['ln_g'][0, 0], I['ln_b'][0, 0])
    xb, hb = stage_k4b(xa, ha, m[0], I['mlp_w1'][0], I['mlp_w2'][0], I['ln_g'][0, 1], I['ln_b'][0, 1])
    pl1, pc1 = stage_k1(xb, hb, m[1], cd_w_in_layout(I['cd_w_in'][0]))
    F = stage_rwf(pl1, pc1, I)
    yf, yb = stage_rws(F)
    att1 = stage_mla(pl1, pc1, I)
    xa1 = stage_k4a1(xb, m[1], yf, yb, F, att1, I)
    xb1, _ = stage_k4b(xa1, None, m[1], I['mlp_w1'][1], I['mlp_w2'][1], I['ln_g'][1, 1], I['ln_b'][1, 1], with_ctx=False)
    return xb1


GROUPS = [[0, 1, 2, 3], [4, 5, 6, 7]]


def build_k0f(C):
    cT = C.din("cT", [128, 8, 2]); w = C.din("w", [2, 1024, 6144]); bia = C.din("b", [2, 2, 6144]); out = C.dout("out", [2, 2, 6144])
    NB = 768
    cs = C.sb("cs", [128, 8, 2]); cs2 = C.sb("cs2", [128, 8, 2]); wt = C.sb("wt", [128, 2, 8, NB]); bt = C.sb("bt", [2, 2, 6144]); ot = C.sb("ot", [2, 2, 6144])
    pss = [C.ps("ps%d" % i, [128, 512]) for i in range(2)]
    P = C.start()
    P.dma("sp", cs[:], cT[:, :, :], writes=["cs"]); P.dma("sp", bt[:], bia[:, :, :], writes=["bt"])
    P.op("act", lambda a: a.activation(cs2[:], cs[:], AF.Silu), reads=["cs"], writes=["cs2"])
    i = 0; bi = 0
    for l in range(2):
        for blk in range(6144 // NB):
            b = bi % 2; bi += 1
            P.dma("sp", wt[:, b, :, :], w[l][:, blk * NB:(blk + 1) * NB].rearrange("(kc p) n -> p kc n", p=128), writes=[("wt", b)])
            for n0 in range(0, NB, 384):
                ps = pss[i % 2]; pk = ("ps", i % 2); i += 1
                for kc in range(8):
                    P.op("pe", lambda t, kc=kc, b=b, n0=n0, ps=ps: t.matmul(ps[0:2, 0:384], cs2[:, kc, :], wt[:, b, kc, n0:n0 + 384], start=(kc == 0), stop=(kc == 7)), reads=["cs2", ("wt", b)], writes=[pk])
                g0 = blk * NB + n0
                P.op("dve", lambda v, l=l, g0=g0, ps=ps: v.tensor_tensor(ot[:, l, g0:g0 + 384], ps[0:2, 0:384], bt[:, l, g0:g0 + 384], ALU.add), reads=[pk, "bt"], writes=["ot"])
    P.dma("sp", out[:, :, :], ot[:], reads=["ot"])
    return C.end()


def build_k1x(C, ncols, nfm, tm, segs):
    W = C.din("w", [1024, ncols])
    xs, mods, outs, outt = [], [], [], []
    for si, (ntok, tile) in enumerate(segs):
        xs.append(C.bind["xv%d" % si] if ("xv%d" % si) in C.bind else C.din("x%d" % si, [1024, ntok]).rearrange("(kc p) t -> p kc t", p=128))
        mods.append(C.din("mod%d" % si, [128, 6, 8])); outs.append(C.dout("p%d" % si, [nfm, ntok]))
        outt.append(C.dout("pt%d" % si, [ntok, tm[1] - tm[0]]) if tm else None)
    TM = max(t for _, t in segs)
    NCH = (nfm + 127) // 128
    wb = C.sb("wb", [128, 8, ncols], BF16); stage = C.sb("stage", [128, 2, ncols])
    xt = C.sb("xt", [128, 2, 8, TM]); ub = C.sb("ub", [128, 2, 8, TM], BF16); ost = C.sb("ost", [128, 4, 512])
    mt = C.sb("mt", [128, len(segs), 6, 8]); sc1 = C.sb("sc1", [128, len(segs), 8])
    pss = [C.ps("ps%d" % i, [128, 512]) for i in range(4)]
    P = C.start()
    for si in range(len(segs)):
        P.dma("sp", mt[:, si, :, :], mods[si][:, :, :], writes=[("mt", si)])
        P.op("dve", lambda v, si=si: v.tensor_scalar_add(sc1[:, si, :], mt[:, si, 1, :], 1.0), reads=[("mt", si)], writes=[("sc1", si)])
    load_w_bf16(C, P, wb, "wb", W, 1024, ncols, stage, "stage")
    it = 0; oi = 0
    for si, (ntok, tile) in enumerate(segs):
        xv = xs[si]
        for t0 in range(0, ntok, tile):
            b = it % 2; it += 1
            P.dma("sp", xt[:, b, :, 0:tile], xv[:, :, t0:t0 + tile], writes=[("xt", b)])
            for kc in range(8):
                P.op(P.alt(), lambda v, b=b, kc=kc, si=si, tile=tile: v.tensor_scalar(ub[:, b, kc, 0:tile], xt[:, b, kc, 0:tile], sc1[:, si, kc:kc + 1], mt[:, si, 0, kc:kc + 1], ALU.mult, ALU.add),
                     reads=[("xt", b), ("sc1", si), ("mt", si)], writes=[("ub", b, kc)])
            for ch in range(NCH):
                c0 = ch * 128; cn = min(128, nfm - c0)
                pi = oi % 4; ob = oi % 4; oi += 1
                ps = pss[pi]
                for kc in range(8):
                    P.op("pe", lambda t, ps=ps, cn=cn, c0=c0, kc=kc, b=b, tile=tile: t.matmul(ps[0:cn, 0:tile], wb[:, kc, c0:c0 + cn], ub[:, b, kc, 0:tile], start=(kc == 0), stop=(kc == 7)),
                         reads=[("wb", kc), ("ub", b, kc)], writes=[("ps", pi)])
                P.op("act" if oi % 2 else "dve", (lambda a, ps=ps, cn=cn, ob=ob, tile=tile: a.copy(ost[0:cn, ob, 0:tile], ps[0:cn, 0:tile])) if oi % 2 else
                     (lambda v, ps=ps, cn=cn, ob=ob, tile=tile: v.tensor_copy(ost[0:cn, ob, 0:tile], ps[0:cn, 0:tile])), reads=[("ps", pi)], writes=[("ost", ob)])
                P.dma("sp", outs[si][c0:c0 + cn, t0:t0 + tile], ost[0:cn, ob, 0:tile], reads=[("ost", ob)])
            if tm:
                nt_ = tm[1] - tm[0]
                for s0 in range(0, tile, 128):
                    sn = min(128, tile - s0)
                    pi = oi % 4; ob = oi % 4; oi += 1
                    ps = pss[pi]
                    for kc in range(8):
                        P.op("pe", lambda t, ps=ps, kc=kc, b=b, s0=s0, sn=sn: t.matmul(ps[0:sn, 0:nt_], ub[:, b, kc, s0:s0 + sn], wb[:, kc, tm[0]:tm[1]], start=(kc == 0), stop=(kc == 7)),
                             reads=[("wb", kc), ("ub", b, kc)], writes=[("ps", pi)])
                    P.op("act", lambda a, ps=ps, ob=ob, sn=sn: a.copy(ost[0:sn, ob, 0:nt_], ps[0:sn, 0:nt_]), reads=[("ps", pi)], writes=[("ost", ob)])
                    P.dma("sp", outt[si][t0 + s0:t0 + s0 + sn, :], ost[0:sn, ob, 0:nt_], reads=[("ost", ob)])
    return C.end()


def build_tr(C, R, Cc, rev_src=False, plain2=False):
    src = C.din("src", [R, Cc]); idn = C.din("idn", [128, 128]); dst = C.dout("dst", [Cc, R])
    src2 = C.din("src2", [R, Cc]) if rev_src else None
    jd = C.din("jdn", [128, 128]) if rev_src else None
    NCc = Cc // 128
    st = C.sb("st", [128, 2, 4, Cc]); st2 = C.sb("st2", [128, 2, 4, Cc]) if rev_src else None
    idt = C.sb("idt", [128, 128]); jt = C.sb("jt", [128, 128]) if rev_src else None
    ot = C.sb("ot", [128, 4, 512])
    pss = [C.ps("ps%d" % i, [128, 512]) for i in range(4)]
    P = C.start()
    P.dma("sp", idt[:], idn[:, :], writes=["idt"])
    if rev_src:
        P.dma("sp", jt[:], jd[:, :], writes=["jt"])
    oi = 0
    for g, r0 in enumerate(range(0, R, 512)):
        nr = min(4, (R - r0) // 128)
        b = g % 2
        P.dma("sp", st[:, b, 0:nr, :], src[r0:r0 + nr * 128, :].rearrange("(n p) c -> p n c", p=128), writes=[("st", b)])
        if rev_src:
            if plain2:
                P.dma("sp", st2[:, b, 0:nr, :], src2[r0:r0 + nr * 128, :].rearrange("(n p) c -> p n c", p=128), writes=[("st2", b)])
            else:
                P.dma("sp", st2[:, b, 0:nr, :], src2[R - r0 - nr * 128:R - r0, :].rearrange("(n p) c -> p n c", p=128), writes=[("st2", b)])
        for c in range(NCc):
            pi = oi % 4; oi += 1
            ps = pss[pi]
            for n in range(nr):
                P.op("pe", lambda t, ps=ps, b=b, n=n, c=c: t.matmul(ps[:, n * 128:(n + 1) * 128], st[:, b, n, c * 128:(c + 1) * 128], idt[:, :], start=True, stop=not rev_src),
                     reads=[("st", b), "idt"], writes=[("ps", pi)])
                if rev_src:
                    P.op("pe", lambda t, ps=ps, b=b, n=n, c=c: t.matmul(ps[:, n * 128:(n + 1) * 128], st2[:, b, (n if plain2 else nr - 1 - n), c * 128:(c + 1) * 128], jt[:, :], start=False, stop=True),
                         reads=[("st2", b), "jt"], writes=[("ps", pi)])
            P.op("act" if oi % 2 else "dve", (lambda a, ps=ps, pi=pi: a.copy(ot[:, pi, 0:nr * 128], ps[:, 0:nr * 128])) if oi % 2 else (lambda v, ps=ps, pi=pi: v.tensor_copy(ot[:, pi, 0:nr * 128], ps[:, 0:nr * 128])),
                 reads=[("ps", pi)], writes=[("ot", pi)])
            P.dma("sp", dst[c * 128:(c + 1) * 128, r0:r0 + nr * 128], ot[:, pi, 0:nr * 128], reads=[("ot", pi)])
    return C.end()


def build_glaepi(C, T=8448):
    oT = C.din("oT", [128, T]); ogT = C.din("ogT", [128, T]); gn_d = C.din("gn", [128, 1]); ones_d = C.din("ones", [128, 128]); W = C.din("w", [128, 1024])
    rs_in = C.dout("rs_in", [4096, 2112])
    wb = C.sb("wb", [128, 1, 1024], BF16); stage = C.sb("stage", [128, 2, 1024]); ones = C.sb("ones", [128, 128]); gn = C.sb("gn", [128, 1])
    o = C.sb("o", [128, 2, 512]); og = C.sb("og", [128, 2, 512]); sq = C.sb("sq", [128, 2, 512]); mixb = C.sb("mixb", [128, 2, 512], BF16); ost = C.sb("ost", [128, 4, 512])
    pss = [C.ps("ps%d" % i, [128, 512]) for i in range(8)]
    P = C.start()
    pctr = [0]

    def newps():
        i = pctr[0] % 8; pctr[0] += 1
        return pss[i], ("ps", i)
    P.dma("sp", ones[:], ones_d[:, :], writes=["ones"]); P.dma("sp", gn[:], gn_d[:, :], writes=["gn"])
    load_w_bf16(C, P, wb, "wb", W, 128, 1024, stage, "stage")
    tiles = [(i * 64, 64, i, 2048) for i in range(4)] + [(256 + n * 512, 512, n // 4, (n % 4) * 512) for n in range(16)]
    oi = 0
    for it, (s0, nt, rank, d0) in enumerate(tiles):
        b = it % 2
        P.dma("sp", o[:, b, 0:nt], oT[:, s0:s0 + nt], writes=[("o", b)]); P.dma("sp", og[:, b, 0:nt], ogT[:, s0:s0 + nt], writes=[("og", b)])
        P.op("act", lambda a, b=b, nt=nt: a.activation(sq[:, b, 0:nt], o[:, b, 0:nt], AF.Square), reads=[("o", b)], writes=[("sq", b)])
        P.op("act", lambda a, b=b, nt=nt: a.activation(og[:, b, 0:nt], og[:, b, 0:nt], AF.Silu), reads=[("og", b)], writes=[("og", b)])
        ps, pk = newps()
        P.op("pe", lambda t, ps=ps, b=b, nt=nt: t.matmul(ps[:, 0:nt], ones[:, :], sq[:, b, 0:nt], start=True, stop=True), reads=[("sq", b), "ones"], writes=[pk])
        P.op("dve", lambda v, ps=ps, b=b, nt=nt: v.tensor_scalar(sq[:, b, 0:nt], ps[:, 0:nt], 1.0 / 128.0, LN_EPS, ALU.mult, ALU.add), reads=[pk], writes=[("sq", b)])
        P.op("act", lambda a, b=b, nt=nt: a.activation(sq[:, b, 0:nt], sq[:, b, 0:nt], AF.Sqrt), reads=[("sq", b)], writes=[("sq", b)])
        P.op("dve", lambda v, b=b, nt=nt: v.reciprocal(sq[:, b, 0:nt], sq[:, b, 0:nt]), reads=[("sq", b)], writes=[("sq", b)])
        P.op("dve", lambda v, b=b, nt=nt: v.tensor_tensor(sq[:, b, 0:nt], sq[:, b, 0:nt], o[:, b, 0:nt], ALU.mult), reads=[("sq", b), ("o", b)], writes=[("sq", b)])
        P.op("dve", lambda v, b=b, nt=nt: v.scalar_tensor_tensor(mixb[:, b, 0:nt], sq[:, b, 0:nt], gn[:, 0:1], og[:, b, 0:nt], ALU.mult, ALU.mult), reads=[("sq", b), ("og", b), "gn"], writes=[("mixb", b)])
        for dc in range(8):
            ps, pk = newps()
            ob = oi % 4; oi += 1
            P.op("pe", lambda t, ps=ps, dc=dc, b=b, nt=nt: t.matmul(ps[:, 0:nt], wb[:, 0, dc * 128:(dc + 1) * 128], mixb[:, b, 0:nt], start=True, stop=True), reads=[("wb", 0), ("mixb", b)], writes=[pk])
            P.op("act" if oi % 2 else "dve", (lambda a, ps=ps, ob=ob, nt=nt: a.copy(ost[:, ob, 0:nt], ps[:, 0:nt])) if oi % 2 else (lambda v, ps=ps, ob=ob, nt=nt: v.tensor_copy(ost[:, ob, 0:nt], ps[:, 0:nt])),
                 reads=[pk], writes=[("ost", ob)])
            P.dma("sp", rs_in[rank * 1024 + dc * 128:rank * 1024 + (dc + 1) * 128, d0:d0 + nt], ost[:, ob, 0:nt], reads=[("ost", ob)])
    return C.end()


def build_k4af(C, segs, KC):
    W = C.din("w", [KC * 128, 1024]); lnd = C.din("ln", [128, 2, 8]); ones_d = C.din("ones", [128, 128])
    xs, mods, outs, rss, ats = [], [], [], [], []
    for si, (ntok, tile) in enumerate(segs):
        xs.append(C.din("x%d" % si, [1024, ntok])); mods.append(C.din("mod%d" % si, [128, 6, 8])); outs.append(C.dout("o%d" % si, [1024, ntok]))
        rss.append(C.din("rs%d" % si, [1024, ntok])); ats.append(C.din("at%d" % si, [KC * 128, ntok]))
    TM = max(t for _, t in segs)
    wb = C.sb("wb", [128, KC, 1024], BF16); stage = C.sb("stage", [128, 2, 1024])
    xt = C.sb("xt", [128, 8, TM]); rt = C.sb("rt", [128, 8, TM]); at = C.sb("at", [128, KC, TM]); mix = C.sb("mix", [128, KC, TM], BF16)
    z = C.sb("z", [128, 8, TM]); zsq = C.sb("zsq", [128, 8, TM]); ost = C.sb("ost", [128, 2, TM])
    ones = C.sb("ones", [128, 128]); lnt = C.sb("lnt", [128, 2, 8]); mt = C.sb("mt", [128, len(segs), 6, 8])
    pss = [C.ps("ps%d" % i, [128, 512]) for i in range(8)]
    P = C.start()
    pctr = [0]

    def newps():
        i = pctr[0] % 8; pctr[0] += 1
        return pss[i], ("ps", i)
    P.dma("sp", ones[:], ones_d[:, :], writes=["ones"]); P.dma("sp", lnt[:], lnd[:, :, :], writes=["ln"])
    for si in range(len(segs)):
        P.dma("sp", mt[:, si, :, :], mods[si][:, :, :], writes=[("mt", si)])
    load_w_bf16(C, P, wb, "wb", W, KC * 128, 1024, stage, "stage")
    v3 = lambda ap: ap.rearrange("(kc p) t -> p kc t", p=128)
    for si, (ntok, tile) in enumerate(segs):
        for t0 in range(0, ntok, tile):
            nt = tile
            P.dma("sp", xt[:, :, 0:nt], v3(xs[si])[:, :, t0:t0 + nt], writes=["xt"])
            P.dma("sp", rt[:, :, 0:nt], v3(rss[si])[:, :, t0:t0 + nt], writes=["rt"])
            P.dma("sp", at[:, :, 0:nt], v3(ats[si])[:, :, t0:t0 + nt], writes=["at"])
            P.op("pool", lambda v: v.tensor_copy(mix[:, :, 0:nt], at[:, :, 0:nt]), reads=["at"], writes=[("mix", k) for k in range(KC)])
            for dc in range(8):
                ps, pk = newps()
                for kc in range(KC):
                    P.op("pe", lambda t, ps=ps, kc=kc, dc=dc: t.matmul(ps[:, 0:nt], wb[:, kc, dc * 128:(dc + 1) * 128], mix[:, kc, 0:nt], start=(kc == 0), stop=(kc == KC - 1)),
                         reads=[("wb", kc), ("mix", kc)], writes=[pk])
                P.op("dve", lambda v, ps=ps, dc=dc: v.tensor_tensor(rt[:, dc, 0:nt], rt[:, dc, 0:nt], ps[:, 0:nt], ALU.add), reads=[pk, "rt"], writes=["rt"])
                P.op("pool", lambda v, dc=dc: v.tensor_scalar_mul(z[:, dc, 0:nt], xt[:, dc, 0:nt], ALPHA), reads=["xt"], writes=[("z", dc)])
                P.op("dve", lambda v, dc=dc, si=si: v.scalar_tensor_tensor(z[:, dc, 0:nt], rt[:, dc, 0:nt], mt[:, si, 2, dc:dc + 1], z[:, dc, 0:nt], ALU.mult, ALU.add),
                     reads=["rt", ("z", dc), ("mt", si)], writes=[("z", dc)])
            ln_tail(P, nt, z, zsq, ones, newps, lnt[:, 0, :], lnt[:, 1, :], outs[si], (t0, t0 + nt), ost, None)
    return C.end()


ADA_PERM = np.array([j * 1024 + kc * 128 + p for p in range(128) for j in range(6) for kc in range(8)])


def fused_program(upto=99):
    C = Ctx(shared=True)
    sc = C.scratch
    T = 8448
    m_scr = sc("m_scr", [2, 2, 6144])
    modv = lambda r, l: m_scr[r, l, :].rearrange("(p j kc) -> p j kc", p=128, j=6)
    C.stage("k0_", {"out": m_scr}); build_k0f(C)
    pT_gla = sc("pT_gla", [2, 384, T]); pt_gla = sc("pt_gla", [2, T, 192])
    for z in range(2):
        C.stage("g%d_" % z, {"mod0": modv(1, 0), "mod1": modv(0, 0), "p0": pT_gla[z, :, 0:256], "p1": pT_gla[z, :, 256:T], "pt0": pt_gla[z, 0:256, :], "pt1": pt_gla[z, 256:T, :]})
        build_k1x(C, 576, 384, (384, 576), [(256, 256), (8192, 512)])
    o_gla = sc("o_gla", [2, T, 128])
    C.stage("gl_", {"qT": pT_gla[:, 0:64, :], "kT": pT_gla[:, 64:128, :], "a1T": pT_gla[:, 128:144, :], "k": pt_gla[:, :, 0:64], "v": pt_gla[:, :, 64:192], "o": o_gla})
    build_gla(T, C=C)
    oT = sc("oT_gla", [128, T])
    C.stage("t0_", {"src": o_gla[0, 0:256, :], "src2": o_gla[1, 0:256, :], "dst": oT[:, 0:256]}); build_tr(C, 256, 128, rev_src=True)
    C.stage("t1_", {"src": o_gla[0, 256:T, :], "src2": o_gla[1, 256:T, :], "dst": oT[:, 256:T]}); build_tr(C, 8192, 128, rev_src=True)
    rs_in = sc("rs_in", [4096, 2112]); rs_out = sc("rs_out", [1024, 2112])
    C.stage("ge_", {"oT": oT, "ogT": pT_gla[0, 256:384, :], "rs_in": rs_in}); build_glaepi(C, T)
    C.P.coll_issue("ReduceScatter", ALU.add, GROUPS, rs_in[:, :], rs_out[:, :])
    pq0 = sc("pq0", [1280, 2304]); pqt0 = sc("pqt0", [2304, 128]); pq1 = sc("pq1", [1280, 256]); pqt1 = sc("pqt1", [256, 128]); pq2 = sc("pq2", [1280, 64]); pqt2 = sc("pqt2", [64, 128])
    C.stage("q_", {"mod0": modv(0, 0), "mod1": modv(1, 0), "mod2": modv(1, 0), "p0": pq0, "pt0": pqt0, "p1": pq1, "pt1": pqt1, "p2": pq2, "pt2": pqt2})
    build_k1x(C, 1408, 1280, (1280, 1408), [(2304, 256), (256, 256), (64, 64)])
    att_tm = sc("att_tm", [2176, 512]); attT = sc("attT", [512, 2176])
    hd = lambda ap: ap.rearrange("(h d) t -> d h t", d=64)
    tmv = lambda ap: ap.rearrange("(n p) (g d) -> p n g d", p=128, d=64)
    C.stage("a_", {"qT": hd(pq0[0:512, 128:2176]), "qsT": hd(pq0[512:1024, 128:2176]), "kT": hd(pq0[1024:1152, :]), "ksT": hd(pq0[1152:1280, :]), "v": tmv(pqt0),
                   "kcT": hd(pq1[1024:1152, :]), "vc": tmv(pqt1), "qcT": hd(pq2[0:512, :]), "o": att_tm[0:2048, :], "oc": att_tm[2048:2176, :]})
    build_gqa(False, False, C=C, oc_rows=128, nmk=4)
    C.stage("ta_", {"src": att_tm, "dst": attT}); build_tr(C, 2176, 512)
    C.P.coll_wait()
    x1T = sc("x1T", [1024, 2048]); h1T = sc("h1T", [1024, 64]); xbT = sc("xbT", [1024, 2048]); hbT = sc("hbT", [1024, 64])
    C.stage("f_", {"mod0": modv(0, 0), "mod1": modv(1, 0), "rs0": rs_out[:, 0:2048], "rs1": rs_out[:, 2048:2112], "at0": attT[:, 0:2048], "at1": attT[:, 2048:2112], "o0": x1T, "o1": h1T})
    build_k4af(C, [(2048, 256), (64, 64)], 4)
    ag_lat = sc("ag_lat", [8, 128, 2048]); ag_ctx = sc("ag_ctx", [1024, 64]); agl_out = sc("agl_out", [8, 512, 2048]); agc_out = sc("agc_out", [4096, 64])
    xb_own = ag_lat.rearrange("kc p t -> (kc p) t")
    C.stage("m0_", {"mod0": modv(0, 0), "mod1": modv(1, 0), "x0": x1T, "x1": h1T, "o0": xb_own, "o1": ag_ctx})
    build_k4b([(2048, 256), (64, 64)], C=C)
    if upto == 0:
        dbg = C.nc.dram_tensor("dbg_xb", [1024, 2048], F32, kind="ExternalOutput").ap()
        dbg2 = C.nc.dram_tensor("dbg_hb", [1024, 64], F32, kind="ExternalOutput").ap()
        C.P.dma("sp", dbg[:, :], xb_own); C.P.dma("sp", dbg2[:, :], ag_ctx[:, :])
        return C.finalize(), C
    for kc in range(8):
        C.P.coll_issue("AllGather", ALU.bypass, GROUPS, ag_lat[kc], agl_out[kc], barrier=(kc == 0))
    C.P.coll_issue("AllGather", ALU.bypass, GROUPS, ag_ctx[:, :], agc_out[:, :], barrier=False)
    p_q = sc("p_q", [640, 2048])
    C.stage("mq_", {"x0": xb_own, "mod0": modv(0, 1), "p0": p_q}); build_k1x(C, 640, 640, None, [(2048, 512)])
    q_own = sc("q_own", [96, 8, 2048]); kn_d = sc("kn_dum", [64, 8, 2048]); kpe_dm = sc("kpe_dum", [32, 2048]); v_dm = sc("v_dum", [2048, 512])
    C.stage("mqf_", {"p": p_q, "q": q_own, "kn": kn_d, "kpe": kpe_dm, "v": v_dm}); build_mlf(2048, 256, C=C, with_kv=False)
    C.P.coll_wait()
    latv = lambda r: agl_out[:, r * 128:(r + 1) * 128, :].rearrange("kc p t -> p kc t")
    ctxv = lambda r: agc_out[r * 1024:(r + 1) * 1024, :]
    pTr_c = sc("pTr_c", [768, 258]); pTr_l = sc("pTr_l", [768, 8194])
    C.stage("z_", {})
    zt = C.sb("zt", [128, 6, 1]); P = C.start()
    P.op("dve", lambda v: v.memset(zt[:], 0.0), writes=["zt"])
    for ap_, col in ((pTr_c, 0), (pTr_c, 257), (pTr_l, 0), (pTr_l, 8193)):
        P.dma("sp", ap_.rearrange("(kc p) t -> p kc t", p=128)[:, :, col:col + 1], zt[:], reads=["zt"], allow_slow_non_contiguous=True)
    C.end()
    bd = {}
    for i in range(4):
        bd["x%d" % i] = ctxv(i); bd["mod%d" % i] = modv(1, 1); bd["p%d" % i] = pTr_c[:, 1 + 64 * i:1 + 64 * (i + 1)]
        bd["xv%d" % (4 + i)] = latv(i); bd["mod%d" % (4 + i)] = modv(0, 1); bd["p%d" % (4 + i)] = pTr_l[:, 1 + 2048 * i:1 + 2048 * (i + 1)]
    C.stage("r_", bd); build_k1x(C, 768, 768, None, [(64, 64)] * 4 + [(2048, 512)] * 4)
    FN = ["r", "km", "v", "an", "bb", "wf", "wb", "g"]
    Ff = {n: sc("F_" + n, [128, T]) for n in FN}
    bd = {"p0": pTr_c, "p1": pTr_l}
    for n in FN:
        bd[n + "0"] = Ff[n][:, 0:256]; bd[n + "1"] = Ff[n][:, 256:T]
    C.stage("rf_", bd); build_rwf([(256, 256), (8192, 512)], C=C, NH=1)
    Ftm = {n: sc("Ftm_" + n, [T, 128]) for n in ("bb", "km", "v", "wf", "wb")}
    bd = {}
    for n in ("bb", "km", "v", "wf", "wb"):
        bd["src_" + n] = Ff[n]; bd["dst_" + n] = Ftm[n]
    C.stage("xt_", bd); build_tr_fm(C, ("bb", "km", "v", "wf", "wb"), T)
    y_scr = sc("y_scr", [2, T, 128])
    C.stage("s_", {"an": Ff["an"], "r": Ff["r"], "bb": Ff["bb"], "km": Ff["km"], "bb_tm": Ftm["bb"], "km_tm": Ftm["km"], "v_tm": Ftm["v"], "wf_tm": Ftm["wf"], "wb_tm": Ftm["wb"], "y": y_scr}); build_rwc2(C, T, 4)
    yT = sc("yT", [128, 8192])
    C.stage("ty_", {"src": y_scr[0, 256:T, :], "src2": y_scr[1, 256:T, :], "dst": yT}); build_tr(C, 8192, 128, rev_src=True, plain2=True)
    rs2_in = sc("rs2_in", [4096, 2048]); rs2_out = sc("rs2_out", [1024, 2048])
    C.stage("re_", {"yT": yT, "r": Ff["r"][:, 256:T], "km": Ff["km"][:, 256:T], "v": Ff["v"][:, 256:T], "g": Ff["g"][:, 256:T], "rs_in": rs2_in}); build_rwepi(C)
    C.P.coll_issue("ReduceScatter", ALU.add, GROUPS, rs2_in[:, :], rs2_out[:, :])
    p_mla = sc("p_mla", [640, T])
    bd = {}
    for i in range(4):
        bd["xv%d" % i] = latv(i); bd["mod%d" % i] = modv(0, 1); bd["p%d" % i] = p_mla[:, 2048 * i:2048 * (i + 1)]
        bd["x%d" % (4 + i)] = ctxv(i); bd["mod%d" % (4 + i)] = modv(1, 1); bd["p%d" % (4 + i)] = p_mla[:, 8192 + 64 * i:8192 + 64 * (i + 1)]
    C.stage("mk_", bd); build_k1x(C, 640, 640, None, [(2048, 512)] * 4 + [(64, 64)] * 4)
    q_all = sc("q_all", [96, 8, T]); kn_all = sc("kn_all", [64, 8, T]); kpe_all = sc("kpe_all", [32, T]); v_all = sc("v_all", [T, 512])
    C.stage("mkf_", {"p": p_mla, "q": q_all, "kn": kn_all, "kpe": kpe_all, "v": v_all}); build_mlf(T, 256, C=C, with_q=False)
    att1 = sc("att1_tm", [2048, 512]); attT1 = sc("attT1", [512, 2048])
    C.stage("ma_", {"q": q_own, "kn": kn_all, "kpe": kpe_all, "v": v_all.rearrange("(n p) (h d) -> p n h d", p=128, d=64), "o": att1}); build_mla(C=C, split_k=True)
    C.stage("tb_", {"src": att1, "dst": attT1}); build_tr(C, 2048, 512)
    C.P.coll_wait()
    C.stage("f1_", {"x0": xb_own, "mod0": modv(0, 1), "rs0": rs2_out, "at0": attT1, "o0": x1T}); build_k4af(C, [(2048, 256)], 4)
    C.stage("m1_", {"x0": x1T, "mod0": modv(0, 1)}); build_k4b([(2048, 256)], C=C)
    return C.finalize(), C


def fused_inputs(I, c):
    b, s = c // 4, c % 4
    f = lambda a: np.ascontiguousarray(a, dtype=np.float32)
    x, ctx = I['x'], I['ctx']
    m = {}
    cc = np.stack([I['c'][b], I['c_ctx']], axis=1)
    m["k0_cT"] = f(cc.reshape(8, 128, 2).transpose(1, 0, 2))
    m["k0_w"] = f(I['ada_w'][:, :, ADA_PERM])
    m["k0_b"] = f(np.broadcast_to(I['ada_b'][None, :, ADA_PERM], (2, 2, 6144)))
    w_in = I['ab_w_in'][0]
    h = s
    z96 = np.zeros((1024, 96), np.float32)
    q_, k_, v_ = w_in[:, h * 64:(h + 1) * 64], w_in[:, 256 + h * 64:256 + (h + 1) * 64], w_in[:, 512 + h * 128:512 + (h + 1) * 128]
    a1f, a1b, og = w_in[:, 1024:1040], w_in[:, 1040:1056], w_in[:, 1056 + h * 128:1056 + (h + 1) * 128]
    m["g0_w"] = f(np.concatenate([q_, k_, a1f, a1b, z96, og, k_, v_], 1))
    m["g1_w"] = f(np.concatenate([q_, k_, a1b, a1f, z96, og, k_, v_], 1))
    m["g0_x0"] = f(ctx[b].T); m["g0_x1"] = f(x[b].T)
    m["g1_x0"] = f(ctx[b][::-1].T); m["g1_x1"] = f(x[b][::-1].T)
    m["gl_a2"] = f(I['gla_a2'][0][:, :, h * 64:(h + 1) * 64]); m["gl_ab"] = f(I['gla_ab'][0][:, None, h * 64:(h + 1) * 64])
    tri, mask = gla_consts()
    m["gl_tri"] = tri; m["gl_mask"] = mask
    idn = np.eye(128, dtype=np.float32); jdn = f(idn[::-1])
    for p in ("t0_", "t1_"):
        m[p + "idn"] = idn; m[p + "jdn"] = jdn
    ones = np.ones((128, 128), np.float32)
    m["ge_gn"] = f(I['gla_norm'][0].reshape(128, 1)); m["ge_ones"] = ones; m["ge_w"] = f(I['ab_w_out'][0][h * 128:(h + 1) * 128])
    qs = np.concatenate([w_in[:, GQ0 + hh * 64 + SWAP64] for hh in range(8)], 1)
    ks = np.concatenate([w_in[:, GK0 + hh * 64 + SWAP64] for hh in range(2)], 1)
    m["q_w"] = f(np.concatenate([w_in[:, GQ0:GQ0 + 512], qs, w_in[:, GK0:GK0 + 128], ks, w_in[:, GV0:GV0 + 128]], 1))
    lo, hi = s * 2048, (s + 1) * 2048
    xh = np.zeros((2304, 1024), np.float32)
    a0, a1 = max(lo - 128, 0), min(hi + 128, 8192)
    xh[a0 - (lo - 128):a1 - (lo - 128)] = x[b, a0:a1]
    m["q_x0"] = f(xh.T); m["q_x1"] = f(ctx[b].T); m["q_x2"] = f(ctx[b, s * 64:(s + 1) * 64].T)
    pos = np.arange(lo, hi); posk = np.clip(np.arange(lo - 128, hi + 128), 0, 8191)
    m["a_cq"], m["a_sq"] = rope_tables(pos); m["a_ck"], m["a_sk"] = rope_tables(posk)
    m["a_sink"] = f(np.broadcast_to(I['gqa_sink'][0][None, :], (128, 8)))
    t = np.arange(128)
    mprev = np.tile((t[:, None] >= t[None, :]).astype(np.float32), (1, 4))
    mnext = np.tile((t[:, None] <= t[None, :]).astype(np.float32), (1, 4))
    m["a_mk"] = f(np.stack([mprev, mnext, mprev * (0.0 if s == 0 else 1.0), mnext * (0.0 if s == 3 else 1.0)], 1))
    m["ta_idn"] = idn
    m["f_w"] = f(I['ab_w_out'][0][512:1024]); m["f_ln"] = ln_layout(I['ln_g'][0, 0], I['ln_b'][0, 0]); m["f_ones"] = ones
    m["f_x0"] = f(x[b, lo:hi].T); m["f_x1"] = f(ctx[b, s * 64:(s + 1) * 64].T)
    m["m0_w1"] = f(I['mlp_w1'][0]); m["m0_w2"] = f(I['mlp_w2'][0]); m["m0_ln"] = ln_layout(I['ln_g'][0, 1], I['ln_b'][0, 1]); m["m0_ones"] = ones
    return m


def build_rws2(C, T=8448, BS=32):
    NCTX = 256 // BS
    NB = T // BS
    fm = {n: C.din(n, [128, T]) for n in ("an", "r", "wf", "wb")}
    tmd = {n: C.din(n, [T, 128]) for n in ("bb", "km", "v")}
    mk_d = C.din("mk2", [128, 2])
    y_d = C.dout("y", [2, T, 128])
    an = C.sb("an", [128, 2, 2, BS]); rr = C.sb("rr", [128, 2, 2, BS]); wt = C.sb("wt", [128, 2, 2, BS])
    ab = C.sb("ab", [128, 2, 2, BS, 2]); rb = C.sb("rb", [128, 2, 2, BS, 2])
    bk = C.sb("bk", [4, 2, 2, BS, 128]); vv = C.sb("vv", [4, 2, 2, BS, 64]); ysb = C.sb("ysb", [2, 2, 2, BS, 64])
    ST = C.sb("ST", [128, 2, 64]); mk = C.sb("mk", [128, 2])
    pa = [C.ps("pa%d" % i, [128, 512]) for i in range(2)]
    po = [C.ps("po%d" % i, [128, 512]) for i in range(2)]
    py = [C.ps("py%d" % i, [128, 512]) for i in range(2)]
    P = C.start()
    P.dma("sp", mk[:], mk_d[:, :], writes=["mk"])
    P.op("dve", lambda v: v.memset(ST[:], 0.0), writes=[("ST", 0), ("ST", 1)])
    P.op("pool", lambda v: v.memset(bk[:], 0.0), writes=[("bk", z, b) for z in range(2) for b in range(2)])
    P.op("pool", lambda v: v.memset(vv[:], 0.0), writes=[("vv", z, b) for z in range(2) for b in range(2)])
    P.op("dve", lambda v: v.memset(ysb[:], 0.0), writes=[("ysb", z, b) for z in range(2) for b in range(2)])
    order = [list(range(NB)), list(range(NCTX - 1, -1, -1)) + list(range(NB - 1, NCTX - 1, -1))]
    for i in range(NB):
        b = i % 2
        for z in range(2):
            blk = order[z][i]; s0 = blk * BS
            wn = "wf" if z == 0 else "wb"
            P.dma("sp", an[:, z, b, :], fm["an"][:, s0:s0 + BS], writes=[("an", z, b)])
            P.dma("sp", rr[:, z, b, :], fm["r"][:, s0:s0 + BS], writes=[("rr", z, b)])
            P.dma("sp", wt[:, z, b, :], fm[wn][:, s0:s0 + BS], writes=[("wt", z, b)])
            for row, (nm, lo) in enumerate((("bb", 0), ("bb", 64), ("km", 0), ("km", 64))):
                P.dma("sp", bk[row:row + 1, z, b, :, lo:lo + 64], tmd[nm][s0:s0 + BS, lo:lo + 64], writes=[("bk", z, b)])
            for row, lo in ((2, 0), (3, 64)):
                P.dma("sp", vv[row:row + 1, z, b, :, :], tmd["v"][s0:s0 + BS, lo:lo + 64], writes=[("vv", z, b)])
            for c in range(2):
                P.op("pool", lambda v, z=z, b=b, c=c: v.tensor_scalar_mul(ab[:, z, b, :, c], an[:, z, b, :], mk[:, c:c + 1]), reads=[("an", z, b), "mk"], writes=[("ab", z, b)])
                P.op("pool", lambda v, z=z, b=b, c=c: v.tensor_scalar_mul(rb[:, z, b, :, c], rr[:, z, b, :], mk[:, c:c + 1]), reads=[("rr", z, b), "mk"], writes=[("rb", z, b)])
        for k in range(BS):
            for z in range(2):
                blk = order[z][i]
                t = k if z == 0 else BS - 1 - k
                P.op("pe", lambda e, z=z, b=b, t=t: e.matmul(pa[z][0:2, 0:64], ab[:, z, b, t, :], ST[:, z, :], start=True, stop=True), reads=[("ab", z, b), ("ST", z)], writes=[("pa", z)])
                P.op("act", lambda a, z=z, b=b, t=t: a.copy(vv[0:2, z, b, t, :], pa[z][0:2, 0:64]), reads=[("pa", z)], writes=[("vv", z, b)])
                P.op("pe", lambda e, z=z, b=b, t=t: e.matmul(po[z][:, 0:64], bk[:, z, b, t, :], vv[:, z, b, t, :], start=True, stop=True), reads=[("bk", z, b), ("vv", z, b)], writes=[("po", z)])
                P.op("dve", lambda v, z=z, b=b, t=t: v.scalar_tensor_tensor(ST[:, z, :], ST[:, z, :], wt[:, z, b, t:t + 1], po[z][:, 0:64], ALU.mult, ALU.add), reads=[("ST", z), ("wt", z, b), ("po", z)], writes=[("ST", z)])
                if blk >= NCTX:
                    P.op("pe", lambda e, z=z, b=b, t=t: e.matmul(py[z][0:2, (t % 8) * 64:(t % 8 + 1) * 64], rb[:, z, b, t, :], ST[:, z, :], start=True, stop=True), reads=[("rb", z, b), ("ST", z)], writes=[("py", z)])
                    last = (t % 8 == 7) if z == 0 else (t % 8 == 0)
                    if last:
                        t8 = (t // 8) * 8
                        P.op("act", lambda a, z=z, b=b, t8=t8: a.copy(ysb[:, z, b, t8:t8 + 8, :], py[z][0:2, 0:512].rearrange("p (a c) -> p a c", c=64)), reads=[("py", z)], writes=[("ysb", z, b)])
        for z in range(2):
            blk = order[z][i]; s0 = blk * BS
            if blk >= NCTX:
                P.dma("sp", y_d[z, s0:s0 + BS, :].rearrange("t (c i) -> c t i", c=2), ysb[:, z, b, :, :], reads=[("ysb", z, b)])
    return C.end()


def build_rwepi(C, N=8192):
    ins = {n: C.din(n, [128, N]) for n in ("yT", "r", "km", "v", "g")}
    sc_d = C.din("sc", [128, 3]); bo_d = C.din("bo", [128, 128]); W = C.din("w", [128, 1024]); rs_in = C.dout("rs_in", [4096, 2048])
    wb = C.sb("wb", [128, 1, 1024], BF16); stage = C.sb("stage", [128, 2, 1024]); bo = C.sb("bo", [128, 128]); sc = C.sb("sc", [128, 3])
    tl = {n: C.sb("t_" + n, [128, 2, 512]) for n in ins}
    sq = C.sb("sq", [128, 2, 512]); mean = C.sb("mean", [128, 512]); var = C.sb("var", [128, 512]); mixb = C.sb("mixb", [128, 2, 512], BF16); ost = C.sb("ost", [128, 4, 512])
    pss = [C.ps("ps%d" % i, [128, 512]) for i in range(8)]
    P = C.start()
    pctr = [0]

    def newps():
        i = pctr[0] % 8; pctr[0] += 1
        return pss[i], ("ps", i)
    P.dma("sp", bo[:], bo_d[:, :], writes=["bo"]); P.dma("sp", sc[:], sc_d[:, :], writes=["sc"])
    load_w_bf16(C, P, wb, "wb", W, 128, 1024, stage, "stage")
    oi = 0
    for it in range(N // 512):
        b = it % 2; t0 = it * 512
        for n in ins:
            P.dma("sp", tl[n][:, b, :], ins[n][:, t0:t0 + 512], writes=[(n, b)])
        y, r_, km_, v_, g_ = (tl[n] for n in ("yT", "r", "km", "v", "g"))
        P.op("act", lambda a, b=b: a.activation(sq[:, b, :], y[:, b, :], AF.Square), reads=[("yT", b)], writes=[("sq", b)])
        P.op("pool", lambda v, b=b: v.tensor_tensor(r_[:, b, :], r_[:, b, :], km_[:, b, :], ALU.mult), reads=[("r", b), ("km", b)], writes=[("r", b)])
        P.op("pool", lambda v, b=b: v.tensor_scalar_mul(r_[:, b, :], r_[:, b, :], sc[:, 0:1]), reads=[("r", b), "sc"], writes=[("r", b)])
        psm, pkm = newps(); psq, pkq = newps(); psb, pkb = newps()
        P.op("pe", lambda t, psm=psm, b=b: t.matmul(psm[:, 0:512], bo[:, :], y[:, b, :], start=True, stop=True), reads=["bo", ("yT", b)], writes=[pkm])
        P.op("pe", lambda t, psq=psq, b=b: t.matmul(psq[:, 0:512], bo[:, :], sq[:, b, :], start=True, stop=True), reads=["bo", ("sq", b)], writes=[pkq])
        P.op("pe", lambda t, psb=psb, b=b: t.matmul(psb[:, 0:512], bo[:, :], r_[:, b, :], start=True, stop=True), reads=["bo", ("r", b)], writes=[pkb])
        P.op("dve", lambda v, psm=psm: v.tensor_scalar_mul(mean[:, :], psm[:, 0:512], 1.0 / 64.0), reads=[pkm], writes=["mean"])
        P.op("dve", lambda v: v.tensor_tensor(var[:, :], mean[:, :], mean[:, :], ALU.mult), reads=["mean"], writes=["var"])
        P.op("dve", lambda v, psq=psq: v.scalar_tensor_tensor(var[:, :], psq[:, 0:512], 1.0 / 64.0, var[:, :], ALU.mult, ALU.subtract), reads=[pkq, "var"], writes=["var"])
        P.op("dve", lambda v: v.tensor_scalar_add(var[:, :], var[:, :], 6.4e-4), reads=["var"], writes=["var"])
        P.op("act", lambda a: a.activation(var[:, :], var[:, :], AF.Sqrt), reads=["var"], writes=["var"])
        P.op("dve", lambda v: v.reciprocal(var[:, :], var[:, :]), reads=["var"], writes=["var"])
        P.op("dve", lambda v, b=b: v.tensor_tensor(sq[:, b, :], y[:, b, :], mean[:, :], ALU.subtract), reads=[("yT", b), "mean"], writes=[("sq", b)])
        P.op("dve", lambda v, b=b: v.tensor_tensor(sq[:, b, :], sq[:, b, :], var[:, :], ALU.mult), reads=[("sq", b), "var"], writes=[("sq", b)])
        P.op("dve", lambda v, b=b: v.tensor_scalar(sq[:, b, :], sq[:, b, :], sc[:, 1:2], sc[:, 2:3], ALU.mult, ALU.add), reads=[("sq", b), "sc"], writes=[("sq", b)])
        P.op("dve", lambda v, psb=psb, b=b: v.tensor_tensor(v_[:, b, :], psb[:, 0:512], v_[:, b, :], ALU.mult), reads=[pkb, ("v", b)], writes=[("v", b)])
        P.op("dve", lambda v, b=b: v.tensor_tensor(sq[:, b, :], sq[:, b, :], v_[:, b, :], ALU.add), reads=[("sq", b), ("v", b)], writes=[("sq", b)])
        P.op("dve", lambda v, b=b: v.tensor_tensor(mixb[:, b, :], sq[:, b, :], g_[:, b, :], ALU.mult), reads=[("sq", b), ("g", b)], writes=[("mixb", b)])
        rank, d0 = t0 // 2048, t0 % 2048
        for dc in range(8):
            ps, pk = newps()
            ob = oi % 4; oi += 1
            P.op("pe", lambda t, ps=ps, dc=dc, b=b: t.matmul(ps[:, 0:512], wb[:, 0, dc * 128:(dc + 1) * 128], mixb[:, b, :], start=True, stop=True), reads=[("wb", 0), ("mixb", b)], writes=[pk])
            P.op("act" if oi % 2 else "dve", (lambda a, ps=ps, ob=ob: a.copy(ost[:, ob, :], ps[:, 0:512])) if oi % 2 else (lambda v, ps=ps, ob=ob: v.tensor_copy(ost[:, ob, :], ps[:, 0:512])), reads=[pk], writes=[("ost", ob)])
            P.dma("sp", rs_in[rank * 1024 + dc * 128:rank * 1024 + (dc + 1) * 128, d0:d0 + 512], ost[:, ob, :], reads=[("ost", ob)])
    return C.end()


def fused_inputs1(I, c, m):
    b, s = c // 4, c % 4
    f = lambda a: np.ascontiguousarray(a, dtype=np.float32)
    hp = s
    w = I['cd_w_in'][0]
    z64 = np.zeros((1024, 64), np.float32); z96 = np.zeros((1024, 96), np.float32)
    pr = slice(hp * 128, (hp + 1) * 128)
    m["r_w"] = f(np.concatenate([w[:, 0:512][:, pr], w[:, 512:1024][:, pr], w[:, 1024:1536][:, pr], w[:, 1536:1664], w[:, 1664:1728], z64, w[:, 1728:1856]], 1))
    w2, a2, g2 = I['rwkv_w2'][0], I['rwkv_a2'][0], I['rwkv_g2'][0]
    wl = np.zeros((128, 512), np.float32)
    wl[0:64, 0:128] = w2[0][:, pr]; wl[64:128, 128:256] = w2[1][:, pr]; wl[0:64, 256:384] = a2[:, pr]; wl[:, 384:512] = g2[:, pr]
    m["rf_wl"] = wl
    mu = I['rwkv_mu'][0]
    col = lambda v: v.reshape(128, 1)
    sc = np.concatenate([col(mu[0:512][pr]), col(mu[512:1024][pr]), col(mu[1024:1536][pr]), col(mu[1536:1664]), col(np.concatenate([mu[1664:1728], np.zeros(64, np.float32)])), col(mu[1728:1856]),
                         col(I['rwkv_w0'][0, 0][pr]), col(I['rwkv_w0'][0, 1][pr]), col(I['rwkv_a0'][0][pr]), col(I['rwkv_kk'][0][pr]), col(I['rwkv_ka'][0][pr])], 1)
    m["rf_sc"] = f(sc)
    t = np.arange(128)
    bo = ((t[:, None] // 64) == (t[None, :] // 64)).astype(np.float32)
    m["rf_bo"] = bo
    idn = np.eye(128, dtype=np.float32)
    m["xt_idn"] = idn
    m["s_U"], m["s_M"] = rwc_consts()
    m["ty_idn"] = idn; m["ty_jdn"] = idn
    m["re_sc"] = f(np.stack([I['rwkv_rk'][0].reshape(512)[pr], I['rwkv_lnx_g'][0][pr], I['rwkv_lnx_b'][0][pr]], 1))
    m["re_bo"] = bo; m["re_w"] = f(I['cd_w_out'][0][hp * 128:(hp + 1) * 128])
    kpe = w[:, 2240:2272]
    sw = np.array([d + 8 if (d % 16) < 8 else d - 8 for d in range(32)])
    wm = f(np.concatenate([w[:, 1856:2112], w[:, 2112:2240], kpe, z96, kpe[:, sw], z96], 1))
    m["mk_w"] = wm; m["mq_w"] = wm
    wuq = I['mla_w_uq'][0]; wukv = I['mla_w_ukv'][0]
    wuq_sw = wuq.reshape(256, 8, 96).copy()
    wuq_sw[:, :, 64:96] = wuq_sw[:, :, 64:96][:, :, sw]
    wuq_all = f(np.concatenate([wuq, wuq_sw.reshape(256, 768)], 1))
    wk = wukv.reshape(128, 8, 128)
    wukv_all = f(np.concatenate([wk[:, :, :64].reshape(128, 512), wk[:, :, 64:].reshape(128, 512)], 1))
    gq = f(np.stack([I['mla_q_norm'][0][:128], I['mla_q_norm'][0][128:], I['mla_kv_norm'][0]], 1))
    ones = np.ones((128, 128), np.float32)

    def tabs(pos, nrope):
        c32, s32 = rope32_tables(pos, True)
        n = len(pos)
        cq = np.ones((96, n), np.float32); sq = np.zeros((96, n), np.float32)
        cq[64:96, :nrope] = c32[:, :nrope]; sq[64:96, :nrope] = s32[:, :nrope]; cq[0:32, :nrope] = c32[:, :nrope]; sq[0:32, :nrope] = s32[:, :nrope]
        return cq, sq
    cqa, sqa = tabs(np.concatenate([np.arange(8192), np.zeros(256, np.int64)]), 8192)
    cqo, sqo = tabs(np.arange(s * 2048, (s + 1) * 2048), 2048)
    for p_, (cq, sq) in (("mkf_", (cqa, sqa)), ("mqf_", (cqo, sqo))):
        m[p_ + "wuq"] = wuq_all; m[p_ + "wukv"] = wukv_all; m[p_ + "gq"] = gq; m[p_ + "ones"] = ones; m[p_ + "cq"] = f(cq); m[p_ + "sq"] = f(sq)
    m["tb_idn"] = idn
    m["f1_w"] = f(I['cd_w_out'][0][512:1024]); m["f1_ln"] = ln_layout(I['ln_g'][1, 0], I['ln_b'][1, 0]); m["f1_ones"] = ones
    m["m1_w1"] = f(I['mlp_w1'][1]); m["m1_w2"] = f(I['mlp_w2'][1]); m["m1_ln"] = ln_layout(I['ln_g'][1, 1], I['ln_b'][1, 1]); m["m1_ones"] = ones
    return m


_FUSED_CACHE = {}


def kernel(**I):
    I = {k: np.asarray(v, dtype=np.float32) for k, v in I.items()}
    if "nc" not in _FUSED_CACHE:
        _FUSED_CACHE["nc"] = fused_program()
    nc, C = _FUSED_CACHE["nc"]
    maps = []
    for c in range(NCORES):
        m = fused_inputs(I, c)
        fused_inputs1(I, c, m)
        maps.append(m)
    res = run_spmd(nc, maps)
    out = np.empty((2, 8192, 1024), np.float32)
    for c in range(NCORES):
        b, s = c // 4, c % 4
        out[b, s * 2048:(s + 1) * 2048] = res[c]["m1_o0"].T
    return out


def rwc_consts():
    t = np.arange(64)
    le = (t[:, None] <= t[None, :]).astype(np.float32); lt = (t[:, None] < t[None, :]).astype(np.float32)
    ge = le.T.copy(); gt = lt.T.copy()
    U = np.stack([le, lt, ge, gt], 1)
    q4 = lambda f, b: np.stack([f, f, b, b], 1)
    return np.ascontiguousarray(U), np.ascontiguousarray(np.stack([q4(lt, gt), q4(gt, lt), q4(le, ge), q4(np.eye(64, dtype=np.float32), np.eye(64, dtype=np.float32))], 1))


def build_rwc(C, T=8448, NCTX=4):
    NB = T // 64
    fm = {n: C.din(n, [128, T]) for n in ("an", "r", "bb", "km")}
    tmd = {n: C.din(n + "_tm", [T, 128]) for n in ("bb", "km", "v", "wf", "wb")}
    U_d = C.din("U", [64, 4, 64]); M_d = C.din("M", [64, 4, 4, 64])
    y_d = C.dout("y", [2, T, 128])
    Uc = C.sb("Uc", [64, 4, 64]); Mc = C.sb("Mc", [64, 4, 4, 64])
    ffm = {n: C.sb("f_" + n, [64, 2, 4, 64]) for n in fm}
    ftm = {n: C.sb("t_" + n, [64, 2, 2, 128]) for n in ("bb", "km", "v", "w")}
    lw = C.sb("lw", [64, 2, 128]); Eg = C.sb("Eg", [64, 4, 64]); Egi = C.sb("Egi", [64, 4, 64]); Egx = C.sb("Egx", [64, 4, 64]); Egt = C.sb("Egt", [64, 2, 128])
    At = C.sb("At", [64, 4, 64]); Rt = C.sb("Rt", [64, 4, 64]); Bt = C.sb("Bt", [64, 4, 64]); Kt = C.sb("Kt", [64, 4, 64])
    Btm = C.sb("Btm", [64, 2, 128]); Ktm = C.sb("Ktm", [64, 2, 128]); gC = C.sb("gC", [64, 4])
    Pm = C.sb("Pm", [64, 2, 4, 64]); PT = C.sb("PT", [64, 2, 4, 64]); Tm = C.sb("Tm", [64, 2, 4, 64])
    Mka = C.sb("Mka", [64, 4, 64]); Mbr = C.sb("Mbr", [64, 4, 64]); Mkr = C.sb("Mkr", [64, 4, 64])
    ST = C.sb("ST", [64, 4, 64]); W1 = C.sb("W1", [64, 4, 64]); UT = C.sb("UT", [64, 4, 64]); tmp = C.sb("tmp", [64, 4, 64]); ysb = C.sb("ysb", [64, 2, 4, 64])
    pss = [C.ps("ps%d" % i, [128, 512]) for i in range(8)]
    P = C.start()
    pctr = [0]

    def newps():
        i = pctr[0] % 8; pctr[0] += 1
        return pss[i], ("ps", i)
    P.dma("sp", Uc[:], U_d[:, :, :], writes=["Uc"]); P.dma("sp", Mc[:], M_d[:, :, :, :], writes=["Mc"])
    P.op("dve", lambda v: v.memset(ST[:], 0.0), writes=["ST"])
    order = [list(range(NB)), list(range(NCTX - 1, -1, -1)) + list(range(NB - 1, NCTX - 1, -1))]
    for i in range(NB):
        b = i % 2
        blk = [order[0][i], order[1][i]]
        for z in range(2):
            s0 = blk[z] * 64
            for n in fm:
                P.dma("sp", ffm[n][:, b, 2 * z:2 * z + 2, :], fm[n][:, s0:s0 + 64].rearrange("(c j) t -> j c t", c=2), writes=[("f" + n, b)])
            for n, src in (("bb", "bb"), ("km", "km"), ("v", "v"), ("w", "wf" if z == 0 else "wb")):
                P.dma("sp", ftm[n][:, b, z, :], tmd[src][s0:s0 + 64, :], writes=[("t" + n, b)])
        P.op("act", lambda a, b=b: a.activation(lw[:, :, :], ftm["w"][:, b, :, :], AF.Ln), reads=[("tw", b)], writes=["lw"])
        pg, pkg = newps(); px, pkx = newps(); pt_, pkt = newps()
        for z in range(2):
            for c in range(2):
                q = 2 * z + c
                P.op("pe", lambda t, z=z, c=c, q=q: t.matmul(pg[0:64, q * 64:(q + 1) * 64], lw[:, z, c * 64:(c + 1) * 64], Uc[:, 2 * z, :], start=True, stop=True), reads=["lw", "Uc"], writes=[pkg])
                P.op("pe", lambda t, z=z, c=c, q=q: t.matmul(px[0:64, q * 64:(q + 1) * 64], lw[:, z, c * 64:(c + 1) * 64], Uc[:, 2 * z + 1, :], start=True, stop=True), reads=["lw", "Uc"], writes=[pkx])
            P.op("pe", lambda t, z=z: t.matmul(pt_[0:64, z * 128:(z + 1) * 128], Uc[:, 2 * z, :], lw[:, z, :], start=True, stop=True), reads=["lw", "Uc"], writes=[pkt])
        f4 = lambda ap: ap.rearrange("p (q t) -> p q t", q=4)
        P.op("act", lambda a: a.activation(Eg[:], f4(pg[0:64, 0:256]), AF.Exp), reads=[pkg], writes=["Eg"])
        P.op("act", lambda a: a.activation(Egi[:], f4(pg[0:64, 0:256]), AF.Exp, scale=-1.0), reads=[pkg], writes=["Egi"])
        P.op("act", lambda a: a.activation(Egx[:], f4(px[0:64, 0:256]), AF.Exp), reads=[pkx], writes=["Egx"])
        P.op("act", lambda a: a.activation(Egt[:], pt_[0:64, 0:256].rearrange("p (z f) -> p z f", z=2), AF.Exp, scale=-1.0), reads=[pkt], writes=["Egt"])
        P.op("dve", lambda v, b=b: v.tensor_tensor(At[:], ffm["an"][:, b, :, :], Egx[:], ALU.mult), reads=[("fan", b), "Egx"], writes=["At"])
        P.op("pool", lambda v, b=b: v.tensor_tensor(Rt[:], ffm["r"][:, b, :, :], Eg[:], ALU.mult), reads=[("fr", b), "Eg"], writes=["Rt"])
        P.op("dve", lambda v, b=b: v.tensor_tensor(Bt[:], ffm["bb"][:, b, :, :], Egi[:], ALU.mult), reads=[("fbb", b), "Egi"], writes=["Bt"])
        P.op("pool", lambda v, b=b: v.tensor_tensor(Kt[:], ffm["km"][:, b, :, :], Egi[:], ALU.mult), reads=[("fkm", b), "Egi"], writes=["Kt"])
        P.op("dve", lambda v, b=b: v.tensor_tensor(Btm[:], ftm["bb"][:, b, :, :], Egt[:], ALU.mult), reads=[("tbb", b), "Egt"], writes=["Btm"])
        P.op("pool", lambda v, b=b: v.tensor_tensor(Ktm[:], ftm["km"][:, b, :, :], Egt[:], ALU.mult), reads=[("tkm", b), "Egt"], writes=["Ktm"])
        P.op("dve", lambda v: v.tensor_copy(gC[:, 0:2], Eg[:, 0:2, 63]), reads=["Eg"], writes=["gC"])
        P.op("dve", lambda v: v.tensor_copy(gC[:, 2:4], Eg[:, 2:4, 0]), reads=["Eg"], writes=["gC"])
        grams = [("N", Bt, At, 0, Pm, 0), ("NT", At, Bt, 1, PT, 0), ("Mka", Kt, At, 0, Mka, None), ("Mbr", Bt, Rt, 2, Mbr, None), ("Mkr", Kt, Rt, 2, Mkr, None)]
        for nm, L, Rr, mi, dst, slot in grams:
            ps, pk = newps()
            for q in range(4):
                P.op("pe", lambda t, ps=ps, q=q, L=L, Rr=Rr: t.matmul(ps[0:64, q * 64:(q + 1) * 64], L[:, q, :], Rr[:, q, :], start=True, stop=True), reads=[L.name if hasattr(L, "name") else nm, "At", "Bt", "Kt", "Rt"], writes=[pk])
            d_ap = dst[:, slot, :, :] if slot is not None else dst[:]
            P.op("dve", lambda v, ps=ps, d_ap=d_ap, mi=mi: v.tensor_tensor(d_ap, f4(ps[0:64, 0:256]), Mc[:, mi, :, :], ALU.mult), reads=[pk, "Mc"], writes=[nm + "m"])
        P.op("dve", lambda v: v.tensor_tensor(Tm[:, 0, :, :], Pm[:, 0, :, :], Mc[:, 3, :, :], ALU.add), reads=["Nm", "Mc"], writes=["Tm0"])
        cur = 0
        for lev in range(5):
            nxt = 1 - cur
            pp, pkp = newps(); pq, pkq = newps()
            for q in range(4):
                P.op("pe", lambda t, q=q, cur=cur: t.matmul(pp[0:64, q * 64:(q + 1) * 64], PT[:, cur, q, :], Pm[:, cur, q, :], start=True, stop=True), reads=["Nm", "NTm", "Pl%d" % cur, "PTl%d" % cur], writes=[pkp])
                P.op("pe", lambda t, q=q, cur=cur: t.matmul(pq[0:64, q * 64:(q + 1) * 64], Pm[:, cur, q, :], PT[:, cur, q, :], start=True, stop=True), reads=["Nm", "NTm", "Pl%d" % cur, "PTl%d" % cur], writes=[pkq])
            P.op("act", lambda a, nxt=nxt: a.copy(Pm[:, nxt, :, :], f4(pp[0:64, 0:256])), reads=[pkp], writes=["Pl%d" % nxt])
            P.op("dve", lambda v, nxt=nxt: v.tensor_copy(PT[:, nxt, :, :], f4(pq[0:64, 0:256])), reads=[pkq], writes=["PTl%d" % nxt])
            pt2, pkt2 = newps()
            for q in range(4):
                P.op("pe", lambda t, q=q, cur=cur, nxt=nxt: t.matmul(pt2[0:64, q * 64:(q + 1) * 64], PT[:, nxt, q, :], Tm[:, cur, q, :], start=True, stop=True), reads=["PTl%d" % nxt, "Tm%d" % cur], writes=[pkt2])
            P.op("dve", lambda v, cur=cur, nxt=nxt: v.tensor_tensor(Tm[:, nxt, :, :], f4(pt2[0:64, 0:256]), Tm[:, cur, :, :], ALU.add), reads=[pkt2, "Tm%d" % cur], writes=["Tm%d" % nxt])
            cur = nxt
        Tf = cur
        vsl = lambda q: ftm["v"][:, b, q // 2, (q % 2) * 64:(q % 2 + 1) * 64]
        pw, pkw = newps()
        for q in range(4):
            P.op("pe", lambda t, q=q: t.matmul(pw[0:64, q * 64:(q + 1) * 64], Mka[:, q, :], vsl(q), start=True, stop=False), reads=["Mkam", ("tv", b)], writes=[pkw])
            P.op("pe", lambda t, q=q: t.matmul(pw[0:64, q * 64:(q + 1) * 64], At[:, q, :], ST[:, q, :], start=False, stop=True), reads=["At", "ST"], writes=[pkw])
        P.op("act", lambda a: a.copy(W1[:], f4(pw[0:64, 0:256])), reads=[pkw], writes=["W1"])
        pu, pku = newps()
        for q in range(4):
            P.op("pe", lambda t, q=q, Tf=Tf: t.matmul(pu[0:64, q * 64:(q + 1) * 64], Tm[:, Tf, q, :], W1[:, q, :], start=True, stop=True), reads=["Tm%d" % Tf, "W1"], writes=[pku])
        P.op("act", lambda a: a.copy(UT[:], f4(pu[0:64, 0:256])), reads=[pku], writes=["UT"])
        psn, pksn = newps()
        for q in range(4):
            P.op("pe", lambda t, q=q: t.matmul(psn[0:64, q * 64:(q + 1) * 64], Ktm[:, q // 2, (q % 2) * 64:(q % 2 + 1) * 64], vsl(q), start=True, stop=False), reads=["Ktm", ("tv", b)], writes=[pksn])
            P.op("pe", lambda t, q=q: t.matmul(psn[0:64, q * 64:(q + 1) * 64], Btm[:, q // 2, (q % 2) * 64:(q % 2 + 1) * 64], UT[:, q, :], start=False, stop=True), reads=["Btm", "UT"], writes=[pksn])
        if min(blk) >= 0:
            py_, pky = newps()
            for q in range(4):
                P.op("pe", lambda t, q=q: t.matmul(py_[0:64, q * 64:(q + 1) * 64], Mkr[:, q, :], vsl(q), start=True, stop=False), reads=["Mkrm", ("tv", b)], writes=[pky])
                P.op("pe", lambda t, q=q: t.matmul(py_[0:64, q * 64:(q + 1) * 64], Rt[:, q, :], ST[:, q, :], start=False, stop=False), reads=["Rt", "ST"], writes=[pky])
                P.op("pe", lambda t, q=q: t.matmul(py_[0:64, q * 64:(q + 1) * 64], Mbr[:, q, :], UT[:, q, :], start=False, stop=True), reads=["Mbrm", "UT"], writes=[pky])
            P.op("act", lambda a, b=b: a.copy(ysb[:, b, :, :], f4(py_[0:64, 0:256])), reads=[pky], writes=[("ysb", b)])
            for z in range(2):
                if blk[z] >= NCTX:
                    P.dma("sp", y_d[z, blk[z] * 64:(blk[z] + 1) * 64, :].rearrange("t (c i) -> t c i", c=2), ysb[:, b, 2 * z:2 * z + 2, :], reads=[("ysb", b)])
        P.op("dve", lambda v: v.tensor_tensor(tmp[:], f4(psn[0:64, 0:256]), ST[:], ALU.add), reads=[pksn, "ST"], writes=["tmp"])
        for q in range(4):
            P.op("dve" if q % 2 else "pool", lambda v, q=q: v.tensor_scalar_mul(ST[:, q, :], tmp[:, q, :], gC[:, q:q + 1]), reads=["tmp", "gC"], writes=["ST"])
    return C.end()


def build_rwc2(C, T=8448, NCTX=4, stop_at=99):
    NB = T // 64
    fm = {n: C.din(n, [128, T]) for n in ("an", "r", "bb", "km")}
    tmd = {n: C.din(n + "_tm", [T, 128]) for n in ("bb", "km", "v", "wf", "wb")}
    U_d = C.din("U", [64, 4, 64]); M_d = C.din("M", [64, 4, 4, 64])
    y_d = C.dout("y", [2, T, 128])
    Uc = C.sb("Uc", [64, 4, 64]); Mc = C.sb("Mc", [64, 4, 4, 64])
    ffm = {n: C.sb("f_" + n, [64, 2, 2, 2, 64]) for n in fm}
    ftm = {n: C.sb("t_" + n, [64, 2, 2, 128]) for n in ("bb", "km", "v", "w")}
    lw = C.sb("lw", [64, 2, 128]); Eg = C.sb("Eg", [64, 2, 2, 64]); Egi = C.sb("Egi", [64, 2, 2, 64]); Egx = C.sb("Egx", [64, 2, 2, 64]); Egt = C.sb("Egt", [64, 2, 128])
    At = C.sb("At", [64, 2, 2, 64], BF16); Rt = C.sb("Rt", [64, 2, 2, 64], BF16); Bt = C.sb("Bt", [64, 2, 2, 64], BF16); Kt = C.sb("Kt", [64, 2, 2, 64], BF16)
    Btm = C.sb("Btm", [64, 2, 128], BF16); Ktm = C.sb("Ktm", [64, 2, 128], BF16); vb = C.sb("vb", [64, 2, 128], BF16); gC = C.sb("gC", [64, 2, 2])
    Pm = C.sb("Pm", [64, 2, 2, 2, 64]); PT = C.sb("PT", [64, 2, 2, 2, 64]); Tm = C.sb("Tm", [64, 2, 2, 2, 64])
    Mka = C.sb("Mka", [64, 2, 2, 64], BF16); Mbr = C.sb("Mbr", [64, 2, 2, 64], BF16); Mkr = C.sb("Mkr", [64, 2, 2, 64], BF16)
    ST = C.sb("ST", [64, 2, 2, 64]); STb = C.sb("STb", [64, 2, 2, 64], BF16); W1 = C.sb("W1", [64, 2, 2, 64]); UT = C.sb("UT", [64, 2, 2, 64], BF16)
    tmp = C.sb("tmp", [64, 2, 2, 64]); ysb = C.sb("ysb", [64, 2, 2, 2, 64])
    pss = [C.ps("ps%d" % i, [128, 512]) for i in range(8)]
    P = C.start()
    pctr = [0]

    def newps():
        i = pctr[0] % 8; pctr[0] += 1
        return pss[i], ("ps", i)
    P.dma("sp", Uc[:], U_d[:, :, :], writes=["Uc"]); P.dma("sp", Mc[:], M_d[:, :, :, :], writes=["Mc"])
    P.op("dve", lambda v: v.memset(ST[:], 0.0), writes=[("ST", 0), ("ST", 1)])
    P.op("dve", lambda v: v.memset(STb[:], 0.0), writes=[("STb", 0), ("STb", 1)])
    order = [list(range(NB)), list(range(NCTX - 1, -1, -1)) + list(range(NB - 1, NCTX - 1, -1))]
    c2 = lambda ap: ap.rearrange("p (c t) -> p c t", c=2)

    def make_steps(i, z):
        b = i % 2
        blk = order[z][i]; s0 = blk * 64
        steps = []
        K = lambda n: (n, z)

        def s_load():
            for n in fm:
                P.dma("sp", ffm[n][:, b, z, :, :], fm[n][:, s0:s0 + 64].rearrange("(c j) t -> j c t", c=2), writes=[("f" + n, b, z)])
            for n, src in (("bb", "bb"), ("km", "km"), ("v", "v"), ("w", "wf" if z == 0 else "wb")):
                P.dma("sp", ftm[n][:, b, z, :], tmd[src][s0:s0 + 64, :], writes=[("t" + n, b, z)])
        steps.append(s_load)

        def s_ln():
            P.op("act", lambda a: a.activation(lw[:, z, :], ftm["w"][:, b, z, :], AF.Ln), reads=[("tw", b, z)], writes=[K("lw")])
            P.op("dve", lambda v: v.tensor_copy(vb[:, z, :], ftm["v"][:, b, z, :]), reads=[("tv", b, z)], writes=[K("vb")])
        steps.append(s_ln)
        st = {}

        def s_cum():
            pg, pkg = newps(); px, pkx = newps(); pt_, pkt = newps()
            st.update(pg=pg, pkg=pkg, px=px, pkx=pkx, pt=pt_, pkt=pkt)
            for c in range(2):
                P.op("pe", lambda t, c=c: t.matmul(pg[0:64, c * 64:(c + 1) * 64], lw[:, z, c * 64:(c + 1) * 64], Uc[:, 2 * z, :], start=True, stop=True), reads=[K("lw"), "Uc"], writes=[pkg])
                P.op("pe", lambda t, c=c: t.matmul(px[0:64, c * 64:(c + 1) * 64], lw[:, z, c * 64:(c + 1) * 64], Uc[:, 2 * z + 1, :], start=True, stop=True), reads=[K("lw"), "Uc"], writes=[pkx])
            P.op("pe", lambda t: t.matmul(pt_[0:64, 0:128], Uc[:, 2 * z, :], lw[:, z, :], start=True, stop=True), reads=[K("lw"), "Uc"], writes=[pkt])
        steps.append(s_cum)

        def s_exp():
            pg, pkg, px, pkx, pt_, pkt = st["pg"], st["pkg"], st["px"], st["pkx"], st["pt"], st["pkt"]
            P.op("act", lambda a: a.activation(Eg[:, z, :, :], c2(pg[0:64, 0:128]), AF.Exp), reads=[pkg], writes=[K("Eg")])
            P.op("act", lambda a: a.activation(Egi[:, z, :, :], c2(pg[0:64, 0:128]), AF.Exp, scale=-1.0), reads=[pkg], writes=[K("Egi")])
            P.op("act", lambda a: a.activation(Egx[:, z, :, :], c2(px[0:64, 0:128]), AF.Exp), reads=[pkx], writes=[K("Egx")])
            P.op("act", lambda a: a.activation(Egt[:, z, :], pt_[0:64, 0:128], AF.Exp, scale=-1.0), reads=[pkt], writes=[K("Egt")])
        steps.append(s_exp)

        def s_prod():
            P.op("dve", lambda v: v.tensor_tensor(At[:, z, :, :], ffm["an"][:, b, z, :, :], Egx[:, z, :, :], ALU.mult), reads=[("fan", b, z), K("Egx")], writes=[K("At")])
            P.op("dve", lambda v: v.tensor_tensor(Rt[:, z, :, :], ffm["r"][:, b, z, :, :], Eg[:, z, :, :], ALU.mult), reads=[("fr", b, z), K("Eg")], writes=[K("Rt")])
            P.op("dve", lambda v: v.tensor_tensor(Bt[:, z, :, :], ffm["bb"][:, b, z, :, :], Egi[:, z, :, :], ALU.mult), reads=[("fbb", b, z), K("Egi")], writes=[K("Bt")])
            P.op("dve", lambda v: v.tensor_tensor(Kt[:, z, :, :], ffm["km"][:, b, z, :, :], Egi[:, z, :, :], ALU.mult), reads=[("fkm", b, z), K("Egi")], writes=[K("Kt")])
            P.op("dve", lambda v: v.tensor_tensor(Btm[:, z, :], ftm["bb"][:, b, z, :], Egt[:, z, :], ALU.mult), reads=[("tbb", b, z), K("Egt")], writes=[K("Btm")])
            P.op("dve", lambda v: v.tensor_tensor(Ktm[:, z, :], ftm["km"][:, b, z, :], Egt[:, z, :], ALU.mult), reads=[("tkm", b, z), K("Egt")], writes=[K("Ktm")])
            col = 63 if z == 0 else 0
            P.op("dve", lambda v: v.tensor_copy(gC[:, z, :], Eg[:, z, :, col]), reads=[K("Eg")], writes=[K("gC")])
        steps.append(s_prod)

        def s_gram():
            pa, pka = newps(); pb, pkb = newps()
            for gi, (L, Rr) in enumerate(((Bt, At), (At, Bt), (Kt, At))):
                for c in range(2):
                    P.op("pe", lambda t, gi=gi, c=c, L=L, Rr=Rr: t.matmul(pa[0:64, gi * 128 + c * 64:gi * 128 + (c + 1) * 64], L[:, z, c, :], Rr[:, z, c, :], start=True, stop=True), reads=[K("At"), K("Bt"), K("Kt")], writes=[pka])
            for gi, (L, Rr) in enumerate(((Bt, Rt), (Kt, Rt))):
                for c in range(2):
                    P.op("pe", lambda t, gi=gi, c=c, L=L, Rr=Rr: t.matmul(pb[0:64, gi * 128 + c * 64:gi * 128 + (c + 1) * 64], L[:, z, c, :], Rr[:, z, c, :], start=True, stop=True), reads=[K("Rt"), K("Bt"), K("Kt")], writes=[pkb])
            mq = slice(2 * z, 2 * z + 2)
            P.op("dve", lambda v: v.tensor_tensor(Pm[:, 0, z, :, :], c2(pa[0:64, 0:128]), Mc[:, 0, mq, :], ALU.mult), reads=[pka, "Mc"], writes=[K("P0")])
            P.op("dve", lambda v: v.tensor_tensor(PT[:, 0, z, :, :], c2(pa[0:64, 128:256]), Mc[:, 1, mq, :], ALU.mult), reads=[pka, "Mc"], writes=[K("PT0")])
            P.op("dve", lambda v: v.tensor_tensor(Mka[:, z, :, :], c2(pa[0:64, 256:384]), Mc[:, 0, mq, :], ALU.mult), reads=[pka, "Mc"], writes=[K("Mka")])
            P.op("dve", lambda v: v.tensor_tensor(Mbr[:, z, :, :], c2(pb[0:64, 0:128]), Mc[:, 2, mq, :], ALU.mult), reads=[pkb, "Mc"], writes=[K("Mbr")])
            P.op("dve", lambda v: v.tensor_tensor(Mkr[:, z, :, :], c2(pb[0:64, 128:256]), Mc[:, 2, mq, :], ALU.mult), reads=[pkb, "Mc"], writes=[K("Mkr")])
            P.op("dve", lambda v: v.tensor_tensor(Tm[:, 0, z, :, :], Pm[:, 0, z, :, :], Mc[:, 3, mq, :], ALU.add), reads=[K("P0"), "Mc"], writes=[K("T0")])
        steps.append(s_gram)
        for lev in range(5):
            cur, nxt = lev % 2, 1 - lev % 2

            def s_sq(cur=cur, nxt=nxt):
                pp, pkp = newps(); pq, pkq = newps()
                for c in range(2):
                    P.op("pe", lambda t, c=c: t.matmul(pp[0:64, c * 64:(c + 1) * 64], PT[:, cur, z, c, :], Pm[:, cur, z, c, :], start=True, stop=True), reads=[K("P%d" % cur), K("PT%d" % cur)], writes=[pkp])
                    P.op("pe", lambda t, c=c: t.matmul(pq[0:64, c * 64:(c + 1) * 64], Pm[:, cur, z, c, :], PT[:, cur, z, c, :], start=True, stop=True), reads=[K("P%d" % cur), K("PT%d" % cur)], writes=[pkq])
                P.op("act", lambda a: a.copy(Pm[:, nxt, z, :, :], c2(pp[0:64, 0:128])), reads=[pkp], writes=[K("P%d" % nxt)])
                P.op("dve", lambda v: v.tensor_copy(PT[:, nxt, z, :, :], c2(pq[0:64, 0:128])), reads=[pkq], writes=[K("PT%d" % nxt)])
            steps.append(s_sq)

            def s_t(cur=cur, nxt=nxt):
                pt2, pkt2 = newps()
                for c in range(2):
                    P.op("pe", lambda t, c=c: t.matmul(pt2[0:64, c * 64:(c + 1) * 64], PT[:, nxt, z, c, :], Tm[:, cur, z, c, :], start=True, stop=True), reads=[K("PT%d" % nxt), K("T%d" % cur)], writes=[pkt2])
                P.op("dve", lambda v: v.tensor_tensor(Tm[:, nxt, z, :, :], c2(pt2[0:64, 0:128]), Tm[:, cur, z, :, :], ALU.add), reads=[pkt2, K("T%d" % cur)], writes=[K("T%d" % nxt)])
            steps.append(s_t)
        Tf = 1

        def s_w1():
            pw, pkw = newps(); st.update(pw=pw, pkw=pkw)
            for c in range(2):
                P.op("pe", lambda t, c=c: t.matmul(pw[0:64, c * 64:(c + 1) * 64], Mka[:, z, c, :], vb[:, z, c * 64:(c + 1) * 64], start=True, stop=False), reads=[K("Mka"), K("vb")], writes=[pkw])
                P.op("pe", lambda t, c=c: t.matmul(pw[0:64, c * 64:(c + 1) * 64], At[:, z, c, :], STb[:, z, c, :], start=False, stop=True), reads=[K("At"), K("STb")], writes=[pkw])
            P.op("act", lambda a: a.copy(W1[:, z, :, :], c2(pw[0:64, 0:128])), reads=[pkw], writes=[K("W1")])
        steps.append(s_w1)

        def s_ut():
            pu, pku = newps()
            for c in range(2):
                P.op("pe", lambda t, c=c: t.matmul(pu[0:64, c * 64:(c + 1) * 64], Tm[:, Tf, z, c, :], W1[:, z, c, :], start=True, stop=True), reads=[K("T%d" % Tf), K("W1")], writes=[pku])
            P.op("act", lambda a: a.copy(UT[:, z, :, :], c2(pu[0:64, 0:128])), reads=[pku], writes=[K("UT")])
        steps.append(s_ut)

        def s_fin():
            psn, pksn = newps()
            for c in range(2):
                P.op("pe", lambda t, c=c: t.matmul(psn[0:64, c * 64:(c + 1) * 64], Ktm[:, z, c * 64:(c + 1) * 64], vb[:, z, c * 64:(c + 1) * 64], start=True, stop=False), reads=[K("Ktm"), K("vb")], writes=[pksn])
                P.op("pe", lambda t, c=c: t.matmul(psn[0:64, c * 64:(c + 1) * 64], Btm[:, z, c * 64:(c + 1) * 64], UT[:, z, c, :], start=False, stop=True), reads=[K("Btm"), K("UT")], writes=[pksn])
            if blk >= NCTX:
                py_, pky = newps()
                for c in range(2):
                    P.op("pe", lambda t, c=c: t.matmul(py_[0:64, c * 64:(c + 1) * 64], Mkr[:, z, c, :], vb[:, z, c * 64:(c + 1) * 64], start=True, stop=False), reads=[K("Mkr"), K("vb")], writes=[pky])
                    P.op("pe", lambda t, c=c: t.matmul(py_[0:64, c * 64:(c + 1) * 64], Rt[:, z, c, :], STb[:, z, c, :], start=False, stop=False), reads=[K("Rt"), K("STb")], writes=[pky])
                    P.op("pe", lambda t, c=c: t.matmul(py_[0:64, c * 64:(c + 1) * 64], Mbr[:, z, c, :], UT[:, z, c, :], start=False, stop=True), reads=[K("Mbr"), K("UT")], writes=[pky])
                P.op("act", lambda a: a.copy(ysb[:, b, z, :, :], c2(py_[0:64, 0:128])), reads=[pky], writes=[("ysb", b, z)])
                P.dma("sp", y_d[z, s0:s0 + 64, :].rearrange("t (c i) -> t c i", c=2), ysb[:, b, z, :, :], reads=[("ysb", b, z)])
            P.op("dve", lambda v: v.tensor_tensor(tmp[:, z, :, :], c2(psn[0:64, 0:128]), ST[:, z, :, :], ALU.add), reads=[pksn, K("ST")], writes=[K("tmp")])
            for c in range(2):
                P.op("dve", lambda v, c=c: v.tensor_scalar_mul(ST[:, z, c, :], tmp[:, z, c, :], gC[:, z, c:c + 1]), reads=[K("tmp"), K("gC")], writes=[K("ST")])
            P.op("act", lambda a: a.copy(STb[:, z, :, :], ST[:, z, :, :]), reads=[K("ST")], writes=[K("STb")])
        steps.append(s_fin)
        return steps

    for i in range(NB):
        sa, sb_ = make_steps(i, 0), make_steps(i, 1)
        for k in range(min(len(sa), stop_at)):
            sa[k](); sb_[k]()
    return C.end()


def build_tr_fm(C, names, T):
    srcs = {n: C.din("src_" + n, [128, T]) for n in names}
    dsts = {n: C.dout("dst_" + n, [T, 128]) for n in names}
    idn = C.din("idn", [128, 128])
    st = C.sb("st", [128, 3, 512]); ot = C.sb("ot", [128, 3, 4, 128]); idt = C.sb("idt", [128, 128])
    pss = [C.ps("ps%d" % i, [128, 512]) for i in range(4)]
    P = C.start()
    P.dma("sp", idt[:], idn[:, :], writes=["idt"])
    it = 0
    for n in names:
        for c0 in range(0, T, 512):
            nr = min(4, (T - c0) // 128)
            b = it % 3; pi = it % 4; it += 1
            P.dma("sp", st[:, b, 0:nr * 128], srcs[n][:, c0:c0 + nr * 128], writes=[("st", b)])
            for k in range(nr):
                P.op("pe", lambda t, b=b, k=k, pi=pi: t.matmul(pss[pi][:, k * 128:(k + 1) * 128], st[:, b, k * 128:(k + 1) * 128], idt[:, :], start=True, stop=True), reads=[("st", b), "idt"], writes=[("ps", pi)])
            P.op("act" if it % 2 else "dve", (lambda a, b=b, pi=pi, nr=nr: a.copy(ot[:, b, 0:nr, :], pss[pi][:, 0:nr * 128].rearrange("p (n c) -> p n c", c=128))) if it % 2 else
                 (lambda v, b=b, pi=pi, nr=nr: v.tensor_copy(ot[:, b, 0:nr, :], pss[pi][:, 0:nr * 128].rearrange("p (n c) -> p n c", c=128))), reads=[("ps", pi)], writes=[("ot", b)])
            P.dma("sp", dsts[n][c0:c0 + nr * 128, :].rearrange("(n p) c -> p n c", p=128), ot[:, b, 0:nr, :], reads=[("ot", b)])
    return C.end()
```

```python
import numpy as np
from contextlib import ExitStack
import concourse.bass as bass
import concourse.mybir as mybir
from concourse.bass_utils import run_bass_kernel_spmd

F32 = mybir.dt.float32
BF16 = mybir.dt.bfloat16
AF = mybir.ActivationFunctionType
ALU = mybir.AluOpType
AX = mybir.AxisListType

D = 1024
NCORES = 8
ALPHA = 4.0 ** 0.25
LN_EPS = 1e-5


class Prog:
    def __init__(self, nc, n_dma_sems=32, same_engine_sync=("act", "pool", "dve")):
        self.nc = nc
        self.eng = {"pe": nc.tensor, "dve": nc.vector, "act": nc.scalar, "pool": nc.gpsimd, "sp": nc.sync}
        self.sem, self.cnt, self.stack = {}, {}, ExitStack()
        for e in self.eng:
            self.sem[e] = self.stack.enter_context(nc.semaphore("c_" + e))
            self.cnt[e] = 0
        self.dsem = [self.stack.enter_context(nc.semaphore("d_%d" % i)) for i in range(n_dma_sems)]
        self.dcnt = [0] * n_dma_sems
        self.dnext = 0
        self.waited = {e: {} for e in self.eng}
        self.last_w, self.readers = {}, {}
        self.same = set(same_engine_sync)
        self.rr = 0
        self.csem = self.stack.enter_context(nc.semaphore("c_coll"))
        self.ccnt = 0

    def _semobj(self, k):
        return self.sem[k] if isinstance(k, str) else self.dsem[k]

    def _wait(self, e, k, v):
        if self.waited[e].get(k, 0) >= v:
            return
        self.eng[e].wait_ge(self._semobj(k), v)
        self.waited[e][k] = v

    def _deps(self, e, reads, writes):
        deps = {}

        def add(kv):
            if kv is not None and deps.get(kv[0], 0) < kv[1]:
                deps[kv[0]] = kv[1]
        for b in reads:
            add(self.last_w.get(b))
        for b in writes:
            add(self.last_w.get(b))
            for kv in self.readers.get(b, {}).items():
                add(kv)
        for k, v in deps.items():
            if k == e and e not in self.same:
                continue
            self._wait(e, k, v)

    def _record(self, mark, reads, writes):
        for b in reads:
            r = self.readers.setdefault(b, {})
            if r.get(mark[0], 0) < mark[1]:
                r[mark[0]] = mark[1]
        for b in writes:
            self.last_w[b] = mark
            self.readers[b] = {}

    def op(self, e, fn, reads=(), writes=()):
        self._deps(e, reads, writes)
        ins = fn(self.eng[e])
        self.cnt[e] += 1
        ins.then_inc(self.sem[e], 1)
        self._record((e, self.cnt[e]), reads, writes)

    def dma(self, e, out, in_, reads=(), writes=(), **kw):
        i = self.dnext
        self.dnext = (self.dnext + 1) % len(self.dsem)
        if self.dcnt[i] > 0:
            self._wait(e, i, self.dcnt[i])
        self._deps(e, reads, writes)
        ins = self.eng[e].dma_start(out=out, in_=in_, **kw)
        self.dcnt[i] += 16
        ins.then_inc(self.dsem[i], 16)
        self._record((i, self.dcnt[i]), reads, writes)

    def finish(self, e="sp"):
        self.coll_wait()
        for i in range(len(self.dsem)):
            if self.dcnt[i] > 0:
                self._wait(e, i, self.dcnt[i])
        for k in self.eng:
            if k != e and self.cnt[k] > 0:
                self._wait(e, k, self.cnt[k])

    def barrier(self):
        for e in self.eng:
            for i in range(len(self.dsem)):
                if self.dcnt[i] > 0:
                    self._wait(e, i, self.dcnt[i])
            for k in self.eng:
                if k != e and self.cnt[k] > 0:
                    self._wait(e, k, self.cnt[k])

    def coll_issue(self, kind, op, groups, in_ap, out_ap, barrier=True):
        if barrier:
            self.barrier()
        ins = self.nc.gpsimd.collective_compute(kind, op, replica_groups=groups, ins=[in_ap], outs=[out_ap])
        self.ccnt += 1
        ins.then_inc(self.csem)

    def coll_wait(self):
        for e in self.eng:
            if self.ccnt > 0 and self.waited[e].get("cc", 0) < self.ccnt:
                self.eng[e].wait_ge(self.csem, self.ccnt)
                self.waited[e]["cc"] = self.ccnt

    def coll(self, kind, op, groups, in_ap, out_ap):
        self.coll_issue(kind, op, groups, in_ap, out_ap)
        self.coll_wait()

    def close(self):
        self.stack.close()

    def alt(self, engs=("dve", "pool")):
        self.rr += 1
        return engs[self.rr % len(engs)]


class Ctx:
    def __init__(self, shared=False):
        self.nc = bass.Bass("TRN2", target_bir_lowering=False)
        self.es = ExitStack()
        self.P = None
        self.shared = shared
        self.bind = {}
        self.pref = ""
        self.ext_in = []

    def _io(self, name, shape, dt, kind):
        if name in self.bind:
            ap = self.bind[name]
            assert [int(x) for x in ap.shape] == [int(x) for x in shape], (name, ap.shape, shape)
            return ap
        full = self.pref + name
        if kind == "ExternalInput":
            self.ext_in.append(full)
        return self.nc.dram_tensor(full, list(shape), dt, kind=kind).ap()

    def din(self, name, shape, dt=F32):
        return self._io(name, shape, dt, "ExternalInput")

    def dout(self, name, shape, dt=F32):
        return self._io(name, shape, dt, "ExternalOutput")

    def scratch(self, name, shape, dt=F32):
        return self.nc.dram_tensor(name, list(shape), dt).ap()

    def sb(self, name, shape, dt=F32):
        return self.es.enter_context(self.nc.sbuf_tensor("s_" + self.pref + name, list(shape), dt))

    def ps(self, name, shape, dt=F32):
        return self.es.enter_context(self.nc.psum_tensor("p_" + self.pref + name, list(shape), dt))

    def start(self, **kw):
        if self.P is None:
            self.P = Prog(self.nc, **kw)
        return self.P

    def stage(self, pref, bind):
        self.pref = pref
        self.bind = dict(bind)

    def end(self):
        if self.shared:
            self.P.barrier()
            self.es.close()
            self.es = ExitStack()
            self.bind = {}
            return None
        self.P.finish("sp")
        self.P.close()
        self.es.close()
        return self.nc

    def finalize(self):
        self.P.finish("sp")
        self.P.close()
        return self.nc


def run_spmd(nc, in_maps):
    res = run_bass_kernel_spmd(nc, in_maps, core_ids=list(range(len(in_maps))))
    return res.results


def load_w_bf16(C, P, wdst, wkey, wsrc, K, ncols, stage, skey, col0=0, engs=("dve", "pool", "act")):
    KC = (K + 127) // 128
    for kc in range(KC):
        kp = min(128, K - kc * 128)
        b = kc % 2
        P.dma("sp", stage[0:kp, b, 0:ncols], wsrc[kc * 128:kc * 128 + kp, col0:col0 + ncols], writes=[(skey, b)])
        e = engs[kc % len(engs)]
        if e == "act":
            P.op("act", lambda a, kp=kp, b=b, kc=kc: a.copy(wdst[0:kp, kc, 0:ncols], stage[0:kp, b, 0:ncols]),
                 reads=[(skey, b)], writes=[(wkey, kc)])
        else:
            P.op(e, lambda v, kp=kp, b=b, kc=kc: v.tensor_copy(wdst[0:kp, kc, 0:ncols], stage[0:kp, b, 0:ncols]),
                 reads=[(skey, b)], writes=[(wkey, kc)])


def build_k0(ncol, C=None):
    C = C or Ctx()
    cT = C.din("cT", [128, 8, 3])
    w = C.din("w", [2, 1024, ncol])
    bia = C.din("b", [3, 2, ncol])
    out = C.dout("out", [3, 2, ncol])
    cs = C.sb("cs", [128, 8, 3])
    cs2 = C.sb("cs2", [128, 8, 3])
    wt = C.sb("wt", [128, 2, 8, ncol])
    bt = C.sb("bt", [3, 2, ncol])
    ot = C.sb("ot", [3, 2, ncol])
    pss = [C.ps("ps%d" % i, [128, 512]) for i in range(2)]
    P = C.start()
    P.dma("sp", cs[:], cT[:, :, :], writes=["cs"])
    P.dma("sp", bt[:], bia[:, :, :], writes=["bt"])
    for l in range(2):
        P.dma("sp", wt[:, l, :, :], w[l].rearrange("(kc p) n -> p kc n", p=128), writes=[("wt", l)])
    P.op("act", lambda a: a.activation(cs2[:], cs[:], AF.Silu), reads=["cs"], writes=["cs2"])
    i = 0
    for l in range(2):
        for n0 in range(0, ncol, 512):
            n1 = min(ncol, n0 + 512)
            ps = pss[i % 2]; pk = "ps%d" % (i % 2); i += 1
            for kc in range(8):
                P.op("pe", lambda t, kc=kc, l=l, n0=n0, n1=n1, ps=ps: t.matmul(ps[0:3, 0:n1 - n0], cs2[:, kc, :], wt[:, l, kc, n0:n1], start=(kc == 0), stop=(kc == 7)),
                     reads=["cs2", ("wt", l)], writes=[pk])
            P.op("dve", lambda v, l=l, n0=n0, n1=n1, ps=ps: v.tensor_tensor(ot[:, l, n0:n1], ps[0:3, 0:n1 - n0], bt[:, l, n0:n1], ALU.add),
                 reads=[pk, "bt"], writes=["ot"])
    P.dma("sp", out[:, :, :], ot[:], reads=["ot"])
    return C.end()


def stage_k0(c, c_ctx, ada_w, ada_b):
    ncol = 6144 // NCORES
    nc = build_k0(ncol)
    cc = np.stack([c[0], c[1], c_ctx], axis=1).astype(np.float32)
    cT = np.ascontiguousarray(cc.reshape(8, 128, 3).transpose(1, 0, 2))
    maps = []
    for i in range(NCORES):
        sl = slice(i * ncol, (i + 1) * ncol)
        maps.append({"cT": cT, "w": np.ascontiguousarray(ada_w[:, :, sl]),
                     "b": np.ascontiguousarray(np.broadcast_to(ada_b[None, :, sl], (3, 2, ncol)))})
    res = run_spmd(nc, maps)
    m = np.concatenate([r["out"] for r in res], axis=2)
    return np.ascontiguousarray(m.transpose(1, 0, 2))


def mod_layout(mrow):
    return np.ascontiguousarray(mrow.reshape(6, 8, 128).transpose(2, 0, 1))


def build_k1(ncols, segs, C=None):
    C = C or Ctx()
    W = C.din("w", [1024, ncols])
    xs, mods, outs = [], [], []
    for si, (ntok, tile) in enumerate(segs):
        xs.append(C.din("x%d" % si, [1024, ntok]))
        mods.append(C.din("mod%d" % si, [128, 6, 8]))
        outs.append(C.dout("p%d" % si, [ncols, ntok]))
    TM = max(t for _, t in segs)
    NCH = (ncols + 127) // 128
    wb = C.sb("wb", [128, 8, ncols], BF16)
    stage = C.sb("stage", [128, 2, ncols])
    xt = C.sb("xt", [128, 2, 8, TM])
    ub = C.sb("ub", [128, 2, 8, TM], BF16)
    ost = C.sb("ost", [128, 4, TM])
    mt = C.sb("mt", [128, len(segs), 6, 8])
    sc1 = C.sb("sc1", [128, len(segs), 8])
    pss = [C.ps("ps%d" % i, [128, 512]) for i in range(4)]
    P = C.start()
    for si in range(len(segs)):
        P.dma("sp", mt[:, si, :, :], mods[si][:, :, :], writes=[("mt", si)])
        P.op("dve", lambda v, si=si: v.tensor_scalar_add(sc1[:, si, :], mt[:, si, 1, :], 1.0), reads=[("mt", si)], writes=[("sc1", si)])
    load_w_bf16(C, P, wb, "wb", W, 1024, ncols, stage, "stage")
    it = 0
    oi = 0
    for si, (ntok, tile) in enumerate(segs):
        xv = xs[si].rearrange("(kc p) t -> p kc t", p=128)
        for t0 in range(0, ntok, tile):
            b = it % 2; it += 1
            P.dma("sp", xt[:, b, :, 0:tile], xv[:, :, t0:t0 + tile], writes=[("xt", b)])
            for kc in range(8):
                e = P.alt()
                P.op(e, lambda v, b=b, kc=kc, si=si, tile=tile: v.tensor_scalar(ub[:, b, kc, 0:tile], xt[:, b, kc, 0:tile], sc1[:, si, kc:kc + 1], mt[:, si, 0, kc:kc + 1], ALU.mult, ALU.add),
                     reads=[("xt", b), ("sc1", si), ("mt", si)], writes=[("ub", b, kc)])
            for ch in range(NCH):
                c0 = ch * 128; cn = min(128, ncols - c0)
                pi = oi % 4; ob = oi % 4; oi += 1
                ps = pss[pi]
                for kc in range(8):
                    P.op("pe", lambda t, ps=ps, cn=cn, c0=c0, kc=kc, b=b, tile=tile: t.matmul(ps[0:cn, 0:tile], wb[:, kc, c0:c0 + cn], ub[:, b, kc, 0:tile], start=(kc == 0), stop=(kc == 7)),
                         reads=[("wb", kc), ("ub", b, kc)], writes=[("ps", pi)])
                if oi % 2:
                    P.op("act", lambda a, ps=ps, cn=cn, ob=ob, tile=tile: a.copy(ost[0:cn, ob, 0:tile], ps[0:cn, 0:tile]), reads=[("ps", pi)], writes=[("ost", ob)])
                else:
                    P.op("dve", lambda v, ps=ps, cn=cn, ob=ob, tile=tile: v.tensor_copy(ost[0:cn, ob, 0:tile], ps[0:cn, 0:tile]), reads=[("ps", pi)], writes=[("ost", ob)])
                P.dma("sp", outs[si][c0:c0 + cn, t0:t0 + tile], ost[0:cn, ob, 0:tile], reads=[("ost", ob)])
    return C.end()


def stage_k1(x, ctx, m_l, w_in):
    ncols = w_in.shape[1]
    nc = build_k1(ncols, [(2048, 512), (64, 64)])
    maps = []
    for c in range(NCORES):
        b, s = c // 4, c % 4
        maps.append({"w": np.ascontiguousarray(w_in),
                     "x0": np.ascontiguousarray(x[b, s * 2048:(s + 1) * 2048].T), "mod0": mod_layout(m_l[b]),
                     "x1": np.ascontiguousarray(ctx[b, s * 64:(s + 1) * 64].T), "mod1": mod_layout(m_l[2])})
    res = run_spmd(nc, maps)
    pl = np.empty((2, 8192, ncols), np.float32)
    pc = np.empty((2, 256, ncols), np.float32)
    for c in range(NCORES):
        b, s = c // 4, c % 4
        pl[b, s * 2048:(s + 1) * 2048] = res[c]["p0"].T
        pc[b, s * 64:(s + 1) * 64] = res[c]["p1"].T
    return pl, pc


def build_gla(T, stop=99, C=None):
    NT = T // 128
    BLK = 6
    assert NT % BLK == 0
    C = C or Ctx()
    qT = C.din("qT", [2, 64, T]); kT = C.din("kT", [2, 64, T])
    kk = C.din("k", [2, T, 64]); vv = C.din("v", [2, T, 128])
    a1T = C.din("a1T", [2, 16, T]); a2 = C.din("a2", [2, 16, 64]); ab = C.din("ab", [2, 1, 64])
    tri_d = C.din("tri", [128, 128]); mask_d = C.din("mask", [128, 128])
    o_d = C.dout("o", [2, T, 128])
    W = BLK * 128
    qs = C.sb("qs", [64, 2, 2, W]); ks = C.sb("ks", [64, 2, 2, W])
    kt = C.sb("kt", [128, 2, 2, BLK, 64]); vt = C.sb("vt", [128, 2, 2, BLK, 128])
    vb = C.sb("vb", [128, 2, 2, 128], BF16)
    a1s = C.sb("a1s", [16, 2, 2, W]); a2s = C.sb("a2s", [16, 2, 64]); abs_ = C.sb("abs", [1, 2, 64])
    ones = C.sb("ones", [1, 128])
    tri = C.sb("tri", [128, 128]); mask = C.sb("mask", [128, 128])
    ez = C.sb("ez", [128, 2, 2, 64]); sp = C.sb("sp", [128, 2, 2, 64])
    ecT = C.sb("ecT", [64, 2, 2, 128]); encT = C.sb("encT", [64, 2, 2, 128]); enc = C.sb("enc", [128, 2, 2, 64])
    el = C.sb("el", [64, 2, 2, 2])
    qa = C.sb("qa", [64, 2, 2, 128], BF16); qb_ = C.sb("qb", [64, 2, 2, 128], BF16)
    kTb = C.sb("kTb", [64, 2, 2, 128], BF16); kb = C.sb("kb", [128, 2, 2, 64], BF16)
    att = C.sb("att", [128, 2, 2, 128], BF16)
    S = C.sb("S", [64, 2, 128]); Sb = C.sb("Sb", [64, 2, 3, 128], BF16); tmp = C.sb("tmp", [64, 2, 128])
    osb = C.sb("osb", [128, 2, 2, 128])
    NPS = 8
    pss = [C.ps("ps%d" % i, [128, 512]) for i in range(NPS)]
    P = C.start()
    pctr = [0]

    def newps():
        i = pctr[0] % NPS; pctr[0] += 1
        return pss[i], ("ps", i)

    P.dma("sp", tri[:], tri_d[:, :], writes=["tri"]); P.dma("sp", mask[:], mask_d[:, :], writes=["mask"])
    P.dma("sp", a2s[:], a2.rearrange("z r d -> r z d"), writes=["a2s"])
    P.dma("sp", abs_[:], ab.rearrange("z o d -> o z d"), writes=["abs"])
    P.op("dve", lambda v: v.memset(ones[:], 1.0), writes=["ones"])
    P.op("dve", lambda v: v.memset(S[:], 0.0), writes=[("S", 0), ("S", 1)])
    P.op("pool", lambda v: v.memset(Sb[:], 0.0), writes=[("Sb", 0, 0), ("Sb", 1, 0), ("Sb", 0, 1), ("Sb", 1, 1), ("Sb", 0, 2), ("Sb", 1, 2)])
    P.op("pool", lambda v: v.memset(qa[:], 0.0), writes=[("qa", z, p) for z in range(2) for p in range(2)])
    P.op("pool", lambda v: v.memset(qb_[:], 0.0), writes=[("qb", z, p) for z in range(2) for p in range(2)])
    sbi = [0, 0]
    for blk in range(NT // BLK):
        bb = blk % 2
        t0 = blk * W
        for z in range(2):
            P.dma("sp", qs[:, z, bb, :], qT[z, :, t0:t0 + W], writes=[("qs", z, bb)])
            P.dma("sp", ks[:, z, bb, :], kT[z, :, t0:t0 + W], writes=[("ks", z, bb)])
            P.dma("sp", kt[:, z, bb, :, :], kk[z, t0:t0 + W, :].rearrange("(n p) d -> p n d", p=128), writes=[("kt", z, bb)])
            P.dma("sp", vt[:, z, bb, :, :], vv[z, t0:t0 + W, :].rearrange("(n p) d -> p n d", p=128), writes=[("vt", z, bb)])
            P.dma("sp", a1s[:, z, bb, :], a1T[z, :, t0:t0 + W], writes=[("a1s", z, bb)])
        for ti in range(BLK):
            tg = blk * BLK + ti
            p = tg % 2
            c0 = ti * 128
            for z in range(2):
                ps, pk = newps()
                P.op("pe", lambda t, ps=ps, z=z, bb=bb, c0=c0: t.matmul(ps[:, 0:64], a1s[:, z, bb, c0:c0 + 128], a2s[:, z, :], start=True, stop=False),
                     reads=[("a1s", z, bb), "a2s"], writes=[pk])
                P.op("pe", lambda t, ps=ps, z=z: t.matmul(ps[:, 0:64], ones[:, :], abs_[:, z, :], start=False, stop=True),
                     reads=["ones", "abs"], writes=[pk])
                P.op("act", lambda a, ps=ps, z=z, p=p: a.activation(ez[:, z, p, :], ps[:, 0:64], AF.Exp, scale=-1.0), reads=[pk], writes=[("ez", z, p)])
                P.op("act", lambda a, z=z, p=p: a.activation(sp[:, z, p, :], ez[:, z, p, :], AF.Ln, bias=1.0), reads=[("ez", z, p)], writes=[("sp", z, p)])
                if stop <= 1:
                    continue
                ps1, pk1 = newps()
                P.op("pe", lambda t, ps1=ps1, z=z, p=p: t.matmul(ps1[0:64, 0:128], sp[:, z, p, :], tri[:, :], start=True, stop=True),
                     reads=[("sp", z, p), "tri"], writes=[pk1])
                ps2, pk2 = newps()
                P.op("pe", lambda t, ps2=ps2, z=z, p=p: t.matmul(ps2[:, 0:64], tri[:, :], sp[:, z, p, :], start=True, stop=True),
                     reads=[("sp", z, p), "tri"], writes=[pk2])
                P.op("act", lambda a, ps1=ps1, z=z, p=p: a.activation(ecT[:, z, p, :], ps1[0:64, 0:128], AF.Exp), reads=[pk1], writes=[("ecT", z, p)])
                P.op("act", lambda a, ps1=ps1, z=z, p=p: a.activation(encT[:, z, p, :], ps1[0:64, 0:128], AF.Exp, scale=-1.0), reads=[pk1], writes=[("encT", z, p)])
                P.op("act", lambda a, ps2=ps2, z=z, p=p: a.activation(enc[:, z, p, :], ps2[:, 0:64], AF.Exp, scale=-1.0), reads=[pk2], writes=[("enc", z, p)])
                P.op("dve", lambda v, z=z, p=p: v.tensor_copy(el[:, z, p, :], ecT[:, z, p, 63:128:64]), reads=[("ecT", z, p)], writes=[("el", z, p)])
                if stop <= 2:
                    continue
                P.op("dve", lambda v, z=z, p=p, bb=bb, c0=c0: v.scalar_tensor_tensor(qa[:, z, p, 0:64], qs[:, z, bb, c0:c0 + 64], 0.125, ecT[:, z, p, 0:64], ALU.mult, ALU.mult),
                     reads=[("qs", z, bb), ("ecT", z, p)], writes=[("qa", z, p)])
                P.op("dve", lambda v, z=z, p=p, bb=bb, c0=c0: v.scalar_tensor_tensor(qb_[:, z, p, 64:128], qs[:, z, bb, c0 + 64:c0 + 128], 0.125, ecT[:, z, p, 64:128], ALU.mult, ALU.mult),
                     reads=[("qs", z, bb), ("ecT", z, p)], writes=[("qb", z, p)])
                P.op("dve", lambda v, z=z, p=p, bb=bb, c0=c0: v.tensor_tensor(kTb[:, z, p, :], ks[:, z, bb, c0:c0 + 128], encT[:, z, p, :], ALU.mult),
                     reads=[("ks", z, bb), ("encT", z, p)], writes=[("kTb", z, p)])
                P.op("dve", lambda v, z=z, p=p, bb=bb, ti=ti: v.tensor_tensor(kb[:, z, p, :], kt[:, z, bb, ti, :], enc[:, z, p, :], ALU.mult),
                     reads=[("kt", z, bb), ("enc", z, p)], writes=[("kb", z, p)])
                P.op("dve", lambda v, z=z, p=p, bb=bb, ti=ti: v.tensor_copy(vb[:, z, p, :], vt[:, z, bb, ti, :]), reads=[("vt", z, bb)], writes=[("vb", z, p)])
                if stop <= 3:
                    continue
                ps3, pk3 = newps()
                P.op("pe", lambda t, ps3=ps3, z=z, p=p: t.matmul(ps3[:, 0:64], kTb[:, z, p, :], qa[:, z, p, 0:64], start=True, stop=True),
                     reads=[("kTb", z, p), ("qa", z, p)], writes=[pk3])
                P.op("pe", lambda t, ps3=ps3, z=z, p=p: t.matmul(ps3[:, 64:128], kTb[:, z, p, :], qb_[:, z, p, 64:128], start=True, stop=True),
                     reads=[("kTb", z, p), ("qb", z, p)], writes=[pk3])
                P.op("dve", lambda v, ps3=ps3, z=z, p=p: v.tensor_tensor(att[:, z, p, :], ps3[:, 0:128], mask[:, :], ALU.mult), reads=[pk3, "mask"], writes=[("att", z, p)])
                if stop <= 4:
                    continue
                ps4, pk4 = newps()
                P.op("pe", lambda t, ps4=ps4, z=z, p=p: t.matmul(ps4[0:64, 0:128], kb[0:64, z, p, :], vb[0:64, z, p, :], start=True, stop=True),
                     reads=[("kb", z, p), ("vb", z, p)], writes=[pk4])
                ps4b, pk4b = newps()
                P.op("pe", lambda t, ps4b=ps4b, z=z, p=p: t.matmul(ps4b[0:64, 128:256], kb[64:128, z, p, :], vb[64:128, z, p, :], start=True, stop=True),
                     reads=[("kb", z, p), ("vb", z, p)], writes=[pk4b])
                s0 = sbi[z]; s1 = (s0 + 1) % 3; s2 = (s0 + 2) % 3
                P.op("dve", lambda v, ps4=ps4, z=z: v.tensor_tensor(tmp[:, z, :], ps4[0:64, 0:128], S[:, z, :], ALU.add), reads=[pk4, ("S", z)], writes=[("tmp", z)])
                P.op("dve", lambda v, z=z, p=p: v.tensor_scalar_mul(S[:, z, :], tmp[:, z, :], el[:, z, p, 0:1]), reads=[("tmp", z), ("el", z, p)], writes=[("S", z)])
                P.op("act", lambda a, z=z, s1=s1: a.copy(Sb[:, z, s1, :], S[:, z, :]), reads=[("S", z)], writes=[("Sb", z, s1)])
                P.op("dve", lambda v, ps4b=ps4b, z=z: v.tensor_tensor(tmp[:, z, :], ps4b[0:64, 128:256], S[:, z, :], ALU.add), reads=[pk4b, ("S", z)], writes=[("tmp", z)])
                P.op("dve", lambda v, z=z, p=p: v.tensor_scalar_mul(S[:, z, :], tmp[:, z, :], el[:, z, p, 1:2]), reads=[("tmp", z), ("el", z, p)], writes=[("S", z)])
                P.op("act", lambda a, z=z, s2=s2: a.copy(Sb[:, z, s2, :], S[:, z, :]), reads=[("S", z)], writes=[("Sb", z, s2)])
                if stop <= 5:
                    continue
                ps5, pk5 = newps()
                P.op("pe", lambda t, ps5=ps5, z=z, p=p: t.matmul(ps5[:, 0:128], att[:, z, p, :], vb[:, z, p, :], start=True, stop=False),
                     reads=[("att", z, p), ("vb", z, p)], writes=[pk5])
                P.op("pe", lambda t, ps5=ps5, z=z, p=p, s0=s0: t.matmul(ps5[:, 0:128], qa[:, z, p, :], Sb[:, z, s0, :], start=False, stop=False),
                     reads=[("qa", z, p), ("Sb", z, s0)], writes=[pk5])
                P.op("pe", lambda t, ps5=ps5, z=z, p=p, s1=s1: t.matmul(ps5[:, 0:128], qb_[:, z, p, :], Sb[:, z, s1, :], start=False, stop=True),
                     reads=[("qb", z, p), ("Sb", z, s1)], writes=[pk5])
                sbi[z] = s2
                P.op("act", lambda a, ps5=ps5, z=z, p=p: a.copy(osb[:, z, p, :], ps5[:, 0:128]), reads=[pk5], writes=[("osb", z, p)])
                P.dma("sp", o_d[z, tg * 128:(tg + 1) * 128, :], osb[:, z, p, :], reads=[("osb", z, p)])
    return C.end()


def gla_consts():
    t = np.arange(128)
    same = (t[:, None] // 64) == (t[None, :] // 64)
    m = (same & (t[:, None] <= t[None, :])).astype(np.float32)
    return np.ascontiguousarray(m * (-1.0 / 16.0)), np.ascontiguousarray(m)


def stage_gla(pl, pc, gla_a2, gla_ab):
    T = 8448
    nc = build_gla(T)
    tri, mask = gla_consts()
    maps = []
    for c in range(NCORES):
        b, h = c // 4, c % 4
        p = np.concatenate([pc[b], pl[b]], axis=0)
        pr = np.concatenate([pc[b][::-1], pl[b][::-1]], axis=0)
        def pack(fn):
            return np.ascontiguousarray(np.stack([fn(p, 0), fn(pr, 1)], 0))
        maps.append({
            "qT": pack(lambda a, z: a[:, h * 64:(h + 1) * 64].T),
            "kT": pack(lambda a, z: a[:, 256 + h * 64:256 + (h + 1) * 64].T),
            "k": pack(lambda a, z: a[:, 256 + h * 64:256 + (h + 1) * 64]),
            "v": pack(lambda a, z: a[:, 512 + h * 128:512 + (h + 1) * 128]),
            "a1T": pack(lambda a, z: a[:, 1024 + 16 * z:1024 + 16 * (z + 1)].T),
            "a2": np.ascontiguousarray(gla_a2[:, :, h * 64:(h + 1) * 64]),
            "ab": np.ascontiguousarray(gla_ab[:, None, h * 64:(h + 1) * 64]),
            "tri": tri, "mask": mask})
    res = run_spmd(nc, maps)
    o_f = np.empty((2, 4, T, 128), np.float32); o_b = np.empty((2, 4, T, 128), np.float32)
    for c in range(NCORES):
        b, h = c // 4, c % 4
        o = res[c]["o"]
        o_f[b, h] = o[0]
        o_b[b, h, :256] = o[1][:256][::-1]
        o_b[b, h, 256:] = o[1][256:][::-1]
    return o_f, o_b


def ln_tail(P, nt, z, zsq, ones, newps, lng, lnb, outT, out_cols, ost, tagp):
    ps_s, pk_s = newps()
    ps_q, pk_q = newps()
    for kc in range(8):
        P.op("act", lambda a, kc=kc: a.activation(zsq[:, kc, 0:nt], z[:, kc, 0:nt], AF.Square), reads=[("z", kc)], writes=[("zsq", kc)])
    for kc in range(8):
        P.op("pe", lambda t, kc=kc: t.matmul(ps_s[:, 0:nt], ones[:, :], z[:, kc, 0:nt], start=(kc == 0), stop=(kc == 7)), reads=[("z", kc), "ones"], writes=[pk_s])
    for kc in range(8):
        P.op("pe", lambda t, kc=kc: t.matmul(ps_q[:, 0:nt], ones[:, :], zsq[:, kc, 0:nt], start=(kc == 0), stop=(kc == 7)), reads=[("zsq", kc), "ones"], writes=[pk_q])
    mean, var = ost[:, 0, 0:nt], ost[:, 1, 0:nt]
    P.op("dve", lambda v: v.tensor_scalar_mul(mean, ps_s[:, 0:nt], 1.0 / 1024.0), reads=[pk_s], writes=[("ost", 0)])
    P.op("dve", lambda v: v.tensor_tensor(var, mean, mean, ALU.mult), reads=[("ost", 0)], writes=[("ost", 1)])
    P.op("dve", lambda v: v.scalar_tensor_tensor(var, ps_q[:, 0:nt], 1.0 / 1024.0, var, ALU.mult, ALU.subtract), reads=[pk_q, ("ost", 1)], writes=[("ost", 1)])
    P.op("dve", lambda v: v.tensor_scalar_add(var, var, LN_EPS), reads=[("ost", 1)], writes=[("ost", 1)])
    P.op("act", lambda a: a.activation(var, var, AF.Sqrt), reads=[("ost", 1)], writes=[("ost", 1)])
    P.op("dve", lambda v: v.reciprocal(var, var), reads=[("ost", 1)], writes=[("ost", 1)])
    for kc in range(8):
        e = P.alt()
        P.op(e, lambda v, kc=kc: v.tensor_tensor(zsq[:, kc, 0:nt], z[:, kc, 0:nt], mean, ALU.subtract), reads=[("z", kc), ("ost", 0)], writes=[("zsq", kc)])
        P.op(e, lambda v, kc=kc: v.tensor_tensor(zsq[:, kc, 0:nt], zsq[:, kc, 0:nt], var, ALU.mult), reads=[("zsq", kc), ("ost", 1)], writes=[("zsq", kc)])
        P.op(e, lambda v, kc=kc: v.tensor_scalar(zsq[:, kc, 0:nt], zsq[:, kc, 0:nt], lng[:, kc:kc + 1], lnb[:, kc:kc + 1], ALU.mult, ALU.add), reads=[("zsq", kc), "ln"], writes=[("zsq", kc)])
    P.dma("sp", outT.rearrange("(kc p) t -> p kc t", p=128)[:, :, out_cols[0]:out_cols[1]], zsq[:, :, 0:nt], reads=[("zsq", kc) for kc in range(8)])


def build_k4b(segs, C=None):
    C = C or Ctx()
    W1 = C.din("w1", [1024, 4096]); W2 = C.din("w2", [4096, 1024])
    lnd = C.din("ln", [128, 2, 8]); ones_d = C.din("ones", [128, 128])
    xs, mods, outs = [], [], []
    for si, (ntok, tile) in enumerate(segs):
        xs.append(C.din("x%d" % si, [1024, ntok])); mods.append(C.din("mod%d" % si, [128, 6, 8])); outs.append(C.dout("o%d" % si, [1024, ntok]))
    TM = max(t for _, t in segs)
    w1b = C.sb("w1b", [128, 8, 4096], BF16); w2b = C.sb("w2b", [128, 32, 1024], BF16)
    stage = C.sb("stage", [128, 2, 2048])
    xt = C.sb("xt", [128, 8, TM]); ub = C.sb("ub", [128, 8, TM], BF16); hid = C.sb("hid", [128, 32, TM], BF16)
    z = C.sb("z", [128, 8, TM]); zsq = C.sb("zsq", [128, 8, TM]); ost = C.sb("ost", [128, 2, TM])
    ones = C.sb("ones", [128, 128]); lnt = C.sb("lnt", [128, 2, 8])
    mt = C.sb("mt", [128, len(segs), 6, 8]); sc1 = C.sb("sc1", [128, len(segs), 8])
    pss = [C.ps("ps%d" % i, [128, 512]) for i in range(8)]
    P = C.start()
    pctr = [0]

    def newps():
        i = pctr[0] % 8; pctr[0] += 1
        return pss[i], ("ps", i)
    P.dma("sp", ones[:], ones_d[:, :], writes=["ones"]); P.dma("sp", lnt[:], lnd[:, :, :], writes=["ln"])
    for si in range(len(segs)):
        P.dma("sp", mt[:, si, :, :], mods[si][:, :, :], writes=[("mt", si)])
        P.op("dve", lambda v, si=si: v.tensor_scalar_add(sc1[:, si, :], mt[:, si, 4, :], 1.0), reads=[("mt", si)], writes=[("sc1", si)])
    for half in range(2):
        load_w_bf16(C, P, w1b[:, :, half * 2048:(half + 1) * 2048], ("w1", half), W1, 1024, 2048, stage, "stage", col0=half * 2048)
    load_w_bf16(C, P, w2b, "w2", W2, 4096, 1024, stage, "stage")
    for si, (ntok, tile) in enumerate(segs):
        xv = xs[si].rearrange("(kc p) t -> p kc t", p=128)
        for t0 in range(0, ntok, tile):
            nt = tile
            P.dma("sp", xt[:, :, 0:nt], xv[:, :, t0:t0 + nt], writes=["xt"])
            for kc in range(8):
                P.op(P.alt(), lambda v, kc=kc, si=si: v.tensor_scalar(ub[:, kc, 0:nt], xt[:, kc, 0:nt], sc1[:, si, kc:kc + 1], mt[:, si, 3, kc:kc + 1], ALU.mult, ALU.add),
                     reads=["xt", ("sc1", si), ("mt", si)], writes=[("ub", kc)])
            for hc in range(32):
                ps, pk = newps()
                for kc in range(8):
                    P.op("pe", lambda t, ps=ps, hc=hc, kc=kc: t.matmul(ps[:, 0:nt], w1b[:, kc, hc * 128:(hc + 1) * 128], ub[:, kc, 0:nt], start=(kc == 0), stop=(kc == 7)),
                         reads=[(("w1", hc // 16), kc), ("ub", kc)], writes=[pk])
                P.op("act", lambda a, ps=ps, hc=hc: a.activation(zsq[:, hc % 8, 0:nt], ps[:, 0:nt], AF.Relu), reads=[pk], writes=[("zsq", hc % 8)])
                P.op(P.alt(), lambda v, hc=hc: v.tensor_tensor(hid[:, hc, 0:nt], zsq[:, hc % 8, 0:nt], zsq[:, hc % 8, 0:nt], ALU.mult), reads=[("zsq", hc % 8)], writes=[("hid", hc)])
            for dc in range(8):
                ps, pk = newps()
                for hc in range(32):
                    P.op("pe", lambda t, ps=ps, hc=hc, dc=dc: t.matmul(ps[:, 0:nt], w2b[:, hc, dc * 128:(dc + 1) * 128], hid[:, hc, 0:nt], start=(hc == 0), stop=(hc == 31)),
                         reads=[("w2", hc), ("hid", hc)], writes=[pk])
                P.op("dve", lambda v, dc=dc: v.tensor_scalar_mul(z[:, dc, 0:nt], xt[:, dc, 0:nt], ALPHA), reads=["xt"], writes=[("z", dc)])
                P.op("dve", lambda v, ps=ps, dc=dc, si=si: v.scalar_tensor_tensor(z[:, dc, 0:nt], ps[:, 0:nt], mt[:, si, 5, dc:dc + 1], z[:, dc, 0:nt], ALU.mult, ALU.add),
                     reads=[pk, ("z", dc), ("mt", si)], writes=[("z", dc)])
            ln_tail(P, nt, z, zsq, ones, newps, lnt[:, 0, :], lnt[:, 1, :], outs[si], (t0, t0 + nt), ost, None)
    return C.end()


def ln_layout(g, b):
    return np.ascontiguousarray(np.stack([g.reshape(8, 128).T, b.reshape(8, 128).T], axis=1))


def shard_tok(a, c):
    b, s = c // 4, c % 4
    return np.ascontiguousarray(a[b, s * 2048:(s + 1) * 2048].T)


def shard_ctx(a, c):
    b, s = c // 4, c % 4
    return np.ascontiguousarray(a[b, s * 64:(s + 1) * 64].T)


def unshard(res, key_lat, key_ctx, F):
    xl = np.empty((2, 8192, F), np.float32); xc = np.empty((2, 256, F), np.float32)
    for c in range(NCORES):
        b, s = c // 4, c % 4
        xl[b, s * 2048:(s + 1) * 2048] = res[c][key_lat].T
        if key_ctx is not None:
            xc[b, s * 64:(s + 1) * 64] = res[c][key_ctx].T
    return xl, xc


def stage_k4b(x1, h1, m_l, w1, w2, g, bb, with_ctx=True):
    segs = [(2048, 256)] + ([(64, 64)] if with_ctx else [])
    nc = build_k4b(segs)
    ones = np.ones((128, 128), np.float32)
    maps = []
    for c in range(NCORES):
        m = {"w1": w1, "w2": w2, "ln": ln_layout(g, bb), "ones": ones, "x0": shard_tok(x1, c), "mod0": mod_layout(m_l[c // 4])}
        if with_ctx:
            m["x1"] = shard_ctx(h1, c); m["mod1"] = mod_layout(m_l[2])
        maps.append(m)
    res = run_spmd(nc, maps)
    return unshard(res, "o0", "o1" if with_ctx else None, 1024)


def build_gqa(first, last, C=None, oc_rows=64, nmk=2):
    NQ, NK = 2048, 2304
    C = C or Ctx()
    qT = C.din("qT", [64, 8, NQ]); qsT = C.din("qsT", [64, 8, NQ])
    kT = C.din("kT", [64, 2, NK]); ksT = C.din("ksT", [64, 2, NK])
    cq = C.din("cq", [64, NQ]); sq = C.din("sq", [64, NQ]); ck = C.din("ck", [64, NK]); sk = C.din("sk", [64, NK])
    vv = C.din("v", [128, 18, 2, 64]); kcT = C.din("kcT", [64, 2, 256]); vc = C.din("vc", [128, 2, 2, 64])
    qcT = C.din("qcT", [64, 8, 64]); sink = C.din("sink", [128, 8]); mk = C.din("mk", [128, nmk, 512])
    o_d = C.dout("o", [NQ, 512]); oc_d = C.dout("oc", [oc_rows, 512])
    q = C.sb("q", [64, 8, NQ]); t1 = C.sb("t1", [64, 2304]); t2 = C.sb("t2", [64, 2304])
    cqs = C.sb("cqs", [64, NQ]); sqs = C.sb("sqs", [64, NQ]); cks = C.sb("cks", [64, NK]); sks = C.sb("sks", [64, NK])
    qr = C.sb("qr", [64, 8, NQ], BF16); kr = C.sb("kr", [64, 2, NK], BF16)
    kc = C.sb("kc", [64, 2, 256]); kcb = C.sb("kcb", [64, 2, 256], BF16)
    qc = C.sb("qc", [64, 8, 64]); qcb = C.sb("qcb", [64, 8, 64], BF16)
    vf = C.sb("vf", [128, 20, 2, 64]); va = C.sb("va", [128, 20, 2, 65], BF16)
    mks = C.sb("mks", [128, nmk, 512]); mkb = C.sb("mkb", [128, nmk, 512], BF16)
    es = C.sb("es", [128, 8]); pt = C.sb("pt", [128, 4, 512], BF16)
    den = C.sb("den", [128, 2, 4]); osb = C.sb("osb", [128, 2, 512])
    pss = [C.ps("ps%d" % i, [128, 512]) for i in range(8)]
    P = C.start()
    pctr = [0]

    def newps():
        i = pctr[0] % 8; pctr[0] += 1
        return pss[i], ("ps", i)
    P.dma("sp", cqs[:], cq[:, :], writes=["cqs"]); P.dma("sp", sqs[:], sq[:, :], writes=["sqs"])
    P.dma("sp", cks[:], ck[:, :], writes=["cks"]); P.dma("sp", sks[:], sk[:, :], writes=["sks"])
    P.dma("sp", vf[:, 0:18, :, :], vv[:, :, :, :], writes=["vf"]); P.dma("sp", vf[:, 18:20, :, :], vc[:, :, :, :], writes=["vf"])
    P.dma("sp", kc[:], kcT[:, :, :], writes=["kc"]); P.dma("sp", qc[:], qcT[:, :, :], writes=["qc"])
    P.dma("sp", mks[:], mk[:, :, :], writes=["mks"]); P.dma("sp", es[:], sink[:, :], writes=["es"])
    P.op("act", lambda a: a.activation(es[:], es[:], AF.Exp), reads=["es"], writes=["es"])
    P.op("dve", lambda v: v.tensor_copy(mkb[:], mks[:]), reads=["mks"], writes=["mkb"])
    P.op("dve", lambda v: v.tensor_copy(kcb[:], kc[:]), reads=["kc"], writes=["kcb"])
    P.op("dve", lambda v: v.tensor_copy(qcb[:], qc[:]), reads=["qc"], writes=["qcb"])
    P.op("pool", lambda v: v.memset(va[:], 1.0), writes=["va"])
    P.op("dve", lambda v: v.tensor_copy(va[:, :, :, 0:64], vf[:]), reads=["vf"], writes=["va"])
    for h in range(8):
        P.dma("sp", q[:, h, :], qT[:, h, :], writes=[("q", h)])
        P.dma("sp", t1[:, 0:NQ], qsT[:, h, :], writes=["t1"])
        P.op("dve", lambda v: v.tensor_tensor(t1[:, 0:NQ], t1[:, 0:NQ], sqs[:], ALU.mult), reads=["t1", "sqs"], writes=["t1"])
        P.op("pool", lambda v, h=h: v.tensor_tensor(q[:, h, :], q[:, h, :], cqs[:], ALU.mult), reads=[("q", h), "cqs"], writes=[("q", h)])
        P.op("dve", lambda v, h=h: v.tensor_tensor(qr[:, h, :], q[:, h, :], t1[:, 0:NQ], ALU.add), reads=[("q", h), "t1"], writes=[("qr", h)])
    for g in range(2):
        P.dma("sp", t2[:, :], kT[:, g, :], writes=["t2"])
        P.dma("sp", t1[:, :], ksT[:, g, :], writes=["t1"])
        P.op("dve", lambda v: v.tensor_tensor(t1[:, :], t1[:, :], sks[:], ALU.mult), reads=["t1", "sks"], writes=["t1"])
        P.op("pool", lambda v: v.tensor_tensor(t2[:, :], t2[:, :], cks[:], ALU.mult), reads=["t2", "cks"], writes=["t2"])
        P.op("dve", lambda v, g=g: v.tensor_tensor(kr[:, g, :], t2[:, :], t1[:, :], ALU.add), reads=["t2", "t1"], writes=[("kr", g)])
    it = 0
    for qb in range(17):
        ob = qb % 2
        for g in range(2):
            if qb < 16:
                tiles = []
                if not (first and qb == 0):
                    tiles.append((qb, 2 if (nmk == 4 and qb == 0) else 0))
                tiles.append((qb + 1, None))
                if not (last and qb == 15):
                    tiles.append((qb + 2, 3 if (nmk == 4 and qb == 15) else 1))
                tiles += [("c", 0), ("c", 1)]
                nq = 128
            else:
                tiles = [("c", 0), ("c", 1)]
                nq = 64
            ops, opk = newps()

            def emit_qk(ti, g=g, qb=qb, nq=nq, tiles=tiles):
                kt_, mi = tiles[ti]
                ps, pk = newps()
                if kt_ == "c":
                    lhs = kcb[:, g, mi * 128:(mi + 1) * 128]; lk = "kcb"; vi = 18 + mi; mi = None
                else:
                    lhs = kr[:, g, kt_ * 128:(kt_ + 1) * 128]; lk = ("kr", g); vi = kt_
                if qb < 16:
                    rhs = qr[:, 4 * g:4 * g + 4, qb * 128:(qb + 1) * 128]; rk_ = [("qr", 4 * g + j) for j in range(4)]
                else:
                    rhs = qcb[:, 4 * g:4 * g + 4, :]; rk_ = ["qcb"]
                P.op("pe", lambda t, ps=ps, lhs=lhs, rhs=rhs, nq=nq: t.matmul(ps[:, 0:4 * nq], lhs, rhs, start=True, stop=True), reads=[lk] + rk_, writes=[pk])
                return ps, pk, vi, mi
            pend = [emit_qk(0)]
            for ti in range(len(tiles)):
                ps, pk, vi, mi = pend.pop(0)
                if ti + 1 < len(tiles):
                    pend.append(emit_qk(ti + 1))
                pb = it % 4; it += 1
                P.op("act", lambda a, ps=ps, pb=pb, nq=nq: a.activation(pt[:, pb, 0:4 * nq], ps[:, 0:4 * nq], AF.Exp, scale=0.125), reads=[pk], writes=[("pt", pb)])
                if mi is not None:
                    P.op("dve", lambda v, pb=pb, mi=mi: v.tensor_tensor(pt[:, pb, :], pt[:, pb, :], mkb[:, mi, :], ALU.mult), reads=[("pt", pb), "mkb"], writes=[("pt", pb)])
                for j in range(4):
                    P.op("pe", lambda t, ops=ops, pb=pb, j=j, vi=vi, g=g, nq=nq, ti=ti, nt_=len(tiles): t.matmul(ops[0:nq, j * 65:(j + 1) * 65], pt[:, pb, j * nq:(j + 1) * nq], va[:, vi, g, :], start=(ti == 0 and j == 0), stop=(ti == nt_ - 1 and j == 3)),
                         reads=[("pt", pb), "va"], writes=[opk])
            P.op("dve", lambda v, ops=ops, g=g, ob=ob, nq=nq: v.tensor_tensor(den[0:nq, ob, :], ops[0:nq, 64:260:65], es[0:nq, 4 * g:4 * g + 4], ALU.add), reads=[opk, "es"], writes=[("den", ob)])
            P.op("dve", lambda v, ob=ob, nq=nq: v.reciprocal(den[0:nq, ob, :], den[0:nq, ob, :]), reads=[("den", ob)], writes=[("den", ob)])
            for j in range(4):
                hh = 4 * g + j
                P.op("dve" if j % 2 else "act", (lambda v, ops=ops, ob=ob, j=j, hh=hh, nq=nq: v.tensor_scalar_mul(osb[0:nq, ob, hh * 64:(hh + 1) * 64], ops[0:nq, j * 65:j * 65 + 64], den[0:nq, ob, j:j + 1])) if j % 2 else
                     (lambda a, ops=ops, ob=ob, j=j, hh=hh, nq=nq: a.activation(osb[0:nq, ob, hh * 64:(hh + 1) * 64], ops[0:nq, j * 65:j * 65 + 64], AF.Copy, scale=den[0:nq, ob, j:j + 1])),
                     reads=[opk, ("den", ob)], writes=[("osb", ob)])
        if qb < 16:
            P.dma("sp", o_d[qb * 128:(qb + 1) * 128, :], osb[:, ob, :], reads=[("osb", ob)])
        else:
            P.dma("sp", oc_d[:, :], osb[0:oc_rows, ob, :], reads=[("osb", ob)])
    return C.end()


GQ0, GK0, GV0 = 1568, 2080, 2208
SWAP64 = np.array([d + 16 if (d % 32) < 16 else d - 16 for d in range(64)])


def rope_tables(pos_tok):
    row = (pos_tok // 64).astype(np.float32); col = (pos_tok % 64).astype(np.float32)
    inv = (10000.0 ** (-np.arange(16, dtype=np.float32) / np.float32(16))).astype(np.float32)
    ar = row[None, :] * inv[:, None]; ac = col[None, :] * inv[:, None]
    Cc = np.concatenate([np.cos(ar), np.cos(ar), np.cos(ac), np.cos(ac)], 0).astype(np.float32)
    Ss = np.concatenate([-np.sin(ar), np.sin(ar), -np.sin(ac), np.sin(ac)], 0).astype(np.float32)
    return Cc, Ss


def stage_gqa(pl, pc, sink):
    ncs = {(f, l): build_gqa(f, l) for (f, l) in [(True, False), (False, False), (False, True)]}
    t = np.arange(128)
    mk = np.stack([np.tile((t[:, None] >= t[None, :]).astype(np.float32), (1, 4)), np.tile((t[:, None] <= t[None, :]).astype(np.float32), (1, 4))], 1)
    outs = {}
    att = np.empty((2, 8192, 512), np.float32); attc = np.empty((2, 256, 512), np.float32)
    maps = []
    for c in range(NCORES):
        b, s = c // 4, c % 4
        lo, hi = s * 2048, (s + 1) * 2048
        pos = np.arange(lo, hi); posk = np.clip(np.arange(lo - 128, hi + 128), 0, 8191)
        cq, sq = rope_tables(pos); ck, sk = rope_tables(posk)
        pk = np.zeros((2304, pl.shape[2]), np.float32)
        a0, a1 = max(lo - 128, 0), min(hi + 128, 8192)
        pk[a0 - (lo - 128):a1 - (lo - 128)] = pl[b, a0:a1]
        m = {"qT": np.ascontiguousarray(pl[b, lo:hi, GQ0:GQ0 + 512].reshape(2048, 8, 64).transpose(2, 1, 0)),
             "qsT": np.ascontiguousarray(pl[b, lo:hi, 2336:2848].reshape(2048, 8, 64).transpose(2, 1, 0)),
             "kT": np.ascontiguousarray(pk[:, GK0:GK0 + 128].reshape(2304, 2, 64).transpose(2, 1, 0)),
             "ksT": np.ascontiguousarray(pk[:, 2848:2976].reshape(2304, 2, 64).transpose(2, 1, 0)),
             "cq": cq, "sq": sq, "ck": ck, "sk": sk,
             "v": np.ascontiguousarray(pk[:, GV0:GV0 + 128].reshape(18, 128, 2, 64).transpose(1, 0, 2, 3)),
             "kcT": np.ascontiguousarray(pc[b, :, GK0:GK0 + 128].reshape(256, 2, 64).transpose(2, 1, 0)),
             "vc": np.ascontiguousarray(pc[b, :, GV0:GV0 + 128].reshape(2, 128, 2, 64).transpose(1, 0, 2, 3)),
             "qcT": np.ascontiguousarray(pc[b, s * 64:(s + 1) * 64, GQ0:GQ0 + 512].reshape(64, 8, 64).transpose(2, 1, 0)),
             "sink": np.ascontiguousarray(np.broadcast_to(sink[None, :], (128, 8))).astype(np.float32), "mk": np.ascontiguousarray(mk)}
        maps.append(m)
    for key, nc in ncs.items():
        res = run_spmd(nc, maps)
        for c in range(NCORES):
            b, s = c // 4, c % 4
            if (s == 0, s == 3) == key:
                att[b, s * 2048:(s + 1) * 2048] = res[c]["o"]
                attc[b, s * 64:(s + 1) * 64] = res[c]["oc"]
    return att, attc


def aug_w_in_ab(w_in):
    qs = np.concatenate([w_in[:, GQ0 + h * 64 + SWAP64] for h in range(8)], 1)
    ks = np.concatenate([w_in[:, GK0 + h * 64 + SWAP64] for h in range(2)], 1)
    return np.ascontiguousarray(np.concatenate([w_in, qs, ks], 1))


def build_k4a0(segs, C=None):
    C = C or Ctx()
    W = C.din("w", [1024, 1024]); lnd = C.din("ln", [128, 2, 8]); ones_d = C.din("ones", [128, 128]); gn_d = C.din("gn", [128, 1])
    xs, mods, outs, ofs, obs, ogs, ats = [], [], [], [], [], [], []
    for si, (ntok, tile) in enumerate(segs):
        xs.append(C.din("x%d" % si, [1024, ntok])); mods.append(C.din("mod%d" % si, [128, 6, 8])); outs.append(C.dout("o%d" % si, [1024, ntok]))
        ofs.append(C.din("of%d" % si, [512, ntok])); obs.append(C.din("ob%d" % si, [512, ntok])); ogs.append(C.din("og%d" % si, [512, ntok])); ats.append(C.din("at%d" % si, [512, ntok]))
    TM = max(t for _, t in segs)
    wb = C.sb("wb", [128, 8, 1024], BF16); stage = C.sb("stage", [128, 2, 1024])
    xt = C.sb("xt", [128, 8, TM]); mix = C.sb("mix", [128, 8, TM], BF16)
    a = C.sb("a", [128, 4, TM]); b_ = C.sb("b", [128, 4, TM]); g_ = C.sb("g", [128, 4, TM]); at = C.sb("at", [128, 4, TM])
    z = C.sb("z", [128, 8, TM]); zsq = C.sb("zsq", [128, 8, TM]); ost = C.sb("ost", [128, 2, TM])
    ones = C.sb("ones", [128, 128]); lnt = C.sb("lnt", [128, 2, 8]); gn = C.sb("gn", [128, 1])
    mt = C.sb("mt", [128, len(segs), 6, 8])
    pss = [C.ps("ps%d" % i, [128, 512]) for i in range(8)]
    P = C.start()
    pctr = [0]

    def newps():
        i = pctr[0] % 8; pctr[0] += 1
        return pss[i], ("ps", i)
    P.dma("sp", ones[:], ones_d[:, :], writes=["ones"]); P.dma("sp", lnt[:], lnd[:, :, :], writes=["ln"]); P.dma("sp", gn[:], gn_d[:, :], writes=["gn"])
    for si in range(len(segs)):
        P.dma("sp", mt[:, si, :, :], mods[si][:, :, :], writes=[("mt", si)])
    load_w_bf16(C, P, wb, "wb", W, 1024, 1024, stage, "stage")
    for si, (ntok, tile) in enumerate(segs):
        v3 = lambda ap: ap.rearrange("(kc p) t -> p kc t", p=128)
        for t0 in range(0, ntok, tile):
            nt = tile
            P.dma("sp", xt[:, :, 0:nt], v3(xs[si])[:, :, t0:t0 + nt], writes=["xt"])
            P.dma("sp", a[:, :, 0:nt], v3(ofs[si])[:, :, t0:t0 + nt], writes=["a"])
            P.dma("sp", b_[:, :, 0:nt], v3(obs[si])[:, :, t0:t0 + nt], writes=["b"])
            P.dma("sp", g_[:, :, 0:nt], v3(ogs[si])[:, :, t0:t0 + nt], writes=["g"])
            P.dma("sp", at[:, :, 0:nt], v3(ats[si])[:, :, t0:t0 + nt], writes=["at"])
            P.op("dve", lambda v: v.tensor_tensor(a[:, :, 0:nt], a[:, :, 0:nt], b_[:, :, 0:nt], ALU.add), reads=["a", "b"], writes=["a"])
            P.op("act", lambda a_: a_.activation(b_[:, :, 0:nt], a[:, :, 0:nt], AF.Square), reads=["a"], writes=["b"])
            P.op("act", lambda a_: a_.activation(g_[:, :, 0:nt], g_[:, :, 0:nt], AF.Silu), reads=["g"], writes=["g"])
            P.op("pool", lambda v: v.tensor_copy(mix[:, 4:8, 0:nt], at[:, :, 0:nt]), reads=["at"], writes=[("mix", k) for k in range(4, 8)])
            for h in range(4):
                ps, pk = newps()
                P.op("pe", lambda t, ps=ps, h=h: t.matmul(ps[:, 0:nt], ones[:, :], b_[:, h, 0:nt], start=True, stop=True), reads=["b", "ones"], writes=[pk])
                P.op("dve", lambda v, ps=ps, h=h: v.tensor_scalar(zsq[:, h, 0:nt], ps[:, 0:nt], 1.0 / 128.0, LN_EPS, ALU.mult, ALU.add), reads=[pk], writes=[("zsq", h)])
                P.op("act", lambda a_, h=h: a_.activation(zsq[:, h, 0:nt], zsq[:, h, 0:nt], AF.Sqrt), reads=[("zsq", h)], writes=[("zsq", h)])
                P.op("dve", lambda v, h=h: v.reciprocal(zsq[:, h, 0:nt], zsq[:, h, 0:nt]), reads=[("zsq", h)], writes=[("zsq", h)])
                P.op("dve", lambda v, h=h: v.tensor_tensor(zsq[:, h, 0:nt], zsq[:, h, 0:nt], a[:, h, 0:nt], ALU.mult), reads=[("zsq", h), "a"], writes=[("zsq", h)])
                P.op("dve", lambda v, h=h: v.scalar_tensor_tensor(mix[:, h, 0:nt], zsq[:, h, 0:nt], gn[:, 0:1], g_[:, h, 0:nt], ALU.mult, ALU.mult), reads=[("zsq", h), "g", "gn"], writes=[("mix", h)])
            for dc in range(8):
                ps, pk = newps()
                for kc in range(8):
                    P.op("pe", lambda t, ps=ps, kc=kc, dc=dc: t.matmul(ps[:, 0:nt], wb[:, kc, dc * 128:(dc + 1) * 128], mix[:, kc, 0:nt], start=(kc == 0), stop=(kc == 7)),
                         reads=[("wb", kc), ("mix", kc)], writes=[pk])
                P.op("dve", lambda v, dc=dc: v.tensor_scalar_mul(z[:, dc, 0:nt], xt[:, dc, 0:nt], ALPHA), reads=["xt"], writes=[("z", dc)])
                P.op("dve", lambda v, ps=ps, dc=dc, si=si: v.scalar_tensor_tensor(z[:, dc, 0:nt], ps[:, 0:nt], mt[:, si, 2, dc:dc + 1], z[:, dc, 0:nt], ALU.mult, ALU.add),
                     reads=[pk, ("z", dc), ("mt", si)], writes=[("z", dc)])
            ln_tail(P, nt, z, zsq, ones, newps, lnt[:, 0, :], lnt[:, 1, :], outs[si], (t0, t0 + nt), ost, None)
    return C.end()


def stage_k4a0(x, ctx, m_l, o_f, o_b, pl, pc, att, attc, w_out, gla_norm, g, bb):
    nc = build_k4a0([(2048, 256), (64, 64)])
    ones = np.ones((128, 128), np.float32)
    def heads_to_tok(o):
        return np.ascontiguousarray(o.transpose(0, 2, 1, 3).reshape(2, o.shape[2], 512))
    ofl, obl = heads_to_tok(o_f[:, :, 256:]), heads_to_tok(o_b[:, :, 256:])
    ofc, obc = heads_to_tok(o_f[:, :, :256]), heads_to_tok(o_b[:, :, :256])
    maps = []
    for c in range(NCORES):
        maps.append({"w": w_out, "ln": ln_layout(g, bb), "ones": ones, "gn": np.ascontiguousarray(gla_norm.reshape(128, 1)),
                     "x0": shard_tok(x, c), "mod0": mod_layout(m_l[c // 4]), "x1": shard_ctx(ctx, c), "mod1": mod_layout(m_l[2]),
                     "of0": shard_tok(ofl, c), "ob0": shard_tok(obl, c), "og0": shard_tok(pl[:, :, 1056:1568], c), "at0": shard_tok(att, c),
                     "of1": shard_ctx(ofc, c), "ob1": shard_ctx(obc, c), "og1": shard_ctx(pc[:, :, 1056:1568], c), "at1": shard_ctx(attc, c)})
    res = run_spmd(nc, maps)
    return unshard(res, "o0", "o1", 1024)


def build_rwf(segs, C=None, NH=4):
    NCHK = 3 * NH + 3
    C = C or Ctx()
    Wl = C.din("wl", [128, 512 * NH]); sc_d = C.din("sc", [128, NCHK + 5 * NH]); bo_d = C.din("bo", [128, 128])
    ps_in, outs = [], []
    names = ["r", "km", "v", "an", "bb", "wf", "wb", "g"]
    for si, (ntok, tile) in enumerate(segs):
        ps_in.append(C.din("p%d" % si, [NCHK * 128, ntok + 2]))
        outs.append({n: C.dout("%s%d" % (n, si), [128 * NH, ntok]) for n in names})
    TM = max(t for _, t in segs)
    wl = C.sb("wl", [128, 1, 512 * NH], BF16); stage = C.sb("stage", [128, 2, 512 * NH])
    sc = C.sb("sc", [128, NCHK + 5 * NH]); hmu = C.sb("hmu", [128, NCHK]); omm = C.sb("omm", [128, NCHK]); omka = C.sb("omka", [128, NH])
    bo = C.sb("bo", [128, 128])
    pt = C.sb("pt", [128, NCHK, TM + 2]); sh = C.sb("sh", [128, NCHK, TM]); tmp = C.sb("tmp", [128, 2, TM])
    th = C.sb("th", [128, 3, TM], BF16)
    o = C.sb("o", [128, 6, NH, TM]); ag = C.sb("ag", [128, NH, TM]); kx = C.sb("kx", [128, 2, TM]); sq = C.sb("sq", [128, 2, TM])
    pss = [C.ps("ps%d" % i, [128, 512]) for i in range(8)]
    P = C.start()
    pctr = [0]

    def newps():
        i = pctr[0] % 8; pctr[0] += 1
        return pss[i], ("ps", i)
    P.dma("sp", sc[:], sc_d[:, :], writes=["sc"]); P.dma("sp", bo[:], bo_d[:, :], writes=["bo"])
    load_w_bf16(C, P, wl, "wl", Wl, 128, 512 * NH, stage, "stage")
    P.op("dve", lambda v: v.tensor_scalar_mul(hmu[:], sc[:, 0:NCHK], 0.5), reads=["sc"], writes=["hmu"])
    P.op("dve", lambda v: v.tensor_scalar(omm[:], sc[:, 0:NCHK], -1.0, 1.0, ALU.mult, ALU.add), reads=["sc"], writes=["omm"])
    P.op("dve", lambda v: v.tensor_scalar(omka[:], sc[:, NCHK + 4 * NH:NCHK + 5 * NH], -1.0, 1.0, ALU.mult, ALU.add), reads=["sc"], writes=["omka"])
    W0F, W0B, A0, KK, KA = NCHK, NCHK + NH, NCHK + 2 * NH, NCHK + 3 * NH, NCHK + 4 * NH
    for si, (ntok, tile) in enumerate(segs):
        pv = ps_in[si].rearrange("(kc p) t -> p kc t", p=128)
        for t0 in range(0, ntok, tile):
            nt = tile
            P.dma("sp", pt[:, :, 0:nt + 2], pv[:, :, t0:t0 + nt + 2], writes=["pt"])
            for ch in range(NCHK):
                e = P.alt(); tb = ch % 2
                P.op(e, lambda v, ch=ch, tb=tb: v.tensor_tensor(tmp[:, tb, 0:nt], pt[:, ch, 0:nt], pt[:, ch, 2:nt + 2], ALU.add), reads=["pt"], writes=[("tmp", tb)])
                P.op(e, lambda v, ch=ch, tb=tb: v.tensor_scalar_mul(tmp[:, tb, 0:nt], tmp[:, tb, 0:nt], hmu[:, ch:ch + 1]), reads=[("tmp", tb), "hmu"], writes=[("tmp", tb)])
                P.op("dve", lambda v, ch=ch, tb=tb: v.scalar_tensor_tensor(sh[:, ch, 0:nt], pt[:, ch, 1:nt + 1], omm[:, ch:ch + 1], tmp[:, tb, 0:nt], ALU.mult, ALU.add), reads=["pt", ("tmp", tb), "omm"], writes=[("sh", ch)])
            P.op("act", lambda a: a.activation(th[:, 0, 0:nt], sh[:, 3 * NH, 0:nt], AF.Tanh), reads=[("sh", 3 * NH)], writes=[("th", 0)])
            P.op("act", lambda a: a.copy(th[:, 1, 0:nt], sh[:, 3 * NH + 1, 0:nt]), reads=[("sh", 3 * NH + 1)], writes=[("th", 1)])
            P.op("act", lambda a: a.activation(th[:, 2, 0:nt], sh[:, 3 * NH + 2, 0:nt], AF.Sigmoid), reads=[("sh", 3 * NH + 2)], writes=[("th", 2)])
            for c in range(NH):
                for (oi, wc0, ti_, bcol) in ((3, 0, 0, W0F), (4, 128 * NH, 0, W0B)):
                    ps, pk = newps()
                    P.op("pe", lambda t, ps=ps, wc0=wc0, c=c, ti_=ti_: t.matmul(ps[:, 0:nt], wl[:, 0, wc0 + c * 128:wc0 + (c + 1) * 128], th[:, ti_, 0:nt], start=True, stop=True), reads=[("wl", 0), ("th", ti_)], writes=[pk])
                    P.op("act", lambda a, ps=ps, oi=oi, c=c, bcol=bcol: a.activation(o[:, oi, c, 0:nt], ps[:, 0:nt], AF.Sigmoid, bias=sc[:, bcol + c:bcol + c + 1]), reads=[pk, "sc"], writes=[("o", oi, c)])
                    P.op("act", lambda a, oi=oi, c=c: a.activation(o[:, oi, c, 0:nt], o[:, oi, c, 0:nt], AF.Exp, scale=-0.6065306597126334), reads=[("o", oi, c)], writes=[("o", oi, c)])
                ps, pk = newps()
                P.op("pe", lambda t, ps=ps, c=c: t.matmul(ps[:, 0:nt], wl[:, 0, 256 * NH + c * 128:256 * NH + (c + 1) * 128], th[:, 1, 0:nt], start=True, stop=True), reads=[("wl", 0), ("th", 1)], writes=[pk])
                P.op("act", lambda a, ps=ps, c=c: a.activation(ag[:, c, 0:nt], ps[:, 0:nt], AF.Sigmoid, bias=sc[:, A0 + c:A0 + c + 1]), reads=[pk, "sc"], writes=[("ag", c)])
                ps, pk = newps()
                P.op("pe", lambda t, ps=ps, c=c: t.matmul(ps[:, 0:nt], wl[:, 0, 384 * NH + c * 128:384 * NH + (c + 1) * 128], th[:, 2, 0:nt], start=True, stop=True), reads=[("wl", 0), ("th", 2)], writes=[pk])
                P.op("act", lambda a, ps=ps, c=c: a.copy(o[:, 5, c, 0:nt], ps[:, 0:nt]), reads=[pk], writes=[("o", 5, c)])
                kb_ = c % 2
                P.op("dve", lambda v, c=c, kb_=kb_: v.tensor_scalar_mul(kx[:, kb_, 0:nt], sh[:, NH + c, 0:nt], sc[:, KK + c:KK + c + 1]), reads=[("sh", NH + c), "sc"], writes=[("kx", kb_)])
                P.op("pool", lambda v, kb_=kb_: v.tensor_tensor(sq[:, kb_, 0:nt], kx[:, kb_, 0:nt], kx[:, kb_, 0:nt], ALU.mult), reads=[("kx", kb_)], writes=[("sq", kb_)])
                ps, pk = newps()
                P.op("pe", lambda t, ps=ps, kb_=kb_: t.matmul(ps[:, 0:nt], bo[:, :], sq[:, kb_, 0:nt], start=True, stop=True), reads=["bo", ("sq", kb_)], writes=[pk])
                P.op("act", lambda a, ps=ps, kb_=kb_: a.activation(sq[:, kb_, 0:nt], ps[:, 0:nt], AF.Sqrt), reads=[pk], writes=[("sq", kb_)])
                P.op("dve", lambda v, kb_=kb_: v.tensor_scalar_max(sq[:, kb_, 0:nt], sq[:, kb_, 0:nt], 1e-12), reads=[("sq", kb_)], writes=[("sq", kb_)])
                P.op("dve", lambda v, kb_=kb_: v.reciprocal(sq[:, kb_, 0:nt], sq[:, kb_, 0:nt]), reads=[("sq", kb_)], writes=[("sq", kb_)])
                P.op("dve", lambda v, c=c, kb_=kb_: v.scalar_tensor_tensor(o[:, 1, c, 0:nt], kx[:, kb_, 0:nt], -1.0, sq[:, kb_, 0:nt], ALU.mult, ALU.mult), reads=[("kx", kb_), ("sq", kb_)], writes=[("o", 1, c)])
                P.op("dve", lambda v, c=c: v.scalar_tensor_tensor(o[:, 2, c, 0:nt], o[:, 1, c, 0:nt], -1.0, ag[:, c, 0:nt], ALU.mult, ALU.mult), reads=[("o", 1, c), ("ag", c)], writes=[("o", 2, c)])
                P.op("pool", lambda v, c=c: v.tensor_scalar(ag[:, c, 0:nt], ag[:, c, 0:nt], sc[:, KA + c:KA + c + 1], omka[:, c:c + 1], ALU.mult, ALU.add), reads=[("ag", c), "sc", "omka"], writes=[("ag", c)])
                P.op("pool", lambda v, c=c: v.tensor_tensor(o[:, 0, c, 0:nt], sh[:, NH + c, 0:nt], ag[:, c, 0:nt], ALU.mult), reads=[("sh", NH + c), ("ag", c)], writes=[("o", 0, c)])
            v3 = lambda ap: ap.rearrange("(kc p) t -> p kc t", p=128)
            P.dma("sp", v3(outs[si]["r"])[:, :, t0:t0 + nt], sh[:, 0:NH, 0:nt], reads=[("sh", k) for k in range(NH)])
            P.dma("sp", v3(outs[si]["v"])[:, :, t0:t0 + nt], sh[:, 2 * NH:3 * NH, 0:nt], reads=[("sh", k) for k in range(2 * NH, 3 * NH)])
            for oi, n in ((0, "km"), (1, "an"), (2, "bb"), (3, "wf"), (4, "wb"), (5, "g")):
                P.dma("sp", v3(outs[si][n])[:, :, t0:t0 + nt], o[:, oi, :, 0:nt], reads=[("o", oi, c) for c in range(NH)])
    return C.end()


def cd_w_in_layout(w):
    z64 = np.zeros((1024, 64), np.float32)
    kpe = w[:, 2240:2272]
    sw = np.array([d + 8 if (d % 16) < 8 else d - 8 for d in range(32)])
    return np.ascontiguousarray(np.concatenate([w[:, 0:1536], w[:, 1536:1664], w[:, 1664:1728], z64, w[:, 1728:1856],
                                                w[:, 1856:2112], w[:, 2112:2240], kpe, kpe[:, sw], z64], 1))


def mu_layout(mu):
    z64 = np.zeros(64, np.float32)
    m = np.concatenate([mu[0:1536], mu[1536:1664], mu[1664:1728], z64, mu[1728:1856]])
    return m.reshape(15, 128).T


def stage_rwf(pl, pc, I):
    nc = build_rwf([(2048, 256), (64, 64)])
    w2, a2, g2 = I['rwkv_w2'][0], I['rwkv_a2'][0], I['rwkv_g2'][0]
    wl = np.zeros((128, 2048), np.float32)
    wl[0:64, 0:512] = w2[0]; wl[64:128, 512:1024] = w2[1]; wl[0:64, 1024:1536] = a2; wl[:, 1536:2048] = g2
    col = lambda v: v.reshape(4, 128).T
    sc = np.concatenate([mu_layout(I['rwkv_mu'][0]), col(I['rwkv_w0'][0, 0]), col(I['rwkv_w0'][0, 1]), col(I['rwkv_a0'][0]), col(I['rwkv_kk'][0]), col(I['rwkv_ka'][0])], 1).astype(np.float32)
    t = np.arange(128)
    bo = ((t[:, None] // 64) == (t[None, :] // 64)).astype(np.float32)
    maps = []
    for c in range(NCORES):
        b, s = c // 4, c % 4
        pp = np.zeros((2050, 1920), np.float32)
        a0, a1 = max(s * 2048 - 1, 0), min((s + 1) * 2048 + 1, 8192)
        pp[a0 - (s * 2048 - 1):a1 - (s * 2048 - 1)] = pl[b, a0:a1, :1920]
        pq = np.zeros((66, 1920), np.float32)
        a0, a1 = max(s * 64 - 1, 0), min((s + 1) * 64 + 1, 256)
        pq[a0 - (s * 64 - 1):a1 - (s * 64 - 1)] = pc[b, a0:a1, :1920]
        maps.append({"wl": wl, "sc": np.ascontiguousarray(sc), "bo": bo, "p0": np.ascontiguousarray(pp.T), "p1": np.ascontiguousarray(pq.T)})
    res = run_spmd(nc, maps)
    out = {}
    for n in ["r", "km", "v", "an", "bb", "wf", "wb", "g"]:
        xl, xc = unshard(res, n + "0", n + "1", 512)
        out[n] = np.concatenate([xc, xl], 1)
    return out


def build_rws(T, Tskip, C=None):
    NB = T // 64
    C = C or Ctx()
    Ab = C.din("Ab", [128, T, 2]); Rb = C.din("Rb", [128, T, 2]); Wt = C.din("Wt", [128, T])
    BK = C.din("BK", [4, T, 128]); VVd = C.din("VV", [4, T, 64])
    y_d = C.dout("y", [2, T, 64])
    ab = C.sb("ab", [128, 2, 64, 2]); rb = C.sb("rb", [128, 2, 64, 2]); wt = C.sb("wt", [128, 2, 64])
    bk = C.sb("bk", [4, 2, 64, 128]); vv = C.sb("vv", [4, 2, 64, 64]); ysb = C.sb("ysb", [2, 2, 64, 64])
    ST = C.sb("ST", [128, 64])
    pa = [C.ps("pa%d" % i, [128, 512]) for i in range(2)]
    po = [C.ps("po%d" % i, [128, 512]) for i in range(2)]
    py = [C.ps("py%d" % i, [128, 512]) for i in range(2)]
    P = C.start()
    P.op("dve", lambda v: v.memset(ST[:], 0.0), writes=["ST"])
    P.op("dve", lambda v: v.memset(ysb[:], 0.0), writes=[("ysb", 0), ("ysb", 1)])
    for blk in range(NB):
        b = blk % 2
        s0 = blk * 64
        P.dma("sp", ab[:, b, :, :], Ab[:, s0:s0 + 64, :], writes=[("ab", b)])
        P.dma("sp", rb[:, b, :, :], Rb[:, s0:s0 + 64, :], writes=[("rb", b)])
        P.dma("sp", wt[:, b, :], Wt[:, s0:s0 + 64], writes=[("wt", b)])
        P.dma("sp", bk[:, b, :, :], BK[:, s0:s0 + 64, :], writes=[("bk", b)])
        P.dma("sp", vv[:, b, :, :], VVd[:, s0:s0 + 64, :], writes=[("vv", b)])
        for t in range(64):
            i2 = t % 2
            P.op("pe", lambda e, b=b, t=t, i2=i2: e.matmul(pa[i2][0:2, 0:64], ab[:, b, t, :], ST[:, :], start=True, stop=True), reads=[("ab", b), "ST"], writes=[("pa", i2)])
            P.op("act", lambda a, b=b, t=t, i2=i2: a.copy(vv[0:2, b, t, :], pa[i2][0:2, 0:64]), reads=[("pa", i2)], writes=[("vv", b)])
            P.op("pe", lambda e, b=b, t=t, i2=i2: e.matmul(po[i2][:, 0:64], bk[:, b, t, :], vv[:, b, t, :], start=True, stop=True), reads=[("bk", b), ("vv", b)], writes=[("po", i2)])
            P.op("dve", lambda v, b=b, t=t, i2=i2: v.scalar_tensor_tensor(ST[:, :], ST[:, :], wt[:, b, t:t + 1], po[i2][:, 0:64], ALU.mult, ALU.add), reads=["ST", ("wt", b), ("po", i2)], writes=["ST"])
            if s0 >= Tskip:
                g8 = (t // 8) % 2
                P.op("pe", lambda e, b=b, t=t, g8=g8: e.matmul(py[g8][0:2, (t % 8) * 64:(t % 8 + 1) * 64], rb[:, b, t, :], ST[:, :], start=True, stop=True), reads=[("rb", b), "ST"], writes=[("py", g8)])
                if t % 8 == 7:
                    P.op("act", lambda a, b=b, t=t, g8=g8: a.copy(ysb[:, b, t - 7:t + 1, :], py[g8][0:2, 0:512].rearrange("p (a c) -> p a c", c=64)), reads=[("py", g8)], writes=[("ysb", b)])
        P.dma("sp", y_d[:, s0:s0 + 64, :], ysb[:, b, :, :], reads=[("ysb", b)])
    return C.end()


def stage_rws(F):
    T = 8448
    nc = build_rws(T, 256)
    ys = []
    for d_, wname in ((0, "wf"), (1, "wb")):
        def st(a):
            return a if d_ == 0 else np.concatenate([a[:, :256][:, ::-1], a[:, 256:][:, ::-1]], 1)
        r, km, v, an, bb, w = [st(F[n]) for n in ("r", "km", "v", "an", "bb", wname)]
        maps = []
        for c in range(NCORES):
            b, hp = c // 4, c % 4
            Ab = np.zeros((128, T, 2), np.float32); Rb = np.zeros((128, T, 2), np.float32); Wt = np.zeros((128, T), np.float32)
            BK = np.zeros((4, T, 128), np.float32); VV = np.zeros((4, T, 64), np.float32)
            for ci in range(2):
                h = 2 * hp + ci
                sl = slice(h * 64, (h + 1) * 64)
                Ab[ci * 64:(ci + 1) * 64, :, ci] = an[b, :, sl].T
                Rb[ci * 64:(ci + 1) * 64, :, ci] = r[b, :, sl].T
                Wt[ci * 64:(ci + 1) * 64, :] = w[b, :, sl].T
                BK[ci, :, ci * 64:(ci + 1) * 64] = bb[b, :, sl]
                BK[2 + ci, :, ci * 64:(ci + 1) * 64] = km[b, :, sl]
                VV[2 + ci] = v[b, :, sl]
            maps.append({"Ab": Ab, "Rb": Rb, "Wt": Wt, "BK": BK, "VV": VV})
        res = run_spmd(nc, maps)
        y = np.empty((2, 8192, 512), np.float32)
        for c in range(NCORES):
            b, hp = c // 4, c % 4
            yy = res[c]["y"][:, 256:]
            if d_ == 1:
                yy = yy[:, ::-1]
            for ci in range(2):
                h = 2 * hp + ci
                y[b, :, h * 64:(h + 1) * 64] = yy[ci]
        ys.append(y)
    return ys[0], ys[1]


def build_mlf(ntok, tile, C=None, with_q=True, with_kv=True):
    C = C or Ctx()
    p_d = C.din("p", [640, ntok]); wuq_d = C.din("wuq", [256, 1536]); wukv_d = C.din("wukv", [128, 1024])
    gq_d = C.din("gq", [128, 3]); ones_d = C.din("ones", [128, 128])
    cq_d = C.din("cq", [96, ntok]); sq_d = C.din("sq", [96, ntok])
    q_o = C.dout("q", [96, 8, ntok]); kn_o = C.dout("kn", [64, 8, ntok]); kpe_o = C.dout("kpe", [32, ntok]); v_o = C.dout("v", [ntok, 512])
    wuq = C.sb("wuq", [128, 2, 1536], BF16); wukv = C.sb("wukv", [128, 1, 1024], BF16); stage = C.sb("stage", [128, 2, 1536])
    gq = C.sb("gq", [128, 3]); ones = C.sb("ones", [128, 128])
    pt = C.sb("pt", [128, 5, tile]); sqq = C.sb("sqq", [128, 3, tile]); rs = C.sb("rs", [128, 2, tile])
    qn = C.sb("qn", [128, 3, tile], BF16)
    cq = C.sb("cq", [96, tile]); sq = C.sb("sq", [96, tile])
    qo = C.sb("qo", [96, 8, tile]); qs = C.sb("qs", [96, 2, tile]); ko = C.sb("ko", [64, 8, tile]); kp = C.sb("kp", [32, 2, tile]); vo = C.sb("vo", [128, 2, 512])
    pss = [C.ps("ps%d" % i, [128, 512]) for i in range(8)]
    P = C.start()
    pctr = [0]

    def newps():
        i = pctr[0] % 8; pctr[0] += 1
        return pss[i], ("ps", i)
    P.dma("sp", gq[:], gq_d[:, :], writes=["gq"]); P.dma("sp", ones[:], ones_d[:, :], writes=["ones"])
    load_w_bf16(C, P, wuq, "wuq", wuq_d, 256, 1536, stage, "stage")
    load_w_bf16(C, P, wukv, "wukv", wukv_d, 128, 1024, stage, "stage")
    pv = p_d.rearrange("(kc p) t -> p kc t", p=128)
    for t0 in range(0, ntok, tile):
        nt = tile
        P.dma("sp", pt[:, :, :], pv[:, :, t0:t0 + nt], writes=["pt"])
        P.dma("sp", cq[:], cq_d[:, t0:t0 + nt], writes=["cq"]); P.dma("sp", sq[:], sq_d[:, t0:t0 + nt], writes=["sq"])
        P.op("act", lambda a: a.activation(sqq[:, :, :], pt[:, 0:3, :], AF.Square), reads=["pt"], writes=["sqq"])
        for (ri, chs, n) in ((0, (0, 1), 256.0), (1, (2,), 128.0)):
            ps, pk = newps()
            for i, ch in enumerate(chs):
                P.op("pe", lambda t, ps=ps, ch=ch, i=i, chs=chs: t.matmul(ps[:, 0:nt], ones[:, :], sqq[:, ch, :], start=(i == 0), stop=(i == len(chs) - 1)), reads=["ones", "sqq"], writes=[pk])
            P.op("dve", lambda v, ps=ps, ri=ri, n=n: v.tensor_scalar(rs[:, ri, :], ps[:, 0:nt], 1.0 / n, 1e-6, ALU.mult, ALU.add), reads=[pk], writes=[("rs", ri)])
            P.op("act", lambda a, ri=ri: a.activation(rs[:, ri, :], rs[:, ri, :], AF.Sqrt), reads=[("rs", ri)], writes=[("rs", ri)])
            P.op("dve", lambda v, ri=ri: v.reciprocal(rs[:, ri, :], rs[:, ri, :]), reads=[("rs", ri)], writes=[("rs", ri)])
            for ch in chs:
                P.op("dve", lambda v, ch=ch, ri=ri: v.scalar_tensor_tensor(qn[:, ch, :], pt[:, ch, :], gq[:, ch:ch + 1], rs[:, ri, :], ALU.mult, ALU.mult), reads=["pt", "gq", ("rs", ri)], writes=[("qn", ch)])
        for h in range(8 if with_q else 0):
            ps, pk = newps()
            for kc in range(2):
                P.op("pe", lambda t, ps=ps, h=h, kc=kc: t.matmul(ps[0:96, 0:nt], wuq[:, kc, h * 96:(h + 1) * 96], qn[:, kc, :], start=(kc == 0), stop=(kc == 1)), reads=[("wuq", kc), ("qn", kc)], writes=[pk])
            P.op("act", lambda a, ps=ps, h=h: a.copy(qo[:, h, :], ps[0:96, 0:nt]), reads=[pk], writes=[("qo", h)])
            ps2, pk2 = newps()
            for kc in range(2):
                P.op("pe", lambda t, ps2=ps2, h=h, kc=kc: t.matmul(ps2[0:96, 0:nt], wuq[:, kc, 768 + h * 96:768 + (h + 1) * 96], qn[:, kc, :], start=(kc == 0), stop=(kc == 1)), reads=[("wuq", kc), ("qn", kc)], writes=[pk2])
            sb_ = h % 2
            P.op("dve", lambda v, ps2=ps2, sb_=sb_: v.tensor_tensor(qs[64:96, sb_, :], ps2[64:96, 0:nt], sq[64:96, :], ALU.mult), reads=[pk2, "sq"], writes=[("qs", sb_)])
            P.op("dve", lambda v, h=h: v.tensor_tensor(qo[64:96, h, :], qo[64:96, h, :], cq[64:96, :], ALU.mult), reads=[("qo", h), "cq"], writes=[("qo", h)])
            P.op("dve", lambda v, h=h, sb_=sb_: v.tensor_tensor(qo[64:96, h, :], qo[64:96, h, :], qs[64:96, sb_, :], ALU.add), reads=[("qo", h), ("qs", sb_)], writes=[("qo", h)])
        for h in range(8 if with_kv else 0):
            ps3, pk3 = newps()
            P.op("pe", lambda t, ps3=ps3, h=h: t.matmul(ps3[0:64, 0:nt], wukv[:, 0, h * 64:(h + 1) * 64], qn[:, 2, :], start=True, stop=True), reads=[("wukv", 0), ("qn", 2)], writes=[pk3])
            P.op("act" if h % 2 else "dve", (lambda a, ps3=ps3, h=h: a.copy(ko[:, h, :], ps3[0:64, 0:nt])) if h % 2 else (lambda v, ps3=ps3, h=h: v.tensor_copy(ko[:, h, :], ps3[0:64, 0:nt])), reads=[pk3], writes=[("ko", h)])
        P.op("dve", lambda v: v.tensor_tensor(kp[:, 0, :], pt[0:32, 3, :], cq[0:32, :], ALU.mult), reads=["pt", "cq"], writes=[("kp", 0)])
        P.op("dve", lambda v: v.tensor_tensor(kp[:, 1, :], pt[0:32, 4, :], sq[0:32, :], ALU.mult), reads=["pt", "sq"], writes=[("kp", 1)])
        P.op("dve", lambda v: v.tensor_tensor(kp[:, 0, :], kp[:, 0, :], kp[:, 1, :], ALU.add), reads=[("kp", 0), ("kp", 1)], writes=[("kp", 0)])
        for sub in range(nt // 128 if with_kv else 0):
            ps4, pk4 = newps()
            vb_ = sub % 2
            P.op("pe", lambda t, ps4=ps4, sub=sub: t.matmul(ps4[:, 0:512], qn[:, 2, sub * 128:(sub + 1) * 128], wukv[:, 0, 512:1024], start=True, stop=True), reads=[("wukv", 0), ("qn", 2)], writes=[pk4])
            P.op("act", lambda a, ps4=ps4, vb_=vb_: a.copy(vo[:, vb_, :], ps4[:, 0:512]), reads=[pk4], writes=[("vo", vb_)])
            P.dma("sp", v_o[t0 + sub * 128:t0 + (sub + 1) * 128, :], vo[:, vb_, :], reads=[("vo", vb_)])
        if with_q:
            P.dma("sp", q_o[:, :, t0:t0 + nt], qo[:, :, :], reads=[("qo", h) for h in range(8)])
        if with_kv:
            P.dma("sp", kn_o[:, :, t0:t0 + nt], ko[:, :, :], reads=[("ko", h) for h in range(8)])
            P.dma("sp", kpe_o[:, t0:t0 + nt], kp[:, 0, :], reads=[("kp", 0)])
    return C.end()


def build_mla(C=None, split_k=False):
    NQ, NKT = 2048, 66
    C = C or Ctx()
    q_d = C.din("q", [96, 8, NQ]); v_d = C.din("v", [128, NKT, 8, 64])
    if split_k:
        kn_d = C.din("kn", [64, 8, NKT * 128]); kpe_d = C.din("kpe", [32, NKT * 128])
    else:
        k_d = C.din("k", [8, 96, NKT * 128])
    o_d = C.dout("o", [NQ, 512])
    qf = C.sb("qf", [96, 2, NQ]); qb = C.sb("qb", [96, 2, NQ], BF16)
    kf = C.sb("kf", [96, 2, 2112]); kb = C.sb("kb", [96, 2, NKT * 128], BF16)
    vf = C.sb("vf", [128, NKT, 64]); va = C.sb("va", [128, 2, NKT, 65], BF16)
    pt = C.sb("pt", [128, 4, 512], BF16); den = C.sb("den", [128, 2, 4]); osb = C.sb("osb", [128, 16, 512])
    pss = [C.ps("ps%d" % i, [128, 512]) for i in range(8)]
    P = C.start()
    P.op("pool", lambda v: v.memset(va[:], 1.0), writes=[("va", 0), ("va", 1)])
    it = 0; sc = 0; oc = 0
    for h in range(8):
        hb = h % 2
        P.dma("sp", qf[:, hb, :], q_d[:, h, :], writes=[("qf", hb)])
        P.op("dve", lambda v, hb=hb: v.tensor_copy(qb[:, hb, :], qf[:, hb, :]), reads=[("qf", hb)], writes=[("qb", hb)])
        for part in range(4):
            pb_ = part % 2
            if split_k:
                P.dma("sp", kf[0:64, pb_, :], kn_d[:, h, part * 2112:(part + 1) * 2112], writes=[("kf", pb_)])
                P.dma("sp", kf[64:96, pb_, :], kpe_d[:, part * 2112:(part + 1) * 2112], writes=[("kf", pb_)])
            else:
                P.dma("sp", kf[:, pb_, :], k_d[h, :, part * 2112:(part + 1) * 2112], writes=[("kf", pb_)])
            P.op("pool" if part % 2 else "dve", lambda v, hb=hb, pb_=pb_, part=part: v.tensor_copy(kb[:, hb, part * 2112:(part + 1) * 2112], kf[:, pb_, :]), reads=[("kf", pb_)], writes=[("kb", hb)])
        P.dma("sp", vf[:, :, :], v_d[:, :, h, :], writes=["vf"])
        P.op("dve", lambda v, hb=hb: v.tensor_copy(va[:, hb, :, 0:64], vf[:, :, :]), reads=["vf"], writes=[("va", hb)])
        for qt in range(4):
            ops = pss[4 + oc % 4]; opk = ("ps", 4 + oc % 4); oc += 1
            def emit_qk(kt, hb=hb, qt=qt):
                nonlocal sc
                ps = pss[sc % 4]; pk = ("ps", sc % 4); sc += 1
                P.op("pe", lambda t, ps=ps, hb=hb, kt=kt, qt=qt: t.matmul(ps[:, 0:512], kb[:, hb, kt * 128:(kt + 1) * 128], qb[:, hb, qt * 512:(qt + 1) * 512], start=True, stop=True), reads=[("kb", hb), ("qb", hb)], writes=[pk])
                return ps, pk
            pend = [emit_qk(0), emit_qk(1)]
            for kt in range(NKT):
                ps, pk = pend.pop(0)
                if kt + 2 < NKT:
                    pend.append(emit_qk(kt + 2))
                pb = it % 4; it += 1
                P.op("act", lambda a, ps=ps, pb=pb: a.activation(pt[:, pb, :], ps[:, 0:512], AF.Exp, scale=96.0 ** -0.5), reads=[pk], writes=[("pt", pb)])
                for j in range(4):
                    P.op("pe", lambda t, ops=ops, pb=pb, j=j, hb=hb, kt=kt: t.matmul(ops[:, j * 65:(j + 1) * 65], pt[:, pb, j * 128:(j + 1) * 128], va[:, hb, kt, :], start=(kt == 0 and j == 0), stop=(kt == NKT - 1 and j == 3)),
                         reads=[("pt", pb), ("va", hb)], writes=[opk])
            db = qt % 2
            P.op("dve", lambda v, ops=ops, db=db: v.reciprocal(den[:, db, :], ops[:, 64:260:65]), reads=[opk], writes=[("den", db)])
            for j in range(4):
                P.op("dve", lambda v, ops=ops, db=db, j=j, qt=qt, h=h: v.tensor_scalar_mul(osb[:, qt * 4 + j, h * 64:(h + 1) * 64], ops[:, j * 65:j * 65 + 64], den[:, db, j:j + 1]), reads=[opk, ("den", db)], writes=[("osb", qt * 4 + j)])
    P.dma("sp", o_d.rearrange("(n p) c -> p n c", p=128), osb[:, :, :], reads=[("osb", i) for i in range(16)])
    return C.end()


def build_k4a1(ntok, tile, C=None):
    C = C or Ctx()
    W = C.din("w", [1024, 1024]); lnd = C.din("ln", [128, 2, 8]); ones_d = C.din("ones", [128, 128]); bo_d = C.din("bo", [128, 128]); sc_d = C.din("sc", [128, 12])
    x_d = C.din("x0", [1024, ntok]); mod_d = C.din("mod0", [128, 6, 8]); out = C.dout("o0", [1024, ntok])
    ins = {n: C.din(n, [512, ntok]) for n in ("yf", "yb", "r", "km", "v", "g", "at")}
    TM = tile
    wb = C.sb("wb", [128, 8, 1024], BF16); stage = C.sb("stage", [128, 2, 1024])
    xt = C.sb("xt", [128, 8, TM]); mix = C.sb("mix", [128, 8, TM], BF16)
    tl = {n: C.sb("t_" + n, [128, 4, TM]) for n in ins}
    z = C.sb("z", [128, 8, TM]); zsq = C.sb("zsq", [128, 8, TM]); ost = C.sb("ost", [128, 2, TM])
    ones = C.sb("ones", [128, 128]); bo = C.sb("bo", [128, 128]); lnt = C.sb("lnt", [128, 2, 8]); sc = C.sb("sc", [128, 12]); mt = C.sb("mt", [128, 1, 6, 8])
    pss = [C.ps("ps%d" % i, [128, 512]) for i in range(8)]
    P = C.start()
    pctr = [0]

    def newps():
        i = pctr[0] % 8; pctr[0] += 1
        return pss[i], ("ps", i)
    P.dma("sp", ones[:], ones_d[:, :], writes=["ones"]); P.dma("sp", bo[:], bo_d[:, :], writes=["bo"]); P.dma("sp", lnt[:], lnd[:, :, :], writes=["ln"])
    P.dma("sp", sc[:], sc_d[:, :], writes=["sc"]); P.dma("sp", mt[:, 0, :, :], mod_d[:, :, :], writes=[("mt", 0)])
    load_w_bf16(C, P, wb, "wb", W, 1024, 1024, stage, "stage")
    v3 = lambda ap: ap.rearrange("(kc p) t -> p kc t", p=128)
    for t0 in range(0, ntok, tile):
        nt = tile
        P.dma("sp", xt[:, :, :], v3(x_d)[:, :, t0:t0 + nt], writes=["xt"])
        for n in ins:
            P.dma("sp", tl[n][:, :, :], v3(ins[n])[:, :, t0:t0 + nt], writes=[n])
        y, yb_, r_, km_, v_, g_, at_ = (tl[n] for n in ("yf", "yb", "r", "km", "v", "g", "at"))
        P.op("dve", lambda v: v.tensor_tensor(y[:, :, :], y[:, :, :], yb_[:, :, :], ALU.add), reads=["yf", "yb"], writes=["yf"])
        P.op("act", lambda a: a.activation(yb_[:, :, :], y[:, :, :], AF.Square), reads=["yf"], writes=["yb"])
        P.op("pool", lambda v: v.tensor_copy(mix[:, 4:8, :], at_[:, :, :]), reads=["at"], writes=[("mix", k) for k in range(4, 8)])
        P.op("pool", lambda v: v.tensor_tensor(r_[:, :, :], r_[:, :, :], km_[:, :, :], ALU.mult), reads=["r", "km"], writes=["r"])
        for c in range(4):
            psm, pkm = newps(); psq, pkq = newps(); psb, pkb = newps()
            P.op("pe", lambda t, psm=psm, c=c: t.matmul(psm[:, 0:nt], bo[:, :], y[:, c, :], start=True, stop=True), reads=["bo", "yf"], writes=[pkm])
            P.op("pe", lambda t, psq=psq, c=c: t.matmul(psq[:, 0:nt], bo[:, :], yb_[:, c, :], start=True, stop=True), reads=["bo", "yb"], writes=[pkq])
            P.op("dve", lambda v, c=c: v.tensor_scalar_mul(km_[:, c, :], r_[:, c, :], sc[:, c:c + 1]), reads=["r", "sc"], writes=[("km", c)])
            P.op("pe", lambda t, psb=psb, c=c: t.matmul(psb[:, 0:nt], bo[:, :], km_[:, c, :], start=True, stop=True), reads=["bo", ("km", c)], writes=[pkb])
            mean, var = ost[:, 0, :], ost[:, 1, :]
            P.op("dve", lambda v, psm=psm: v.tensor_scalar_mul(mean, psm[:, 0:nt], 1.0 / 64.0), reads=[pkm], writes=[("ost", 0)])
            P.op("dve", lambda v: v.tensor_tensor(var, mean, mean, ALU.mult), reads=[("ost", 0)], writes=[("ost", 1)])
            P.op("dve", lambda v, psq=psq: v.scalar_tensor_tensor(var, psq[:, 0:nt], 1.0 / 64.0, var, ALU.mult, ALU.subtract), reads=[pkq, ("ost", 1)], writes=[("ost", 1)])
            P.op("dve", lambda v: v.tensor_scalar_add(var, var, 6.4e-4), reads=[("ost", 1)], writes=[("ost", 1)])
            P.op("act", lambda a: a.activation(var, var, AF.Sqrt), reads=[("ost", 1)], writes=[("ost", 1)])
            P.op("dve", lambda v: v.reciprocal(var, var), reads=[("ost", 1)], writes=[("ost", 1)])
            P.op("dve", lambda v, c=c: v.tensor_tensor(zsq[:, c, :], y[:, c, :], mean, ALU.subtract), reads=["yf", ("ost", 0)], writes=[("zsq", c)])
            P.op("dve", lambda v, c=c: v.tensor_tensor(zsq[:, c, :], zsq[:, c, :], var, ALU.mult), reads=[("zsq", c), ("ost", 1)], writes=[("zsq", c)])
            P.op("dve", lambda v, c=c: v.tensor_scalar(zsq[:, c, :], zsq[:, c, :], sc[:, 4 + c:5 + c], sc[:, 8 + c:9 + c], ALU.mult, ALU.add), reads=[("zsq", c), "sc"], writes=[("zsq", c)])
            P.op("dve", lambda v, psb=psb, c=c: v.tensor_tensor(z[:, c, :], psb[:, 0:nt], v_[:, c, :], ALU.mult), reads=[pkb, "v"], writes=[("z", c)])
            P.op("dve", lambda v, c=c: v.tensor_tensor(zsq[:, c, :], zsq[:, c, :], z[:, c, :], ALU.add), reads=[("zsq", c), ("z", c)], writes=[("zsq", c)])
            P.op("dve", lambda v, c=c: v.tensor_tensor(mix[:, c, :], zsq[:, c, :], g_[:, c, :], ALU.mult), reads=[("zsq", c), "g"], writes=[("mix", c)])
        for dc in range(8):
            ps, pk = newps()
            for kc in range(8):
                P.op("pe", lambda t, ps=ps, kc=kc, dc=dc: t.matmul(ps[:, 0:nt], wb[:, kc, dc * 128:(dc + 1) * 128], mix[:, kc, :], start=(kc == 0), stop=(kc == 7)), reads=[("wb", kc), ("mix", kc)], writes=[pk])
            P.op("dve", lambda v, dc=dc: v.tensor_scalar_mul(z[:, dc, :], xt[:, dc, :], ALPHA), reads=["xt"], writes=[("z", dc)])
            P.op("dve", lambda v, ps=ps, dc=dc: v.scalar_tensor_tensor(z[:, dc, :], ps[:, 0:nt], mt[:, 0, 2, dc:dc + 1], z[:, dc, :], ALU.mult, ALU.add), reads=[pk, ("z", dc), ("mt", 0)], writes=[("z", dc)])
        ln_tail(P, nt, z, zsq, ones, newps, lnt[:, 0, :], lnt[:, 1, :], out, (t0, t0 + nt), ost, None)
    return C.end()


def cd_w_in_layout(w):
    z64 = np.zeros((1024, 64), np.float32); z96 = np.zeros((1024, 96), np.float32)
    kpe = w[:, 2240:2272]
    sw = np.array([d + 8 if (d % 16) < 8 else d - 8 for d in range(32)])
    return np.ascontiguousarray(np.concatenate([w[:, 0:1536], w[:, 1536:1664], w[:, 1664:1728], z64, w[:, 1728:1856],
                                                w[:, 1856:2112], w[:, 2112:2240], kpe, z96, kpe[:, sw], z96], 1))


def rope32_tables(pos_tok, rope):
    n = len(pos_tok)
    if not rope:
        return np.ones((32, n), np.float32), np.zeros((32, n), np.float32)
    row = (pos_tok // 64).astype(np.float32); col = (pos_tok % 64).astype(np.float32)
    inv = (10000.0 ** (-np.arange(8, dtype=np.float32) / np.float32(8))).astype(np.float32)
    ar = row[None, :] * inv[:, None]; ac = col[None, :] * inv[:, None]
    Cc = np.concatenate([np.cos(ar), np.cos(ar), np.cos(ac), np.cos(ac)], 0).astype(np.float32)
    Ss = np.concatenate([-np.sin(ar), np.sin(ar), -np.sin(ac), np.sin(ac)], 0).astype(np.float32)
    return Cc, Ss


def stage_mla(pl, pc, I):
    wuq = I['mla_w_uq'][0]; wukv = I['mla_w_ukv'][0]
    sw = np.array([d + 8 if (d % 16) < 8 else d - 8 for d in range(32)])
    wuq_sw = wuq.reshape(256, 8, 96).copy()
    wuq_sw[:, :, 64:96] = wuq_sw[:, :, 64:96][:, :, sw]
    wuq_all = np.ascontiguousarray(np.concatenate([wuq, wuq_sw.reshape(256, 768)], 1))
    wk = wukv.reshape(128, 8, 128)
    wukv_all = np.ascontiguousarray(np.concatenate([wk[:, :, :64].reshape(128, 512), wk[:, :, 64:].reshape(128, 512)], 1))
    gq = np.stack([I['mla_q_norm'][0][:128], I['mla_q_norm'][0][128:], I['mla_kv_norm'][0]], 1).astype(np.float32)
    ones = np.ones((128, 128), np.float32)
    ncf = build_mlf(2048, 256); ncc = build_mlf(64, 64)
    def tabs(pos, rope):
        c32, s32 = rope32_tables(pos, rope)
        cq = np.ones((96, len(pos)), np.float32); sq = np.zeros((96, len(pos)), np.float32)
        cq[64:96] = c32; sq[64:96] = s32; cq[0:32] = c32; sq[0:32] = s32
        return cq, sq
    mapsl, mapsc = [], []
    for c in range(NCORES):
        b, s = c // 4, c % 4
        cq, sq = tabs(np.arange(s * 2048, (s + 1) * 2048), True)
        mapsl.append({"p": np.ascontiguousarray(pl[b, s * 2048:(s + 1) * 2048, 1920:2560].T), "wuq": wuq_all, "wukv": wukv_all, "gq": gq, "ones": ones, "cq": cq, "sq": sq})
        cq, sq = tabs(np.arange(64), False)
        mapsc.append({"p": np.ascontiguousarray(pc[b, s * 64:(s + 1) * 64, 1920:2560].T), "wuq": wuq_all, "wukv": wukv_all, "gq": gq, "ones": ones, "cq": cq, "sq": sq})
    rl = run_spmd(ncf, mapsl); rc = run_spmd(ncc, mapsc)
    KT = np.empty((2, 8, 96, 8448), np.float32); V = np.empty((2, 8448, 512), np.float32)
    for c in range(NCORES):
        b, s = c // 4, c % 4
        sl = slice(s * 2048, (s + 1) * 2048); sc_ = slice(8192 + s * 64, 8192 + (s + 1) * 64)
        KT[b, :, 0:64, sl] = rl[c]["kn"].transpose(1, 0, 2); KT[b, :, 64:96, sl] = rl[c]["kpe"][None]
        KT[b, :, 0:64, sc_] = rc[c]["kn"].transpose(1, 0, 2); KT[b, :, 64:96, sc_] = rc[c]["kpe"][None]
        V[b, sl] = rl[c]["v"]; V[b, sc_] = rc[c]["v"]
    nca = build_mla()
    maps = []
    for c in range(NCORES):
        b = c // 4
        maps.append({"q": rl[c]["q"], "k": KT[b], "v": np.ascontiguousarray(V[b].reshape(66, 128, 8, 64).transpose(1, 0, 2, 3))})
    ra = run_spmd(nca, maps)
    att = np.empty((2, 8192, 512), np.float32)
    for c in range(NCORES):
        b, s = c // 4, c % 4
        att[b, s * 2048:(s + 1) * 2048] = ra[c]["o"]
    return att


def stage_k4a1(x, m_l, yf, yb, F, att, I):
    nc = build_k4a1(2048, 256)
    ones = np.ones((128, 128), np.float32)
    t = np.arange(128)
    bo = ((t[:, None] // 64) == (t[None, :] // 64)).astype(np.float32)
    col = lambda v: v.reshape(4, 128).T
    sc = np.ascontiguousarray(np.concatenate([col(I['rwkv_rk'][0].reshape(512)), col(I['rwkv_lnx_g'][0]), col(I['rwkv_lnx_b'][0])], 1).astype(np.float32))
    maps = []
    for c in range(NCORES):
        maps.append({"w": I['cd_w_out'][0], "ln": ln_layout(I['ln_g'][1, 0], I['ln_b'][1, 0]), "ones": ones, "bo": bo, "sc": sc,
                     "x0": shard_tok(x, c), "mod0": mod_layout(m_l[c // 4]),
                     "yf": shard_tok(yf, c), "yb": shard_tok(yb, c), "r": shard_tok(F["r"][:, 256:], c), "km": shard_tok(F["km"][:, 256:], c),
                     "v": shard_tok(F["v"][:, 256:], c), "g": shard_tok(F["g"][:, 256:], c), "at": shard_tok(att, c)})
    res = run_spmd(nc, maps)
    return unshard(res, "o0", None, 1024)[0]


def kernel(**I):
    I = {k: np.asarray(v, dtype=np.float32) for k, v in I.items()}
    m = stage_k0(I['c'], I['c_ctx'], I['ada_w'], I['ada_b'])
    pl, pc = stage_k1(I['x'], I['ctx'], m[0], aug_w_in_ab(I['ab_w_in'][0]))
    o_f, o_b = stage_gla(pl, pc, I['gla_a2'][0], I['gla_ab'][0])
    att, attc = stage_gqa(pl, pc, I['gqa_sink'][0])
    xa, ha = stage_k4a0(I['x'], I['ctx'], m[0], o_f, o_b, pl, pc, att, attc, I['ab_w_out'][0], I['gla_norm'][0], I['ln_g'][0, 0], I['ln_b'][0, 0])
    xb, hb = stage_k4b(xa, ha, m[0], I['mlp_w1'][0], I['mlp_w2'][0], I['ln_g'][0, 1], I['ln_b'][0, 1])
    pl1, pc1 = stage_k1(xb, hb, m[1], cd_w_in_layout(I['cd_w_in'][0]))
    F = stage_rwf(pl1, pc1, I)
    yf, yb = stage_rws(F)
    att1 = stage_mla(pl1, pc1, I)
    xa1 = stage_k4a1(xb, m[1], yf, yb, F, att1, I)
    xb1, _ = stage_k4b(xa1, None, m[1], I['mlp_w1'][1], I['mlp_w2'][1], I['ln_g'][1, 1], I['ln_b'][1, 1], with_ctx=False)
    return xb1


GROUPS = [[0, 1, 2, 3], [4, 5, 6, 7]]


def build_k0f(C):
    cT = C.din("cT", [128, 8, 2]); w = C.din("w", [2, 1024, 6144]); bia = C.din("b", [2, 2, 6144]); out = C.dout("out", [2, 2, 6144])
    NB = 768
    cs = C.sb("cs", [128, 8, 2]); cs2 = C.sb("cs2", [128, 8, 2]); wt = C.sb("wt", [128, 2, 8, NB]); bt = C.sb("bt", [2, 2, 6144]); ot = C.sb("ot", [2, 2, 6144])
    pss = [C.ps("ps%d" % i, [128, 512]) for i in range(2)]
    P = C.start()
    P.dma("sp", cs[:], cT[:, :, :], writes=["cs"]); P.dma("sp", bt[:], bia[:, :, :], writes=["bt"])
    P.op("act", lambda a: a.activation(cs2[:], cs[:], AF.Silu), reads=["cs"], writes=["cs2"])
    i = 0; bi = 0
    for l in range(2):
        for blk in range(6144 // NB):
            b = bi % 2; bi += 1
            P.dma("sp", wt[:, b, :, :], w[l][:, blk * NB:(blk + 1) * NB].rearrange("(kc p) n -> p kc n", p=128), writes=[("wt", b)])
            for n0 in range(0, NB, 384):
                ps = pss[i % 2]; pk = ("ps", i % 2); i += 1
                for kc in range(8):
                    P.op("pe", lambda t, kc=kc, b=b, n0=n0, ps=ps: t.matmul(ps[0:2, 0:384], cs2[:, kc, :], wt[:, b, kc, n0:n0 + 384], start=(kc == 0), stop=(kc == 7)), reads=["cs2", ("wt", b)], writes=[pk])
                g0 = blk * NB + n0
                P.op("dve", lambda v, l=l, g0=g0, ps=ps: v.tensor_tensor(ot[:, l, g0:g0 + 384], ps[0:2, 0:384], bt[:, l, g0:g0 + 384], ALU.add), reads=[pk, "bt"], writes=["ot"])
    P.dma("sp", out[:, :, :], ot[:], reads=["ot"])
    return C.end()


def build_k1x(C, ncols, nfm, tm, segs):
    W = C.din("w", [1024, ncols])
    xs, mods, outs, outt = [], [], [], []
    for si, (ntok, tile) in enumerate(segs):
        xs.append(C.bind["xv%d" % si] if ("xv%d" % si) in C.bind else C.din("x%d" % si, [1024, ntok]).rearrange("(kc p) t -> p kc t", p=128))
        mods.append(C.din("mod%d" % si, [128, 6, 8])); outs.append(C.dout("p%d" % si, [nfm, ntok]))
        outt.append(C.dout("pt%d" % si, [ntok, tm[1] - tm[0]]) if tm else None)
    TM = max(t for _, t in segs)
    NCH = (nfm + 127) // 128
    wb = C.sb("wb", [128, 8, ncols], BF16); stage = C.sb("stage", [128, 2, ncols])
    xt = C.sb("xt", [128, 2, 8, TM]); ub = C.sb("ub", [128, 2, 8, TM], BF16); ost = C.sb("ost", [128, 4, 512])
    mt = C.sb("mt", [128, len(segs), 6, 8]); sc1 = C.sb("sc1", [128, len(segs), 8])
    pss = [C.ps("ps%d" % i, [128, 512]) for i in range(4)]
    P = C.start()
    for si in range(len(segs)):
        P.dma("sp", mt[:, si, :, :], mods[si][:, :, :], writes=[("mt", si)])
        P.op("dve", lambda v, si=si: v.tensor_scalar_add(sc1[:, si, :], mt[:, si, 1, :], 1.0), reads=[("mt", si)], writes=[("sc1", si)])
    load_w_bf16(C, P, wb, "wb", W, 1024, ncols, stage, "stage")
    it = 0; oi = 0
    flat = [(si, t0, tile) for si, (ntok, tile) in enumerate(segs) for t0 in range(0, ntok, tile)]

    def issue_load(idx):
        si_, t0_, tile_ = flat[idx]
        P.dma("sp", xt[:, idx % 2, :, 0:tile_], xs[si_][:, :, t0_:t0_ + tile_], writes=[("xt", idx % 2)])
    issue_load(0)
    for idx, (si, t0, tile) in enumerate(flat):
        if True:
            b = it % 2; it += 1
            if idx + 1 < len(flat):
                issue_load(idx + 1)
            for kc in range(8):
                P.op(P.alt(), lambda v, b=b, kc=kc, si=si, tile=tile: v.tensor_scalar(ub[:, b, kc, 0:tile], xt[:, b, kc, 0:tile], sc1[:, si, kc:kc + 1], mt[:, si, 0, kc:kc + 1], ALU.mult, ALU.add),
                     reads=[("xt", b), ("sc1", si), ("mt", si)], writes=[("ub", b, kc)])
            for ch in range(NCH):
                c0 = ch * 128; cn = min(128, nfm - c0)
                pi = oi % 4; ob = oi % 4; oi += 1
                ps = pss[pi]
                for kc in range(8):
                    P.op("pe", lambda t, ps=ps, cn=cn, c0=c0, kc=kc, b=b, tile=tile: t.matmul(ps[0:cn, 0:tile], wb[:, kc, c0:c0 + cn], ub[:, b, kc, 0:tile], start=(kc == 0), stop=(kc == 7)),
                         reads=[("wb", kc), ("ub", b, kc)], writes=[("ps", pi)])
                P.op("act" if oi % 2 else "dve", (lambda a, ps=ps, cn=cn, ob=ob, tile=tile: a.copy(ost[0:cn, ob, 0:tile], ps[0:cn, 0:tile])) if oi % 2 else
                     (lambda v, ps=ps, cn=cn, ob=ob, tile=tile: v.tensor_copy(ost[0:cn, ob, 0:tile], ps[0:cn, 0:tile])), reads=[("ps", pi)], writes=[("ost", ob)])
                P.dma("sp", outs[si][c0:c0 + cn, t0:t0 + tile], ost[0:cn, ob, 0:tile], reads=[("ost", ob)])
            if tm:
                nt_ = tm[1] - tm[0]
                for s0 in range(0, tile, 128):
                    sn = min(128, tile - s0)
                    pi = oi % 4; ob = oi % 4; oi += 1
                    ps = pss[pi]
                    for kc in range(8):
                        P.op("pe", lambda t, ps=ps, kc=kc, b=b, s0=s0, sn=sn: t.matmul(ps[0:sn, 0:nt_], ub[:, b, kc, s0:s0 + sn], wb[:, kc, tm[0]:tm[1]], start=(kc == 0), stop=(kc == 7)),
                             reads=[("wb", kc), ("ub", b, kc)], writes=[("ps", pi)])
                    P.op("act", lambda a, ps=ps, ob=ob, sn=sn: a.copy(ost[0:sn, ob, 0:nt_], ps[0:sn, 0:nt_]), reads=[("ps", pi)], writes=[("ost", ob)])
                    P.dma("sp", outt[si][t0 + s0:t0 + s0 + sn, :], ost[0:sn, ob, 0:nt_], reads=[("ost", ob)])
    return C.end()


def build_tr(C, R, Cc, rev_src=False, plain2=False):
    src = C.din("src", [R, Cc]); idn = C.din("idn", [128, 128]); dst = C.dout("dst", [Cc, R])
    src2 = C.din("src2", [R, Cc]) if rev_src else None
    jd = C.din("jdn", [128, 128]) if rev_src else None
    NCc = Cc // 128
    st = C.sb("st", [128, 2, 4, Cc]); st2 = C.sb("st2", [128, 2, 4, Cc]) if rev_src else None
    idt = C.sb("idt", [128, 128]); jt = C.sb("jt", [128, 128]) if rev_src else None
    ot = C.sb("ot", [128, 4, 512])
    pss = [C.ps("ps%d" % i, [128, 512]) for i in range(4)]
    P = C.start()
    P.dma("sp", idt[:], idn[:, :], writes=["idt"])
    if rev_src:
        P.dma("sp", jt[:], jd[:, :], writes=["jt"])
    oi = 0
    for g, r0 in enumerate(range(0, R, 512)):
        nr = min(4, (R - r0) // 128)
        b = g % 2
        P.dma("sp", st[:, b, 0:nr, :], src[r0:r0 + nr * 128, :].rearrange("(n p) c -> p n c", p=128), writes=[("st", b)])
        if rev_src:
            if plain2:
                P.dma("sp", st2[:, b, 0:nr, :], src2[r0:r0 + nr * 128, :].rearrange("(n p) c -> p n c", p=128), writes=[("st2", b)])
            else:
                P.dma("sp", st2[:, b, 0:nr, :], src2[R - r0 - nr * 128:R - r0, :].rearrange("(n p) c -> p n c", p=128), writes=[("st2", b)])
        for c in range(NCc):
            pi = oi % 4; oi += 1
            ps = pss[pi]
            for n in range(nr):
                P.op("pe", lambda t, ps=ps, b=b, n=n, c=c: t.matmul(ps[:, n * 128:(n + 1) * 128], st[:, b, n, c * 128:(c + 1) * 128], idt[:, :], start=True, stop=not rev_src),
                     reads=[("st", b), "idt"], writes=[("ps", pi)])
                if rev_src:
                    P.op("pe", lambda t, ps=ps, b=b, n=n, c=c: t.matmul(ps[:, n * 128:(n + 1) * 128], st2[:, b, (n if plain2 else nr - 1 - n), c * 128:(c + 1) * 128], jt[:, :], start=False, stop=True),
                         reads=[("st2", b), "jt"], writes=[("ps", pi)])
            P.op("act" if oi % 2 else "dve", (lambda a, ps=ps, pi=pi: a.copy(ot[:, pi, 0:nr * 128], ps[:, 0:nr * 128])) if oi % 2 else (lambda v, ps=ps, pi=pi: v.tensor_copy(ot[:, pi, 0:nr * 128], ps[:, 0:nr * 128])),
                 reads=[("ps", pi)], writes=[("ot", pi)])
            P.dma("sp", dst[c * 128:(c + 1) * 128, r0:r0 + nr * 128], ot[:, pi, 0:nr * 128], reads=[("ot", pi)])
    return C.end()


def build_glaepi(C, T=8448):
    oT = C.din("oT", [128, T]); ogT = C.din("ogT", [128, T]); gn_d = C.din("gn", [128, 1]); ones_d = C.din("ones", [128, 128]); W = C.din("w", [128, 1024])
    rs_in = C.dout("rs_in", [4096, 2112])
    wb = C.sb("wb", [128, 1, 1024], BF16); stage = C.sb("stage", [128, 2, 1024]); ones = C.sb("ones", [128, 128]); gn = C.sb("gn", [128, 1])
    o = C.sb("o", [128, 2, 512]); og = C.sb("og", [128, 2, 512]); sq = C.sb("sq", [128, 2, 512]); mixb = C.sb("mixb", [128, 2, 512], BF16); ost = C.sb("ost", [128, 4, 512])
    pss = [C.ps("ps%d" % i, [128, 512]) for i in range(8)]
    P = C.start()
    pctr = [0]

    def newps():
        i = pctr[0] % 8; pctr[0] += 1
        return pss[i], ("ps", i)
    P.dma("sp", ones[:], ones_d[:, :], writes=["ones"]); P.dma("sp", gn[:], gn_d[:, :], writes=["gn"])
    load_w_bf16(C, P, wb, "wb", W, 128, 1024, stage, "stage")
    tiles = [(i * 64, 64, i, 2048) for i in range(4)] + [(256 + n * 512, 512, n // 4, (n % 4) * 512) for n in range(16)]
    oi = 0
    for it, (s0, nt, rank, d0) in enumerate(tiles):
        b = it % 2
        P.dma("sp", o[:, b, 0:nt], oT[:, s0:s0 + nt], writes=[("o", b)]); P.dma("sp", og[:, b, 0:nt], ogT[:, s0:s0 + nt], writes=[("og", b)])
        P.op("act", lambda a, b=b, nt=nt: a.activation(sq[:, b, 0:nt], o[:, b, 0:nt], AF.Square), reads=[("o", b)], writes=[("sq", b)])
        P.op("act", lambda a, b=b, nt=nt: a.activation(og[:, b, 0:nt], og[:, b, 0:nt], AF.Silu), reads=[("og", b)], writes=[("og", b)])
        ps, pk = newps()
        P.op("pe", lambda t, ps=ps, b=b, nt=nt: t.matmul(ps[:, 0:nt], ones[:, :], sq[:, b, 0:nt], start=True, stop=True), reads=[("sq", b), "ones"], writes=[pk])
        P.op("dve", lambda v, ps=ps, b=b, nt=nt: v.tensor_scalar(sq[:, b, 0:nt], ps[:, 0:nt], 1.0 / 128.0, LN_EPS, ALU.mult, ALU.add), reads=[pk], writes=[("sq", b)])
        P.op("act", lambda a, b=b, nt=nt: a.activation(sq[:, b, 0:nt], sq[:, b, 0:nt], AF.Sqrt), reads=[("sq", b)], writes=[("sq", b)])
        P.op("dve", lambda v, b=b, nt=nt: v.reciprocal(sq[:, b, 0:nt], sq[:, b, 0:nt]), reads=[("sq", b)], writes=[("sq", b)])
        P.op("dve", lambda v, b=b, nt=nt: v.tensor_tensor(sq[:, b, 0:nt], sq[:, b, 0:nt], o[:, b, 0:nt], ALU.mult), reads=[("sq", b), ("o", b)], writes=[("sq", b)])
        P.op("dve", lambda v, b=b, nt=nt: v.scalar_tensor_tensor(mixb[:, b, 0:nt], sq[:, b, 0:nt], gn[:, 0:1], og[:, b, 0:nt], ALU.mult, ALU.mult), reads=[("sq", b), ("og", b), "gn"], writes=[("mixb", b)])
        for dc in range(8):
            ps, pk = newps()
            ob = oi % 4; oi += 1
            P.op("pe", lambda t, ps=ps, dc=dc, b=b, nt=nt: t.matmul(ps[:, 0:nt], wb[:, 0, dc * 128:(dc + 1) * 128], mixb[:, b, 0:nt], start=True, stop=True), reads=[("wb", 0), ("mixb", b)], writes=[pk])
            P.op("act" if oi % 2 else "dve", (lambda a, ps=ps, ob=ob, nt=nt: a.copy(ost[:, ob, 0:nt], ps[:, 0:nt])) if oi % 2 else (lambda v, ps=ps, ob=ob, nt=nt: v.tensor_copy(ost[:, ob, 0:nt], ps[:, 0:nt])),
                 reads=[pk], writes=[("ost", ob)])
            P.dma("sp", rs_in[rank * 1024 + dc * 128:rank * 1024 + (dc + 1) * 128, d0:d0 + nt], ost[:, ob, 0:nt], reads=[("ost", ob)])
    return C.end()


def build_k4af(C, segs, KC):
    W = C.din("w", [KC * 128, 1024]); lnd = C.din("ln", [128, 2, 8]); ones_d = C.din("ones", [128, 128])
    xs, mods, outs, rss, ats = [], [], [], [], []
    for si, (ntok, tile) in enumerate(segs):
        xs.append(C.din("x%d" % si, [1024, ntok])); mods.append(C.din("mod%d" % si, [128, 6, 8])); outs.append(C.dout("o%d" % si, [1024, ntok]))
        rss.append(C.din("rs%d" % si, [1024, ntok])); ats.append(C.din("at%d" % si, [KC * 128, ntok]))
    TM = max(t for _, t in segs)
    wb = C.sb("wb", [128, KC, 1024], BF16); stage = C.sb("stage", [128, 2, 1024])
    xt = C.sb("xt", [128, 8, TM]); rt = C.sb("rt", [128, 8, TM]); at = C.sb("at", [128, KC, TM]); mix = C.sb("mix", [128, KC, TM], BF16)
    z = C.sb("z", [128, 8, TM]); zsq = C.sb("zsq", [128, 8, TM]); ost = C.sb("ost", [128, 2, TM])
    ones = C.sb("ones", [128, 128]); lnt = C.sb("lnt", [128, 2, 8]); mt = C.sb("mt", [128, len(segs), 6, 8])
    pss = [C.ps("ps%d" % i, [128, 512]) for i in range(8)]
    P = C.start()
    pctr = [0]

    def newps():
        i = pctr[0] % 8; pctr[0] += 1
        return pss[i], ("ps", i)
    P.dma("sp", ones[:], ones_d[:, :], writes=["ones"]); P.dma("sp", lnt[:], lnd[:, :, :], writes=["ln"])
    for si in range(len(segs)):
        P.dma("sp", mt[:, si, :, :], mods[si][:, :, :], writes=[("mt", si)])
    load_w_bf16(C, P, wb, "wb", W, KC * 128, 1024, stage, "stage")
    v3 = lambda ap: ap.rearrange("(kc p) t -> p kc t", p=128)
    for si, (ntok, tile) in enumerate(segs):
        for t0 in range(0, ntok, tile):
            nt = tile
            P.dma("sp", xt[:, :, 0:nt], v3(xs[si])[:, :, t0:t0 + nt], writes=["xt"])
            P.dma("sp", rt[:, :, 0:nt], v3(rss[si])[:, :, t0:t0 + nt], writes=["rt"])
            P.dma("sp", at[:, :, 0:nt], v3(ats[si])[:, :, t0:t0 + nt], writes=["at"])
            P.op("pool", lambda v: v.tensor_copy(mix[:, :, 0:nt], at[:, :, 0:nt]), reads=["at"], writes=[("mix", k) for k in range(KC)])
            for dc in range(8):
                ps, pk = newps()
                for kc in range(KC):
                    P.op("pe", lambda t, ps=ps, kc=kc, dc=dc: t.matmul(ps[:, 0:nt], wb[:, kc, dc * 128:(dc + 1) * 128], mix[:, kc, 0:nt], start=(kc == 0), stop=(kc == KC - 1)),
                         reads=[("wb", kc), ("mix", kc)], writes=[pk])
                P.op("dve", lambda v, ps=ps, dc=dc: v.tensor_tensor(rt[:, dc, 0:nt], rt[:, dc, 0:nt], ps[:, 0:nt], ALU.add), reads=[pk, "rt"], writes=["rt"])
                P.op("pool", lambda v, dc=dc: v.tensor_scalar_mul(z[:, dc, 0:nt], xt[:, dc, 0:nt], ALPHA), reads=["xt"], writes=[("z", dc)])
                P.op("dve", lambda v, dc=dc, si=si: v.scalar_tensor_tensor(z[:, dc, 0:nt], rt[:, dc, 0:nt], mt[:, si, 2, dc:dc + 1], z[:, dc, 0:nt], ALU.mult, ALU.add),
                     reads=["rt", ("z", dc), ("mt", si)], writes=[("z", dc)])
            ln_tail(P, nt, z, zsq, ones, newps, lnt[:, 0, :], lnt[:, 1, :], outs[si], (t0, t0 + nt), ost, None)
    return C.end()


ADA_PERM = np.array([j * 1024 + kc * 128 + p for p in range(128) for j in range(6) for kc in range(8)])


def fused_program(upto=99):
    C = Ctx(shared=True)
    sc = C.scratch
    T = 8448
    m_scr = sc("m_scr", [2, 2, 6144])
    modv = lambda r, l: m_scr[r, l, :].rearrange("(p j kc) -> p j kc", p=128, j=6)
    C.stage("k0_", {"out": m_scr}); build_k0f(C)
    pT_gla = sc("pT_gla", [2, 384, T]); pt_gla = sc("pt_gla", [2, T, 192])
    for z in range(2):
        C.stage("g%d_" % z, {"mod0": modv(1, 0), "mod1": modv(0, 0), "p0": pT_gla[z, :, 0:256], "p1": pT_gla[z, :, 256:T], "pt0": pt_gla[z, 0:256, :], "pt1": pt_gla[z, 256:T, :]})
        build_k1x(C, 576, 384, (384, 576), [(256, 256), (8192, 512)])
    o_gla = sc("o_gla", [2, T, 128])
    C.stage("gl_", {"qT": pT_gla[:, 0:64, :], "kT": pT_gla[:, 64:128, :], "a1T": pT_gla[:, 128:144, :], "k": pt_gla[:, :, 0:64], "v": pt_gla[:, :, 64:192], "o": o_gla})
    build_gla(T, C=C)
    oT = sc("oT_gla", [128, T])
    C.stage("t0_", {"src": o_gla[0, 0:256, :], "src2": o_gla[1, 0:256, :], "dst": oT[:, 0:256]}); build_tr(C, 256, 128, rev_src=True)
    C.stage("t1_", {"src": o_gla[0, 256:T, :], "src2": o_gla[1, 256:T, :], "dst": oT[:, 256:T]}); build_tr(C, 8192, 128, rev_src=True)
    rs_in = sc("rs_in", [4096, 2112]); rs_out = sc("rs_out", [1024, 2112])
    C.stage("ge_", {"oT": oT, "ogT": pT_gla[0, 256:384, :], "rs_in": rs_in}); build_glaepi(C, T)
    C.P.coll_issue("ReduceScatter", ALU.add, GROUPS, rs_in[:, :], rs_out[:, :])
    pq0 = sc("pq0", [1280, 2304]); pqt0 = sc("pqt0", [2304, 128]); pq1 = sc("pq1", [1280, 256]); pqt1 = sc("pqt1", [256, 128]); pq2 = sc("pq2", [1280, 64]); pqt2 = sc("pqt2", [64, 128])
    C.stage("q_", {"mod0": modv(0, 0), "mod1": modv(1, 0), "mod2": modv(1, 0), "p0": pq0, "pt0": pqt0, "p1": pq1, "pt1": pqt1, "p2": pq2, "pt2": pqt2})
    build_k1x(C, 1408, 1280, (1280, 1408), [(2304, 256), (256, 256), (64, 64)])
    att_tm = sc("att_tm", [2176, 512]); attT = sc("attT", [512, 2176])
    hd = lambda ap: ap.rearrange("(h d) t -> d h t", d=64)
    tmv = lambda ap: ap.rearrange("(n p) (g d) -> p n g d", p=128, d=64)
    C.stage("a_", {"qT": hd(pq0[0:512, 128:2176]), "qsT": hd(pq0[512:1024, 128:2176]), "kT": hd(pq0[1024:1152, :]), "ksT": hd(pq0[1152:1280, :]), "v": tmv(pqt0),
                   "kcT": hd(pq1[1024:1152, :]), "vc": tmv(pqt1), "qcT": hd(pq2[0:512, :]), "o": att_tm[0:2048, :], "oc": att_tm[2048:2176, :]})
    build_gqa(False, False, C=C, oc_rows=128, nmk=4)
    C.stage("ta_", {"src": att_tm, "dst": attT}); build_tr(C, 2176, 512)
    C.P.coll_wait()
    x1T = sc("x1T", [1024, 2048]); h1T = sc("h1T", [1024, 64]); xbT = sc("xbT", [1024, 2048]); hbT = sc("hbT", [1024, 64])
    C.stage("f_", {"mod0": modv(0, 0), "mod1": modv(1, 0), "rs0": rs_out[:, 0:2048], "rs1": rs_out[:, 2048:2112], "at0": attT[:, 0:2048], "at1": attT[:, 2048:2112], "o0": x1T, "o1": h1T})
    build_k4af(C, [(2048, 256), (64, 64)], 4)
    ag_lat = sc("ag_lat", [8, 128, 2048]); ag_ctx = sc("ag_ctx", [1024, 64]); agl_out = sc("agl_out", [8, 512, 2048]); agc_out = sc("agc_out", [4096, 64])
    xb_own = ag_lat.rearrange("kc p t -> (kc p) t")
    C.stage("m0_", {"mod0": modv(0, 0), "mod1": modv(1, 0), "x0": x1T, "x1": h1T, "o0": xb_own, "o1": ag_ctx})
    build_k4b([(2048, 256), (64, 64)], C=C)
    if upto == 0:
        dbg = C.nc.dram_tensor("dbg_xb", [1024, 2048], F32, kind="ExternalOutput").ap()
        dbg2 = C.nc.dram_tensor("dbg_hb", [1024, 64], F32, kind="ExternalOutput").ap()
        C.P.dma("sp", dbg[:, :], xb_own); C.P.dma("sp", dbg2[:, :], ag_ctx[:, :])
        return C.finalize(), C
    for kc in range(8):
        C.P.coll_issue("AllGather", ALU.bypass, GROUPS, ag_lat[kc], agl_out[kc], barrier=(kc == 0))
    C.P.coll_issue("AllGather", ALU.bypass, GROUPS, ag_ctx[:, :], agc_out[:, :], barrier=False)
    p_q = sc("p_q", [640, 2048])
    C.stage("mq_", {"x0": xb_own, "mod0": modv(0, 1), "p0": p_q}); build_k1x(C, 640, 640, None, [(2048, 512)])
    q_own = sc("q_own", [96, 8, 2048]); kn_d = sc("kn_dum", [64, 8, 2048]); kpe_dm = sc("kpe_dum", [32, 2048]); v_dm = sc("v_dum", [2048, 512])
    C.stage("mqf_", {"p": p_q, "q": q_own, "kn": kn_d, "kpe": kpe_dm, "v": v_dm}); build_mlf(2048, 256, C=C, with_kv=False)
    C.P.coll_wait()
    latv = lambda r: agl_out[:, r * 128:(r + 1) * 128, :].rearrange("kc p t -> p kc t")
    ctxv = lambda r: agc_out[r * 1024:(r + 1) * 1024, :]
    pTr_c = sc("pTr_c", [768, 258]); pTr_l = sc("pTr_l", [768, 8194])
    C.stage("z_", {})
    zt = C.sb("zt", [128, 6, 1]); P = C.start()
    P.op("dve", lambda v: v.memset(zt[:], 0.0), writes=["zt"])
    for ap_, col in ((pTr_c, 0), (pTr_c, 257), (pTr_l, 0), (pTr_l, 8193)):
        P.dma("sp", ap_.rearrange("(kc p) t -> p kc t", p=128)[:, :, col:col + 1], zt[:], reads=["zt"], allow_slow_non_contiguous=True)
    C.end()
    bd = {}
    for i in range(4):
        bd["x%d" % i] = ctxv(i); bd["mod%d" % i] = modv(1, 1); bd["p%d" % i] = pTr_c[:, 1 + 64 * i:1 + 64 * (i + 1)]
        bd["xv%d" % (4 + i)] = latv(i); bd["mod%d" % (4 + i)] = modv(0, 1); bd["p%d" % (4 + i)] = pTr_l[:, 1 + 2048 * i:1 + 2048 * (i + 1)]
    C.stage("r_", bd); build_k1x(C, 768, 768, None, [(64, 64)] * 4 + [(2048, 512)] * 4)
    FN = ["r", "km", "v", "an", "bb", "wf", "wb", "g"]
    Ff = {n: sc("F_" + n, [128, T]) for n in FN}
    bd = {"p0": pTr_c, "p1": pTr_l}
    for n in FN:
        bd[n + "0"] = Ff[n][:, 0:256]; bd[n + "1"] = Ff[n][:, 256:T]
    C.stage("rf_", bd); build_rwf([(256, 256), (8192, 512)], C=C, NH=1)
    Ftm = {n: sc("Ftm_" + n, [T, 128]) for n in ("bb", "km", "v", "wf", "wb")}
    bd = {}
    for n in ("bb", "km", "v", "wf", "wb"):
        bd["src_" + n] = Ff[n]; bd["dst_" + n] = Ftm[n]
    C.stage("xt_", bd); build_tr_fm(C, ("bb", "km", "v", "wf", "wb"), T)
    y_scr = sc("y_scr", [2, T, 128])
    C.stage("s_", {"an": Ff["an"], "r": Ff["r"], "bb": Ff["bb"], "km": Ff["km"], "bb_tm": Ftm["bb"], "km_tm": Ftm["km"], "v_tm": Ftm["v"], "wf_tm": Ftm["wf"], "wb_tm": Ftm["wb"], "y": y_scr}); build_rwc2(C, T, 4)
    yT = sc("yT", [128, 8192])
    C.stage("ty_", {"src": y_scr[0, 256:T, :], "src2": y_scr[1, 256:T, :], "dst": yT}); build_tr(C, 8192, 128, rev_src=True, plain2=True)
    rs2_in = sc("rs2_in", [4096, 2048]); rs2_out = sc("rs2_out", [1024, 2048])
    C.stage("re_", {"yT": yT, "r": Ff["r"][:, 256:T], "km": Ff["km"][:, 256:T], "v": Ff["v"][:, 256:T], "g": Ff["g"][:, 256:T], "rs_in": rs2_in}); build_rwepi(C)
    C.P.coll_issue("ReduceScatter", ALU.add, GROUPS, rs2_in[:, :], rs2_out[:, :])
    p_mla = sc("p_mla", [640, T])
    bd = {}
    for i in range(4):
        bd["xv%d" % i] = latv(i); bd["mod%d" % i] = modv(0, 1); bd["p%d" % i] = p_mla[:, 2048 * i:2048 * (i + 1)]
        bd["x%d" % (4 + i)] = ctxv(i); bd["mod%d" % (4 + i)] = modv(1, 1); bd["p%d" % (4 + i)] = p_mla[:, 8192 + 64 * i:8192 + 64 * (i + 1)]
    C.stage("mk_", bd); build_k1x(C, 640, 640, None, [(2048, 512)] * 4 + [(64, 64)] * 4)
    q_all = sc("q_all", [96, 8, T]); kn_all = sc("kn_all", [64, 8, T]); kpe_all = sc("kpe_all", [32, T]); v_all = sc("v_all", [T, 512])
    C.stage("mkf_", {"p": p_mla, "q": q_all, "kn": kn_all, "kpe": kpe_all, "v": v_all}); build_mlf(T, 256, C=C, with_q=False)
    att1 = sc("att1_tm", [2048, 512]); attT1 = sc("attT1", [512, 2048])
    C.stage("ma_", {"q": q_own, "kn": kn_all, "kpe": kpe_all, "v": v_all.rearrange("(n p) (h d) -> p n h d", p=128, d=64), "o": att1}); build_mla(C=C, split_k=True)
    C.stage("tb_", {"src": att1, "dst": attT1}); build_tr(C, 2048, 512)
    C.P.coll_wait()
    C.stage("f1_", {"x0": xb_own, "mod0": modv(0, 1), "rs0": rs2_out, "at0": attT1, "o0": x1T}); build_k4af(C, [(2048, 256)], 4)
    C.stage("m1_", {"x0": x1T, "mod0": modv(0, 1)}); build_k4b([(2048, 256)], C=C)
    return C.finalize(), C


def fused_inputs(I, c):
    b, s = c // 4, c % 4
    f = lambda a: np.ascontiguousarray(a, dtype=np.float32)
    x, ctx = I['x'], I['ctx']
    m = {}
    cc = np.stack([I['c'][b], I['c_ctx']], axis=1)
    m["k0_cT"] = f(cc.reshape(8, 128, 2).transpose(1, 0, 2))
    m["k0_w"] = f(I['ada_w'][:, :, ADA_PERM])
    m["k0_b"] = f(np.broadcast_to(I['ada_b'][None, :, ADA_PERM], (2, 2, 6144)))
    w_in = I['ab_w_in'][0]
    h = s
    z96 = np.zeros((1024, 96), np.float32)
    q_, k_, v_ = w_in[:, h * 64:(h + 1) * 64], w_in[:, 256 + h * 64:256 + (h + 1) * 64], w_in[:, 512 + h * 128:512 + (h + 1) * 128]
    a1f, a1b, og = w_in[:, 1024:1040], w_in[:, 1040:1056], w_in[:, 1056 + h * 128:1056 + (h + 1) * 128]
    m["g0_w"] = f(np.concatenate([q_, k_, a1f, a1b, z96, og, k_, v_], 1))
    m["g1_w"] = f(np.concatenate([q_, k_, a1b, a1f, z96, og, k_, v_], 1))
    m["g0_x0"] = f(ctx[b].T); m["g0_x1"] = f(x[b].T)
    m["g1_x0"] = f(ctx[b][::-1].T); m["g1_x1"] = f(x[b][::-1].T)
    m["gl_a2"] = f(I['gla_a2'][0][:, :, h * 64:(h + 1) * 64]); m["gl_ab"] = f(I['gla_ab'][0][:, None, h * 64:(h + 1) * 64])
    tri, mask = gla_consts()
    m["gl_tri"] = tri; m["gl_mask"] = mask
    idn = np.eye(128, dtype=np.float32); jdn = f(idn[::-1])
    for p in ("t0_", "t1_"):
        m[p + "idn"] = idn; m[p + "jdn"] = jdn
    ones = np.ones((128, 128), np.float32)
    m["ge_gn"] = f(I['gla_norm'][0].reshape(128, 1)); m["ge_ones"] = ones; m["ge_w"] = f(I['ab_w_out'][0][h * 128:(h + 1) * 128])
    qs = np.concatenate([w_in[:, GQ0 + hh * 64 + SWAP64] for hh in range(8)], 1)
    ks = np.concatenate([w_in[:, GK0 + hh * 64 + SWAP64] for hh in range(2)], 1)
    m["q_w"] = f(np.concatenate([w_in[:, GQ0:GQ0 + 512], qs, w_in[:, GK0:GK0 + 128], ks, w_in[:, GV0:GV0 + 128]], 1))
    lo, hi = s * 2048, (s + 1) * 2048
    xh = np.zeros((2304, 1024), np.float32)
    a0, a1 = max(lo - 128, 0), min(hi + 128, 8192)
    xh[a0 - (lo - 128):a1 - (lo - 128)] = x[b, a0:a1]
    m["q_x0"] = f(xh.T); m["q_x1"] = f(ctx[b].T); m["q_x2"] = f(ctx[b, s * 64:(s + 1) * 64].T)
    pos = np.arange(lo, hi); posk = np.clip(np.arange(lo - 128, hi + 128), 0, 8191)
    m["a_cq"], m["a_sq"] = rope_tables(pos); m["a_ck"], m["a_sk"] = rope_tables(posk)
    m["a_sink"] = f(np.broadcast_to(I['gqa_sink'][0][None, :], (128, 8)))
    t = np.arange(128)
    mprev = np.tile((t[:, None] >= t[None, :]).astype(np.float32), (1, 4))
    mnext = np.tile((t[:, None] <= t[None, :]).astype(np.float32), (1, 4))
    m["a_mk"] = f(np.stack([mprev, mnext, mprev * (0.0 if s == 0 else 1.0), mnext * (0.0 if s == 3 else 1.0)], 1))
    m["ta_idn"] = idn
    m["f_w"] = f(I['ab_w_out'][0][512:1024]); m["f_ln"] = ln_layout(I['ln_g'][0, 0], I['ln_b'][0, 0]); m["f_ones"] = ones
    m["f_x0"] = f(x[b, lo:hi].T); m["f_x1"] = f(ctx[b, s * 64:(s + 1) * 64].T)
    m["m0_w1"] = f(I['mlp_w1'][0]); m["m0_w2"] = f(I['mlp_w2'][0]); m["m0_ln"] = ln_layout(I['ln_g'][0, 1], I['ln_b'][0, 1]); m["m0_ones"] = ones
    return m


def build_rws2(C, T=8448, BS=32):
    NCTX = 256 // BS
    NB = T // BS
    fm = {n: C.din(n, [128, T]) for n in ("an", "r", "wf", "wb")}
    tmd = {n: C.din(n, [T, 128]) for n in ("bb", "km", "v")}
    mk_d = C.din("mk2", [128, 2])
    y_d = C.dout("y", [2, T, 128])
    an = C.sb("an", [128, 2, 2, BS]); rr = C.sb("rr", [128, 2, 2, BS]); wt = C.sb("wt", [128, 2, 2, BS])
    ab = C.sb("ab", [128, 2, 2, BS, 2]); rb = C.sb("rb", [128, 2, 2, BS, 2])
    bk = C.sb("bk", [4, 2, 2, BS, 128]); vv = C.sb("vv", [4, 2, 2, BS, 64]); ysb = C.sb("ysb", [2, 2, 2, BS, 64])
    ST = C.sb("ST", [128, 2, 64]); mk = C.sb("mk", [128, 2])
    pa = [C.ps("pa%d" % i, [128, 512]) for i in range(2)]
    po = [C.ps("po%d" % i, [128, 512]) for i in range(2)]
    py = [C.ps("py%d" % i, [128, 512]) for i in range(2)]
    P = C.start()
    P.dma("sp", mk[:], mk_d[:, :], writes=["mk"])
    P.op("dve", lambda v: v.memset(ST[:], 0.0), writes=[("ST", 0), ("ST", 1)])
    P.op("pool", lambda v: v.memset(bk[:], 0.0), writes=[("bk", z, b) for z in range(2) for b in range(2)])
    P.op("pool", lambda v: v.memset(vv[:], 0.0), writes=[("vv", z, b) for z in range(2) for b in range(2)])
    P.op("dve", lambda v: v.memset(ysb[:], 0.0), writes=[("ysb", z, b) for z in range(2) for b in range(2)])
    order = [list(range(NB)), list(range(NCTX - 1, -1, -1)) + list(range(NB - 1, NCTX - 1, -1))]
    for i in range(NB):
        b = i % 2
        for z in range(2):
            blk = order[z][i]; s0 = blk * BS
            wn = "wf" if z == 0 else "wb"
            P.dma("sp", an[:, z, b, :], fm["an"][:, s0:s0 + BS], writes=[("an", z, b)])
            P.dma("sp", rr[:, z, b, :], fm["r"][:, s0:s0 + BS], writes=[("rr", z, b)])
            P.dma("sp", wt[:, z, b, :], fm[wn][:, s0:s0 + BS], writes=[("wt", z, b)])
            for row, (nm, lo) in enumerate((("bb", 0), ("bb", 64), ("km", 0), ("km", 64))):
                P.dma("sp", bk[row:row + 1, z, b, :, lo:lo + 64], tmd[nm][s0:s0 + BS, lo:lo + 64], writes=[("bk", z, b)])
            for row, lo in ((2, 0), (3, 64)):
                P.dma("sp", vv[row:row + 1, z, b, :, :], tmd["v"][s0:s0 + BS, lo:lo + 64], writes=[("vv", z, b)])
            for c in range(2):
                P.op("pool", lambda v, z=z, b=b, c=c: v.tensor_scalar_mul(ab[:, z, b, :, c], an[:, z, b, :], mk[:, c:c + 1]), reads=[("an", z, b), "mk"], writes=[("ab", z, b)])
                P.op("pool", lambda v, z=z, b=b, c=c: v.tensor_scalar_mul(rb[:, z, b, :, c], rr[:, z, b, :], mk[:, c:c + 1]), reads=[("rr", z, b), "mk"], writes=[("rb", z, b)])
        for k in range(BS):
            for z in range(2):
                blk = order[z][i]
                t = k if z == 0 else BS - 1 - k
                P.op("pe", lambda e, z=z, b=b, t=t: e.matmul(pa[z][0:2, 0:64], ab[:, z, b, t, :], ST[:, z, :], start=True, stop=True), reads=[("ab", z, b), ("ST", z)], writes=[("pa", z)])
                P.op("act", lambda a, z=z, b=b, t=t: a.copy(vv[0:2, z, b, t, :], pa[z][0:2, 0:64]), reads=[("pa", z)], writes=[("vv", z, b)])
                P.op("pe", lambda e, z=z, b=b, t=t: e.matmul(po[z][:, 0:64], bk[:, z, b, t, :], vv[:, z, b, t, :], start=True, stop=True), reads=[("bk", z, b), ("vv", z, b)], writes=[("po", z)])
                P.op("dve", lambda v, z=z, b=b, t=t: v.scalar_tensor_tensor(ST[:, z, :], ST[:, z, :], wt[:, z, b, t:t + 1], po[z][:, 0:64], ALU.mult, ALU.add), reads=[("ST", z), ("wt", z, b), ("po", z)], writes=[("ST", z)])
                if blk >= NCTX:
                    P.op("pe", lambda e, z=z, b=b, t=t: e.matmul(py[z][0:2, (t % 8) * 64:(t % 8 + 1) * 64], rb[:, z, b, t, :], ST[:, z, :], start=True, stop=True), reads=[("rb", z, b), ("ST", z)], writes=[("py", z)])
                    last = (t % 8 == 7) if z == 0 else (t % 8 == 0)
                    if last:
                        t8 = (t // 8) * 8
                        P.op("act", lambda a, z=z, b=b, t8=t8: a.copy(ysb[:, z, b, t8:t8 + 8, :], py[z][0:2, 0:512].rearrange("p (a c) -> p a c", c=64)), reads=[("py", z)], writes=[("ysb", z, b)])
        for z in range(2):
            blk = order[z][i]; s0 = blk * BS
            if blk >= NCTX:
                P.dma("sp", y_d[z, s0:s0 + BS, :].rearrange("t (c i) -> c t i", c=2), ysb[:, z, b, :, :], reads=[("ysb", z, b)])
    return C.end()


def build_rwepi(C, N=8192):
    ins = {n: C.din(n, [128, N]) for n in ("yT", "r", "km", "v", "g")}
    sc_d = C.din("sc", [128, 3]); bo_d = C.din("bo", [128, 128]); W = C.din("w", [128, 1024]); rs_in = C.dout("rs_in", [4096, 2048])
    wb = C.sb("wb", [128, 1, 1024], BF16); stage = C.sb("stage", [128, 2, 1024]); bo = C.sb("bo", [128, 128]); sc = C.sb("sc", [128, 3])
    tl = {n: C.sb("t_" + n, [128, 2, 512]) for n in ins}
    sq = C.sb("sq", [128, 2, 512]); mean = C.sb("mean", [128, 512]); var = C.sb("var", [128, 512]); mixb = C.sb("mixb", [128, 2, 512], BF16); ost = C.sb("ost", [128, 4, 512])
    pss = [C.ps("ps%d" % i, [128, 512]) for i in range(8)]
    P = C.start()
    pctr = [0]

    def newps():
        i = pctr[0] % 8; pctr[0] += 1
        return pss[i], ("ps", i)
    P.dma("sp", bo[:], bo_d[:, :], writes=["bo"]); P.dma("sp", sc[:], sc_d[:, :], writes=["sc"])
    load_w_bf16(C, P, wb, "wb", W, 128, 1024, stage, "stage")
    oi = 0
    for it in range(N // 512):
        b = it % 2; t0 = it * 512
        for n in ins:
            P.dma("sp", tl[n][:, b, :], ins[n][:, t0:t0 + 512], writes=[(n, b)])
        y, r_, km_, v_, g_ = (tl[n] for n in ("yT", "r", "km", "v", "g"))
        P.op("act", lambda a, b=b: a.activation(sq[:, b, :], y[:, b, :], AF.Square), reads=[("yT", b)], writes=[("sq", b)])
        P.op("pool", lambda v, b=b: v.tensor_tensor(r_[:, b, :], r_[:, b, :], km_[:, b, :], ALU.mult), reads=[("r", b), ("km", b)], writes=[("r", b)])
        P.op("pool", lambda v, b=b: v.tensor_scalar_mul(r_[:, b, :], r_[:, b, :], sc[:, 0:1]), reads=[("r", b), "sc"], writes=[("r", b)])
        psm, pkm = newps(); psq, pkq = newps(); psb, pkb = newps()
        P.op("pe", lambda t, psm=psm, b=b: t.matmul(psm[:, 0:512], bo[:, :], y[:, b, :], start=True, stop=True), reads=["bo", ("yT", b)], writes=[pkm])
        P.op("pe", lambda t, psq=psq, b=b: t.matmul(psq[:, 0:512], bo[:, :], sq[:, b, :], start=True, stop=True), reads=["bo", ("sq", b)], writes=[pkq])
        P.op("pe", lambda t, psb=psb, b=b: t.matmul(psb[:, 0:512], bo[:, :], r_[:, b, :], start=True, stop=True), reads=["bo", ("r", b)], writes=[pkb])
        P.op("dve", lambda v, psm=psm: v.tensor_scalar_mul(mean[:, :], psm[:, 0:512], 1.0 / 64.0), reads=[pkm], writes=["mean"])
        P.op("dve", lambda v: v.tensor_tensor(var[:, :], mean[:, :], mean[:, :], ALU.mult), reads=["mean"], writes=["var"])
        P.op("dve", lambda v, psq=psq: v.scalar_tensor_tensor(var[:, :], psq[:, 0:512], 1.0 / 64.0, var[:, :], ALU.mult, ALU.subtract), reads=[pkq, "var"], writes=["var"])
        P.op("dve", lambda v: v.tensor_scalar_add(var[:, :], var[:, :], 6.4e-4), reads=["var"], writes=["var"])
        P.op("act", lambda a: a.activation(var[:, :], var[:, :], AF.Sqrt), reads=["var"], writes=["var"])
        P.op("dve", lambda v: v.reciprocal(var[:, :], var[:, :]), reads=["var"], writes=["var"])
        P.op("dve", lambda v, b=b: v.tensor_tensor(sq[:, b, :], y[:, b, :], mean[:, :], ALU.subtract), reads=[("yT", b), "mean"], writes=[("sq", b)])
        P.op("dve", lambda v, b=b: v.tensor_tensor(sq[:, b, :], sq[:, b, :], var[:, :], ALU.mult), reads=[("sq", b), "var"], writes=[("sq", b)])
        P.op("dve", lambda v, b=b: v.tensor_scalar(sq[:, b, :], sq[:, b, :], sc[:, 1:2], sc[:, 2:3], ALU.mult, ALU.add), reads=[("sq", b), "sc"], writes=[("sq", b)])
        P.op("dve", lambda v, psb=psb, b=b: v.tensor_tensor(v_[:, b, :], psb[:, 0:512], v_[:, b, :], ALU.mult), reads=[pkb, ("v", b)], writes=[("v", b)])
        P.op("dve", lambda v, b=b: v.tensor_tensor(sq[:, b, :], sq[:, b, :], v_[:, b, :], ALU.add), reads=[("sq", b), ("v", b)], writes=[("sq", b)])
        P.op("dve", lambda v, b=b: v.tensor_tensor(mixb[:, b, :], sq[:, b, :], g_[:, b, :], ALU.mult), reads=[("sq", b), ("g", b)], writes=[("mixb", b)])
        rank, d0 = t0 // 2048, t0 % 2048
        for dc in range(8):
            ps, pk = newps()
            ob = oi % 4; oi += 1
            P.op("pe", lambda t, ps=ps, dc=dc, b=b: t.matmul(ps[:, 0:512], wb[:, 0, dc * 128:(dc + 1) * 128], mixb[:, b, :], start=True, stop=True), reads=[("wb", 0), ("mixb", b)], writes=[pk])
            P.op("act" if oi % 2 else "dve", (lambda a, ps=ps, ob=ob: a.copy(ost[:, ob, :], ps[:, 0:512])) if oi % 2 else (lambda v, ps=ps, ob=ob: v.tensor_copy(ost[:, ob, :], ps[:, 0:512])), reads=[pk], writes=[("ost", ob)])
            P.dma("sp", rs_in[rank * 1024 + dc * 128:rank * 1024 + (dc + 1) * 128, d0:d0 + 512], ost[:, ob, :], reads=[("ost", ob)])
    return C.end()


def fused_inputs1(I, c, m):
    b, s = c // 4, c % 4
    f = lambda a: np.ascontiguousarray(a, dtype=np.float32)
    hp = s
    w = I['cd_w_in'][0]
    z64 = np.zeros((1024, 64), np.float32); z96 = np.zeros((1024, 96), np.float32)
    pr = slice(hp * 128, (hp + 1) * 128)
    m["r_w"] = f(np.concatenate([w[:, 0:512][:, pr], w[:, 512:1024][:, pr], w[:, 1024:1536][:, pr], w[:, 1536:1664], w[:, 1664:1728], z64, w[:, 1728:1856]], 1))
    w2, a2, g2 = I['rwkv_w2'][0], I['rwkv_a2'][0], I['rwkv_g2'][0]
    wl = np.zeros((128, 512), np.float32)
    wl[0:64, 0:128] = w2[0][:, pr]; wl[64:128, 128:256] = w2[1][:, pr]; wl[0:64, 256:384] = a2[:, pr]; wl[:, 384:512] = g2[:, pr]
    m["rf_wl"] = wl
    mu = I['rwkv_mu'][0]
    col = lambda v: v.reshape(128, 1)
    sc = np.concatenate([col(mu[0:512][pr]), col(mu[512:1024][pr]), col(mu[1024:1536][pr]), col(mu[1536:1664]), col(np.concatenate([mu[1664:1728], np.zeros(64, np.float32)])), col(mu[1728:1856]),
                         col(I['rwkv_w0'][0, 0][pr]), col(I['rwkv_w0'][0, 1][pr]), col(I['rwkv_a0'][0][pr]), col(I['rwkv_kk'][0][pr]), col(I['rwkv_ka'][0][pr])], 1)
    m["rf_sc"] = f(sc)
    t = np.arange(128)
    bo = ((t[:, None] // 64) == (t[None, :] // 64)).astype(np.float32)
    m["rf_bo"] = bo
    idn = np.eye(128, dtype=np.float32)
    m["xt_idn"] = idn
    m["s_U"], m["s_M"] = rwc_consts()
    m["ty_idn"] = idn; m["ty_jdn"] = idn
    m["re_sc"] = f(np.stack([I['rwkv_rk'][0].reshape(512)[pr], I['rwkv_lnx_g'][0][pr], I['rwkv_lnx_b'][0][pr]], 1))
    m["re_bo"] = bo; m["re_w"] = f(I['cd_w_out'][0][hp * 128:(hp + 1) * 128])
    kpe = w[:, 2240:2272]
    sw = np.array([d + 8 if (d % 16) < 8 else d - 8 for d in range(32)])
    wm = f(np.concatenate([w[:, 1856:2112], w[:, 2112:2240], kpe, z96, kpe[:, sw], z96], 1))
    m["mk_w"] = wm; m["mq_w"] = wm
    wuq = I['mla_w_uq'][0]; wukv = I['mla_w_ukv'][0]
    wuq_sw = wuq.reshape(256, 8, 96).copy()
    wuq_sw[:, :, 64:96] = wuq_sw[:, :, 64:96][:, :, sw]
    wuq_all = f(np.concatenate([wuq, wuq_sw.reshape(256, 768)], 1))
    wk = wukv.reshape(128, 8, 128)
    wukv_all = f(np.concatenate([wk[:, :, :64].reshape(128, 512), wk[:, :, 64:].reshape(128, 512)], 1))
    gq = f(np.stack([I['mla_q_norm'][0][:128], I['mla_q_norm'][0][128:], I['mla_kv_norm'][0]], 1))
    ones = np.ones((128, 128), np.float32)

    def tabs(pos, nrope):
        c32, s32 = rope32_tables(pos, True)
        n = len(pos)
        cq = np.ones((96, n), np.float32); sq = np.zeros((96, n), np.float32)
        cq[64:96, :nrope] = c32[:, :nrope]; sq[64:96, :nrope] = s32[:, :nrope]; cq[0:32, :nrope] = c32[:, :nrope]; sq[0:32, :nrope] = s32[:, :nrope]
        return cq, sq
    cqa, sqa = tabs(np.concatenate([np.arange(8192), np.zeros(256, np.int64)]), 8192)
    cqo, sqo = tabs(np.arange(s * 2048, (s + 1) * 2048), 2048)
    for p_, (cq, sq) in (("mkf_", (cqa, sqa)), ("mqf_", (cqo, sqo))):
        m[p_ + "wuq"] = wuq_all; m[p_ + "wukv"] = wukv_all; m[p_ + "gq"] = gq; m[p_ + "ones"] = ones; m[p_ + "cq"] = f(cq); m[p_ + "sq"] = f(sq)
    m["tb_idn"] = idn
    m["f1_w"] = f(I['cd_w_out'][0][512:1024]); m["f1_ln"] = ln_layout(I['ln_g'][1, 0], I['ln_b'][1, 0]); m["f1_ones"] = ones
    m["m1_w1"] = f(I['mlp_w1'][1]); m["m1_w2"] = f(I['mlp_w2'][1]); m["m1_ln"] = ln_layout(I['ln_g'][1, 1], I['ln_b'][1, 1]); m["m1_ones"] = ones
    return m


_FUSED_CACHE = {}


def kernel(**I):
    I = {k: np.asarray(v, dtype=np.float32) for k, v in I.items()}
    if "nc" not in _FUSED_CACHE:
        _FUSED_CACHE["nc"] = fused_program()
    nc, C = _FUSED_CACHE["nc"]
    maps = []
    for c in range(NCORES):
        m = fused_inputs(I, c)
        fused_inputs1(I, c, m)
        maps.append(m)
    res = run_spmd(nc, maps)
    out = np.empty((2, 8192, 1024), np.float32)
    for c in range(NCORES):
        b, s = c // 4, c % 4
        out[b, s * 2048:(s + 1) * 2048] = res[c]["m1_o0"].T
    return out


def rwc_consts():
    t = np.arange(64)
    le = (t[:, None] <= t[None, :]).astype(np.float32); lt = (t[:, None] < t[None, :]).astype(np.float32)
    ge = le.T.copy(); gt = lt.T.copy()
    U = np.stack([le, lt, ge, gt], 1)
    q4 = lambda f, b: np.stack([f, f, b, b], 1)
    return np.ascontiguousarray(U), np.ascontiguousarray(np.stack([q4(lt, gt), q4(gt, lt), q4(le, ge), q4(np.eye(64, dtype=np.float32), np.eye(64, dtype=np.float32))], 1))


def build_rwc(C, T=8448, NCTX=4):
    NB = T // 64
    fm = {n: C.din(n, [128, T]) for n in ("an", "r", "bb", "km")}
    tmd = {n: C.din(n + "_tm", [T, 128]) for n in ("bb", "km", "v", "wf", "wb")}
    U_d = C.din("U", [64, 4, 64]); M_d = C.din("M", [64, 4, 4, 64])
    y_d = C.dout("y", [2, T, 128])
    Uc = C.sb("Uc", [64, 4, 64]); Mc = C.sb("Mc", [64, 4, 4, 64])
    ffm = {n: C.sb("f_" + n, [64, 2, 4, 64]) for n in fm}
    ftm = {n: C.sb("t_" + n, [64, 2, 2, 128]) for n in ("bb", "km", "v", "w")}
    lw = C.sb("lw", [64, 2, 128]); Eg = C.sb("Eg", [64, 4, 64]); Egi = C.sb("Egi", [64, 4, 64]); Egx = C.sb("Egx", [64, 4, 64]); Egt = C.sb("Egt", [64, 2, 128])
    At = C.sb("At", [64, 4, 64]); Rt = C.sb("Rt", [64, 4, 64]); Bt = C.sb("Bt", [64, 4, 64]); Kt = C.sb("Kt", [64, 4, 64])
    Btm = C.sb("Btm", [64, 2, 128]); Ktm = C.sb("Ktm", [64, 2, 128]); gC = C.sb("gC", [64, 4])
    Pm = C.sb("Pm", [64, 2, 4, 64]); PT = C.sb("PT", [64, 2, 4, 64]); Tm = C.sb("Tm", [64, 2, 4, 64])
    Mka = C.sb("Mka", [64, 4, 64]); Mbr = C.sb("Mbr", [64, 4, 64]); Mkr = C.sb("Mkr", [64, 4, 64])
    ST = C.sb("ST", [64, 4, 64]); W1 = C.sb("W1", [64, 4, 64]); UT = C.sb("UT", [64, 4, 64]); tmp = C.sb("tmp", [64, 4, 64]); ysb = C.sb("ysb", [64, 2, 4, 64])
    pss = [C.ps("ps%d" % i, [128, 512]) for i in range(8)]
    P = C.start()
    pctr = [0]

    def newps():
        i = pctr[0] % 8; pctr[0] += 1
        return pss[i], ("ps", i)
    P.dma("sp", Uc[:], U_d[:, :, :], writes=["Uc"]); P.dma("sp", Mc[:], M_d[:, :, :, :], writes=["Mc"])
    P.op("dve", lambda v: v.memset(ST[:], 0.0), writes=["ST"])
    order = [list(range(NB)), list(range(NCTX - 1, -1, -1)) + list(range(NB - 1, NCTX - 1, -1))]
    for i in range(NB):
        b = i % 2
        blk = [order[0][i], order[1][i]]
        for z in range(2):
            s0 = blk[z] * 64
            for n in fm:
                P.dma("sp", ffm[n][:, b, 2 * z:2 * z + 2, :], fm[n][:, s0:s0 + 64].rearrange("(c j) t -> j c t", c=2), writes=[("f" + n, b)])
            for n, src in (("bb", "bb"), ("km", "km"), ("v", "v"), ("w", "wf" if z == 0 else "wb")):
                P.dma("sp", ftm[n][:, b, z, :], tmd[src][s0:s0 + 64, :], writes=[("t" + n, b)])
        P.op("act", lambda a, b=b: a.activation(lw[:, :, :], ftm["w"][:, b, :, :], AF.Ln), reads=[("tw", b)], writes=["lw"])
        pg, pkg = newps(); px, pkx = newps(); pt_, pkt = newps()
        for z in range(2):
            for c in range(2):
                q = 2 * z + c
                P.op("pe", lambda t, z=z, c=c, q=q: t.matmul(pg[0:64, q * 64:(q + 1) * 64], lw[:, z, c * 64:(c + 1) * 64], Uc[:, 2 * z, :], start=True, stop=True), reads=["lw", "Uc"], writes=[pkg])
                P.op("pe", lambda t, z=z, c=c, q=q: t.matmul(px[0:64, q * 64:(q + 1) * 64], lw[:, z, c * 64:(c + 1) * 64], Uc[:, 2 * z + 1, :], start=True, stop=True), reads=["lw", "Uc"], writes=[pkx])
            P.op("pe", lambda t, z=z: t.matmul(pt_[0:64, z * 128:(z + 1) * 128], Uc[:, 2 * z, :], lw[:, z, :], start=True, stop=True), reads=["lw", "Uc"], writes=[pkt])
        f4 = lambda ap: ap.rearrange("p (q t) -> p q t", q=4)
        P.op("act", lambda a: a.activation(Eg[:], f4(pg[0:64, 0:256]), AF.Exp), reads=[pkg], writes=["Eg"])
        P.op("act", lambda a: a.activation(Egi[:], f4(pg[0:64, 0:256]), AF.Exp, scale=-1.0), reads=[pkg], writes=["Egi"])
        P.op("act", lambda a: a.activation(Egx[:], f4(px[0:64, 0:256]), AF.Exp), reads=[pkx], writes=["Egx"])
        P.op("act", lambda a: a.activation(Egt[:], pt_[0:64, 0:256].rearrange("p (z f) -> p z f", z=2), AF.Exp, scale=-1.0), reads=[pkt], writes=["Egt"])
        P.op("dve", lambda v, b=b: v.tensor_tensor(At[:], ffm["an"][:, b, :, :], Egx[:], ALU.mult), reads=[("fan", b), "Egx"], writes=["At"])
        P.op("pool", lambda v, b=b: v.tensor_tensor(Rt[:], ffm["r"][:, b, :, :], Eg[:], ALU.mult), reads=[("fr", b), "Eg"], writes=["Rt"])
        P.op("dve", lambda v, b=b: v.tensor_tensor(Bt[:], ffm["bb"][:, b, :, :], Egi[:], ALU.mult), reads=[("fbb", b), "Egi"], writes=["Bt"])
        P.op("pool", lambda v, b=b: v.tensor_tensor(Kt[:], ffm["km"][:, b, :, :], Egi[:], ALU.mult), reads=[("fkm", b), "Egi"], writes=["Kt"])
        P.op("dve", lambda v, b=b: v.tensor_tensor(Btm[:], ftm["bb"][:, b, :, :], Egt[:], ALU.mult), reads=[("tbb", b), "Egt"], writes=["Btm"])
        P.op("pool", lambda v, b=b: v.tensor_tensor(Ktm[:], ftm["km"][:, b, :, :], Egt[:], ALU.mult), reads=[("tkm", b), "Egt"], writes=["Ktm"])
        P.op("dve", lambda v: v.tensor_copy(gC[:, 0:2], Eg[:, 0:2, 63]), reads=["Eg"], writes=["gC"])
        P.op("dve", lambda v: v.tensor_copy(gC[:, 2:4], Eg[:, 2:4, 0]), reads=["Eg"], writes=["gC"])
        grams = [("N", Bt, At, 0, Pm, 0), ("NT", At, Bt, 1, PT, 0), ("Mka", Kt, At, 0, Mka, None), ("Mbr", Bt, Rt, 2, Mbr, None), ("Mkr", Kt, Rt, 2, Mkr, None)]
        for nm, L, Rr, mi, dst, slot in grams:
            ps, pk = newps()
            for q in range(4):
                P.op("pe", lambda t, ps=ps, q=q, L=L, Rr=Rr: t.matmul(ps[0:64, q * 64:(q + 1) * 64], L[:, q, :], Rr[:, q, :], start=True, stop=True), reads=[L.name if hasattr(L, "name") else nm, "At", "Bt", "Kt", "Rt"], writes=[pk])
            d_ap = dst[:, slot, :, :] if slot is not None else dst[:]
            P.op("dve", lambda v, ps=ps, d_ap=d_ap, mi=mi: v.tensor_tensor(d_ap, f4(ps[0:64, 0:256]), Mc[:, mi, :, :], ALU.mult), reads=[pk, "Mc"], writes=[nm + "m"])
        P.op("dve", lambda v: v.tensor_tensor(Tm[:, 0, :, :], Pm[:, 0, :, :], Mc[:, 3, :, :], ALU.add), reads=["Nm", "Mc"], writes=["Tm0"])
        cur = 0
        for lev in range(5):
            nxt = 1 - cur
            pp, pkp = newps(); pq, pkq = newps()
            for q in range(4):
                P.op("pe", lambda t, q=q, cur=cur: t.matmul(pp[0:64, q * 64:(q + 1) * 64], PT[:, cur, q, :], Pm[:, cur, q, :], start=True, stop=True), reads=["Nm", "NTm", "Pl%d" % cur, "PTl%d" % cur], writes=[pkp])
                P.op("pe", lambda t, q=q, cur=cur: t.matmul(pq[0:64, q * 64:(q + 1) * 64], Pm[:, cur, q, :], PT[:, cur, q, :], start=True, stop=True), reads=["Nm", "NTm", "Pl%d" % cur, "PTl%d" % cur], writes=[pkq])
            P.op("act", lambda a, nxt=nxt: a.copy(Pm[:, nxt, :, :], f4(pp[0:64, 0:256])), reads=[pkp], writes=["Pl%d" % nxt])
            P.op("dve", lambda v, nxt=nxt: v.tensor_copy(PT[:, nxt, :, :], f4(pq[0:64, 0:256])), reads=[pkq], writes=["PTl%d" % nxt])
            pt2, pkt2 = newps()
            for q in range(4):
                P.op("pe", lambda t, q=q, cur=cur, nxt=nxt: t.matmul(pt2[0:64, q * 64:(q + 1) * 64], PT[:, nxt, q, :], Tm[:, cur, q, :], start=True, stop=True), reads=["PTl%d" % nxt, "Tm%d" % cur], writes=[pkt2])
            P.op("dve", lambda v, cur=cur, nxt=nxt: v.tensor_tensor(Tm[:, nxt, :, :], f4(pt2[0:64, 0:256]), Tm[:, cur, :, :], ALU.add), reads=[pkt2, "Tm%d" % cur], writes=["Tm%d" % nxt])
            cur = nxt
        Tf = cur
        vsl = lambda q: ftm["v"][:, b, q // 2, (q % 2) * 64:(q % 2 + 1) * 64]
        pw, pkw = newps()
        for q in range(4):
            P.op("pe", lambda t, q=q: t.matmul(pw[0:64, q * 64:(q + 1) * 64], Mka[:, q, :], vsl(q), start=True, stop=False), reads=["Mkam", ("tv", b)], writes=[pkw])
            P.op("pe", lambda t, q=q: t.matmul(pw[0:64, q * 64:(q + 1) * 64], At[:, q, :], ST[:, q, :], start=False, stop=True), reads=["At", "ST"], writes=[pkw])
        P.op("act", lambda a: a.copy(W1[:], f4(pw[0:64, 0:256])), reads=[pkw], writes=["W1"])
        pu, pku = newps()
        for q in range(4):
            P.op("pe", lambda t, q=q, Tf=Tf: t.matmul(pu[0:64, q * 64:(q + 1) * 64], Tm[:, Tf, q, :], W1[:, q, :], start=True, stop=True), reads=["Tm%d" % Tf, "W1"], writes=[pku])
        P.op("act", lambda a: a.copy(UT[:], f4(pu[0:64, 0:256])), reads=[pku], writes=["UT"])
        psn, pksn = newps()
        for q in range(4):
            P.op("pe", lambda t, q=q: t.matmul(psn[0:64, q * 64:(q + 1) * 64], Ktm[:, q // 2, (q % 2) * 64:(q % 2 + 1) * 64], vsl(q), start=True, stop=False), reads=["Ktm", ("tv", b)], writes=[pksn])
            P.op("pe", lambda t, q=q: t.matmul(psn[0:64, q * 64:(q + 1) * 64], Btm[:, q // 2, (q % 2) * 64:(q % 2 + 1) * 64], UT[:, q, :], start=False, stop=True), reads=["Btm", "UT"], writes=[pksn])
        if min(blk) >= 0:
            py_, pky = newps()
            for q in range(4):
                P.op("pe", lambda t, q=q: t.matmul(py_[0:64, q * 64:(q + 1) * 64], Mkr[:, q, :], vsl(q), start=True, stop=False), reads=["Mkrm", ("tv", b)], writes=[pky])
                P.op("pe", lambda t, q=q: t.matmul(py_[0:64, q * 64:(q + 1) * 64], Rt[:, q, :], ST[:, q, :], start=False, stop=False), reads=["Rt", "ST"], writes=[pky])
                P.op("pe", lambda t, q=q: t.matmul(py_[0:64, q * 64:(q + 1) * 64], Mbr[:, q, :], UT[:, q, :], start=False, stop=True), reads=["Mbrm", "UT"], writes=[pky])
            P.op("act", lambda a, b=b: a.copy(ysb[:, b, :, :], f4(py_[0:64, 0:256])), reads=[pky], writes=[("ysb", b)])
            for z in range(2):
                if blk[z] >= NCTX:
                    P.dma("sp", y_d[z, blk[z] * 64:(blk[z] + 1) * 64, :].rearrange("t (c i) -> t c i", c=2), ysb[:, b, 2 * z:2 * z + 2, :], reads=[("ysb", b)])
        P.op("dve", lambda v: v.tensor_tensor(tmp[:], f4(psn[0:64, 0:256]), ST[:], ALU.add), reads=[pksn, "ST"], writes=["tmp"])
        for q in range(4):
            P.op("dve" if q % 2 else "pool", lambda v, q=q: v.tensor_scalar_mul(ST[:, q, :], tmp[:, q, :], gC[:, q:q + 1]), reads=["tmp", "gC"], writes=["ST"])
    return C.end()


def build_rwc2(C, T=8448, NCTX=4, stop_at=99):
    NB = T // 64
    fm = {n: C.din(n, [128, T]) for n in ("an", "r", "bb", "km")}
    tmd = {n: C.din(n + "_tm", [T, 128]) for n in ("bb", "km", "v", "wf", "wb")}
    U_d = C.din("U", [64, 4, 64]); M_d = C.din("M", [64, 4, 4, 64])
    y_d = C.dout("y", [2, T, 128])
    Uc = C.sb("Uc", [64, 4, 64]); Mc = C.sb("Mc", [64, 4, 4, 64])
    ffm = {n: C.sb("f_" + n, [64, 2, 2, 2, 64]) for n in fm}
    ftm = {n: C.sb("t_" + n, [64, 2, 2, 128]) for n in ("bb", "km", "v", "w")}
    lw = C.sb("lw", [64, 2, 128]); Eg = C.sb("Eg", [64, 2, 2, 64]); Egi = C.sb("Egi", [64, 2, 2, 64]); Egx = C.sb("Egx", [64, 2, 2, 64]); Egt = C.sb("Egt", [64, 2, 128])
    At = C.sb("At", [64, 2, 2, 64], BF16); Rt = C.sb("Rt", [64, 2, 2, 64], BF16); Bt = C.sb("Bt", [64, 2, 2, 64], BF16); Kt = C.sb("Kt", [64, 2, 2, 64], BF16)
    Btm = C.sb("Btm", [64, 2, 128], BF16); Ktm = C.sb("Ktm", [64, 2, 128], BF16); vb = C.sb("vb", [64, 2, 128], BF16); gC = C.sb("gC", [64, 2, 2])
    Pm = C.sb("Pm", [64, 2, 2, 2, 64]); PT = C.sb("PT", [64, 2, 2, 2, 64]); Tm = C.sb("Tm", [64, 2, 2, 2, 64])
    Mka = C.sb("Mka", [64, 2, 2, 64], BF16); Mbr = C.sb("Mbr", [64, 2, 2, 64], BF16); Mkr = C.sb("Mkr", [64, 2, 2, 64], BF16)
    ST = C.sb("ST", [64, 2, 2, 64]); STb = C.sb("STb", [64, 2, 2, 64], BF16); W1 = C.sb("W1", [64, 2, 2, 64]); UT = C.sb("UT", [64, 2, 2, 64], BF16)
    tmp = C.sb("tmp", [64, 2, 2, 64]); ysb = C.sb("ysb", [64, 2, 2, 2, 64])
    pss = [C.ps("ps%d" % i, [128, 512]) for i in range(8)]
    P = C.start()
    pctr = [0]

    def newps():
        i = pctr[0] % 8; pctr[0] += 1
        return pss[i], ("ps", i)
    P.dma("sp", Uc[:], U_d[:, :, :], writes=["Uc"]); P.dma("sp", Mc[:], M_d[:, :, :, :], writes=["Mc"])
    P.op("dve", lambda v: v.memset(ST[:], 0.0), writes=[("ST", 0), ("ST", 1)])
    P.op("dve", lambda v: v.memset(STb[:], 0.0), writes=[("STb", 0), ("STb", 1)])
    order = [list(range(NB)), list(range(NCTX - 1, -1, -1)) + list(range(NB - 1, NCTX - 1, -1))]
    c2 = lambda ap: ap.rearrange("p (c t) -> p c t", c=2)

    def make_steps(i, z):
        b = i % 2
        blk = order[z][i]; s0 = blk * 64
        steps = []
        K = lambda n: (n, z)

        def s_load():
            for n in fm:
                P.dma("sp", ffm[n][:, b, z, :, :], fm[n][:, s0:s0 + 64].rearrange("(c j) t -> j c t", c=2), writes=[("f" + n, b, z)])
            for n, src in (("bb", "bb"), ("km", "km"), ("v", "v"), ("w", "wf" if z == 0 else "wb")):
                P.dma("sp", ftm[n][:, b, z, :], tmd[src][s0:s0 + 64, :], writes=[("t" + n, b, z)])
        steps.append(s_load)

        def s_ln():
            P.op("act", lambda a: a.activation(lw[:, z, :], ftm["w"][:, b, z, :], AF.Ln), reads=[("tw", b, z)], writes=[K("lw")])
            P.op("dve", lambda v: v.tensor_copy(vb[:, z, :], ftm["v"][:, b, z, :]), reads=[("tv", b, z)], writes=[K("vb")])
        steps.append(s_ln)
        st = {}

        def s_cum():
            pg, pkg = newps(); px, pkx = newps(); pt_, pkt = newps()
            st.update(pg=pg, pkg=pkg, px=px, pkx=pkx, pt=pt_, pkt=pkt)
            for c in range(2):
                P.op("pe", lambda t, c=c: t.matmul(pg[0:64, c * 64:(c + 1) * 64], lw[:, z, c * 64:(c + 1) * 64], Uc[:, 2 * z, :], start=True, stop=True), reads=[K("lw"), "Uc"], writes=[pkg])
                P.op("pe", lambda t, c=c: t.matmul(px[0:64, c * 64:(c + 1) * 64], lw[:, z, c * 64:(c + 1) * 64], Uc[:, 2 * z + 1, :], start=True, stop=True), reads=[K("lw"), "Uc"], writes=[pkx])
            P.op("pe", lambda t: t.matmul(pt_[0:64, 0:128], Uc[:, 2 * z, :], lw[:, z, :], start=True, stop=True), reads=[K("lw"), "Uc"], writes=[pkt])
        steps.append(s_cum)

        def s_exp():
            pg, pkg, px, pkx, pt_, pkt = st["pg"], st["pkg"], st["px"], st["pkx"], st["pt"], st["pkt"]
            P.op("act", lambda a: a.activation(Eg[:, z, :, :], c2(pg[0:64, 0:128]), AF.Exp), reads=[pkg], writes=[K("Eg")])
            P.op("act", lambda a: a.activation(Egi[:, z, :, :], c2(pg[0:64, 0:128]), AF.Exp, scale=-1.0), reads=[pkg], writes=[K("Egi")])
            P.op("act", lambda a: a.activation(Egx[:, z, :, :], c2(px[0:64, 0:128]), AF.Exp), reads=[pkx], writes=[K("Egx")])
            P.op("act", lambda a: a.activation(Egt[:, z, :], pt_[0:64, 0:128], AF.Exp, scale=-1.0), reads=[pkt], writes=[K("Egt")])
        steps.append(s_exp)

        def s_prod():
            P.op("dve", lambda v: v.tensor_tensor(At[:, z, :, :], ffm["an"][:, b, z, :, :], Egx[:, z, :, :], ALU.mult), reads=[("fan", b, z), K("Egx")], writes=[K("At")])
            P.op("dve", lambda v: v.tensor_tensor(Rt[:, z, :, :], ffm["r"][:, b, z, :, :], Eg[:, z, :, :], ALU.mult), reads=[("fr", b, z), K("Eg")], writes=[K("Rt")])
            P.op("dve", lambda v: v.tensor_tensor(Bt[:, z, :, :], ffm["bb"][:, b, z, :, :], Egi[:, z, :, :], ALU.mult), reads=[("fbb", b, z), K("Egi")], writes=[K("Bt")])
            P.op("dve", lambda v: v.tensor_tensor(Kt[:, z, :, :], ffm["km"][:, b, z, :, :], Egi[:, z, :, :], ALU.mult), reads=[("fkm", b, z), K("Egi")], writes=[K("Kt")])
            P.op("dve", lambda v: v.tensor_tensor(Btm[:, z, :], ftm["bb"][:, b, z, :], Egt[:, z, :], ALU.mult), reads=[("tbb", b, z), K("Egt")], writes=[K("Btm")])
            P.op("dve", lambda v: v.tensor_tensor(Ktm[:, z, :], ftm["km"][:, b, z, :], Egt[:, z, :], ALU.mult), reads=[("tkm", b, z), K("Egt")], writes=[K("Ktm")])
            col = 63 if z == 0 else 0
            P.op("dve", lambda v: v.tensor_copy(gC[:, z, :], Eg[:, z, :, col]), reads=[K("Eg")], writes=[K("gC")])
        steps.append(s_prod)

        def s_gram():
            pa, pka = newps(); pb, pkb = newps()
            for gi, (L, Rr) in enumerate(((Bt, At), (At, Bt), (Kt, At))):
                for c in range(2):
                    P.op("pe", lambda t, gi=gi, c=c, L=L, Rr=Rr: t.matmul(pa[0:64, gi * 128 + c * 64:gi * 128 + (c + 1) * 64], L[:, z, c, :], Rr[:, z, c, :], start=True, stop=True), reads=[K("At"), K("Bt"), K("Kt")], writes=[pka])
            for gi, (L, Rr) in enumerate(((Bt, Rt), (Kt, Rt))):
                for c in range(2):
                    P.op("pe", lambda t, gi=gi, c=c, L=L, Rr=Rr: t.matmul(pb[0:64, gi * 128 + c * 64:gi * 128 + (c + 1) * 64], L[:, z, c, :], Rr[:, z, c, :], start=True, stop=True), reads=[K("Rt"), K("Bt"), K("Kt")], writes=[pkb])
            mq = slice(2 * z, 2 * z + 2)
            P.op("dve", lambda v: v.tensor_tensor(Pm[:, 0, z, :, :], c2(pa[0:64, 0:128]), Mc[:, 0, mq, :], ALU.mult), reads=[pka, "Mc"], writes=[K("P0")])
            P.op("dve", lambda v: v.tensor_tensor(PT[:, 0, z, :, :], c2(pa[0:64, 128:256]), Mc[:, 1, mq, :], ALU.mult), reads=[pka, "Mc"], writes=[K("PT0")])
            P.op("dve", lambda v: v.tensor_tensor(Mka[:, z, :, :], c2(pa[0:64, 256:384]), Mc[:, 0, mq, :], ALU.mult), reads=[pka, "Mc"], writes=[K("Mka")])
            P.op("dve", lambda v: v.tensor_tensor(Mbr[:, z, :, :], c2(pb[0:64, 0:128]), Mc[:, 2, mq, :], ALU.mult), reads=[pkb, "Mc"], writes=[K("Mbr")])
            P.op("dve", lambda v: v.tensor_tensor(Mkr[:, z, :, :], c2(pb[0:64, 128:256]), Mc[:, 2, mq, :], ALU.mult), reads=[pkb, "Mc"], writes=[K("Mkr")])
            P.op("dve", lambda v: v.tensor_tensor(Tm[:, 0, z, :, :], Pm[:, 0, z, :, :], Mc[:, 3, mq, :], ALU.add), reads=[K("P0"), "Mc"], writes=[K("T0")])
        steps.append(s_gram)
        for lev in range(5):
            cur, nxt = lev % 2, 1 - lev % 2

            def s_sq(cur=cur, nxt=nxt):
                pp, pkp = newps(); pq, pkq = newps()
                for c in range(2):
                    P.op("pe", lambda t, c=c: t.matmul(pp[0:64, c * 64:(c + 1) * 64], PT[:, cur, z, c, :], Pm[:, cur, z, c, :], start=True, stop=True), reads=[K("P%d" % cur), K("PT%d" % cur)], writes=[pkp])
                    P.op("pe", lambda t, c=c: t.matmul(pq[0:64, c * 64:(c + 1) * 64], Pm[:, cur, z, c, :], PT[:, cur, z, c, :], start=True, stop=True), reads=[K("P%d" % cur), K("PT%d" % cur)], writes=[pkq])
                P.op("act", lambda a: a.copy(Pm[:, nxt, z, :, :], c2(pp[0:64, 0:128])), reads=[pkp], writes=[K("P%d" % nxt)])
                P.op("dve", lambda v: v.tensor_copy(PT[:, nxt, z, :, :], c2(pq[0:64, 0:128])), reads=[pkq], writes=[K("PT%d" % nxt)])
            steps.append(s_sq)

            def s_t(cur=cur, nxt=nxt):
                pt2, pkt2 = newps()
                for c in range(2):
                    P.op("pe", lambda t, c=c: t.matmul(pt2[0:64, c * 64:(c + 1) * 64], PT[:, nxt, z, c, :], Tm[:, cur, z, c, :], start=True, stop=True), reads=[K("PT%d" % nxt), K("T%d" % cur)], writes=[pkt2])
                P.op("dve", lambda v: v.tensor_tensor(Tm[:, nxt, z, :, :], c2(pt2[0:64, 0:128]), Tm[:, cur, z, :, :], ALU.add), reads=[pkt2, K("T%d" % cur)], writes=[K("T%d" % nxt)])
            steps.append(s_t)
        Tf = 1

        def s_w1():
            pw, pkw = newps(); st.update(pw=pw, pkw=pkw)
            for c in range(2):
                P.op("pe", lambda t, c=c: t.matmul(pw[0:64, c * 64:(c + 1) * 64], Mka[:, z, c, :], vb[:, z, c * 64:(c + 1) * 64], start=True, stop=False), reads=[K("Mka"), K("vb")], writes=[pkw])
                P.op("pe", lambda t, c=c: t.matmul(pw[0:64, c * 64:(c + 1) * 64], At[:, z, c, :], STb[:, z, c, :], start=False, stop=True), reads=[K("At"), K("STb")], writes=[pkw])
            P.op("act", lambda a: a.copy(W1[:, z, :, :], c2(pw[0:64, 0:128])), reads=[pkw], writes=[K("W1")])
        steps.append(s_w1)

        def s_ut():
            pu, pku = newps()
            for c in range(2):
                P.op("pe", lambda t, c=c: t.matmul(pu[0:64, c * 64:(c + 1) * 64], Tm[:, Tf, z, c, :], W1[:, z, c, :], start=True, stop=True), reads=[K("T%d" % Tf), K("W1")], writes=[pku])
            P.op("act", lambda a: a.copy(UT[:, z, :, :], c2(pu[0:64, 0:128])), reads=[pku], writes=[K("UT")])
        steps.append(s_ut)

        def s_fin():
            psn, pksn = newps()
            for c in range(2):
                P.op("pe", lambda t, c=c: t.matmul(psn[0:64, c * 64:(c + 1) * 64], Ktm[:, z, c * 64:(c + 1) * 64], vb[:, z, c * 64:(c + 1) * 64], start=True, stop=False), reads=[K("Ktm"), K("vb")], writes=[pksn])
                P.op("pe", lambda t, c=c: t.matmul(psn[0:64, c * 64:(c + 1) * 64], Btm[:, z, c * 64:(c + 1) * 64], UT[:, z, c, :], start=False, stop=True), reads=[K("Btm"), K("UT")], writes=[pksn])
            if blk >= NCTX:
                py_, pky = newps()
                for c in range(2):
                    P.op("pe", lambda t, c=c: t.matmul(py_[0:64, c * 64:(c + 1) * 64], Mkr[:, z, c, :], vb[:, z, c * 64:(c + 1) * 64], start=True, stop=False), reads=[K("Mkr"), K("vb")], writes=[pky])
                    P.op("pe", lambda t, c=c: t.matmul(py_[0:64, c * 64:(c + 1) * 64], Rt[:, z, c, :], STb[:, z, c, :], start=False, stop=False), reads=[K("Rt"), K("STb")], writes=[pky])
                    P.op("pe", lambda t, c=c: t.matmul(py_[0:64, c * 64:(c + 1) * 64], Mbr[:, z, c, :], UT[:, z, c, :], start=False, stop=True), reads=[K("Mbr"), K("UT")], writes=[pky])
                P.op("act", lambda a: a.copy(ysb[:, b, z, :, :], c2(py_[0:64, 0:128])), reads=[pky], writes=[("ysb", b, z)])
                P.dma("sp", y_d[z, s0:s0 + 64, :].rearrange("t (c i) -> t c i", c=2), ysb[:, b, z, :, :], reads=[("ysb", b, z)])
            P.op("dve", lambda v: v.tensor_tensor(tmp[:, z, :, :], c2(psn[0:64, 0:128]), ST[:, z, :, :], ALU.add), reads=[pksn, K("ST")], writes=[K("tmp")])
            for c in range(2):
                P.op("dve", lambda v, c=c: v.tensor_scalar_mul(ST[:, z, c, :], tmp[:, z, c, :], gC[:, z, c:c + 1]), reads=[K("tmp"), K("gC")], writes=[K("ST")])
            P.op("act", lambda a: a.copy(STb[:, z, :, :], ST[:, z, :, :]), reads=[K("ST")], writes=[K("STb")])
        steps.append(s_fin)
        return steps

    for i in range(NB):
        sa, sb_ = make_steps(i, 0), make_steps(i, 1)
        for k in range(min(len(sa), stop_at)):
            sa[k](); sb_[k]()
    return C.end()


def build_tr_fm(C, names, T):
    srcs = {n: C.din("src_" + n, [128, T]) for n in names}
    dsts = {n: C.dout("dst_" + n, [T, 128]) for n in names}
    idn = C.din("idn", [128, 128])
    st = C.sb("st", [128, 3, 512]); ot = C.sb("ot", [128, 3, 4, 128]); idt = C.sb("idt", [128, 128])
    pss = [C.ps("ps%d" % i, [128, 512]) for i in range(4)]
    P = C.start()
    P.dma("sp", idt[:], idn[:, :], writes=["idt"])
    it = 0
    for n in names:
        for c0 in range(0, T, 512):
            nr = min(4, (T - c0) // 128)
            b = it % 3; pi = it % 4; it += 1
            P.dma("sp", st[:, b, 0:nr * 128], srcs[n][:, c0:c0 + nr * 128], writes=[("st", b)])
            for k in range(nr):
                P.op("pe", lambda t, b=b, k=k, pi=pi: t.matmul(pss[pi][:, k * 128:(k + 1) * 128], st[:, b, k * 128:(k + 1) * 128], idt[:, :], start=True, stop=True), reads=[("st", b), "idt"], writes=[("ps", pi)])
            P.op("act" if it % 2 else "dve", (lambda a, b=b, pi=pi, nr=nr: a.copy(ot[:, b, 0:nr, :], pss[pi][:, 0:nr * 128].rearrange("p (n c) -> p n c", c=128))) if it % 2 else
                 (lambda v, b=b, pi=pi, nr=nr: v.tensor_copy(ot[:, b, 0:nr, :], pss[pi][:, 0:nr * 128].rearrange("p (n c) -> p n c", c=128))), reads=[("ps", pi)], writes=[("ot", b)])
            P.dma("sp", dsts[n][c0:c0 + nr * 128, :].rearrange("(n p) c -> p n c", p=128), ot[:, b, 0:nr, :], reads=[("ot", b)])
    return C.end()
```
